# Optimizing a Trainium2 kernel written in Bass

```python
import math
import jax, jax.numpy as jnp
from jax import lax
import numpy as np

D_MODEL = 1024
BATCH = 8
SEQ = 2048
DEPTH = 4

MEM_LEN = 256
EPS = 1e-6
SSM_HEADS = 16
SSM_HEAD_DIM = 64
D_SSM = SSM_HEADS * SSM_HEAD_DIM
SSM_GROUPS = 4
SSM_STATE = 128
SSM_CONV = 4
SSM_CHUNK = 128
CONV_CH = D_SSM + 2 * SSM_GROUPS * SSM_STATE
MLA_HEADS = 16
QK_NOPE = 64
QK_ROPE = 32
V_DIM = 64
Q_LORA = 384
KV_LORA = 256
D_ATTN = MLA_HEADS * V_DIM
ROPE_THETA = 10000.0
Q_BLOCK = 128
D_MIX = D_SSM + D_ATTN
_O1 = D_SSM
_O2 = _O1 + CONV_CH
_O3 = _O2 + SSM_HEADS
_O4 = _O3 + Q_LORA
_O5 = _O4 + KV_LORA
D_IN = _O5 + QK_ROPE
IN_SPLITS = (_O1, _O2, _O3, _O4, _O5)
MEM_HEADS = 4
MEM_HEAD_DIM = D_MODEL // MEM_HEADS
D_FF = 2816
FFN_CONV = 3

kernel_name = "hymba_ssd_mla_memxattn_convffn"


def rmsnorm(x, g):
    xf = x.astype(jnp.float32)
    var = jnp.mean(xf * xf, axis=-1, keepdims=True)
    return (xf * lax.rsqrt(var + EPS) * g.astype(jnp.float32)).astype(x.dtype)


def causal_dwconv(x, w, b):
    k = w.shape[0]
    s = x.shape[1]
    xp = jnp.pad(x, ((0, 0), (k - 1, 0), (0, 0)))
    y = xp[:, 0:s] * w[0]
    for j in range(1, k):
        y = y + xp[:, j:j + s] * w[j]
    return y + b


def rope_tables(positions):
    inv_freq = 1.0 / (ROPE_THETA ** (jnp.arange(0, QK_ROPE, 2, dtype=jnp.float32) / QK_ROPE))
    ang = positions.astype(jnp.float32)[..., None] * inv_freq
    return jnp.cos(ang), jnp.sin(ang)


def apply_rope(t, cos, sin):
    half = t.shape[-1] // 2
    t1, t2 = t[..., :half], t[..., half:]
    out = jnp.concatenate([t1 * cos - t2 * sin, t2 * cos + t1 * sin], axis=-1)
    return out.astype(t.dtype)


def segsum_exp(a):
    t = a.shape[-1]
    cs = jnp.cumsum(a, axis=-1)
    diff = cs[..., :, None] - cs[..., None, :]
    mask = jnp.tril(jnp.ones((t, t), dtype=bool))
    return jnp.exp(jnp.where(mask, diff, -jnp.inf))


def ssd_scan(x, dt, a, bm, cm):
    out_dtype = x.dtype
    x = x.astype(jnp.float32)
    bm = bm.astype(jnp.float32)
    cm = cm.astype(jnp.float32)
    b, l, h, p = x.shape
    g, n = bm.shape[-2:]
    r = h // g
    c = l // SSM_CHUNK
    xd = (x * dt[..., None]).reshape(b, c, SSM_CHUNK, g, r, p)
    ad = jnp.moveaxis((dt * a).reshape(b, c, SSM_CHUNK, g, r), 2, -1)
    a_cs = jnp.cumsum(ad, axis=-1)
    bc = bm.reshape(b, c, SSM_CHUNK, g, n)
    cc = cm.reshape(b, c, SSM_CHUNK, g, n)
    lmat = segsum_exp(ad)
    cb = jnp.einsum('bclgn,bcsgn->bcgls', cc, bc)
    y_diag = jnp.einsum('bcgls,bcgrls,bcsgrp->bclgrp', cb, lmat, xd)
    decay_states = jnp.exp(a_cs[..., -1:] - a_cs)
    states = jnp.einsum('bclgn,bcgrl,bclgrp->bcgrpn', bc, decay_states, xd)
    chunk_decay = jnp.exp(a_cs[..., -1])

    def step(prev, inp):
        st, dec = inp
        return prev * dec[..., None, None] + st, prev

    init = jnp.zeros((b, g, r, p, n), jnp.float32)
    _, prev_states = lax.scan(step, init, (jnp.moveaxis(states, 1, 0), jnp.moveaxis(chunk_decay, 1, 0)))
    prev_states = jnp.moveaxis(prev_states, 0, 1)
    y_off = jnp.einsum('bclgn,bcgrpn,bcgrl->bclgrp', cc, prev_states, jnp.exp(a_cs))
    return (y_diag + y_off).reshape(b, l, h, p).astype(out_dtype)


def mla_attention(c_q, c_kv, k_rope, q_norm, w_uq, kv_norm, w_ukv, cos, sin):
    b, s, _ = c_q.shape
    q = (rmsnorm(c_q, q_norm) @ w_uq).reshape(b, s, MLA_HEADS, QK_NOPE + QK_ROPE)
    q_nope = q[..., :QK_NOPE]
    q_pe = apply_rope(q[..., QK_NOPE:], cos[:, :, None, :], sin[:, :, None, :])
    kv = (rmsnorm(c_kv, kv_norm) @ w_ukv).reshape(b, s, MLA_HEADS, QK_NOPE + V_DIM)
    k_nope, v = kv[..., :QK_NOPE], kv[..., QK_NOPE:]
    k_pe = apply_rope(k_rope, cos, sin)
    scale = (QK_NOPE + QK_ROPE) ** -0.5
    outs = []
    for i in range(s // Q_BLOCK):
        q0 = i * Q_BLOCK
        kend = q0 + Q_BLOCK
        sc = (jnp.einsum('bqhd,bkhd->bhqk', q_nope[:, q0:kend], k_nope[:, :kend])
              + jnp.einsum('bqhr,bkr->bhqk', q_pe[:, q0:kend], k_pe[:, :kend]))
        sc = sc.astype(jnp.float32) * scale
        mask = (q0 + jnp.arange(Q_BLOCK))[:, None] >= jnp.arange(kend)[None, :]
        pr = jax.nn.softmax(jnp.where(mask, sc, -jnp.inf), axis=-1).astype(v.dtype)
        outs.append(jnp.einsum('bhqk,bkhd->bqhd', pr, v[:, :kend]))
    return jnp.concatenate(outs, axis=1).reshape(b, s, D_ATTN)


def memory_attention(h, m, w_q, w_k, w_v, w_o):
    b, s, _ = h.shape
    ml = m.shape[1]
    q = (h @ w_q).reshape(b, s, MEM_HEADS, MEM_HEAD_DIM)
    k = (m @ w_k).reshape(b, ml, MEM_HEADS, MEM_HEAD_DIM)
    v = (m @ w_v).reshape(b, ml, MEM_HEADS, MEM_HEAD_DIM)
    sc = jnp.einsum('bqhd,bkhd->bhqk', q, k).astype(jnp.float32) * (MEM_HEAD_DIM ** -0.5)
    pr = jax.nn.softmax(sc, axis=-1).astype(v.dtype)
    o = jnp.einsum('bhqk,bkhd->bqhd', pr, v).reshape(b, s, D_MODEL)
    return o @ w_o


def conv_glu_ffn(h, w_up, conv_w, conv_b, w_down):
    u = causal_dwconv(h @ w_up, conv_w, conv_b)
    gate, val = u[..., :D_FF], u[..., D_FF:]
    return (jax.nn.silu(gate) * val) @ w_down


def setup_inputs(seed: int = 0) -> dict:
    key = jax.random.key(seed)
    ks = iter(jax.random.split(key, 64))

    def nrm(shape, scale):
        return jax.random.normal(next(ks), shape, jnp.float32) * scale

    def gain(shape):
        return 1.0 + nrm(shape, 0.02)

    L = DEPTH
    dt0 = jnp.exp(jax.random.uniform(next(ks), (L, SSM_HEADS), jnp.float32,
                                     math.log(1e-3), math.log(1e-1)))
    dt_bias = dt0 + jnp.log(-jnp.expm1(-dt0))
    a_log = jnp.log(jax.random.uniform(next(ks), (L, SSM_HEADS), jnp.float32, 1.0, 16.0))
    offsets = jax.random.randint(next(ks), (BATCH, 1), 0, 1024, dtype=jnp.int32)
    positions = offsets + jnp.arange(SEQ, dtype=jnp.int32)[None, :]
    return {
        "x": nrm((BATCH, SEQ, D_MODEL), 1.0),
        "mem": nrm((BATCH, MEM_LEN, D_MODEL), 1.0),
        "positions": positions,
        "norm_mix": gain((L, D_MODEL)),
        "w_in": nrm((L, D_MODEL, D_IN), D_MODEL ** -0.5),
        "ssm_conv_w": nrm((L, SSM_CONV, CONV_CH), SSM_CONV ** -0.5),
        "ssm_conv_b": nrm((L, CONV_CH), 0.02),
        "dt_bias": dt_bias,
        "a_log": a_log,
        "d_skip": 1.0 + nrm((L, SSM_HEADS), 0.1),
        "ssm_norm": gain((L, D_SSM)),
        "q_norm": gain((L, Q_LORA)),
        "w_uq": nrm((L, Q_LORA, MLA_HEADS * (QK_NOPE + QK_ROPE)), Q_LORA ** -0.5),
        "kv_norm": gain((L, KV_LORA)),
        "w_ukv": nrm((L, KV_LORA, MLA_HEADS * (QK_NOPE + V_DIM)), KV_LORA ** -0.5),
        "attn_out_norm": gain((L, D_ATTN)),
        "w_out": nrm((L, D_MIX, D_MODEL), D_MIX ** -0.5),
        "norm_mem_q": gain((L, D_MODEL)),
        "norm_mem_kv": gain((L, D_MODEL)),
        "w_mq": nrm((L, D_MODEL, D_MODEL), D_MODEL ** -0.5),
        "w_mk": nrm((L, D_MODEL, D_MODEL), D_MODEL ** -0.5),
        "w_mv": nrm((L, D_MODEL, D_MODEL), D_MODEL ** -0.5),
        "w_mo": nrm((L, D_MODEL, D_MODEL), D_MODEL ** -0.5),
        "norm_ffn": gain((L, D_MODEL)),
        "w_up": nrm((L, D_MODEL, 2 * D_FF), D_MODEL ** -0.5),
        "ffn_conv_w": nrm((L, FFN_CONV, 2 * D_FF), FFN_CONV ** -0.5),
        "ffn_conv_b": nrm((L, 2 * D_FF), 0.02),
        "w_down": nrm((L, D_FF, D_MODEL), D_FF ** -0.5),
        "final_norm": gain((D_MODEL,)),
    }


def reference(x, mem, positions, norm_mix, w_in, ssm_conv_w, ssm_conv_b, dt_bias, a_log,
              d_skip, ssm_norm, q_norm, w_uq, kv_norm, w_ukv, attn_out_norm, w_out,
              norm_mem_q, norm_mem_kv, w_mq, w_mk, w_mv, w_mo, norm_ffn, w_up,
              ffn_conv_w, ffn_conv_b, w_down, final_norm):
    b, s, _ = x.shape
    cos, sin = rope_tables(positions)
    for i in range(DEPTH):
        h = rmsnorm(x, norm_mix[i])
        proj = h @ w_in[i]
        z, xbc, dt_raw, c_q, c_kv, k_rope = jnp.split(proj, IN_SPLITS, axis=-1)
        xbc = jax.nn.silu(causal_dwconv(xbc, ssm_conv_w[i], ssm_conv_b[i]))
        xs = xbc[..., :D_SSM].reshape(b, s, SSM_HEADS, SSM_HEAD_DIM)
        bm = xbc[..., D_SSM:D_SSM + SSM_GROUPS * SSM_STATE].reshape(b, s, SSM_GROUPS, SSM_STATE)
        cm = xbc[..., D_SSM + SSM_GROUPS * SSM_STATE:].reshape(b, s, SSM_GROUPS, SSM_STATE)
        dt = jax.nn.softplus(dt_raw.astype(jnp.float32) + dt_bias[i].astype(jnp.float32))
        a = -jnp.exp(a_log[i].astype(jnp.float32))
        y = ssd_scan(xs, dt, a, bm, cm) + xs * d_skip[i][:, None]
        y_ssm = rmsnorm(y.reshape(b, s, D_SSM) * jax.nn.silu(z), ssm_norm[i])
        y_att = mla_attention(c_q, c_kv, k_rope, q_norm[i], w_uq[i], kv_norm[i], w_ukv[i], cos, sin)
        y_att = rmsnorm(y_att, attn_out_norm[i])
        x = x + jnp.concatenate([y_ssm, y_att], axis=-1) @ w_out[i]
        x = x + memory_attention(rmsnorm(x, norm_mem_q[i]), rmsnorm(mem, norm_mem_kv[i]),
                                 w_mq[i], w_mk[i], w_mv[i], w_mo[i])
        x = x + conv_glu_ffn(rmsnorm(x, norm_ffn[i]), w_up[i], ffn_conv_w[i], ffn_conv_b[i], w_down[i])
    return rmsnorm(x, final_norm)
```

```python
import contextlib
import math
import numpy as np

import concourse.bass as bass
import concourse.mybir as mybir
from concourse.alu_op_type import AluOpType as ALU
from concourse.bass_utils import run_bass_kernel_spmd

DT = mybir.dt
AF = mybir.ActivationFunctionType
F32, BF16, I32 = DT.float32, DT.bfloat16, DT.int32

DEPTH = 4
S = 2048
D = 1024
NT = S // 128
KC = D // 128
MEM = 256
EPS = 1e-6
NH = 16
HD = 64
NG = 4
NST = 128
Q_LORA, KV_LORA, ROPE = 384, 256, 32
D_IN = 3760
O_Z, O_XBC, O_DT, O_CQ, O_CKV, O_KR = 0, 1024, 3072, 3088, 3472, 3728
D_FF = 2816
NFF = D_FF // 128
N_CORES = 8

ENGS = ("sp", "pe", "dve", "act", "pool")


class Op:
    __slots__ = ("eng", "fn", "deps", "signal", "sigval", "dma_sem", "dma_val", "id")


class Buf:
    def __init__(self, name, t):
        self.name = name
        self.t = t
        self.writers = []
        self.readers = []
        self.overlaps = []
        self.excl = False

    def __getitem__(self, idx):
        return self.t[idx]


def _conf(a, b):
    if a is None or b is None:
        return True
    for x, y in zip(a, b):
        if x is not None and y is not None and x != y:
            return False
    return True


def _covers(a, b):
    if a is None:
        return True
    if b is None:
        return False
    for x, y in zip(a, b):
        if x is not None and x != y:
            return False
    return True


class MK:
    def __init__(self, nc):
        self.nc = nc
        self.streams = {e: [] for e in ENGS}
        self.eng_sem = {}
        self.dma_sems = {}
        self.nops = 0

    def sbuf_at(self, name, shape, dtype, offset):
        return Buf(name, self.nc.alloc_sbuf_tensor_at(name, list(shape), dtype, offset=offset))

    def psum(self, name, shape, dtype=F32):
        b = Buf(name, self.nc.alloc_psum_tensor(name, list(shape), dtype))
        b.excl = True
        return b

    def dram(self, name, shape, dtype, kind="Internal"):
        return Buf(name, self.nc.dram_tensor(name, list(shape), dtype, kind=kind))

    def overlap(self, a, b):
        a.overlaps.append(b)
        b.overlaps.append(a)

    @staticmethod
    def _norm(lst):
        out = []
        for x in lst:
            if isinstance(x, Buf):
                out.append((x, None))
            else:
                out.append((x[0], tuple(x[1]) if x[1] is not None else None))
        return out

    @staticmethod
    def _absorb(b):
        for y in b.overlaps:
            if y.writers or y.readers:
                ops = [p for _, p in y.writers] + [p for _, p in y.readers]
                y.writers = []
                y.readers = []
                for p in ops:
                    b.writers.append((None, p))
                latest = {}
                keep = []
                for k2, p in b.writers:
                    if k2 is None and p.dma_sem is None:
                        if p.eng not in latest or latest[p.eng].id < p.id:
                            latest[p.eng] = p
                    else:
                        keep.append((k2, p))
                b.writers = keep + [(None, p) for p in latest.values()]

    def op(self, eng, fn, reads=(), writes=(), is_dma=False):
        o = Op()
        o.eng, o.fn, o.deps = eng, fn, []
        o.signal, o.sigval, o.dma_sem, o.dma_val = False, None, None, None
        o.id = self.nops
        self.nops += 1
        reads = self._norm(reads)
        writes = self._norm(writes)
        for b, _ in reads + writes:
            if b.overlaps:
                self._absorb(b)
        seen = set()

        def add(p, raw):
            if p is o or (p.id, raw) in seen:
                return
            seen.add((p.id, raw))
            o.deps.append((p, raw))

        for b, key in reads:
            for k2, p in b.writers:
                if _conf(key, k2):
                    add(p, True)
            if b.excl:
                for k2, p in b.readers:
                    if p.eng != eng:
                        add(p, False)
        for b, key in writes:
            for k2, p in b.writers:
                if _conf(key, k2):
                    add(p, False)
            for k2, p in b.readers:
                if _conf(key, k2):
                    add(p, False)
        for b, key in reads:
            if not is_dma:
                b.readers = [(k2, p) for (k2, p) in b.readers
                             if not (k2 == key and p.eng == eng and p.dma_sem is None)]
            b.readers.append((key, o))
        for b, key in writes:
            b.writers = [(k2, p) for (k2, p) in b.writers if not _covers(key, k2)]
            b.readers = [(k2, p) for (k2, p) in b.readers if (not _covers(key, k2)) or p is o]
            b.writers.append((key, o))
        self.streams[eng].append(o)
        return o

    def I(self, eng, name, reads, writes, *args, **kw):
        def fn(e):
            return getattr(e, name)(*args, **kw)
        return self.op(eng, fn, reads, writes)

    def dma(self, queue, out_ap, in_ap, reads, writes, sem_key, **kw):
        ent = self.dma_sems.get(sem_key)
        if ent is None:
            ent = [None, 0, None]
            self.dma_sems[sem_key] = ent
        prev = ent[2]

        def fn(e):
            return e.dma_start(out=out_ap, in_=in_ap, **kw)

        o = self.op(queue, fn, reads, writes, is_dma=True)
        ent[1] += 16
        o.dma_sem = sem_key
        o.dma_val = ent[1]
        if prev is not None:
            o.deps.append((prev, True))
        ent[2] = o
        return o

    def emit(self):
        nc = self.nc
        for e, ops in self.streams.items():
            for o in ops:
                for p, raw in o.deps:
                    if p.dma_sem is not None:
                        continue
                    if p.eng == o.eng:
                        if o.eng != "pe":
                            p.signal = True
                    else:
                        p.signal = True
        for e, ops in self.streams.items():
            c = 0
            for o in ops:
                if o.dma_sem is None and o.signal:
                    c += 1
                    o.sigval = c
        with contextlib.ExitStack() as st:
            for e in ENGS:
                self.eng_sem[e] = st.enter_context(nc.semaphore("sem_" + e))
            for i, (k, ent) in enumerate(self.dma_sems.items()):
                ent[0] = st.enter_context(nc.semaphore("dsem_%d" % i))
            block = st.enter_context(nc.Block())
            mk = self

            def make(ename):
                def body(eng):
                    waited = {}
                    for o in mk.streams[ename]:
                        for p, raw in o.deps:
                            if p.dma_sem is not None:
                                sem, val, key = mk.dma_sems[p.dma_sem][0], p.dma_val, ("d", p.dma_sem)
                            else:
                                if p.eng == ename and ename == "pe":
                                    continue
                                sem, val, key = mk.eng_sem[p.eng], p.sigval, ("e", p.eng)
                            if waited.get(key, 0) >= val:
                                continue
                            waited[key] = val
                            eng.wait_ge(sem, val)
                        inst = o.fn(eng)
                        if o.dma_sem is not None:
                            inst.then_inc(mk.dma_sems[o.dma_sem][0], 16)
                        elif o.signal:
                            inst.then_inc(mk.eng_sem[ename], 1)
                    for k, ent in mk.dma_sems.items():
                        if ent[2] is not None and ent[2].eng == ename and waited.get(("d", k), 0) < ent[1]:
                            eng.wait_ge(ent[0], ent[1])
                return body

            block.sync(make("sp"))
            block.tensor(make("pe"))
            block.vector(make("dve"))
            block.scalar(make("act"))
            block.gpsimd(make("pool"))


def make_params(DEPTH):
  return [
    ("norm_mix", (DEPTH, D)), ("w_in", (DEPTH, D, D_IN)), ("ssm_conv_w", (DEPTH, 4, 2048)),
    ("ssm_conv_b", (DEPTH, 2048)), ("dt_bias", (DEPTH, NH)), ("a_log", (DEPTH, NH)),
    ("d_skip", (DEPTH, NH)), ("ssm_norm", (DEPTH, D)), ("q_norm", (DEPTH, Q_LORA)),
    ("w_uq", (DEPTH, Q_LORA, NH * 96)), ("kv_norm", (DEPTH, KV_LORA)),
    ("w_ukv", (DEPTH, KV_LORA, NH * 128)), ("attn_out_norm", (DEPTH, D)),
    ("w_out", (DEPTH, 2 * D, D)), ("norm_mem_q", (DEPTH, D)), ("norm_mem_kv", (DEPTH, D)),
    ("w_mq", (DEPTH, D, D)), ("w_mk", (DEPTH, D, D)), ("w_mv", (DEPTH, D, D)), ("w_mo", (DEPTH, D, D)),
    ("norm_ffn", (DEPTH, D)), ("w_up", (DEPTH, D, 2 * D_FF)), ("ffn_conv_w", (DEPTH, 3, 2 * D_FF)),
    ("ffn_conv_b", (DEPTH, 2 * D_FF)), ("w_down", (DEPTH, D_FF, D)), ("final_norm", (D,)),
  ]


PARAMS = make_params(DEPTH)
VEC_ROWS = [("norm_mix", 8), ("ssm_conv_w", 64), ("ssm_conv_b", 16), ("ssm_norm", 8), ("q_norm", 3),
            ("kv_norm", 2), ("attn_out_norm", 8), ("norm_mem_q", 8), ("norm_mem_kv", 8),
            ("norm_ffn", 8), ("ffn_conv_w", 132), ("ffn_conv_b", 44)]


def build_program(depth=DEPTH, debug=None, stop_after=None):
    DEPTH = max(depth, 1)
    PARAMS = make_params(DEPTH)
    nc = bass.Bass("TRN2", target_bir_lowering=False)
    mk = MK(nc)
    I = mk.I

    x_in = mk.dram("x", [S, D], F32, kind="ExternalInput")
    mem_in = mk.dram("mem", [MEM, D], F32, kind="ExternalInput")
    pos_in = mk.dram("positions", [1, S], I32, kind="ExternalInput")
    P = {}
    for name, shp in PARAMS:
        P[name] = mk.dram(name, list(shp), F32, kind="ExternalInput")
    c_ident = mk.dram("c_ident", [128, 128], F32, kind="ExternalInput")
    c_U = mk.dram("c_U", [128, 128], F32, kind="ExternalInput")
    c_SL = mk.dram("c_SL", [128, 128], F32, kind="ExternalInput")
    c_rope = mk.dram("c_rope", [128, 2], F32, kind="ExternalInput")
    out_d = mk.dram("out", [S, D], F32, kind="ExternalOutput")
    xres = mk.dram("xres", [S, D], F32)
    sz_d = mk.dram("sz_d", [S, D], BF16)
    xs_d = mk.dram("xs_d", [S, D], BF16)
    bt_d = mk.dram("bt_d", [S, 512], BF16)
    bT_d = mk.dram("bT_d", [4, 128, S], BF16)
    cT_d = mk.dram("cT_d", [4, 128, S], BF16)
    dbg = {}
    if debug:
        for nm, shp in debug.items():
            dbg[nm] = mk.dram("dbg_" + nm, list(shp), F32, kind="ExternalOutput")

    BASE = 16512
    TOP = 229344
    cur = [BASE]
    all_bufs = []

    def alloc(name, shape, dtype):
        esz = 4 if dtype in (F32, I32) else 2
        n = 1
        for s_ in shape[1:]:
            n *= s_
        nbytes = (n * esz + 31) // 32 * 32
        off = cur[0]
        cur[0] += nbytes
        assert off + nbytes <= TOP, (name, off, nbytes, TOP)
        b = mk.sbuf_at(name, shape, dtype, off)
        b.off, b.nbytes = off, nbytes
        for o in all_bufs:
            if b.off < o.off + o.nbytes and o.off < b.off + b.nbytes:
                mk.overlap(b, o)
        all_bufs.append(b)
        return b

    ident_f = alloc("ident_f", [128, 128], F32)
    ident_b = alloc("ident_b", [128, 128], BF16)
    U_f = alloc("U_f", [128, 128], F32)
    U_b = alloc("U_b", [128, 128], BF16)
    SL_f = alloc("SL_f", [128, 128], F32)
    ones_f = alloc("ones_f", [128, 128], F32)
    ones_b = alloc("ones_b", [128, 128], BF16)
    zeros_b = alloc("zeros_b", [128, 512], BF16)
    cst = alloc("cst", [128, 4], F32)
    NVR = sum(r for _, r in VEC_ROWS) * DEPTH
    NVB = (NVR + 127) // 128
    vecsT = alloc("vecsT", [128, NVB * 128], F32)
    bc16 = alloc("bc16", [128, 3, DEPTH, NH], F32)
    cs2 = alloc("cs2", [128, S], F32)
    sn2 = alloc("sn2", [128, S], F32)
    memT = alloc("memT", [128, KC, MEM], BF16)
    stat = alloc("stat", [128, 96], F32)
    WSLOT = 6144
    wslots = [alloc("wslot%d" % i, [128, WSLOT], BF16) for i in range(2)]
    T0 = cur[0]

    PB = [mk.psum("pb%d" % i, [128, 512], F32) for i in range(8)]
    bank_rr = [0]
    tbank_rr = [0]

    def nextbank():
        b = PB[bank_rr[0] % 4]
        bank_rr[0] += 1
        return b

    def nexttbank():
        b = PB[6 + tbank_rr[0] % 2]
        tbank_rr[0] += 1
        return b

    stat_rr = [0]

    def statcol(n=3):
        c = stat_rr[0]
        if c + n > 96:
            c = 0
        stat_rr[0] = c + n
        return c

    vec_r0 = {}
    r = 0
    for name, rows in VEC_ROWS:
        vec_r0[name] = r
        r += rows * DEPTH
    VR = dict(VEC_ROWS)

    def vcol(name, l, idx):
        c = vec_r0[name] + l * VR[name] + idx
        return vecsT[:, c:c + 1]

    def vcols(name, l, i0, n):
        c = vec_r0[name] + l * VR[name] + i0
        return vecsT[:, c:c + n]

    wrr = [0]

    def nextslot():
        s_ = wslots[wrr[0] % 2]
        wrr[0] += 1
        return s_

    def wload(src_ap, kcn, ncols, src_buf):
        slot = nextslot()
        assert kcn * ncols <= WSLOT
        view = slot[:, 0:kcn * ncols].rearrange("p (k n) -> p k n", k=kcn)
        mk.dma("pool", view, src_ap, [src_buf], [slot], ("w", slot.name))
        return slot, view

    def wsrc(buf, l, c0, ncols, k0=0, kcn=KC):
        return buf[l][k0 * 128:(k0 + kcn) * 128, c0:c0 + ncols].rearrange("(kc p) n -> p kc n", p=128)

    stg = alloc("stg", [128, NVB, 128], F32)
    ang = alloc("ang", [128, S], F32)
    ang2 = alloc("ang2", [128, S], F32)
    kf = alloc("kf", [128, S], F32)
    ki = alloc("ki", [128, S], I32)
    posi = alloc("posi", [128, S], I32)
    crope = alloc("crope", [128, 2], F32)
    memt = alloc("memt", [128, 2, D], F32)
    memn = alloc("memn", [128, 2, D], BF16)
    junk0 = alloc("junk0", [128, D], BF16)

    mk.dma("sp", ident_f[:], c_ident[:], [c_ident], [ident_f], "c0")
    mk.dma("sp", U_f[:], c_U[:], [c_U], [U_f], "c1")
    mk.dma("sp", SL_f[:], c_SL[:], [c_SL], [SL_f], "c2")
    mk.dma("sp", crope[:], c_rope[:], [c_rope], [crope], "c3")
    I("dve", "tensor_copy", [ident_f], [ident_b], out=ident_b[:], in_=ident_f[:])
    I("dve", "tensor_copy", [U_f], [U_b], out=U_b[:], in_=U_f[:])
    I("dve", "memset", [], [ones_f], ones_f[:], 1.0)
    I("dve", "memset", [], [ones_b], ones_b[:], 1.0)
    I("dve", "memset", [], [zeros_b], zeros_b[:], 0.0)
    I("dve", "memset", [], [cst], cst[:, 0:1], EPS)
    I("dve", "memset", [cst], [cst], cst[:, 1:2], 1.0)
    I("dve", "memset", [], [stg], stg[:], 0.0)
    I("dve", "memset", [], [cs2], cs2[:], 0.0)
    I("dve", "memset", [], [sn2], sn2[:], 0.0)
    di = 0
    for name, rows in VEC_ROWS:
        src = P[name]
        shp = dict(PARAMS)[name]
        if len(shp) == 2:
            flat = src[:].rearrange("l (c q) -> (l c) q", q=128)
        else:
            flat = src[:].rearrange("l j (c q) -> (l j c) q", q=128)
        total = rows * DEPTH
        r0 = vec_r0[name]
        done = 0
        while done < total:
            rr = r0 + done
            blk, pp = rr // 128, rr % 128
            n = min(total - done, 128 - pp)
            mk.dma("sp", stg[pp:pp + n, blk, :], flat[done:done + n, :], [src], [stg], "v%d" % (di % 4))
            di += 1
            done += n
    for blk in range(NVB):
        pb = nextbank()
        I("pe", "transpose", [stg, ident_f], [pb], out=pb[:, 0:128], in_=stg[:, blk, :], identity=ident_f[:])
        I("dve", "tensor_copy", [pb], [(vecsT, (blk,))], out=vecsT[:, blk * 128:(blk + 1) * 128], in_=pb[:, 0:128])
    for i, name in enumerate(("dt_bias", "a_log", "d_skip")):
        mk.dma("sp", bc16[:, i, :, :].rearrange("p l h -> p (l h)"),
               P[name][:].rearrange("l h -> (l h)").unsqueeze(0).broadcast_to([128, DEPTH * NH]),
               [P[name]], [bc16], "b%d" % i)
    I("act", "activation", [bc16], [bc16], out=bc16[:, 1, :, :], in_=bc16[:, 1, :, :], func=AF.Exp)
    I("dve", "tensor_scalar", [bc16], [bc16], out=bc16[:, 1, :, :], in0=bc16[:, 1, :, :], scalar1=-1.0,
      scalar2=None, op0=ALU.mult)

    R0, R1 = 64, 128
    mk.dma("sp", posi[R0:R1, :], pos_in[0:1, :].broadcast_to([64, S]), [pos_in], [posi], "c4")
    I("dve", "tensor_copy", [posi], [ang], out=ang[R0:R1, :], in_=posi[R0:R1, :])
    I("dve", "tensor_scalar", [ang, crope], [ang], out=ang[R0:R1, :], in0=ang[R0:R1, :],
      scalar1=crope[R0:R1, 0:1], scalar2=None, op0=ALU.mult)
    TWO_PI = 2.0 * math.pi
    C1 = 6.28125
    C2 = TWO_PI - C1

    def sin_of(src, dst, shift):
        I("dve", "tensor_scalar", [src], [ang2], out=ang2[R0:R1, :], in0=src[R0:R1, :], scalar1=shift,
          scalar2=None, op0=ALU.add)
        I("dve", "tensor_scalar", [ang2], [ki], out=ki[R0:R1, :], in0=ang2[R0:R1, :], scalar1=1.0 / TWO_PI,
          scalar2=None, op0=ALU.mult)
        I("dve", "tensor_copy", [ki], [kf], out=kf[R0:R1, :], in_=ki[R0:R1, :])
        I("dve", "scalar_tensor_tensor", [kf, ang2], [ang2], out=ang2[R0:R1, :], in0=kf[R0:R1, :], scalar=-C1,
          in1=ang2[R0:R1, :], op0=ALU.mult, op1=ALU.add)
        I("dve", "scalar_tensor_tensor", [kf, ang2], [ang2], out=ang2[R0:R1, :], in0=kf[R0:R1, :], scalar=-C2,
          in1=ang2[R0:R1, :], op0=ALU.mult, op1=ALU.add)
        I("dve", "tensor_scalar", [ang2], [ang2], out=ang2[R0:R1, :], in0=ang2[R0:R1, :], scalar1=math.pi,
          scalar2=-math.pi, op0=ALU.min, op1=ALU.max)
        I("act", "activation", [ang2], [dst], out=dst[R0:R1, :], in_=ang2[R0:R1, :], func=AF.Sin)

    sin_of(ang, cs2, math.pi / 2.0)
    sin_of(ang, sn2, 0.0)
    I("dve", "tensor_scalar", [sn2, crope], [sn2], out=sn2[R0:R1, :], in0=sn2[R0:R1, :],
      scalar1=crope[R0:R1, 1:2], scalar2=None, op0=ALU.mult)

    for kt in range(2):
        mk.dma("sp", memt[:, kt, :], mem_in[kt * 128:(kt + 1) * 128, :], [mem_in], [(memt, (kt,))], "m%d" % kt)
        c = statcol(3)
        I("act", "activation", [(memt, (kt,))], [junk0, (stat, (c,))], out=junk0[:], in_=memt[:, kt, :], func=AF.Square,
          accum_out=stat[:, c:c + 1])
        I("act", "activation", [(stat, (c,)), cst], [(stat, (c,))], out=stat[:, c + 1:c + 2], in_=stat[:, c:c + 1],
          func=AF.Sqrt, scale=1.0 / D, bias=cst[:, 0:1])
        I("dve", "reciprocal", [(stat, (c,))], [(stat, (c,))], out=stat[:, c + 2:c + 3], in_=stat[:, c + 1:c + 2])
        I("dve", "tensor_scalar", [(memt, (kt,)), (stat, (c,))], [(memn, (kt,))], out=memn[:, kt, :], in0=memt[:, kt, :],
          scalar1=stat[:, c + 2:c + 3], scalar2=None, op0=ALU.mult)
        tb_ = nexttbank()
        tv = tb_[:].bitcast(BF16)
        for kc in range(KC):
            I("pe", "transpose", [(memn, (kt,)), ident_b], [tb_], out=tv[:, kc * 128:(kc + 1) * 128],
              in_=memn[:, kt, kc * 128:(kc + 1) * 128], identity=ident_b[:])
        I("dve", "tensor_copy", [tb_], [memT], out=memT[:, :, kt * 128:(kt + 1) * 128],
          in_=tv[:, 0:1024].rearrange("p (k t) -> p k t", k=KC))

    cur[0] = T0
    hT = alloc("hT", [128, KC, S], BF16)
    R_A = cur[0]
    xt = [alloc("xt%d" % i, [128, D], F32) for i in range(2)]
    xn = [alloc("xn%d" % i, [128, D], BF16) for i in range(2)]
    junk = alloc("junk", [128, D], BF16)
    R_A_END = cur[0]
    rx = [alloc("rx%d" % i, [128, 256], F32) for i in range(3)]
    R_B = cur[0]

    raw = [alloc("raw%d" % i, [128, S], F32) for i in range(2)]
    acc = [alloc("acc%d" % i, [128, S], F32) for i in range(2)]
    tmpT = [alloc("tmpT%d" % i, [128, S], BF16) for i in range(2)]
    zst = [alloc("zst%d" % i, [128, 512], BF16) for i in range(2)]
    tkst = [alloc("tkst%d" % i, [128, 8, 128], BF16) for i in range(2)]
    R_X_END = cur[0]
    cqnT = alloc("cqnT", [128, 3, S], BF16)
    ckvnT = alloc("ckvnT", [128, 2, S], BF16)
    kpeT = alloc("kpeT", [128, S], BF16)
    dtt = alloc("dtt", [128, NT, NH], F32)
    adt = alloc("adt", [128, NT, NH], F32)
    ssqa = alloc("ssqa", [128, NT], F32)
    rsta = alloc("rsta", [128, NT], F32)
    R_Y = cur[0]
    sqb = [alloc("sqb%d" % i, [128, S], BF16) for i in range(2)]
    rstd = alloc("rstd", [128, S], F32)
    wkr = alloc("wkr", [128, KC, 128], BF16)
    rtmp = alloc("rtmp", [128, 2, 512], F32)
    cur[0] = R_Y
    yattT = alloc("yattT", [128, KC, S], BF16)
    MIX_END = cur[0]

    cur[0] = R_B
    st_f = alloc("st_f", [128, D], F32)
    st_b = alloc("st_b", [128, D], BF16)
    adU = alloc("adU", [128, NH, 128], F32)
    Eexp = [alloc("Eexp%d" % i, [128, 512], F32) for i in range(2)]
    MT = [alloc("MT%d" % i, [128, 512], BF16) for i in range(2)]
    cbm = [alloc("cbm%d" % i, [128, 128], F32) for i in range(2)]
    xsc = [alloc("xsc%d" % i, [128, D], BF16) for i in range(2)]
    btc = [alloc("btc%d" % i, [128, 512], BF16) for i in range(2)]
    bTc = [alloc("bTc%d" % i, [128, 4, 128], BF16) for i in range(2)]
    cTc = [alloc("cTc%d" % i, [128, 4, 128], BF16) for i in range(2)]
    szc = [alloc("szc%d" % i, [128, D], BF16) for i in range(2)]
    assert cur[0] <= R_X_END, (cur[0], R_X_END)
    cur[0] = R_Y
    xd = alloc("xd", [128, D], BF16)
    xdd = alloc("xdd", [128, D], BF16)
    xsk = alloc("xsk", [128, D], F32)
    yf = alloc("yf", [128, D], F32)
    yo = alloc("yo", [128, D], F32)
    ygb = alloc("ygb", [128, D], BF16)
    sm16 = alloc("sm16", [128, 8, NH], F32)
    assert cur[0] <= MIX_END, (cur[0], MIX_END)
    cur[0] = R_B
    QT = [alloc("QT%d" % i, [128, S], BF16) for i in range(2)]
    KT = [alloc("KT%d" % i, [128, S], BF16) for i in range(2)]
    Vh = [alloc("Vh%d" % i, [128, NT, 96], BF16) for i in range(2)]
    PT = [alloc("PT%d" % i, [128, 512], BF16) for i in range(3)]
    qtmp = alloc("qtmp", [128, 4, 512], F32)
    rsum = alloc("rsum", [128, 8], F32)
    onrm = alloc("onrm", [128, 4, HD], F32)
    osq = alloc("osq", [128, 4, HD], F32)
    ypair = [alloc("ypair%d" % i, [128, NT, 128], BF16) for i in range(2)]
    assert cur[0] <= R_X_END, (cur[0], R_X_END)

    cur[0] = R_B
    qT = alloc("qT", [128, KC, S], BF16)
    oT = alloc("oT", [128, KC, S], BF16)
    mnT = alloc("mnT", [128, KC, MEM], BF16)
    KmT = alloc("KmT", [128, KC, MEM], BF16)
    Vm = alloc("Vm", [128, 2, D], BF16)
    PTm = [alloc("PTm%d" % i, [128, 2, 512], BF16) for i in range(2)]
    rinvb = [alloc("rinvb%d" % i, [128, 512], F32) for i in range(2)]

    cur[0] = R_A
    fraw = alloc("fraw", [128, S], F32)
    assert cur[0] <= R_A_END
    cur[0] = R_B
    aT = alloc("aT", [128, NFF, S], BF16)
    facc = alloc("facc", [128, S], F32)
    sgate = alloc("sgate", [128, S], BF16)
    cur[0] = R_B
    gfin = alloc("gfin", [128, D], F32)
    ofin = [alloc("ofin%d" % i, [128, D], F32) for i in range(2)]

    def rms_tile_stats(src_ap, src_reads, width, jbuf=None):
        jb = junk if jbuf is None else jbuf
        c = statcol(3)
        I("act", "activation", src_reads, [jb, (stat, (c,))], out=jb[:, 0:width], in_=src_ap, func=AF.Square,
          accum_out=stat[:, c:c + 1])
        I("act", "activation", [(stat, (c,)), cst], [(stat, (c,))], out=stat[:, c + 1:c + 2], in_=stat[:, c:c + 1],
          func=AF.Sqrt, scale=1.0 / width, bias=cst[:, 0:1])
        I("dve", "reciprocal", [(stat, (c,))], [(stat, (c,))], out=stat[:, c + 2:c + 3], in_=stat[:, c + 1:c + 2])
        return stat[:, c + 2:c + 3], (stat, (c,))

    def transpose_tile_to(src_buf, dstT, dkey, tile_i, gain_name, l):
        tb_ = nexttbank()
        tv = tb_[:].bitcast(BF16)
        for kc in range(KC):
            I("pe", "transpose", [src_buf, ident_b], [tb_], out=tv[:, kc * 128:(kc + 1) * 128],
              in_=src_buf[:, kc * 128:(kc + 1) * 128], identity=ident_b[:])
        g = vcols(gain_name, l, 0, KC).unsqueeze(2).broadcast_to([128, KC, 128])
        I("dve", "tensor_tensor", [tb_, vecsT], [(dstT, dkey)],
          out=dstT[:, :, tile_i * 128:(tile_i + 1) * 128],
          in0=tv[:, 0:1024].rearrange("p (k t) -> p k t", k=KC), in1=g, op=ALU.mult)

    def norm_phase(xsrc, l, gain_name):
        for i in range(NT):
            t = xt[i % 2]
            mk.dma("sp", t[:], xsrc[i * 128:(i + 1) * 128, :], [(xsrc, (i,))], [t], ("xt", i % 2))
            rinv, skey = rms_tile_stats(t[:], [t], D)
            n = xn[i % 2]
            I("dve", "tensor_scalar", [t, skey], [n], out=n[:], in0=t[:], scalar1=rinv, scalar2=None, op0=ALU.mult)
            transpose_tile_to(n, hT, (None, i // 4), i, gain_name, l)

    def resid_phase(xsrc, l, wname, parts, scaled_last=None):
        NB = 256
        nk_tot = sum(nk for _, nk in parts)
        for cb in range(D // NB):
            slot, wv = wload(wsrc(P[wname], l, cb * NB, NB, 0, nk_tot), nk_tot, NB, P[wname])
            for i in range(NT):
                u = cb * NT + i
                t = rx[u % 3]
                mk.dma("sp", t[:], xsrc[i * 128:(i + 1) * 128, cb * NB:(cb + 1) * NB], [(xsrc, (i,))], [t], ("rx", u % 3))
                k0 = 0
                for pi, (actT, nk) in enumerate(parts):
                    last = (pi == len(parts) - 1)
                    sep = last and scaled_last is not None
                    if pi == 0 or sep:
                        pb = nextbank()
                    for kc in range(nk):
                        st_flag = (kc == 0) if (pi == 0 or sep) else False
                        sp_flag = (kc == nk - 1) and (last or (scaled_last is not None and pi == len(parts) - 2))
                        I("pe", "matmul", [(actT, (None, i // 4)), slot], [pb], pb[:, 0:NB],
                          lhsT=actT[:, kc, i * 128:(i + 1) * 128], rhs=wv[:, k0 + kc, :], start=st_flag, stop=sp_flag)
                    k0 += nk
                    if sep:
                        I("dve", "scalar_tensor_tensor", [pb, t, scaled_last], [t], out=t[:], in0=pb[:, 0:NB],
                          scalar=scaled_last[:, i:i + 1], in1=t[:], op0=ALU.mult, op1=ALU.add)
                    elif last or (scaled_last is not None and pi == len(parts) - 2):
                        I("dve", "tensor_tensor", [pb, t], [t], out=t[:], in0=pb[:, 0:NB], in1=t[:], op=ALU.add)
                mk.dma("sp", xres[i * 128:(i + 1) * 128, cb * NB:(cb + 1) * NB], t[:], [t], [(xres, (i,))], ("rxo", u % 3))

    def proj_fm(wv, col0, m, nk, actT, slot, cb_fn):
        for tb in range(4):
            pb = nextbank()
            for kc in range(nk):
                I("pe", "matmul", [slot, (actT, (None, tb))], [pb], pb[0:m, :],
                  lhsT=wv[:, kc, col0:col0 + m], rhs=actT[:, kc, tb * 512:(tb + 1) * 512],
                  start=(kc == 0), stop=(kc == nk - 1))
            cb_fn(tb, pb)

    def rope_rows(ps, dst, dkey, ts_):
        I("dve", "tensor_tensor", [ps, cs2], [qtmp], out=qtmp[64:96, 0, :], in0=ps[64:96, :], in1=cs2[64:96, ts_], op=ALU.mult)
        I("dve", "tensor_tensor", [ps, sn2], [qtmp], out=qtmp[64:96, 1, :], in0=ps[96:128, :], in1=sn2[96:128, ts_], op=ALU.mult)
        I("dve", "tensor_tensor", [qtmp], [(dst, dkey)], out=dst[64:96, ts_], in0=qtmp[64:96, 0, :], in1=qtmp[64:96, 1, :],
          op=ALU.add)
        I("dve", "tensor_tensor", [ps, cs2], [qtmp], out=qtmp[96:128, 2, :], in0=ps[96:128, :], in1=cs2[96:128, ts_], op=ALU.mult)
        I("dve", "tensor_tensor", [ps, sn2], [qtmp], out=qtmp[96:128, 3, :], in0=ps[64:96, :], in1=sn2[64:96, ts_], op=ALU.mult)
        I("dve", "tensor_tensor", [qtmp], [(dst, dkey)], out=dst[96:128, ts_], in0=qtmp[96:128, 2, :], in1=qtmp[96:128, 3, :],
          op=ALU.add)

    xsrc = x_in
    STAGES = ["norm", "z", "xbc", "prep", "ssd", "mla", "wout", "memattn", "ffn"]
    sub_stop = 99
    if stop_after and ":" in stop_after:
        stop_after, ss_ = stop_after.split(":")
        sub_stop = int(ss_)
    stop_idx = STAGES.index(stop_after) if stop_after else 99

    class _Stop(Exception):
        pass

    def layer_body(l, xsrc):
        norm_phase(xsrc, l, "norm_mix")
        if stop_idx <= 0:
            raise _Stop()

        for cb in range(2):
            slot, wv = wload(wsrc(P["w_in"], l, O_Z + cb * 512, 512), KC, 512, P["w_in"])
            for i in range(NT):
                pb = nextbank()
                for kc in range(KC):
                    I("pe", "matmul", [(hT, (None, i // 4)), slot], [pb], pb[:, :],
                      lhsT=hT[:, kc, i * 128:(i + 1) * 128], rhs=wv[:, kc, :], start=(kc == 0), stop=(kc == KC - 1))
                u = cb * NT + i
                z = zst[u % 2]
                I("act", "activation", [pb], [z], out=z[:], in_=pb[:, :], func=AF.Silu)
                mk.dma("sp", sz_d[i * 128:(i + 1) * 128, cb * 512:(cb + 1) * 512], z[:], [z], [(sz_d, (i,))], ("zst", u % 2))

        if stop_idx <= 1:
            raise _Stop()
        for cg in range(8):
            slot, wv = wload(wsrc(P["w_in"], l, O_XBC + cg * 256, 256), KC, 256, P["w_in"])
            for sub in range(2):
                c = cg * 2 + sub
                rw = raw[c % 2]
                ac = acc[c % 2]

                def evac(tb, pb, rw=rw):
                    I("act", "copy", [pb], [(rw, (tb,))], out=rw[:, tb * 512:(tb + 1) * 512], in_=pb[:, :])
                proj_fm(wv, sub * 128, 128, KC, hT, slot, evac)
                w3, w2, w1, w0 = (vcol("ssm_conv_w", l, j * 16 + c) for j in (3, 2, 1, 0))
                bb = vcol("ssm_conv_b", l, c)
                I("dve", "tensor_scalar", [rw, vecsT], [ac], out=ac[:], in0=rw[:], scalar1=w3, scalar2=bb,
                  op0=ALU.mult, op1=ALU.add)
                for sh, wj in ((1, w2), (2, w1), (3, w0)):
                    I("dve", "scalar_tensor_tensor", [rw, ac, vecsT], [ac], out=ac[:, sh:S], in0=rw[:, 0:S - sh],
                      scalar=wj, in1=ac[:, sh:S], op0=ALU.mult, op1=ALU.add)
                tt = tmpT[c % 2]
                I("act", "activation", [ac], [tt], out=tt[:], in_=ac[:], func=AF.Silu)
                if c < 12:
                    for half in range(2):
                        tk = tkst[half]
                        tb_ = nexttbank()
                        tv = tb_[:].bitcast(BF16)
                        for j in range(8):
                            i = half * 8 + j
                            I("pe", "transpose", [tt, ident_b], [tb_], out=tv[:, j * 128:(j + 1) * 128],
                              in_=tt[:, i * 128:(i + 1) * 128], identity=ident_b[:])
                        I("dve", "tensor_copy", [tb_], [tk], out=tk[:], in_=tv[:, 0:1024].rearrange("p (j t) -> p j t", j=8))
                        rows = slice(half * 1024, (half + 1) * 1024)
                        if c < 8:
                            dst = xs_d[rows, c * 128:(c + 1) * 128].rearrange("(i p) f -> p i f", p=128)
                            mk.dma("sp", dst, tk[:], [tk], [xs_d], ("tk", half))
                        else:
                            dst = bt_d[rows, (c - 8) * 128:(c - 7) * 128].rearrange("(i p) f -> p i f", p=128)
                            mk.dma("sp", dst, tk[:], [tk], [bt_d], ("tk", half))
                if 8 <= c < 12:
                    mk.dma("sp", bT_d[c - 8], tt[:], [tt], [bT_d], ("tT", c % 2))
                if c >= 12:
                    mk.dma("sp", cT_d[c - 12], tt[:], [tt], [cT_d], ("tT", c % 2))

        if stop_idx <= 2:
            raise _Stop()
        slot = nextslot()
        wv = slot[:, 0:KC * 64].rearrange("p (k n) -> p k n", k=KC)
        mk.dma("pool", wv[:, :, 0:NH], wsrc(P["w_in"], l, O_DT, NH), [P["w_in"]], [slot], ("w", slot.name))
        import os
        DTV = int(os.environ.get("MK_DT", "9"))
        for i in range(NT if DTV >= 2 else 0):
            pb = nextbank()
            for kc in range(KC):
                I("pe", "matmul", [(hT, (None, i // 4)), slot], [pb], pb[:, 0:NH],
                  lhsT=hT[:, kc, i * 128:(i + 1) * 128], rhs=wv[:, kc, 0:NH], start=(kc == 0), stop=(kc == KC - 1))
            I("dve", "tensor_tensor", [pb, bc16], [(dtt, (i,))], out=dtt[:, i, :], in0=pb[:, 0:NH], in1=bc16[:, 0, l, :],
              op=ALU.add)
            if DTV >= 3:
                I("act", "activation", [(dtt, (i,))], [(dtt, (i,))], out=dtt[:, i, :], in_=dtt[:, i, :], func=AF.Exp)
            if DTV >= 4:
                I("act", "activation", [(dtt, (i,)), cst], [(dtt, (i,))], out=dtt[:, i, :], in_=dtt[:, i, :], func=AF.Ln,
                  bias=cst[:, 1:2], scale=1.0)
            if DTV >= 5:
                I("dve", "tensor_tensor", [(dtt, (i,)), bc16], [(adt, (i,))], out=adt[:, i, :], in0=dtt[:, i, :],
                  in1=bc16[:, 1, l, :], op=ALU.mult)

        if stop_idx == 3 and sub_stop <= 0:
            raise _Stop()

        def latent(wv, slot, col0, nch, dstT, gname, width):
            for c in range(nch):
                sq = sqb[c % 2]

                def evac(tb, pb, c=c, sq=sq):
                    I("act", "activation", [pb], [(sq, (tb,))], out=sq[:, tb * 512:(tb + 1) * 512], in_=pb[:, :],
                      func=AF.Square)
                    I("dve", "tensor_scalar", [pb, vecsT], [(dstT, (c, tb))], out=dstT[:, c, tb * 512:(tb + 1) * 512],
                      in0=pb[:, :], scalar1=vcol(gname, l, c), scalar2=None, op0=ALU.mult)
                proj_fm(wv, col0 + c * 128, 128, KC, hT, slot, evac)
                for tb in range(4):
                    pb = nextbank()
                    I("pe", "matmul", [(sq, (tb,)), ones_b], [pb], pb[:, :], lhsT=ones_b[:],
                      rhs=sq[:, tb * 512:(tb + 1) * 512], start=True, stop=True)
                    if c == 0:
                        I("dve", "tensor_copy", [pb], [(rstd, (tb,))], out=rstd[:, tb * 512:(tb + 1) * 512], in_=pb[:, :])
                    else:
                        I("dve", "tensor_tensor", [pb, (rstd, (tb,))], [(rstd, (tb,))],
                          out=rstd[:, tb * 512:(tb + 1) * 512], in0=pb[:, :], in1=rstd[:, tb * 512:(tb + 1) * 512],
                          op=ALU.add)
            I("act", "activation", [rstd, cst], [rstd], out=rstd[:], in_=rstd[:], func=AF.Sqrt, scale=1.0 / width,
              bias=cst[:, 0:1])
            I("dve", "reciprocal", [rstd], [rstd], out=rstd[:], in_=rstd[:])
            for c in range(nch):
                I("dve", "tensor_tensor", [(dstT, (c, None)), rstd], [(dstT, (c, None))], out=dstT[:, c, :],
                  in0=dstT[:, c, :], in1=rstd[:], op=ALU.mult)

        slot, wv = wload(wsrc(P["w_in"], l, O_CQ, 384), KC, 384, P["w_in"])
        latent(wv, slot, 0, 3, cqnT, "q_norm", Q_LORA)
        if stop_idx == 3 and sub_stop <= 1:
            raise _Stop()
        slot = nextslot()
        wv = slot[:, 0:KC * 64].rearrange("p (k n) -> p k n", k=KC)
        mk.dma("pool", wv[:, :, 0:ROPE], wsrc(P["w_in"], l, O_KR, ROPE), [P["w_in"]], [slot], ("w", slot.name))
        I("dve", "memset", [], [wkr], wkr[:], 0.0)
        I("dve", "tensor_copy", [slot, wkr], [wkr], out=wkr[:, :, 64:80], in_=wv[:, :, 0:16])
        I("dve", "tensor_copy", [slot, wkr], [wkr], out=wkr[:, :, 96:112], in_=wv[:, :, 16:32])
        slot, wv = wload(wsrc(P["w_in"], l, O_CKV, 256), KC, 256, P["w_in"])
        if stop_idx == 3 and sub_stop <= 2:
            raise _Stop()
        latent(wv, slot, 0, 2, ckvnT, "kv_norm", KV_LORA)
        if stop_idx == 3 and sub_stop <= 3:
            raise _Stop()
        for tb in range(4):
            pa = nextbank()
            for kc in range(KC):
                I("pe", "matmul", [wkr, (hT, (None, tb))], [pa], pa[:, :], lhsT=wkr[:, kc, :],
                  rhs=hT[:, kc, tb * 512:(tb + 1) * 512], start=(kc == 0), stop=(kc == KC - 1))
            rope_rows(pa, kpeT, (tb,), slice(tb * 512, (tb + 1) * 512))

        if stop_idx <= 3:
            raise _Stop()
        I("dve", "memset", [], [st_f], st_f[:], 0.0)
        I("dve", "memset", [], [st_b], st_b[:], 0.0)
        for c in range(NT):
            cs_ = slice(c * 128, (c + 1) * 128)
            xs_c, bt_c, bT_c, cT_c, sz_c = xsc[c % 2], btc[c % 2], bTc[c % 2], cTc[c % 2], szc[c % 2]
            mk.dma("sp", xs_c[:], xs_d[cs_, :], [xs_d], [xs_c], ("xsc", c % 2))
            mk.dma("sp", bt_c[:], bt_d[cs_, :], [bt_d], [bt_c], ("btc", c % 2))
            mk.dma("sp", bT_c[:], bT_d[:, :, cs_].rearrange("g n t -> n g t"), [bT_d], [bT_c], ("bTc", c % 2))
            mk.dma("sp", cT_c[:], cT_d[:, :, cs_].rearrange("g n t -> n g t"), [cT_d], [cT_c], ("cTc", c % 2))
            mk.dma("sp", sz_c[:], sz_d[cs_, :], [(sz_d, (c,))], [sz_c], ("szc", c % 2))
            I("pool", "tensor_tensor", [(adt, (c,)), U_f], [adU], out=adU[:],
              in0=adt[:, c, :].unsqueeze(2).broadcast_to([128, NH, 128]),
              in1=U_f[:].unsqueeze(1).broadcast_to([128, NH, 128]), op=ALU.mult)
            pc = nextbank()
            I("pe", "matmul", [U_f, (adt, (c,))], [pc], pc[:, 0:NH], lhsT=U_f[:], rhs=adt[:, c, :], start=True, stop=True)
            I("pe", "matmul", [ones_f, (adt, (c,))], [pc], pc[:, 64:64 + NH], lhsT=ones_f[:], rhs=adt[:, c, :],
              start=True, stop=True, skip_group_check=True)
            I("dve", "tensor_copy", [pc], [sm16], out=sm16[:, 3:5, :], in_=pc[:, 0:128].rearrange("p (a h) -> p a h", a=2)[:, :, 0:NH])
            I("dve", "tensor_tensor", [sm16], [sm16], out=sm16[:, 5, :], in0=sm16[:, 4, :], in1=sm16[:, 3, :], op=ALU.subtract)
            I("act", "activation", [sm16], [sm16], out=sm16[:, 0:3, :], in_=sm16[:, 3:6, :], func=AF.Exp)
            x3 = xs_c[:].rearrange("p (h d) -> p h d", h=NH)
            I("dve", "tensor_tensor", [xs_c, (dtt, (c,))], [xd], out=xd[:].rearrange("p (h d) -> p h d", h=NH), in0=x3,
              in1=dtt[:, c, :].unsqueeze(2).broadcast_to([128, NH, HD]), op=ALU.mult)
            I("dve", "tensor_tensor", [xd, sm16], [xdd], out=xdd[:].rearrange("p (h d) -> p h d", h=NH),
              in0=xd[:].rearrange("p (h d) -> p h d", h=NH),
              in1=sm16[:, 2, :].unsqueeze(2).broadcast_to([128, NH, HD]), op=ALU.mult)
            I("pool", "tensor_tensor", [xs_c, bc16], [xsk], out=xsk[:].rearrange("p (h d) -> p h d", h=NH), in0=x3,
              in1=bc16[:, 2, l, :].unsqueeze(2).broadcast_to([128, NH, HD]), op=ALU.mult)
            for hh in range(2):
                hs = slice(hh * 512, (hh + 1) * 512)
                py, po = PB[4], PB[5]
                for g in (2 * hh, 2 * hh + 1):
                    pd = nextbank()
                    I("pe", "matmul", [SL_f, adU], [pd], pd[:, :], lhsT=SL_f[:],
                      rhs=adU[:, g * 4:(g + 1) * 4, :].rearrange("p h l -> p (h l)"), start=True, stop=True)
                    E = Eexp[g % 2]
                    I("act", "activation", [pd], [E], out=E[:], in_=pd[:, :], func=AF.Exp)
                    pcb = nextbank()
                    I("pe", "matmul", [bT_c, cT_c], [pcb], pcb[:, 0:128], lhsT=bT_c[:, g, :], rhs=cT_c[:, g, :],
                      start=True, stop=True)
                    cb_ = cbm[g % 2]
                    I("dve", "tensor_tensor", [pcb, U_f], [cb_], out=cb_[:], in0=pcb[:, 0:128], in1=U_f[:], op=ALU.mult)
                    M = MT[g % 2]
                    I("dve", "tensor_tensor", [E, cb_], [M], out=M[:].rearrange("p (r l) -> p r l", r=4),
                      in0=E[:].rearrange("p (r l) -> p r l", r=4), in1=cb_[:].unsqueeze(1).broadcast_to([128, 4, 128]),
                      op=ALU.mult)
                    for r_ in range(4):
                        h = g * 4 + r_
                        oc = (g % 2) * 256 + r_ * 64
                        I("pe", "matmul", [M, xd], [py], py[:, oc:oc + 64], lhsT=M[:, r_ * 128:(r_ + 1) * 128],
                          rhs=xd[:, h * HD:(h + 1) * HD], start=True, stop=True, skip_group_check=True)
                    oc = (g % 2) * 256
                    I("pe", "matmul", [cT_c, st_b], [po], po[:, oc:oc + 256], lhsT=cT_c[:, g, :],
                      rhs=st_b[:, g * 256:(g + 1) * 256], start=True, stop=True, skip_group_check=True)
                I("dve", "tensor_tensor", [po, sm16], [yo], out=yo[:, hs].rearrange("p (h d) -> p h d", h=8),
                  in0=po[:, :].rearrange("p (h d) -> p h d", h=8),
                  in1=sm16[:, 0, hh * 8:(hh + 1) * 8].unsqueeze(2).broadcast_to([128, 8, HD]), op=ALU.mult)
                I("dve", "tensor_tensor", [py, yo], [yf], out=yf[:, hs], in0=py[:, :], in1=yo[:, hs], op=ALU.add)
            I("dve", "tensor_tensor", [yf, xsk], [yf], out=yf[:], in0=yf[:], in1=xsk[:], op=ALU.add)
            I("dve", "tensor_tensor", [st_f, sm16], [st_f], out=st_f[:].rearrange("p (h d) -> p h d", h=NH),
              in0=st_f[:].rearrange("p (h d) -> p h d", h=NH),
              in1=sm16[:, 1, :].unsqueeze(2).broadcast_to([128, NH, HD]), op=ALU.mult)
            for hh in range(2):
                hs = slice(hh * 512, (hh + 1) * 512)
                ps_ = nextbank()
                for g in (2 * hh, 2 * hh + 1):
                    oc = (g % 2) * 256
                    I("pe", "matmul", [bt_c, xdd], [ps_], ps_[:, oc:oc + 256], lhsT=bt_c[:, g * 128:(g + 1) * 128],
                      rhs=xdd[:, g * 256:(g + 1) * 256], start=True, stop=True, skip_group_check=True)
                I("dve", "tensor_tensor", [ps_, st_f], [st_f], out=st_f[:, hs], in0=ps_[:, :], in1=st_f[:, hs], op=ALU.add)
            I("act", "copy", [st_f], [st_b], out=st_b[:], in_=st_f[:])
            I("dve", "tensor_tensor", [yf, sz_c], [yf], out=yf[:], in0=yf[:], in1=sz_c[:], op=ALU.mult)
            rinv, skey = rms_tile_stats(yf[:], [yf], D)
            I("dve", "tensor_scalar", [yf, skey], [ygb], out=ygb[:], in0=yf[:], scalar1=rinv, scalar2=None, op0=ALU.mult)
            transpose_tile_to(ygb, hT, (None, c // 4), c, "ssm_norm", l)

        if stop_idx <= 4:
            raise _Stop()
        wqs_ = nextslot()
        wq4 = wqs_[:, 0:3 * NH * 128].rearrange("p (k h d) -> p k h d", k=3, h=NH)
        I("dve", "memset", [], [wqs_], wqs_[:], 0.0)
        for kc in range(3):
            s4 = P["w_uq"][l][kc * 128:(kc + 1) * 128, :].rearrange("p (h d) -> p h d", h=NH)
            mk.dma("pool", wq4[:, kc, :, 0:80], s4[:, :, 0:80], [P["w_uq"]], [wqs_], ("w", wqs_.name))
            mk.dma("pool", wq4[:, kc, :, 96:112], s4[:, :, 80:96], [P["w_uq"]], [wqs_], ("w", wqs_.name))
        wkvs_ = nextslot()
        wkv = wkvs_[:, 0:2 * NH * 128].rearrange("p (k n) -> p k n", k=2)
        for kc in range(2):
            mk.dma("pool", wkv[:, kc, :], P["w_ukv"][l][kc * 128:(kc + 1) * 128, :], [P["w_ukv"]], [wkvs_],
                   ("w", wkvs_.name))
        I("dve", "memset", [], [ssqa], ssqa[:], 0.0)
        scale = 96.0 ** -0.5
        for h in range(NH):
            Qh, Kh, V_ = QT[h % 2], KT[h % 2], Vh[h % 2]
            yp = ypair[(h // 2) % 2]
            for tb in range(4):
                ts_ = slice(tb * 512, (tb + 1) * 512)
                pa = nextbank()
                for kc in range(3):
                    I("pe", "matmul", [wqs_, (cqnT, (None, tb))], [pa], pa[:, :],
                      lhsT=wq4[:, kc, h, :], rhs=cqnT[:, kc, ts_], start=(kc == 0), stop=(kc == 2))
                I("act", "copy", [pa], [(Qh, (tb, 0))], out=Qh[0:64, ts_], in_=pa[0:64, :])
                rope_rows(pa, Qh, (tb, 1), ts_)
                pk = nextbank()
                for kc in range(2):
                    I("pe", "matmul", [wkvs_, (ckvnT, (None, tb))], [pk], pk[0:64, :],
                      lhsT=wkv[:, kc, h * 128:h * 128 + 64], rhs=ckvnT[:, kc, ts_], start=(kc == 0), stop=(kc == 1))
                I("act", "copy", [pk], [(Kh, (tb, 0))], out=Kh[0:64, ts_], in_=pk[0:64, :])
                I("pool", "tensor_copy", [(kpeT, (tb,))], [(Kh, (tb, 1))], out=Kh[64:128, ts_], in_=kpeT[64:128, ts_])
            I("pool", "memset", [], [V_], V_[:, :, 64:65], 1.0)
            for i in range(NT):
                pv = nextbank()
                for kc in range(2):
                    I("pe", "matmul", [(ckvnT, (None, i // 4)), wkvs_], [pv], pv[:, 0:64],
                      lhsT=ckvnT[:, kc, i * 128:(i + 1) * 128], rhs=wkv[:, kc, h * 128 + 64:h * 128 + 128],
                      start=(kc == 0), stop=(kc == 1))
                I("act", "copy", [pv], [V_], out=V_[:, i, 0:64], in_=pv[:, 0:64])
            for qb in range(4):
                pacc = PB[4 + (h * 4 + qb) % 2]
                I("pe", "matmul", [zeros_b], [pacc], pacc[:, :], lhsT=zeros_b[:, 0:128], rhs=zeros_b[:, :],
                  start=True, stop=False)
                nkt = (qb + 1) * 4
                for kt in range(nkt):
                    q0 = max(qb * 512, kt * 128)
                    q1 = (qb + 1) * 512
                    n = q1 - q0
                    psc = nextbank()
                    I("pe", "matmul", [Kh, Qh], [psc], psc[:, 0:n], lhsT=Kh[:, kt * 128:(kt + 1) * 128],
                      rhs=Qh[:, q0:q1], start=True, stop=True)
                    pt = PT[(qb * 16 + kt) % 3]
                    I("act", "activation", [psc], [pt], out=pt[:, 0:n], in_=psc[:, 0:n], func=AF.Exp, scale=scale)
                    if kt * 128 >= qb * 512:
                        I("pool", "tensor_tensor", [pt, U_b], [pt], out=pt[:, 0:128], in0=pt[:, 0:128], in1=U_b[:], op=ALU.mult)
                    for j in range(n // 128):
                        qt = (q0 - qb * 512) // 128 + j
                        I("pe", "matmul", [pt, V_], [pacc], pacc[:, qt * 128:qt * 128 + 65], lhsT=pt[:, j * 128:(j + 1) * 128],
                          rhs=V_[:, kt, 0:65], start=False, stop=False, skip_group_check=True)
                I("pe", "matmul", [zeros_b], [pacc], pacc[:, :], lhsT=zeros_b[:, 0:128], rhs=zeros_b[:, :],
                  start=False, stop=True)
                a3 = pacc[:, :].rearrange("p (q e) -> p q e", q=4)
                I("dve", "reciprocal", [pacc], [rsum], out=rsum[:, 0:4], in_=a3[:, :, 64])
                I("dve", "tensor_tensor", [pacc, rsum], [onrm], out=onrm[:], in0=a3[:, :, 0:64],
                  in1=rsum[:, 0:4].unsqueeze(2).broadcast_to([128, 4, HD]), op=ALU.mult)
                I("pool", "tensor_copy", [onrm], [(yp, (qb, h % 2))],
                  out=yp[:, qb * 4:(qb + 1) * 4, (h % 2) * 64:(h % 2) * 64 + 64], in_=onrm[:])
                I("dve", "tensor_tensor", [onrm], [osq], out=osq[:], in0=onrm[:], in1=onrm[:], op=ALU.mult)
                I("dve", "tensor_reduce", [osq], [rsum], out=rsum[:, 4:8], in_=osq[:], axis=mybir.AxisListType.X, op=ALU.add)
                I("dve", "tensor_tensor", [rsum, ssqa], [ssqa], out=ssqa[:, qb * 4:(qb + 1) * 4], in0=ssqa[:, qb * 4:(qb + 1) * 4],
                  in1=rsum[:, 4:8], op=ALU.add)
            if h % 2 == 1:
                hp = h // 2
                gcol = vcol("attn_out_norm", l, hp)
                for half in range(2):
                    tb_ = nexttbank()
                    tv = tb_[:].bitcast(BF16)
                    for j in range(8):
                        i = half * 8 + j
                        I("pe", "transpose", [yp, ident_b], [tb_], out=tv[:, j * 128:(j + 1) * 128], in_=yp[:, i, :],
                          identity=ident_b[:])
                    I("dve", "tensor_scalar", [tb_, vecsT], [(yattT, (hp, None))],
                      out=yattT[:, hp, half * 1024:(half + 1) * 1024], in0=tv[:, 0:1024], scalar1=gcol, scalar2=None,
                      op0=ALU.mult)
        I("act", "activation", [ssqa, cst], [rsta], out=rsta[:], in_=ssqa[:], func=AF.Sqrt, scale=1.0 / D, bias=cst[:, 0:1])
        I("dve", "reciprocal", [rsta], [rsta], out=rsta[:], in_=rsta[:])

        if stop_idx <= 5:
            raise _Stop()
        resid_phase(xsrc, l, "w_out", [(hT, KC), (yattT, KC)], scaled_last=rsta)
        xsrc = xres
        cur_x[0] = xres

        if stop_idx <= 6:
            raise _Stop()
        norm_phase(xsrc, l, "norm_mem_q")
        I("dve", "tensor_tensor", [memT, vecsT], [mnT], out=mnT[:], in0=memT[:],
          in1=vcols("norm_mem_kv", l, 0, KC).unsqueeze(2).broadcast_to([128, KC, MEM]), op=ALU.mult)
        for cg in range(4):
            slot, wv = wload(wsrc(P["w_mk"], l, cg * 256, 256), KC, 256, P["w_mk"])
            for sub in range(2):
                dc = cg * 2 + sub
                pb = nextbank()
                for kc in range(KC):
                    I("pe", "matmul", [slot, mnT], [pb], pb[:, 0:MEM], lhsT=wv[:, kc, sub * 128:(sub + 1) * 128],
                      rhs=mnT[:, kc, :], start=(kc == 0), stop=(kc == KC - 1))
                I("act", "copy", [pb], [(KmT, (dc,))], out=KmT[:, dc, :], in_=pb[:, 0:MEM])
        for cb in range(2):
            slot, wv = wload(wsrc(P["w_mv"], l, cb * 512, 512), KC, 512, P["w_mv"])
            for kt in range(2):
                pb = nextbank()
                for kc in range(KC):
                    I("pe", "matmul", [mnT, slot], [pb], pb[:, :], lhsT=mnT[:, kc, kt * 128:(kt + 1) * 128], rhs=wv[:, kc, :],
                      start=(kc == 0), stop=(kc == KC - 1))
                I("act", "copy", [pb], [(Vm, (kt, cb))], out=Vm[:, kt, cb * 512:(cb + 1) * 512], in_=pb[:, :])
        for cg in range(4):
            slot, wv = wload(wsrc(P["w_mq"], l, cg * 256, 256), KC, 256, P["w_mq"])
            for sub in range(2):
                dc = cg * 2 + sub

                def evac(tb, pb, dc=dc):
                    I("act", "copy", [pb], [(qT, (dc, tb))], out=qT[:, dc, tb * 512:(tb + 1) * 512], in_=pb[:, :])
                proj_fm(wv, sub * 128, 128, KC, hT, slot, evac)
        mscale = 256.0 ** -0.5
        for hd in range(4):
            for tb in range(4):
                ts_ = slice(tb * 512, (tb + 1) * 512)
                ptm = PTm[(hd * 4 + tb) % 2]
                for kt in range(2):
                    psc = nextbank()
                    for d2 in range(2):
                        dc = hd * 2 + d2
                        I("pe", "matmul", [(KmT, (dc,)), (qT, (dc, tb))], [psc], psc[:, :],
                          lhsT=KmT[:, dc, kt * 128:(kt + 1) * 128], rhs=qT[:, dc, ts_], start=(d2 == 0), stop=(d2 == 1))
                    I("act", "activation", [psc], [(ptm, (kt,))], out=ptm[:, kt, :], in_=psc[:, :], func=AF.Exp, scale=mscale)
                pr = nextbank()
                for kt in range(2):
                    I("pe", "matmul", [ones_b, (ptm, (kt,))], [pr], pr[:, :], lhsT=ones_b[:], rhs=ptm[:, kt, :],
                      start=(kt == 0), stop=(kt == 1))
                rb = rinvb[(hd * 4 + tb) % 2]
                I("dve", "reciprocal", [pr], [rb], out=rb[:], in_=pr[:, :])
                for d2 in range(2):
                    dc = hd * 2 + d2
                    po_ = nextbank()
                    for kt in range(2):
                        I("pe", "matmul", [(Vm, (kt, None)), (ptm, (kt,))], [po_], po_[:, :],
                          lhsT=Vm[:, kt, dc * 128:(dc + 1) * 128], rhs=ptm[:, kt, :], start=(kt == 0), stop=(kt == 1))
                    I("dve", "tensor_tensor", [po_, rb], [(oT, (dc, tb))], out=oT[:, dc, ts_], in0=po_[:, :], in1=rb[:],
                      op=ALU.mult)
        resid_phase(xsrc, l, "w_mo", [(oT, KC)])

        if stop_idx <= 7:
            raise _Stop()
        norm_phase(xsrc, l, "norm_ffn")
        for jg in range(NFF // 2):
            slot_g, wg = wload(wsrc(P["w_up"], l, jg * 256, 256), KC, 256, P["w_up"])
            slot_v, wvv = wload(wsrc(P["w_up"], l, D_FF + jg * 256, 256), KC, 256, P["w_up"])
            for sub in range(2):
                j = jg * 2 + sub
                for which, (slot, wv_) in enumerate(((slot_g, wg), (slot_v, wvv))):
                    cidx = which * NFF + j

                    def evac(tb, pb):
                        I("act", "copy", [pb], [(fraw, (tb,))], out=fraw[:, tb * 512:(tb + 1) * 512], in_=pb[:, :])
                    proj_fm(wv_, sub * 128, 128, KC, hT, slot, evac)
                    w2, w1, w0 = (vcol("ffn_conv_w", l, jj * 44 + cidx) for jj in (2, 1, 0))
                    bb = vcol("ffn_conv_b", l, cidx)
                    I("dve", "tensor_scalar", [fraw, vecsT], [facc], out=facc[:], in0=fraw[:], scalar1=w2, scalar2=bb,
                      op0=ALU.mult, op1=ALU.add)
                    for sh, wj in ((1, w1), (2, w0)):
                        I("dve", "scalar_tensor_tensor", [fraw, facc, vecsT], [facc], out=facc[:, sh:S], in0=fraw[:, 0:S - sh],
                          scalar=wj, in1=facc[:, sh:S], op0=ALU.mult, op1=ALU.add)
                    if which == 0:
                        I("act", "activation", [facc], [sgate], out=sgate[:], in_=facc[:], func=AF.Silu)
                    else:
                        I("dve", "tensor_tensor", [facc, sgate], [(aT, (j, None))], out=aT[:, j, :], in0=facc[:], in1=sgate[:],
                          op=ALU.mult)
        resid_phase(xsrc, l, "w_down", [(aT, NFF)])


    cur_x = [x_in]
    try:
        for l in range(depth):
            layer_body(l, cur_x[0])
            cur_x[0] = xres
    except _Stop:
        pass
    xsrc = cur_x[0]

    mk.dma("sp", gfin[:], P["final_norm"][:].unsqueeze(0).broadcast_to([128, D]), [P["final_norm"]], [gfin], "gf")
    for i in range(NT):
        t = xt[i % 2]
        mk.dma("sp", t[:], xsrc[i * 128:(i + 1) * 128, :], [(xsrc, (i,))], [t], ("xt", i % 2))
        rinv, skey = rms_tile_stats(t[:], [t], D)
        o_ = ofin[i % 2]
        I("dve", "scalar_tensor_tensor", [t, skey, gfin], [o_], out=o_[:], in0=t[:], scalar=rinv, in1=gfin[:],
          op0=ALU.mult, op1=ALU.mult)
        mk.dma("sp", out_d[i * 128:(i + 1) * 128, :], o_[:], [o_], [(out_d, (i,))], ("of", i % 2))

    mk.emit()
    nc.mk_bufs = {b.name: b.t.name for b in all_bufs}
    return nc


_CACHE = {}


def _consts():
    k = np.arange(128)
    U = (k[:, None] <= k[None, :]).astype(np.float32)
    SL = (k[:, None] > k[None, :]).astype(np.float32)
    rope = np.zeros((128, 2), np.float32)
    inv_freq = (1.0 / (np.float32(10000.0) ** (np.arange(0, ROPE, 2, dtype=np.float32) / np.float32(ROPE)))).astype(np.float32)
    rope[64:80, 0] = inv_freq
    rope[96:112, 0] = inv_freq
    rope[64:80, 1] = 1.0
    rope[96:112, 1] = -1.0
    return {"c_ident": np.eye(128, dtype=np.float32), "c_U": U, "c_SL": SL, "c_rope": rope}


def kernel(**inputs):
    return run_depth(inputs, DEPTH)


def run_depth(inputs, depth, trace=False, ncores=N_CORES, stop_after=None):
    key = (depth, stop_after)
    if key not in _CACHE:
        _CACHE[key] = build_program(depth, stop_after=stop_after)
    nc = _CACHE[key]
    consts = _consts()
    x = np.ascontiguousarray(np.asarray(inputs["x"], dtype=np.float32))
    mem = np.ascontiguousarray(np.asarray(inputs["mem"], dtype=np.float32))
    pos = np.ascontiguousarray(np.asarray(inputs["positions"], dtype=np.int32))
    dd = max(depth, 1)
    shared = {}
    for name, shp in PARAMS:
        a = np.asarray(inputs[name], dtype=np.float32)
        if len(shp) > 1 and dd != DEPTH:
            a = a[:dd]
        shared[name] = np.ascontiguousarray(a)
    shared.update(consts)
    in_maps = []
    for b in range(ncores):
        m = dict(shared)
        m["x"] = x[b]
        m["mem"] = mem[b]
        m["positions"] = pos[b:b + 1]
        in_maps.append(m)
    res = run_bass_kernel_spmd(nc, in_maps, core_ids=list(range(ncores)), trace=trace)
    if trace:
        print("exec_time_ns", res.exec_time_ns)
    out = np.stack([np.asarray(r["out"]) for r in res.results], axis=0).astype(np.float32)
    return out
```

```python
import contextlib
import math
import numpy as np

import concourse.bass as bass
import concourse.mybir as mybir
from concourse.alu_op_type import AluOpType as ALU
from concourse.bass_utils import run_bass_kernel_spmd

DT = mybir.dt
AF = mybir.ActivationFunctionType
F32, BF16, I32 = DT.float32, DT.bfloat16, DT.int32

DEPTH = 4
S = 2048
D = 1024
NT = S // 128
KC = D // 128
MEM = 256
EPS = 1e-6
NH = 16
HD = 64
NG = 4
NST = 128
Q_LORA, KV_LORA, ROPE = 384, 256, 32
D_IN = 3760
O_Z, O_XBC, O_DT, O_CQ, O_CKV, O_KR = 0, 1024, 3072, 3088, 3472, 3728
D_FF = 2816
NFF = D_FF // 128
N_CORES = 8

ENGS = ("sp", "pe", "dve", "act", "pool")


class Op:
    __slots__ = ("eng", "fn", "deps", "signal", "sigval", "dma_sem", "dma_val", "id")


class Buf:
    def __init__(self, name, t):
        self.name = name
        self.t = t
        self.writers = []
        self.readers = []
        self.overlaps = []
        self.excl = False

    def __getitem__(self, idx):
        return self.t[idx]


def _conf(a, b):
    if a is None or b is None:
        return True
    for x, y in zip(a, b):
        if x is not None and y is not None and x != y:
            return False
    return True


def _covers(a, b):
    if a is None:
        return True
    if b is None:
        return False
    for x, y in zip(a, b):
        if x is not None and x != y:
            return False
    return True


class MK:
    def __init__(self, nc):
        self.nc = nc
        self.streams = {e: [] for e in ENGS}
        self.eng_sem = {}
        self.dma_sems = {}
        self.nops = 0

    def sbuf_at(self, name, shape, dtype, offset):
        return Buf(name, self.nc.alloc_sbuf_tensor_at(name, list(shape), dtype, offset=offset))

    def psum(self, name, shape, dtype=F32):
        b = Buf(name, self.nc.alloc_psum_tensor(name, list(shape), dtype))
        b.excl = True
        return b

    def dram(self, name, shape, dtype, kind="Internal"):
        return Buf(name, self.nc.dram_tensor(name, list(shape), dtype, kind=kind))

    def overlap(self, a, b):
        a.overlaps.append(b)
        b.overlaps.append(a)

    @staticmethod
    def _norm(lst):
        out = []
        for x in lst:
            if isinstance(x, Buf):
                out.append((x, None))
            else:
                out.append((x[0], tuple(x[1]) if x[1] is not None else None))
        return out

    @staticmethod
    def _absorb(b):
        for y in b.overlaps:
            if y.writers or y.readers:
                ops = [p for _, p in y.writers] + [p for _, p in y.readers]
                y.writers = []
                y.readers = []
                for p in ops:
                    b.writers.append((None, p))
                latest = {}
                keep = []
                for k2, p in b.writers:
                    if k2 is None and p.dma_sem is None:
                        if p.eng not in latest or latest[p.eng].id < p.id:
                            latest[p.eng] = p
                    else:
                        keep.append((k2, p))
                b.writers = keep + [(None, p) for p in latest.values()]

    def op(self, eng, fn, reads=(), writes=(), is_dma=False):
        o = Op()
        o.eng, o.fn, o.deps = eng, fn, []
        o.signal, o.sigval, o.dma_sem, o.dma_val = False, None, None, None
        o.id = self.nops
        self.nops += 1
        reads = self._norm(reads)
        writes = self._norm(writes)
        for b, _ in reads + writes:
            if b.overlaps:
                self._absorb(b)
        seen = set()

        def add(p, raw):
            if p is o or (p.id, raw) in seen:
                return
            seen.add((p.id, raw))
            o.deps.append((p, raw))

        for b, key in reads:
            for k2, p in b.writers:
                if _conf(key, k2):
                    add(p, True)
            if b.excl:
                for k2, p in b.readers:
                    if p.eng != eng:
                        add(p, False)
        for b, key in writes:
            for k2, p in b.writers:
                if _conf(key, k2):
                    add(p, False)
            for k2, p in b.readers:
                if _conf(key, k2):
                    add(p, False)
        for b, key in reads:
            if not is_dma:
                b.readers = [(k2, p) for (k2, p) in b.readers
                             if not (k2 == key and p.eng == eng and p.dma_sem is None)]
            b.readers.append((key, o))
        for b, key in writes:
            b.writers = [(k2, p) for (k2, p) in b.writers if not _covers(key, k2)]
            b.readers = [(k2, p) for (k2, p) in b.readers if (not _covers(key, k2)) or p is o]
            b.writers.append((key, o))
        self.streams[eng].append(o)
        return o

    def I(self, eng, name, reads, writes, *args, **kw):
        def fn(e):
            return getattr(e, name)(*args, **kw)
        return self.op(eng, fn, reads, writes)

    def dma(self, queue, out_ap, in_ap, reads, writes, sem_key, **kw):
        ent = self.dma_sems.get(sem_key)
        if ent is None:
            ent = [None, 0, None]
            self.dma_sems[sem_key] = ent
        prev = ent[2]

        def fn(e):
            return e.dma_start(out=out_ap, in_=in_ap, **kw)

        o = self.op(queue, fn, reads, writes, is_dma=True)
        ent[1] += 16
        o.dma_sem = sem_key
        o.dma_val = ent[1]
        if prev is not None:
            o.deps.append((prev, True))
        ent[2] = o
        return o

    def emit(self):
        nc = self.nc
        for e, ops in self.streams.items():
            for o in ops:
                for p, raw in o.deps:
                    if p.dma_sem is not None:
                        continue
                    if p.eng == o.eng:
                        if o.eng != "pe":
                            p.signal = True
                    else:
                        p.signal = True
        for e, ops in self.streams.items():
            c = 0
            for o in ops:
                if o.dma_sem is None and o.signal:
                    c += 1
                    o.sigval = c
        with contextlib.ExitStack() as st:
            for e in ENGS:
                self.eng_sem[e] = st.enter_context(nc.semaphore("sem_" + e))
            for i, (k, ent) in enumerate(self.dma_sems.items()):
                ent[0] = st.enter_context(nc.semaphore("dsem_%d" % i))
            block = st.enter_context(nc.Block())
            mk = self

            def make(ename):
                def body(eng):
                    waited = {}
                    for o in mk.streams[ename]:
                        for p, raw in o.deps:
                            if p.dma_sem is not None:
                                sem, val, key = mk.dma_sems[p.dma_sem][0], p.dma_val, ("d", p.dma_sem)
                            else:
                                if p.eng == ename and ename == "pe":
                                    continue
                                sem, val, key = mk.eng_sem[p.eng], p.sigval, ("e", p.eng)
                            if waited.get(key, 0) >= val:
                                continue
                            waited[key] = val
                            eng.wait_ge(sem, val)
                        inst = o.fn(eng)
                        if o.dma_sem is not None:
                            inst.then_inc(mk.dma_sems[o.dma_sem][0], 16)
                        elif o.signal:
                            inst.then_inc(mk.eng_sem[ename], 1)
                    for k, ent in mk.dma_sems.items():
                        if ent[2] is not None and ent[2].eng == ename and waited.get(("d", k), 0) < ent[1]:
                            eng.wait_ge(ent[0], ent[1])
                return body

            block.sync(make("sp"))
            block.tensor(make("pe"))
            block.vector(make("dve"))
            block.scalar(make("act"))
            block.gpsimd(make("pool"))


def make_params(DEPTH):
  return [
    ("norm_mix", (DEPTH, D)), ("w_in", (DEPTH, D, D_IN)), ("ssm_conv_w", (DEPTH, 4, 2048)),
    ("ssm_conv_b", (DEPTH, 2048)), ("dt_bias", (DEPTH, NH)), ("a_log", (DEPTH, NH)),
    ("d_skip", (DEPTH, NH)), ("ssm_norm", (DEPTH, D)), ("q_norm", (DEPTH, Q_LORA)),
    ("w_uq", (DEPTH, Q_LORA, NH * 96)), ("kv_norm", (DEPTH, KV_LORA)),
    ("w_ukv", (DEPTH, KV_LORA, NH * 128)), ("attn_out_norm", (DEPTH, D)),
    ("w_out", (DEPTH, 2 * D, D)), ("norm_mem_q", (DEPTH, D)), ("norm_mem_kv", (DEPTH, D)),
    ("w_mq", (DEPTH, D, D)), ("w_mk", (DEPTH, D, D)), ("w_mv", (DEPTH, D, D)), ("w_mo", (DEPTH, D, D)),
    ("norm_ffn", (DEPTH, D)), ("w_up", (DEPTH, D, 2 * D_FF)), ("ffn_conv_w", (DEPTH, 3, 2 * D_FF)),
    ("ffn_conv_b", (DEPTH, 2 * D_FF)), ("w_down", (DEPTH, D_FF, D)), ("final_norm", (D,)),
  ]


PARAMS = make_params(DEPTH)
VEC_ROWS = [("norm_mix", 8), ("ssm_conv_w", 64), ("ssm_conv_b", 16), ("ssm_norm", 8), ("q_norm", 3),
            ("kv_norm", 2), ("attn_out_norm", 8), ("norm_mem_q", 8), ("norm_mem_kv", 8),
            ("norm_ffn", 8), ("ffn_conv_w", 132), ("ffn_conv_b", 44)]


def build_program(depth=DEPTH, debug=None, stop_after=None):
    DEPTH = max(depth, 1)
    PARAMS = make_params(DEPTH)
    nc = bass.Bass("TRN2", target_bir_lowering=False)
    mk = MK(nc)
    I = mk.I

    x_in = mk.dram("x", [S, D], F32, kind="ExternalInput")
    mem_in = mk.dram("mem", [MEM, D], F32, kind="ExternalInput")
    pos_in = mk.dram("positions", [1, S], I32, kind="ExternalInput")
    P = {}
    for name, shp in PARAMS:
        P[name] = mk.dram(name, list(shp), F32, kind="ExternalInput")
    c_ident = mk.dram("c_ident", [128, 128], F32, kind="ExternalInput")
    c_U = mk.dram("c_U", [128, 128], F32, kind="ExternalInput")
    c_SL = mk.dram("c_SL", [128, 128], F32, kind="ExternalInput")
    c_rope = mk.dram("c_rope", [128, 2], F32, kind="ExternalInput")
    out_d = mk.dram("out", [S, D], F32, kind="ExternalOutput")
    xres = mk.dram("xres", [S, D], F32)
    sz_d = mk.dram("sz_d", [S, D], BF16)
    xs_d = mk.dram("xs_d", [S, D], BF16)
    bt_d = mk.dram("bt_d", [S, 512], BF16)
    bT_d = mk.dram("bT_d", [4, 128, S], BF16)
    cT_d = mk.dram("cT_d", [4, 128, S], BF16)
    dbg = {}
    if debug:
        for nm, shp in debug.items():
            dbg[nm] = mk.dram("dbg_" + nm, list(shp), F32, kind="ExternalOutput")

    BASE = 16512
    TOP = 229344
    cur = [BASE]
    all_bufs = []

    def alloc(name, shape, dtype):
        esz = 4 if dtype in (F32, I32) else 2
        n = 1
        for s_ in shape[1:]:
            n *= s_
        nbytes = (n * esz + 31) // 32 * 32
        off = cur[0]
        cur[0] += nbytes
        assert off + nbytes <= TOP, (name, off, nbytes, TOP)
        b = mk.sbuf_at(name, shape, dtype, off)
        b.off, b.nbytes = off, nbytes
        for o in all_bufs:
            if b.off < o.off + o.nbytes and o.off < b.off + b.nbytes:
                mk.overlap(b, o)
        all_bufs.append(b)
        return b

    ident_f = alloc("ident_f", [128, 128], F32)
    ident_b = alloc("ident_b", [128, 128], BF16)
    U_f = alloc("U_f", [128, 128], F32)
    U_b = alloc("U_b", [128, 128], BF16)
    SL_f = alloc("SL_f", [128, 128], F32)
    ones_f = alloc("ones_f", [128, 128], F32)
    ones_b = alloc("ones_b", [128, 128], BF16)
    zeros_b = alloc("zeros_b", [128, 512], BF16)
    cst = alloc("cst", [128, 4], F32)
    NVR = sum(r for _, r in VEC_ROWS) * DEPTH
    NVB = (NVR + 127) // 128
    vecsT = alloc("vecsT", [128, NVB * 128], F32)
    bc16 = alloc("bc16", [128, 3, DEPTH, NH], F32)
    cs2 = alloc("cs2", [128, S], F32)
    sn2 = alloc("sn2", [128, S], F32)
    memT = alloc("memT", [128, KC, MEM], BF16)
    stat = alloc("stat", [128, 96], F32)
    WSLOT = 6144
    wslots = [alloc("wslot%d" % i, [128, WSLOT], BF16) for i in range(2)]
    T0 = cur[0]

    PB = [mk.psum("pb%d" % i, [128, 512], F32) for i in range(8)]
    bank_rr = [0]
    tbank_rr = [0]

    def nextbank():
        b = PB[bank_rr[0] % 4]
        bank_rr[0] += 1
        return b

    def nexttbank():
        b = PB[6 + tbank_rr[0] % 2]
        tbank_rr[0] += 1
        return b

    stat_rr = [0]

    def statcol(n=3):
        c = stat_rr[0]
        if c + n > 96:
            c = 0
        stat_rr[0] = c + n
        return c

    vec_r0 = {}
    r = 0
    for name, rows in VEC_ROWS:
        vec_r0[name] = r
        r += rows * DEPTH
    VR = dict(VEC_ROWS)

    def vcol(name, l, idx):
        c = vec_r0[name] + l * VR[name] + idx
        return vecsT[:, c:c + 1]

    def vcols(name, l, i0, n):
        c = vec_r0[name] + l * VR[name] + i0
        return vecsT[:, c:c + n]

    wrr = [0]

    def nextslot():
        s_ = wslots[wrr[0] % 2]
        wrr[0] += 1
        return s_

    def wload(src_ap, kcn, ncols, src_buf):
        slot = nextslot()
        assert kcn * ncols <= WSLOT
        view = slot[:, 0:kcn * ncols].rearrange("p (k n) -> p k n", k=kcn)
        mk.dma("pool", view, src_ap, [src_buf], [slot], ("w", slot.name))
        return slot, view

    def wsrc(buf, l, c0, ncols, k0=0, kcn=KC):
        return buf[l][k0 * 128:(k0 + kcn) * 128, c0:c0 + ncols].rearrange("(kc p) n -> p kc n", p=128)

    stg = alloc("stg", [128, NVB, 128], F32)
    ang = alloc("ang", [128, S], F32)
    ang2 = alloc("ang2", [128, S], F32)
    kf = alloc("kf", [128, S], F32)
    ki = alloc("ki", [128, S], I32)
    posi = alloc("posi", [128, S], I32)
    crope = alloc("crope", [128, 2], F32)
    memt = alloc("memt", [128, 2, D], F32)
    memn = alloc("memn", [128, 2, D], BF16)
    junk0 = alloc("junk0", [128, D], BF16)

    mk.dma("sp", ident_f[:], c_ident[:], [c_ident], [ident_f], "c0")
    mk.dma("sp", U_f[:], c_U[:], [c_U], [U_f], "c1")
    mk.dma("sp", SL_f[:], c_SL[:], [c_SL], [SL_f], "c2")
    mk.dma("sp", crope[:], c_rope[:], [c_rope], [crope], "c3")
    I("dve", "tensor_copy", [ident_f], [ident_b], out=ident_b[:], in_=ident_f[:])
    I("dve", "tensor_copy", [U_f], [U_b], out=U_b[:], in_=U_f[:])
    I("dve", "memset", [], [ones_f], ones_f[:], 1.0)
    I("dve", "memset", [], [ones_b], ones_b[:], 1.0)
    I("dve", "memset", [], [zeros_b], zeros_b[:], 0.0)
    I("dve", "memset", [], [cst], cst[:, 0:1], EPS)
    I("dve", "memset", [cst], [cst], cst[:, 1:2], 1.0)
    I("dve", "memset", [], [stg], stg[:], 0.0)
    I("dve", "memset", [], [cs2], cs2[:], 0.0)
    I("dve", "memset", [], [sn2], sn2[:], 0.0)
    di = 0
    for name, rows in VEC_ROWS:
        src = P[name]
        shp = dict(PARAMS)[name]
        if len(shp) == 2:
            flat = src[:].rearrange("l (c q) -> (l c) q", q=128)
        else:
            flat = src[:].rearrange("l j (c q) -> (l j c) q", q=128)
        total = rows * DEPTH
        r0 = vec_r0[name]
        done = 0
        while done < total:
            rr = r0 + done
            blk, pp = rr // 128, rr % 128
            n = min(total - done, 128 - pp)
            mk.dma("sp", stg[pp:pp + n, blk, :], flat[done:done + n, :], [src], [stg], "v%d" % (di % 4))
            di += 1
            done += n
    for blk in range(NVB):
        pb = nextbank()
        I("pe", "transpose", [stg, ident_f], [pb], out=pb[:, 0:128], in_=stg[:, blk, :], identity=ident_f[:])
        I("dve", "tensor_copy", [pb], [(vecsT, (blk,))], out=vecsT[:, blk * 128:(blk + 1) * 128], in_=pb[:, 0:128])
    for i, name in enumerate(("dt_bias", "a_log", "d_skip")):
        mk.dma("sp", bc16[:, i, :, :].rearrange("p l h -> p (l h)"),
               P[name][:].rearrange("l h -> (l h)").unsqueeze(0).broadcast_to([128, DEPTH * NH]),
               [P[name]], [bc16], "b%d" % i)
    I("act", "activation", [bc16], [bc16], out=bc16[:, 1, :, :], in_=bc16[:, 1, :, :], func=AF.Exp)
    I("dve", "tensor_scalar", [bc16], [bc16], out=bc16[:, 1, :, :], in0=bc16[:, 1, :, :], scalar1=-1.0,
      scalar2=None, op0=ALU.mult)

    R0, R1 = 64, 128
    mk.dma("sp", posi[R0:R1, :], pos_in[0:1, :].broadcast_to([64, S]), [pos_in], [posi], "c4")
    I("dve", "tensor_copy", [posi], [ang], out=ang[R0:R1, :], in_=posi[R0:R1, :])
    I("dve", "tensor_scalar", [ang, crope], [ang], out=ang[R0:R1, :], in0=ang[R0:R1, :],
      scalar1=crope[R0:R1, 0:1], scalar2=None, op0=ALU.mult)
    TWO_PI = 2.0 * math.pi
    C1 = 6.28125
    C2 = TWO_PI - C1

    def sin_of(src, dst, shift):
        I("dve", "tensor_scalar", [src], [ang2], out=ang2[R0:R1, :], in0=src[R0:R1, :], scalar1=shift,
          scalar2=None, op0=ALU.add)
        I("dve", "tensor_scalar", [ang2], [ki], out=ki[R0:R1, :], in0=ang2[R0:R1, :], scalar1=1.0 / TWO_PI,
          scalar2=None, op0=ALU.mult)
        I("dve", "tensor_copy", [ki], [kf], out=kf[R0:R1, :], in_=ki[R0:R1, :])
        I("dve", "scalar_tensor_tensor", [kf, ang2], [ang2], out=ang2[R0:R1, :], in0=kf[R0:R1, :], scalar=-C1,
          in1=ang2[R0:R1, :], op0=ALU.mult, op1=ALU.add)
        I("dve", "scalar_tensor_tensor", [kf, ang2], [ang2], out=ang2[R0:R1, :], in0=kf[R0:R1, :], scalar=-C2,
          in1=ang2[R0:R1, :], op0=ALU.mult, op1=ALU.add)
        I("dve", "tensor_scalar", [ang2], [ang2], out=ang2[R0:R1, :], in0=ang2[R0:R1, :], scalar1=math.pi,
          scalar2=-math.pi, op0=ALU.min, op1=ALU.max)
        I("act", "activation", [ang2], [dst], out=dst[R0:R1, :], in_=ang2[R0:R1, :], func=AF.Sin)

    sin_of(ang, cs2, math.pi / 2.0)
    sin_of(ang, sn2, 0.0)
    I("dve", "tensor_scalar", [sn2, crope], [sn2], out=sn2[R0:R1, :], in0=sn2[R0:R1, :],
      scalar1=crope[R0:R1, 1:2], scalar2=None, op0=ALU.mult)

    for kt in range(2):
        mk.dma("sp", memt[:, kt, :], mem_in[kt * 128:(kt + 1) * 128, :], [mem_in], [(memt, (kt,))], "m%d" % kt)
        c = statcol(3)
        I("act", "activation", [(memt, (kt,))], [junk0, (stat, (c,))], out=junk0[:], in_=memt[:, kt, :], func=AF.Square,
          accum_out=stat[:, c:c + 1])
        I("act", "activation", [(stat, (c,)), cst], [(stat, (c,))], out=stat[:, c + 1:c + 2], in_=stat[:, c:c + 1],
          func=AF.Sqrt, scale=1.0 / D, bias=cst[:, 0:1])
        I("dve", "reciprocal", [(stat, (c,))], [(stat, (c,))], out=stat[:, c + 2:c + 3], in_=stat[:, c + 1:c + 2])
        I("dve", "tensor_scalar", [(memt, (kt,)), (stat, (c,))], [(memn, (kt,))], out=memn[:, kt, :], in0=memt[:, kt, :],
          scalar1=stat[:, c + 2:c + 3], scalar2=None, op0=ALU.mult)
        tb_ = nexttbank()
        tv = tb_[:].bitcast(BF16)
        for kc in range(KC):
            I("pe", "transpose", [(memn, (kt,)), ident_b], [tb_], out=tv[:, kc * 128:(kc + 1) * 128],
              in_=memn[:, kt, kc * 128:(kc + 1) * 128], identity=ident_b[:])
        I("dve", "tensor_copy", [tb_], [memT], out=memT[:, :, kt * 128:(kt + 1) * 128],
          in_=tv[:, 0:1024].rearrange("p (k t) -> p k t", k=KC))

    cur[0] = T0
    hT = alloc("hT", [128, KC, S], BF16)
    R_A = cur[0]
    xt = [alloc("xt%d" % i, [128, D], F32) for i in range(2)]
    xn = [alloc("xn%d" % i, [128, D], BF16) for i in range(2)]
    junk = alloc("junk", [128, D], BF16)
    R_A_END = cur[0]
    rx = [alloc("rx%d" % i, [128, 256], F32) for i in range(3)]
    R_B = cur[0]

    raw = [alloc("raw%d" % i, [128, S], F32) for i in range(2)]
    acc = [alloc("acc%d" % i, [128, S], F32) for i in range(2)]
    tmpT = [alloc("tmpT%d" % i, [128, S], BF16) for i in range(2)]
    zst = [alloc("zst%d" % i, [128, 512], BF16) for i in range(2)]
    tkst = [alloc("tkst%d" % i, [128, 8, 128], BF16) for i in range(2)]
    R_X_END = cur[0]
    cqnT = alloc("cqnT", [128, 3, S], BF16)
    ckvnT = alloc("ckvnT", [128, 2, S], BF16)
    kpeT = alloc("kpeT", [128, S], BF16)
    dtt = alloc("dtt", [128, NT, NH], F32)
    adt = alloc("adt", [128, NT, NH], F32)
    ssqa = alloc("ssqa", [128, NT], F32)
    rsta = alloc("rsta", [128, NT], F32)
    R_Y = cur[0]
    sqb = [alloc("sqb%d" % i, [128, S], BF16) for i in range(2)]
    rstd = alloc("rstd", [128, S], F32)
    wkr = alloc("wkr", [128, KC, 128], BF16)
    rtmp = alloc("rtmp", [128, 2, 512], F32)
    cur[0] = R_Y
    yattT = alloc("yattT", [128, KC, S], BF16)
    MIX_END = cur[0]

    cur[0] = R_B
    st_f = alloc("st_f", [128, D], F32)
    st_b = alloc("st_b", [128, D], BF16)
    adU = alloc("adU", [128, NH, 128], F32)
    Eexp = [alloc("Eexp%d" % i, [128, 512], F32) for i in range(2)]
    MT = [alloc("MT%d" % i, [128, 512], BF16) for i in range(2)]
    cbm = [alloc("cbm%d" % i, [128, 128], F32) for i in range(2)]
    xsc = [alloc("xsc%d" % i, [128, D], BF16) for i in range(2)]
    btc = [alloc("btc%d" % i, [128, 512], BF16) for i in range(2)]
    bTc = [alloc("bTc%d" % i, [128, 4, 128], BF16) for i in range(2)]
    cTc = [alloc("cTc%d" % i, [128, 4, 128], BF16) for i in range(2)]
    szc = [alloc("szc%d" % i, [128, D], BF16) for i in range(2)]
    assert cur[0] <= R_X_END, (cur[0], R_X_END)
    cur[0] = R_Y
    xd = alloc("xd", [128, D], BF16)
    xdd = alloc("xdd", [128, D], BF16)
    xsk = alloc("xsk", [128, D], F32)
    yf = alloc("yf", [128, D], F32)
    yo = alloc("yo", [128, D], F32)
    ygb = alloc("ygb", [128, D], BF16)
    sm16 = alloc("sm16", [128, 8, NH], F32)
    assert cur[0] <= MIX_END, (cur[0], MIX_END)
    cur[0] = R_B
    QT = [alloc("QT%d" % i, [128, S], BF16) for i in range(2)]
    KT = [alloc("KT%d" % i, [128, S], BF16) for i in range(2)]
    Vh = [alloc("Vh%d" % i, [128, NT, 96], BF16) for i in range(2)]
    PT = [alloc("PT%d" % i, [128, 512], BF16) for i in range(3)]
    qtmp = alloc("qtmp", [128, 4, 512], F32)
    rsum = alloc("rsum", [128, 8], F32)
    onrm = alloc("onrm", [128, 4, HD], F32)
    osq = alloc("osq", [128, 4, HD], F32)
    ypair = [alloc("ypair%d" % i, [128, NT, 128], BF16) for i in range(2)]
    assert cur[0] <= R_X_END, (cur[0], R_X_END)

    cur[0] = R_B
    qT = alloc("qT", [128, KC, S], BF16)
    oT = alloc("oT", [128, KC, S], BF16)
    mnT = alloc("mnT", [128, KC, MEM], BF16)
    KmT = alloc("KmT", [128, KC, MEM], BF16)
    Vm = alloc("Vm", [128, 2, D], BF16)
    PTm = [alloc("PTm%d" % i, [128, 2, 512], BF16) for i in range(2)]
    rinvb = [alloc("rinvb%d" % i, [128, 512], F32) for i in range(2)]

    cur[0] = R_A
    fraw = alloc("fraw", [128, S], F32)
    assert cur[0] <= R_A_END
    cur[0] = R_B
    aT = alloc("aT", [128, NFF, S], BF16)
    facc = alloc("facc", [128, S], F32)
    sgate = alloc("sgate", [128, S], BF16)
    cur[0] = R_B
    gfin = alloc("gfin", [128, D], F32)
    ofin = [alloc("ofin%d" % i, [128, D], F32) for i in range(2)]

    def rms_tile_stats(src_ap, src_reads, width, jbuf=None):
        jb = junk if jbuf is None else jbuf
        c = statcol(3)
        I("act", "activation", src_reads, [jb, (stat, (c,))], out=jb[:, 0:width], in_=src_ap, func=AF.Square,
          accum_out=stat[:, c:c + 1])
        I("act", "activation", [(stat, (c,)), cst], [(stat, (c,))], out=stat[:, c + 1:c + 2], in_=stat[:, c:c + 1],
          func=AF.Sqrt, scale=1.0 / width, bias=cst[:, 0:1])
        I("dve", "reciprocal", [(stat, (c,))], [(stat, (c,))], out=stat[:, c + 2:c + 3], in_=stat[:, c + 1:c + 2])
        return stat[:, c + 2:c + 3], (stat, (c,))

    def transpose_tile_to(src_buf, dstT, dkey, tile_i, gain_name, l):
        tb_ = nexttbank()
        tv = tb_[:].bitcast(BF16)
        for kc in range(KC):
            I("pe", "transpose", [src_buf, ident_b], [tb_], out=tv[:, kc * 128:(kc + 1) * 128],
              in_=src_buf[:, kc * 128:(kc + 1) * 128], identity=ident_b[:])
        g = vcols(gain_name, l, 0, KC).unsqueeze(2).broadcast_to([128, KC, 128])
        I("dve", "tensor_tensor", [tb_, vecsT], [(dstT, dkey)],
          out=dstT[:, :, tile_i * 128:(tile_i + 1) * 128],
          in0=tv[:, 0:1024].rearrange("p (k t) -> p k t", k=KC), in1=g, op=ALU.mult)

    def norm_phase(xsrc, l, gain_name):
        for i in range(NT):
            t = xt[i % 2]
            mk.dma("sp", t[:], xsrc[i * 128:(i + 1) * 128, :], [(xsrc, (i,))], [t], ("xt", i % 2))
            rinv, skey = rms_tile_stats(t[:], [t], D)
            n = xn[i % 2]
            I("dve", "tensor_scalar", [t, skey], [n], out=n[:], in0=t[:], scalar1=rinv, scalar2=None, op0=ALU.mult)
            transpose_tile_to(n, hT, (None, i // 4), i, gain_name, l)

    def resid_phase(xsrc, l, wname, parts, scaled_last=None):
        NB = 256
        nk_tot = sum(nk for _, nk in parts)
        for cb in range(D // NB):
            slot, wv = wload(wsrc(P[wname], l, cb * NB, NB, 0, nk_tot), nk_tot, NB, P[wname])
            for i in range(NT):
                u = cb * NT + i
                t = rx[u % 3]
                mk.dma("sp", t[:], xsrc[i * 128:(i + 1) * 128, cb * NB:(cb + 1) * NB], [(xsrc, (i,))], [t], ("rx", u % 3))
                k0 = 0
                for pi, (actT, nk) in enumerate(parts):
                    last = (pi == len(parts) - 1)
                    sep = last and scaled_last is not None
                    if pi == 0 or sep:
                        pb = nextbank()
                    for kc in range(nk):
                        st_flag = (kc == 0) if (pi == 0 or sep) else False
                        sp_flag = (kc == nk - 1) and (last or (scaled_last is not None and pi == len(parts) - 2))
                        I("pe", "matmul", [(actT, (None, i // 4)), slot], [pb], pb[:, 0:NB],
                          lhsT=actT[:, kc, i * 128:(i + 1) * 128], rhs=wv[:, k0 + kc, :], start=st_flag, stop=sp_flag)
                    k0 += nk
                    if sep:
                        I("dve", "scalar_tensor_tensor", [pb, t, scaled_last], [t], out=t[:], in0=pb[:, 0:NB],
                          scalar=scaled_last[:, i:i + 1], in1=t[:], op0=ALU.mult, op1=ALU.add)
                    elif last or (scaled_last is not None and pi == len(parts) - 2):
                        I("dve", "tensor_tensor", [pb, t], [t], out=t[:], in0=pb[:, 0:NB], in1=t[:], op=ALU.add)
                mk.dma("sp", xres[i * 128:(i + 1) * 128, cb * NB:(cb + 1) * NB], t[:], [t], [(xres, (i,))], ("rxo", u % 3))

    def proj_fm(wv, col0, m, nk, actT, slot, cb_fn):
        for tb in range(4):
            pb = nextbank()
            for kc in range(nk):
                I("pe", "matmul", [slot, (actT, (None, tb))], [pb], pb[0:m, :],
                  lhsT=wv[:, kc, col0:col0 + m], rhs=actT[:, kc, tb * 512:(tb + 1) * 512],
                  start=(kc == 0), stop=(kc == nk - 1))
            cb_fn(tb, pb)

    def rope_rows(ps, dst, dkey, ts_):
        I("dve", "tensor_tensor", [ps, cs2], [qtmp], out=qtmp[64:96, 0, :], in0=ps[64:96, :], in1=cs2[64:96, ts_], op=ALU.mult)
        I("dve", "tensor_tensor", [ps, sn2], [qtmp], out=qtmp[64:96, 1, :], in0=ps[96:128, :], in1=sn2[96:128, ts_], op=ALU.mult)
        I("dve", "tensor_tensor", [qtmp], [(dst, dkey)], out=dst[64:96, ts_], in0=qtmp[64:96, 0, :], in1=qtmp[64:96, 1, :],
          op=ALU.add)
        I("dve", "tensor_tensor", [ps, cs2], [qtmp], out=qtmp[96:128, 2, :], in0=ps[96:128, :], in1=cs2[96:128, ts_], op=ALU.mult)
        I("dve", "tensor_tensor", [ps, sn2], [qtmp], out=qtmp[96:128, 3, :], in0=ps[64:96, :], in1=sn2[64:96, ts_], op=ALU.mult)
        I("dve", "tensor_tensor", [qtmp], [(dst, dkey)], out=dst[96:128, ts_], in0=qtmp[96:128, 2, :], in1=qtmp[96:128, 3, :],
          op=ALU.add)

    xsrc = x_in
    STAGES = ["norm", "z", "xbc", "prep", "ssd", "mla", "wout", "memattn", "ffn"]
    sub_stop = 99
    if stop_after and ":" in stop_after:
        stop_after, ss_ = stop_after.split(":")
        sub_stop = int(ss_)
    stop_idx = STAGES.index(stop_after) if stop_after else 99

    class _Stop(Exception):
        pass

    def layer_body(l, xsrc):
        norm_phase(xsrc, l, "norm_mix")
        if stop_idx <= 0:
            raise _Stop()

        for cb in range(2):
            slot, wv = wload(wsrc(P["w_in"], l, O_Z + cb * 512, 512), KC, 512, P["w_in"])
            for i in range(NT):
                pb = nextbank()
                for kc in range(KC):
                    I("pe", "matmul", [(hT, (None, i // 4)), slot], [pb], pb[:, :],
                      lhsT=hT[:, kc, i * 128:(i + 1) * 128], rhs=wv[:, kc, :], start=(kc == 0), stop=(kc == KC - 1))
                u = cb * NT + i
                z = zst[u % 2]
                I("act", "activation", [pb], [z], out=z[:], in_=pb[:, :], func=AF.Silu)
                mk.dma("sp", sz_d[i * 128:(i + 1) * 128, cb * 512:(cb + 1) * 512], z[:], [z], [(sz_d, (i,))], ("zst", u % 2))

        if stop_idx <= 1:
            raise _Stop()
        for cg in range(8):
            slot, wv = wload(wsrc(P["w_in"], l, O_XBC + cg * 256, 256), KC, 256, P["w_in"])
            for sub in range(2):
                c = cg * 2 + sub
                rw = raw[c % 2]
                ac = acc[c % 2]

                def evac(tb, pb, rw=rw):
                    I("act", "copy", [pb], [(rw, (tb,))], out=rw[:, tb * 512:(tb + 1) * 512], in_=pb[:, :])
                proj_fm(wv, sub * 128, 128, KC, hT, slot, evac)
                w3, w2, w1, w0 = (vcol("ssm_conv_w", l, j * 16 + c) for j in (3, 2, 1, 0))
                bb = vcol("ssm_conv_b", l, c)
                I("dve", "tensor_scalar", [rw, vecsT], [ac], out=ac[:], in0=rw[:], scalar1=w3, scalar2=bb,
                  op0=ALU.mult, op1=ALU.add)
                for sh, wj in ((1, w2), (2, w1), (3, w0)):
                    I("dve", "scalar_tensor_tensor", [rw, ac, vecsT], [ac], out=ac[:, sh:S], in0=rw[:, 0:S - sh],
                      scalar=wj, in1=ac[:, sh:S], op0=ALU.mult, op1=ALU.add)
                tt = tmpT[c % 2]
                I("act", "activation", [ac], [tt], out=tt[:], in_=ac[:], func=AF.Silu)
                if c < 12:
                    for half in range(2):
                        tk = tkst[half]
                        tb_ = nexttbank()
                        tv = tb_[:].bitcast(BF16)
                        for j in range(8):
                            i = half * 8 + j
                            I("pe", "transpose", [tt, ident_b], [tb_], out=tv[:, j * 128:(j + 1) * 128],
                              in_=tt[:, i * 128:(i + 1) * 128], identity=ident_b[:])
                        I("dve", "tensor_copy", [tb_], [tk], out=tk[:], in_=tv[:, 0:1024].rearrange("p (j t) -> p j t", j=8))
                        rows = slice(half * 1024, (half + 1) * 1024)
                        if c < 8:
                            dst = xs_d[rows, c * 128:(c + 1) * 128].rearrange("(i p) f -> p i f", p=128)
                            mk.dma("sp", dst, tk[:], [tk], [xs_d], ("tk", half))
                        else:
                            dst = bt_d[rows, (c - 8) * 128:(c - 7) * 128].rearrange("(i p) f -> p i f", p=128)
                            mk.dma("sp", dst, tk[:], [tk], [bt_d], ("tk", half))
                if 8 <= c < 12:
                    mk.dma("sp", bT_d[c - 8], tt[:], [tt], [bT_d], ("tT", c % 2))
                if c >= 12:
                    mk.dma("sp", cT_d[c - 12], tt[:], [tt], [cT_d], ("tT", c % 2))

        if stop_idx <= 2:
            raise _Stop()
        slot = nextslot()
        wv = slot[:, 0:KC * 64].rearrange("p (k n) -> p k n", k=KC)
        mk.dma("pool", wv[:, :, 0:NH], wsrc(P["w_in"], l, O_DT, NH), [P["w_in"]], [slot], ("w", slot.name))
        import os
        DTV = int(os.environ.get("MK_DT", "9"))
        for i in range(NT if DTV >= 2 else 0):
            pb = nextbank()
            for kc in range(KC):
                I("pe", "matmul", [(hT, (None, i // 4)), slot], [pb], pb[:, 0:NH],
                  lhsT=hT[:, kc, i * 128:(i + 1) * 128], rhs=wv[:, kc, 0:NH], start=(kc == 0), stop=(kc == KC - 1))
            I("dve", "tensor_tensor", [pb, bc16], [(dtt, (i,))], out=dtt[:, i, :], in0=pb[:, 0:NH], in1=bc16[:, 0, l, :],
              op=ALU.add)
            if DTV >= 3:
                I("act", "activation", [(dtt, (i,))], [(dtt, (i,))], out=dtt[:, i, :], in_=dtt[:, i, :], func=AF.Exp)
            if DTV >= 4:
                I("act", "activation", [(dtt, (i,)), cst], [(dtt, (i,))], out=dtt[:, i, :], in_=dtt[:, i, :], func=AF.Ln,
                  bias=cst[:, 1:2], scale=1.0)
            if DTV >= 5:
                I("dve", "tensor_tensor", [(dtt, (i,)), bc16], [(adt, (i,))], out=adt[:, i, :], in0=dtt[:, i, :],
                  in1=bc16[:, 1, l, :], op=ALU.mult)

        if stop_idx == 3 and sub_stop <= 0:
            raise _Stop()

        def latent(wv, slot, col0, nch, dstT, gname, width):
            for c in range(nch):
                sq = sqb[c % 2]

                def evac(tb, pb, c=c, sq=sq):
                    I("act", "activation", [pb], [(sq, (tb,))], out=sq[:, tb * 512:(tb + 1) * 512], in_=pb[:, :],
                      func=AF.Square)
                    I("dve", "tensor_scalar", [pb, vecsT], [(dstT, (c, tb))], out=dstT[:, c, tb * 512:(tb + 1) * 512],
                      in0=pb[:, :], scalar1=vcol(gname, l, c), scalar2=None, op0=ALU.mult)
                proj_fm(wv, col0 + c * 128, 128, KC, hT, slot, evac)
                for tb in range(4):
                    pb = nextbank()
                    I("pe", "matmul", [(sq, (tb,)), ones_b], [pb], pb[:, :], lhsT=ones_b[:],
                      rhs=sq[:, tb * 512:(tb + 1) * 512], start=True, stop=True)
                    if c == 0:
                        I("dve", "tensor_copy", [pb], [(rstd, (tb,))], out=rstd[:, tb * 512:(tb + 1) * 512], in_=pb[:, :])
                    else:
                        I("dve", "tensor_tensor", [pb, (rstd, (tb,))], [(rstd, (tb,))],
                          out=rstd[:, tb * 512:(tb + 1) * 512], in0=pb[:, :], in1=rstd[:, tb * 512:(tb + 1) * 512],
                          op=ALU.add)
            I("act", "activation", [rstd, cst], [rstd], out=rstd[:], in_=rstd[:], func=AF.Sqrt, scale=1.0 / width,
              bias=cst[:, 0:1])
            I("dve", "reciprocal", [rstd], [rstd], out=rstd[:], in_=rstd[:])
            for c in range(nch):
                I("dve", "tensor_tensor", [(dstT, (c, None)), rstd], [(dstT, (c, None))], out=dstT[:, c, :],
                  in0=dstT[:, c, :], in1=rstd[:], op=ALU.mult)

        slot, wv = wload(wsrc(P["w_in"], l, O_CQ, 384), KC, 384, P["w_in"])
        latent(wv, slot, 0, 3, cqnT, "q_norm", Q_LORA)
        if stop_idx == 3 and sub_stop <= 1:
            raise _Stop()
        slot = nextslot()
        wv = slot[:, 0:KC * 64].rearrange("p (k n) -> p k n", k=KC)
        mk.dma("pool", wv[:, :, 0:ROPE], wsrc(P["w_in"], l, O_KR, ROPE), [P["w_in"]], [slot], ("w", slot.name))
        I("dve", "memset", [], [wkr], wkr[:], 0.0)
        I("dve", "tensor_copy", [slot, wkr], [wkr], out=wkr[:, :, 64:80], in_=wv[:, :, 0:16])
        I("dve", "tensor_copy", [slot, wkr], [wkr], out=wkr[:, :, 96:112], in_=wv[:, :, 16:32])
        slot, wv = wload(wsrc(P["w_in"], l, O_CKV, 256), KC, 256, P["w_in"])
        if stop_idx == 3 and sub_stop <= 2:
            raise _Stop()
        latent(wv, slot, 0, 2, ckvnT, "kv_norm", KV_LORA)
        if stop_idx == 3 and sub_stop <= 3:
            raise _Stop()
        for tb in range(4):
            pa = nextbank()
            for kc in range(KC):
                I("pe", "matmul", [wkr, (hT, (None, tb))], [pa], pa[:, :], lhsT=wkr[:, kc, :],
                  rhs=hT[:, kc, tb * 512:(tb + 1) * 512], start=(kc == 0), stop=(kc == KC - 1))
            rope_rows(pa, kpeT, (tb,), slice(tb * 512, (tb + 1) * 512))

        if stop_idx <= 3:
            raise _Stop()
        I("dve", "memset", [], [st_f], st_f[:], 0.0)
        I("dve", "memset", [], [st_b], st_b[:], 0.0)
        for c in range(NT):
            cs_ = slice(c * 128, (c + 1) * 128)
            xs_c, bt_c, bT_c, cT_c, sz_c = xsc[c % 2], btc[c % 2], bTc[c % 2], cTc[c % 2], szc[c % 2]
            mk.dma("sp", xs_c[:], xs_d[cs_, :], [xs_d], [xs_c], ("xsc", c % 2))
            mk.dma("sp", bt_c[:], bt_d[cs_, :], [bt_d], [bt_c], ("btc", c % 2))
            mk.dma("sp", bT_c[:], bT_d[:, :, cs_].rearrange("g n t -> n g t"), [bT_d], [bT_c], ("bTc", c % 2))
            mk.dma("sp", cT_c[:], cT_d[:, :, cs_].rearrange("g n t -> n g t"), [cT_d], [cT_c], ("cTc", c % 2))
            mk.dma("sp", sz_c[:], sz_d[cs_, :], [(sz_d, (c,))], [sz_c], ("szc", c % 2))
            I("pool", "tensor_tensor", [(adt, (c,)), U_f], [adU], out=adU[:],
              in0=adt[:, c, :].unsqueeze(2).broadcast_to([128, NH, 128]),
              in1=U_f[:].unsqueeze(1).broadcast_to([128, NH, 128]), op=ALU.mult)
            pc = nextbank()
            I("pe", "matmul", [U_f, (adt, (c,))], [pc], pc[:, 0:NH], lhsT=U_f[:], rhs=adt[:, c, :], start=True, stop=True)
            I("pe", "matmul", [ones_f, (adt, (c,))], [pc], pc[:, 64:64 + NH], lhsT=ones_f[:], rhs=adt[:, c, :],
              start=True, stop=True, skip_group_check=True)
            I("dve", "tensor_copy", [pc], [sm16], out=sm16[:, 3:5, :], in_=pc[:, 0:128].rearrange("p (a h) -> p a h", a=2)[:, :, 0:NH])
            I("dve", "tensor_tensor", [sm16], [sm16], out=sm16[:, 5, :], in0=sm16[:, 4, :], in1=sm16[:, 3, :], op=ALU.subtract)
            I("act", "activation", [sm16], [sm16], out=sm16[:, 0:3, :], in_=sm16[:, 3:6, :], func=AF.Exp)
            x3 = xs_c[:].rearrange("p (h d) -> p h d", h=NH)
            I("dve", "tensor_tensor", [xs_c, (dtt, (c,))], [xd], out=xd[:].rearrange("p (h d) -> p h d", h=NH), in0=x3,
              in1=dtt[:, c, :].unsqueeze(2).broadcast_to([128, NH, HD]), op=ALU.mult)
            I("dve", "tensor_tensor", [xd, sm16], [xdd], out=xdd[:].rearrange("p (h d) -> p h d", h=NH),
              in0=xd[:].rearrange("p (h d) -> p h d", h=NH),
              in1=sm16[:, 2, :].unsqueeze(2).broadcast_to([128, NH, HD]), op=ALU.mult)
            I("pool", "tensor_tensor", [xs_c, bc16], [xsk], out=xsk[:].rearrange("p (h d) -> p h d", h=NH), in0=x3,
              in1=bc16[:, 2, l, :].unsqueeze(2).broadcast_to([128, NH, HD]), op=ALU.mult)
            for hh in range(2):
                hs = slice(hh * 512, (hh + 1) * 512)
                py, po = PB[4], PB[5]
                for g in (2 * hh, 2 * hh + 1):
                    pd = nextbank()
                    I("pe", "matmul", [SL_f, adU], [pd], pd[:, :], lhsT=SL_f[:],
                      rhs=adU[:, g * 4:(g + 1) * 4, :].rearrange("p h l -> p (h l)"), start=True, stop=True)
                    E = Eexp[g % 2]
                    I("act", "activation", [pd], [E], out=E[:], in_=pd[:, :], func=AF.Exp)
                    pcb = nextbank()
                    I("pe", "matmul", [bT_c, cT_c], [pcb], pcb[:, 0:128], lhsT=bT_c[:, g, :], rhs=cT_c[:, g, :],
                      start=True, stop=True)
                    cb_ = cbm[g % 2]
                    I("dve", "tensor_tensor", [pcb, U_f], [cb_], out=cb_[:], in0=pcb[:, 0:128], in1=U_f[:], op=ALU.mult)
                    M = MT[g % 2]
                    I("dve", "tensor_tensor", [E, cb_], [M], out=M[:].rearrange("p (r l) -> p r l", r=4),
                      in0=E[:].rearrange("p (r l) -> p r l", r=4), in1=cb_[:].unsqueeze(1).broadcast_to([128, 4, 128]),
                      op=ALU.mult)
                    for r_ in range(4):
                        h = g * 4 + r_
                        oc = (g % 2) * 256 + r_ * 64
                        I("pe", "matmul", [M, xd], [py], py[:, oc:oc + 64], lhsT=M[:, r_ * 128:(r_ + 1) * 128],
                          rhs=xd[:, h * HD:(h + 1) * HD], start=True, stop=True, skip_group_check=True)
                    oc = (g % 2) * 256
                    I("pe", "matmul", [cT_c, st_b], [po], po[:, oc:oc + 256], lhsT=cT_c[:, g, :],
                      rhs=st_b[:, g * 256:(g + 1) * 256], start=True, stop=True, skip_group_check=True)
                I("dve", "tensor_tensor", [po, sm16], [yo], out=yo[:, hs].rearrange("p (h d) -> p h d", h=8),
                  in0=po[:, :].rearrange("p (h d) -> p h d", h=8),
                  in1=sm16[:, 0, hh * 8:(hh + 1) * 8].unsqueeze(2).broadcast_to([128, 8, HD]), op=ALU.mult)
                I("dve", "tensor_tensor", [py, yo], [yf], out=yf[:, hs], in0=py[:, :], in1=yo[:, hs], op=ALU.add)
            I("dve", "tensor_tensor", [yf, xsk], [yf], out=yf[:], in0=yf[:], in1=xsk[:], op=ALU.add)
            I("dve", "tensor_tensor", [st_f, sm16], [st_f], out=st_f[:].rearrange("p (h d) -> p h d", h=NH),
              in0=st_f[:].rearrange("p (h d) -> p h d", h=NH),
              in1=sm16[:, 1, :].unsqueeze(2).broadcast_to([128, NH, HD]), op=ALU.mult)
            for hh in range(2):
                hs = slice(hh * 512, (hh + 1) * 512)
                ps_ = nextbank()
                for g in (2 * hh, 2 * hh + 1):
                    oc = (g % 2) * 256
                    I("pe", "matmul", [bt_c, xdd], [ps_], ps_[:, oc:oc + 256], lhsT=bt_c[:, g * 128:(g + 1) * 128],
                      rhs=xdd[:, g * 256:(g + 1) * 256], start=True, stop=True, skip_group_check=True)
                I("dve", "tensor_tensor", [ps_, st_f], [st_f], out=st_f[:, hs], in0=ps_[:, :], in1=st_f[:, hs], op=ALU.add)
            I("act", "copy", [st_f], [st_b], out=st_b[:], in_=st_f[:])
            I("dve", "tensor_tensor", [yf, sz_c], [yf], out=yf[:], in0=yf[:], in1=sz_c[:], op=ALU.mult)
            rinv, skey = rms_tile_stats(yf[:], [yf], D)
            I("dve", "tensor_scalar", [yf, skey], [ygb], out=ygb[:], in0=yf[:], scalar1=rinv, scalar2=None, op0=ALU.mult)
            transpose_tile_to(ygb, hT, (None, c // 4), c, "ssm_norm", l)

        if stop_idx <= 4:
            raise _Stop()
        wqs_ = nextslot()
        wq4 = wqs_[:, 0:3 * NH * 128].rearrange("p (k h d) -> p k h d", k=3, h=NH)
        I("dve", "memset", [], [wqs_], wqs_[:], 0.0)
        for kc in range(3):
            s4 = P["w_uq"][l][kc * 128:(kc + 1) * 128, :].rearrange("p (h d) -> p h d", h=NH)
            mk.dma("pool", wq4[:, kc, :, 0:80], s4[:, :, 0:80], [P["w_uq"]], [wqs_], ("w", wqs_.name))
            mk.dma("pool", wq4[:, kc, :, 96:112], s4[:, :, 80:96], [P["w_uq"]], [wqs_], ("w", wqs_.name))
        wkvs_ = nextslot()
        wkv = wkvs_[:, 0:2 * NH * 128].rearrange("p (k n) -> p k n", k=2)
        for kc in range(2):
            mk.dma("pool", wkv[:, kc, :], P["w_ukv"][l][kc * 128:(kc + 1) * 128, :], [P["w_ukv"]], [wkvs_],
                   ("w", wkvs_.name))
        I("dve", "memset", [], [ssqa], ssqa[:], 0.0)
        scale = 96.0 ** -0.5
        def mla_proj(h):
            Qh, Kh, V_ = QT[h % 2], KT[h % 2], Vh[h % 2]
            for tb in range(4):
                ts_ = slice(tb * 512, (tb + 1) * 512)
                pa = nextbank()
                for kc in range(3):
                    I("pe", "matmul", [wqs_, (cqnT, (None, tb))], [pa], pa[:, :],
                      lhsT=wq4[:, kc, h, :], rhs=cqnT[:, kc, ts_], start=(kc == 0), stop=(kc == 2))
                I("act", "copy", [pa], [(Qh, (tb, 0))], out=Qh[0:64, ts_], in_=pa[0:64, :])
                rope_rows(pa, Qh, (tb, 1), ts_)
                pk = nextbank()
                for kc in range(2):
                    I("pe", "matmul", [wkvs_, (ckvnT, (None, tb))], [pk], pk[0:64, :],
                      lhsT=wkv[:, kc, h * 128:h * 128 + 64], rhs=ckvnT[:, kc, ts_], start=(kc == 0), stop=(kc == 1))
                I("act", "copy", [pk], [(Kh, (tb, 0))], out=Kh[0:64, ts_], in_=pk[0:64, :])
                I("pool", "tensor_copy", [(kpeT, (tb,))], [(Kh, (tb, 1))], out=Kh[64:128, ts_], in_=kpeT[64:128, ts_])
            I("pool", "memset", [], [V_], V_[:, :, 64:65], 1.0)
            for i in range(NT):
                pv = nextbank()
                for kc in range(2):
                    I("pe", "matmul", [(ckvnT, (None, i // 4)), wkvs_], [pv], pv[:, 0:64],
                      lhsT=ckvnT[:, kc, i * 128:(i + 1) * 128], rhs=wkv[:, kc, h * 128 + 64:h * 128 + 128],
                      start=(kc == 0), stop=(kc == 1))
                I("act", "copy", [pv], [V_], out=V_[:, i, 0:64], in_=pv[:, 0:64])

        def mla_attn(h):
            Qh, Kh, V_ = QT[h % 2], KT[h % 2], Vh[h % 2]
            yp = ypair[(h // 2) % 2]

            def emit_scores(qb, kt, sidx):
                q0 = max(qb * 512, kt * 128)
                q1 = (qb + 1) * 512
                n = q1 - q0
                psc = nextbank()
                I("pe", "matmul", [Kh, Qh], [psc], psc[:, 0:n], lhsT=Kh[:, kt * 128:(kt + 1) * 128],
                  rhs=Qh[:, q0:q1], start=True, stop=True)
                pt = PT[sidx % 3]
                I("act", "activation", [psc], [pt], out=pt[:, 0:n], in_=psc[:, 0:n], func=AF.Exp, scale=scale)
                if kt * 128 >= qb * 512:
                    I("pool", "tensor_tensor", [pt, U_b], [pt], out=pt[:, 0:128], in0=pt[:, 0:128], in1=U_b[:], op=ALU.mult)
                return pt, n, q0

            def emit_pv(qb, kt, pt, n, q0):
                pacc = PB[4 + (h * 4 + qb) % 2]
                nkt = (qb + 1) * 4
                if kt == 0:
                    I("pe", "matmul", [zeros_b], [pacc], pacc[:, :], lhsT=zeros_b[:, 0:128], rhs=zeros_b[:, :],
                      start=True, stop=False)
                for j in range(n // 128):
                    qt = (q0 - qb * 512) // 128 + j
                    I("pe", "matmul", [pt, V_], [pacc], pacc[:, qt * 128:qt * 128 + 65], lhsT=pt[:, j * 128:(j + 1) * 128],
                      rhs=V_[:, kt, 0:65], start=False, stop=False, skip_group_check=True)
                if kt != nkt - 1:
                    return
                I("pe", "matmul", [zeros_b], [pacc], pacc[:, :], lhsT=zeros_b[:, 0:128], rhs=zeros_b[:, :],
                  start=False, stop=True)
                a3 = pacc[:, :].rearrange("p (q e) -> p q e", q=4)
                I("dve", "reciprocal", [pacc], [rsum], out=rsum[:, 0:4], in_=a3[:, :, 64])
                I("dve", "tensor_tensor", [pacc, rsum], [onrm], out=onrm[:], in0=a3[:, :, 0:64],
                  in1=rsum[:, 0:4].unsqueeze(2).broadcast_to([128, 4, HD]), op=ALU.mult)
                I("pool", "tensor_copy", [onrm], [(yp, (qb, h % 2))],
                  out=yp[:, qb * 4:(qb + 1) * 4, (h % 2) * 64:(h % 2) * 64 + 64], in_=onrm[:])
                I("dve", "tensor_tensor", [onrm], [osq], out=osq[:], in0=onrm[:], in1=onrm[:], op=ALU.mult)
                I("dve", "tensor_reduce", [osq], [rsum], out=rsum[:, 4:8], in_=osq[:], axis=mybir.AxisListType.X, op=ALU.add)
                I("dve", "tensor_tensor", [rsum, ssqa], [ssqa], out=ssqa[:, qb * 4:(qb + 1) * 4], in0=ssqa[:, qb * 4:(qb + 1) * 4],
                  in1=rsum[:, 4:8], op=ALU.add)

            steps = [(qb, kt) for qb in range(4) for kt in range((qb + 1) * 4)]
            prev = None
            for sidx, (qb, kt) in enumerate(steps):
                cur = emit_scores(qb, kt, sidx)
                if prev is not None:
                    emit_pv(*prev)
                prev = (qb, kt) + cur
            emit_pv(*prev)
            if h % 2 == 1:
                hp = h // 2
                gcol = vcol("attn_out_norm", l, hp)
                for half in range(2):
                    tb_ = nexttbank()
                    tv = tb_[:].bitcast(BF16)
                    for j in range(8):
                        i = half * 8 + j
                        I("pe", "transpose", [yp, ident_b], [tb_], out=tv[:, j * 128:(j + 1) * 128], in_=yp[:, i, :],
                          identity=ident_b[:])
                    I("dve", "tensor_scalar", [tb_, vecsT], [(yattT, (hp, None))],
                      out=yattT[:, hp, half * 1024:(half + 1) * 1024], in0=tv[:, 0:1024], scalar1=gcol, scalar2=None,
                      op0=ALU.mult)
        mla_proj(0)
        for h in range(NH):
            if h + 1 < NH:
                mla_proj(h + 1)
            mla_attn(h)
        I("act", "activation", [ssqa, cst], [rsta], out=rsta[:], in_=ssqa[:], func=AF.Sqrt, scale=1.0 / D, bias=cst[:, 0:1])
        I("dve", "reciprocal", [rsta], [rsta], out=rsta[:], in_=rsta[:])

        if stop_idx <= 5:
            raise _Stop()
        resid_phase(xsrc, l, "w_out", [(hT, KC), (yattT, KC)], scaled_last=rsta)
        xsrc = xres
        cur_x[0] = xres

        if stop_idx <= 6:
            raise _Stop()
        norm_phase(xsrc, l, "norm_mem_q")
        I("dve", "tensor_tensor", [memT, vecsT], [mnT], out=mnT[:], in0=memT[:],
          in1=vcols("norm_mem_kv", l, 0, KC).unsqueeze(2).broadcast_to([128, KC, MEM]), op=ALU.mult)
        for cg in range(4):
            slot, wv = wload(wsrc(P["w_mk"], l, cg * 256, 256), KC, 256, P["w_mk"])
            for sub in range(2):
                dc = cg * 2 + sub
                pb = nextbank()
                for kc in range(KC):
                    I("pe", "matmul", [slot, mnT], [pb], pb[:, 0:MEM], lhsT=wv[:, kc, sub * 128:(sub + 1) * 128],
                      rhs=mnT[:, kc, :], start=(kc == 0), stop=(kc == KC - 1))
                I("act", "copy", [pb], [(KmT, (dc,))], out=KmT[:, dc, :], in_=pb[:, 0:MEM])
        for cb in range(2):
            slot, wv = wload(wsrc(P["w_mv"], l, cb * 512, 512), KC, 512, P["w_mv"])
            for kt in range(2):
                pb = nextbank()
                for kc in range(KC):
                    I("pe", "matmul", [mnT, slot], [pb], pb[:, :], lhsT=mnT[:, kc, kt * 128:(kt + 1) * 128], rhs=wv[:, kc, :],
                      start=(kc == 0), stop=(kc == KC - 1))
                I("act", "copy", [pb], [(Vm, (kt, cb))], out=Vm[:, kt, cb * 512:(cb + 1) * 512], in_=pb[:, :])
        for cg in range(4):
            slot, wv = wload(wsrc(P["w_mq"], l, cg * 256, 256), KC, 256, P["w_mq"])
            for sub in range(2):
                dc = cg * 2 + sub

                def evac(tb, pb, dc=dc):
                    I("act", "copy", [pb], [(qT, (dc, tb))], out=qT[:, dc, tb * 512:(tb + 1) * 512], in_=pb[:, :])
                proj_fm(wv, sub * 128, 128, KC, hT, slot, evac)
        mscale = 256.0 ** -0.5
        for hd in range(4):
            for tb in range(4):
                ts_ = slice(tb * 512, (tb + 1) * 512)
                ptm = PTm[(hd * 4 + tb) % 2]
                for kt in range(2):
                    psc = nextbank()
                    for d2 in range(2):
                        dc = hd * 2 + d2
                        I("pe", "matmul", [(KmT, (dc,)), (qT, (dc, tb))], [psc], psc[:, :],
                          lhsT=KmT[:, dc, kt * 128:(kt + 1) * 128], rhs=qT[:, dc, ts_], start=(d2 == 0), stop=(d2 == 1))
                    I("act", "activation", [psc], [(ptm, (kt,))], out=ptm[:, kt, :], in_=psc[:, :], func=AF.Exp, scale=mscale)
                pr = nextbank()
                for kt in range(2):
                    I("pe", "matmul", [ones_b, (ptm, (kt,))], [pr], pr[:, :], lhsT=ones_b[:], rhs=ptm[:, kt, :],
                      start=(kt == 0), stop=(kt == 1))
                rb = rinvb[(hd * 4 + tb) % 2]
                I("dve", "reciprocal", [pr], [rb], out=rb[:], in_=pr[:, :])
                for d2 in range(2):
                    dc = hd * 2 + d2
                    po_ = nextbank()
                    for kt in range(2):
                        I("pe", "matmul", [(Vm, (kt, None)), (ptm, (kt,))], [po_], po_[:, :],
                          lhsT=Vm[:, kt, dc * 128:(dc + 1) * 128], rhs=ptm[:, kt, :], start=(kt == 0), stop=(kt == 1))
                    I("dve", "tensor_tensor", [po_, rb], [(oT, (dc, tb))], out=oT[:, dc, ts_], in0=po_[:, :], in1=rb[:],
                      op=ALU.mult)
        resid_phase(xsrc, l, "w_mo", [(oT, KC)])

        if stop_idx <= 7:
            raise _Stop()
        norm_phase(xsrc, l, "norm_ffn")
        for jg in range(NFF // 2):
            slot_g, wg = wload(wsrc(P["w_up"], l, jg * 256, 256), KC, 256, P["w_up"])
            slot_v, wvv = wload(wsrc(P["w_up"], l, D_FF + jg * 256, 256), KC, 256, P["w_up"])
            for sub in range(2):
                j = jg * 2 + sub
                for which, (slot, wv_) in enumerate(((slot_g, wg), (slot_v, wvv))):
                    cidx = which * NFF + j

                    def evac(tb, pb):
                        I("act", "copy", [pb], [(fraw, (tb,))], out=fraw[:, tb * 512:(tb + 1) * 512], in_=pb[:, :])
                    proj_fm(wv_, sub * 128, 128, KC, hT, slot, evac)
                    w2, w1, w0 = (vcol("ffn_conv_w", l, jj * 44 + cidx) for jj in (2, 1, 0))
                    bb = vcol("ffn_conv_b", l, cidx)
                    I("dve", "tensor_scalar", [fraw, vecsT], [facc], out=facc[:], in0=fraw[:], scalar1=w2, scalar2=bb,
                      op0=ALU.mult, op1=ALU.add)
                    for sh, wj in ((1, w1), (2, w0)):
                        I("dve", "scalar_tensor_tensor", [fraw, facc, vecsT], [facc], out=facc[:, sh:S], in0=fraw[:, 0:S - sh],
                          scalar=wj, in1=facc[:, sh:S], op0=ALU.mult, op1=ALU.add)
                    if which == 0:
                        I("act", "activation", [facc], [sgate], out=sgate[:], in_=facc[:], func=AF.Silu)
                    else:
                        I("dve", "tensor_tensor", [facc, sgate], [(aT, (j, None))], out=aT[:, j, :], in0=facc[:], in1=sgate[:],
                          op=ALU.mult)
        resid_phase(xsrc, l, "w_down", [(aT, NFF)])


    cur_x = [x_in]
    try:
        for l in range(depth):
            layer_body(l, cur_x[0])
            cur_x[0] = xres
    except _Stop:
        pass
    xsrc = cur_x[0]

    mk.dma("sp", gfin[:], P["final_norm"][:].unsqueeze(0).broadcast_to([128, D]), [P["final_norm"]], [gfin], "gf")
    for i in range(NT):
        t = xt[i % 2]
        mk.dma("sp", t[:], xsrc[i * 128:(i + 1) * 128, :], [(xsrc, (i,))], [t], ("xt", i % 2))
        rinv, skey = rms_tile_stats(t[:], [t], D)
        o_ = ofin[i % 2]
        I("dve", "scalar_tensor_tensor", [t, skey, gfin], [o_], out=o_[:], in0=t[:], scalar=rinv, in1=gfin[:],
          op0=ALU.mult, op1=ALU.mult)
        mk.dma("sp", out_d[i * 128:(i + 1) * 128, :], o_[:], [o_], [(out_d, (i,))], ("of", i % 2))

    mk.emit()
    nc.mk_bufs = {b.name: b.t.name for b in all_bufs}
    return nc


_CACHE = {}


def _consts():
    k = np.arange(128)
    U = (k[:, None] <= k[None, :]).astype(np.float32)
    SL = (k[:, None] > k[None, :]).astype(np.float32)
    rope = np.zeros((128, 2), np.float32)
    inv_freq = (1.0 / (np.float32(10000.0) ** (np.arange(0, ROPE, 2, dtype=np.float32) / np.float32(ROPE)))).astype(np.float32)
    rope[64:80, 0] = inv_freq
    rope[96:112, 0] = inv_freq
    rope[64:80, 1] = 1.0
    rope[96:112, 1] = -1.0
    return {"c_ident": np.eye(128, dtype=np.float32), "c_U": U, "c_SL": SL, "c_rope": rope}


def kernel(**inputs):
    return run_depth(inputs, DEPTH)


def run_depth(inputs, depth, trace=False, ncores=N_CORES, stop_after=None):
    key = (depth, stop_after)
    if key not in _CACHE:
        _CACHE[key] = build_program(depth, stop_after=stop_after)
    nc = _CACHE[key]
    consts = _consts()
    x = np.ascontiguousarray(np.asarray(inputs["x"], dtype=np.float32))
    mem = np.ascontiguousarray(np.asarray(inputs["mem"], dtype=np.float32))
    pos = np.ascontiguousarray(np.asarray(inputs["positions"], dtype=np.int32))
    dd = max(depth, 1)
    shared = {}
    for name, shp in PARAMS:
        a = np.asarray(inputs[name], dtype=np.float32)
        if len(shp) > 1 and dd != DEPTH:
            a = a[:dd]
        shared[name] = np.ascontiguousarray(a)
    shared.update(consts)
    in_maps = []
    for b in range(ncores):
        m = dict(shared)
        m["x"] = x[b]
        m["mem"] = mem[b]
        m["positions"] = pos[b:b + 1]
        in_maps.append(m)
    res = run_bass_kernel_spmd(nc, in_maps, core_ids=list(range(ncores)), trace=trace)
    if trace:
        print("exec_time_ns", res.exec_time_ns)
    out = np.stack([np.asarray(r["out"]) for r in res.results], axis=0).astype(np.float32)
    return out
```

```python
import contextlib
import math
import numpy as np

import concourse.bass as bass
import concourse.mybir as mybir
from concourse.alu_op_type import AluOpType as ALU
from concourse.bass_utils import run_bass_kernel_spmd

DT = mybir.dt
AF = mybir.ActivationFunctionType
F32, BF16, I32 = DT.float32, DT.bfloat16, DT.int32

DEPTH = 4
S = 2048
D = 1024
NT = S // 128
KC = D // 128
MEM = 256
EPS = 1e-6
NH = 16
HD = 64
NG = 4
NST = 128
Q_LORA, KV_LORA, ROPE = 384, 256, 32
D_IN = 3760
O_Z, O_XBC, O_DT, O_CQ, O_CKV, O_KR = 0, 1024, 3072, 3088, 3472, 3728
D_FF = 2816
NFF = D_FF // 128
N_CORES = 8

ENGS = ("sp", "pe", "dve", "act", "pool")


class Op:
    __slots__ = ("eng", "fn", "deps", "signal", "sigval", "dma_sem", "dma_val", "id")


class Buf:
    def __init__(self, name, t):
        self.name = name
        self.t = t
        self.writers = []
        self.readers = []
        self.overlaps = []
        self.excl = False

    def __getitem__(self, idx):
        return self.t[idx]


def _conf(a, b):
    if a is None or b is None:
        return True
    for x, y in zip(a, b):
        if x is not None and y is not None and x != y:
            return False
    return True


def _covers(a, b):
    if a is None:
        return True
    if b is None:
        return False
    for x, y in zip(a, b):
        if x is not None and x != y:
            return False
    return True


class MK:
    def __init__(self, nc):
        self.nc = nc
        self.streams = {e: [] for e in ENGS}
        self.eng_sem = {}
        self.dma_sems = {}
        self.nops = 0

    def sbuf_at(self, name, shape, dtype, offset):
        return Buf(name, self.nc.alloc_sbuf_tensor_at(name, list(shape), dtype, offset=offset))

    def psum(self, name, shape, dtype=F32):
        b = Buf(name, self.nc.alloc_psum_tensor(name, list(shape), dtype))
        b.excl = True
        return b

    def dram(self, name, shape, dtype, kind="Internal"):
        return Buf(name, self.nc.dram_tensor(name, list(shape), dtype, kind=kind))

    def overlap(self, a, b):
        a.overlaps.append(b)
        b.overlaps.append(a)

    @staticmethod
    def _norm(lst):
        out = []
        for x in lst:
            if isinstance(x, Buf):
                out.append((x, None))
            else:
                out.append((x[0], tuple(x[1]) if x[1] is not None else None))
        return out

    @staticmethod
    def _absorb(b):
        for y in b.overlaps:
            if y.writers or y.readers:
                ops = [p for _, p in y.writers] + [p for _, p in y.readers]
                y.writers = []
                y.readers = []
                for p in ops:
                    b.writers.append((None, p))
                latest = {}
                keep = []
                for k2, p in b.writers:
                    if k2 is None and p.dma_sem is None:
                        if p.eng not in latest or latest[p.eng].id < p.id:
                            latest[p.eng] = p
                    else:
                        keep.append((k2, p))
                b.writers = keep + [(None, p) for p in latest.values()]

    def op(self, eng, fn, reads=(), writes=(), is_dma=False):
        o = Op()
        o.eng, o.fn, o.deps = eng, fn, []
        o.signal, o.sigval, o.dma_sem, o.dma_val = False, None, None, None
        o.id = self.nops
        self.nops += 1
        reads = self._norm(reads)
        writes = self._norm(writes)
        for b, _ in reads + writes:
            if b.overlaps:
                self._absorb(b)
        seen = set()

        def add(p, raw):
            if p is o or (p.id, raw) in seen:
                return
            seen.add((p.id, raw))
            o.deps.append((p, raw))

        for b, key in reads:
            for k2, p in b.writers:
                if _conf(key, k2):
                    add(p, True)
            if b.excl:
                for k2, p in b.readers:
                    if p.eng != eng:
                        add(p, False)
        for b, key in writes:
            for k2, p in b.writers:
                if _conf(key, k2):
                    add(p, False)
            for k2, p in b.readers:
                if _conf(key, k2):
                    add(p, False)
        for b, key in reads:
            if not is_dma:
                b.readers = [(k2, p) for (k2, p) in b.readers
                             if not (k2 == key and p.eng == eng and p.dma_sem is None)]
            b.readers.append((key, o))
        for b, key in writes:
            b.writers = [(k2, p) for (k2, p) in b.writers if not _covers(key, k2)]
            b.readers = [(k2, p) for (k2, p) in b.readers if (not _covers(key, k2)) or p is o]
            b.writers.append((key, o))
        self.streams[eng].append(o)
        return o

    def I(self, eng, name, reads, writes, *args, **kw):
        def fn(e):
            return getattr(e, name)(*args, **kw)
        return self.op(eng, fn, reads, writes)

    def dma(self, queue, out_ap, in_ap, reads, writes, sem_key, **kw):
        ent = self.dma_sems.get(sem_key)
        if ent is None:
            ent = [None, 0, None]
            self.dma_sems[sem_key] = ent
        prev = ent[2]

        def fn(e):
            return e.dma_start(out=out_ap, in_=in_ap, **kw)

        o = self.op(queue, fn, reads, writes, is_dma=True)
        ent[1] += 16
        o.dma_sem = sem_key
        o.dma_val = ent[1]
        if prev is not None:
            o.deps.append((prev, True))
        ent[2] = o
        return o

    def emit(self):
        nc = self.nc
        for e, ops in self.streams.items():
            for o in ops:
                for p, raw in o.deps:
                    if p.dma_sem is not None:
                        continue
                    if p.eng == o.eng:
                        if o.eng != "pe":
                            p.signal = True
                    else:
                        p.signal = True
        for e, ops in self.streams.items():
            c = 0
            for o in ops:
                if o.dma_sem is None and o.signal:
                    c += 1
                    o.sigval = c
        with contextlib.ExitStack() as st:
            for e in ENGS:
                self.eng_sem[e] = st.enter_context(nc.semaphore("sem_" + e))
            for i, (k, ent) in enumerate(self.dma_sems.items()):
                ent[0] = st.enter_context(nc.semaphore("dsem_%d" % i))
            block = st.enter_context(nc.Block())
            mk = self

            def make(ename):
                def body(eng):
                    waited = {}
                    for o in mk.streams[ename]:
                        for p, raw in o.deps:
                            if p.dma_sem is not None:
                                sem, val, key = mk.dma_sems[p.dma_sem][0], p.dma_val, ("d", p.dma_sem)
                            else:
                                if p.eng == ename and ename == "pe":
                                    continue
                                sem, val, key = mk.eng_sem[p.eng], p.sigval, ("e", p.eng)
                            if waited.get(key, 0) >= val:
                                continue
                            waited[key] = val
                            eng.wait_ge(sem, val)
                        inst = o.fn(eng)
                        if o.dma_sem is not None:
                            inst.then_inc(mk.dma_sems[o.dma_sem][0], 16)
                        elif o.signal:
                            inst.then_inc(mk.eng_sem[ename], 1)
                    for k, ent in mk.dma_sems.items():
                        if ent[2] is not None and ent[2].eng == ename and waited.get(("d", k), 0) < ent[1]:
                            eng.wait_ge(ent[0], ent[1])
                return body

            block.sync(make("sp"))
            block.tensor(make("pe"))
            block.vector(make("dve"))
            block.scalar(make("act"))
            block.gpsimd(make("pool"))


def make_params(DEPTH):
  return [
    ("norm_mix", (DEPTH, D)), ("w_in", (DEPTH, D, D_IN)), ("ssm_conv_w", (DEPTH, 4, 2048)),
    ("ssm_conv_b", (DEPTH, 2048)), ("dt_bias", (DEPTH, NH)), ("a_log", (DEPTH, NH)),
    ("d_skip", (DEPTH, NH)), ("ssm_norm", (DEPTH, D)), ("q_norm", (DEPTH, Q_LORA)),
    ("w_uq", (DEPTH, Q_LORA, NH * 96)), ("kv_norm", (DEPTH, KV_LORA)),
    ("w_ukv", (DEPTH, KV_LORA, NH * 128)), ("attn_out_norm", (DEPTH, D)),
    ("w_out", (DEPTH, 2 * D, D)), ("norm_mem_q", (DEPTH, D)), ("norm_mem_kv", (DEPTH, D)),
    ("w_mq", (DEPTH, D, D)), ("w_mk", (DEPTH, D, D)), ("w_mv", (DEPTH, D, D)), ("w_mo", (DEPTH, D, D)),
    ("norm_ffn", (DEPTH, D)), ("w_up", (DEPTH, D, 2 * D_FF)), ("ffn_conv_w", (DEPTH, 3, 2 * D_FF)),
    ("ffn_conv_b", (DEPTH, 2 * D_FF)), ("w_down", (DEPTH, D_FF, D)), ("final_norm", (D,)),
  ]


PARAMS = make_params(DEPTH)
VEC_ROWS = [("norm_mix", 8), ("ssm_conv_w", 64), ("ssm_conv_b", 16), ("ssm_norm", 8), ("q_norm", 3),
            ("kv_norm", 2), ("attn_out_norm", 8), ("norm_mem_q", 8), ("norm_mem_kv", 8),
            ("norm_ffn", 8), ("ffn_conv_w", 132), ("ffn_conv_b", 44)]


def build_program(depth=DEPTH, debug=None, stop_after=None):
    DEPTH = max(depth, 1)
    PARAMS = make_params(DEPTH)
    nc = bass.Bass("TRN2", target_bir_lowering=False)
    mk = MK(nc)
    I = mk.I

    x_in = mk.dram("x", [S, D], F32, kind="ExternalInput")
    mem_in = mk.dram("mem", [MEM, D], F32, kind="ExternalInput")
    pos_in = mk.dram("positions", [1, S], I32, kind="ExternalInput")
    P = {}
    for name, shp in PARAMS:
        P[name] = mk.dram(name, list(shp), F32, kind="ExternalInput")
    c_ident = mk.dram("c_ident", [128, 128], F32, kind="ExternalInput")
    c_U = mk.dram("c_U", [128, 128], F32, kind="ExternalInput")
    c_SL = mk.dram("c_SL", [128, 128], F32, kind="ExternalInput")
    c_rope = mk.dram("c_rope", [128, 2], F32, kind="ExternalInput")
    out_d = mk.dram("out", [S, D], F32, kind="ExternalOutput")
    xres = mk.dram("xres", [S, D], F32)
    sz_d = mk.dram("sz_d", [S, D], BF16)
    xs_d = mk.dram("xs_d", [S, D], BF16)
    bt_d = mk.dram("bt_d", [S, 512], BF16)
    bT_d = mk.dram("bT_d", [4, 128, S], BF16)
    cT_d = mk.dram("cT_d", [4, 128, S], BF16)
    dbg = {}
    if debug:
        for nm, shp in debug.items():
            dbg[nm] = mk.dram("dbg_" + nm, list(shp), F32, kind="ExternalOutput")

    BASE = 16512
    TOP = 229344
    cur = [BASE]
    all_bufs = []

    def alloc(name, shape, dtype):
        esz = 4 if dtype in (F32, I32) else 2
        n = 1
        for s_ in shape[1:]:
            n *= s_
        nbytes = (n * esz + 31) // 32 * 32
        off = cur[0]
        cur[0] += nbytes
        assert off + nbytes <= TOP, (name, off, nbytes, TOP)
        b = mk.sbuf_at(name, shape, dtype, off)
        b.off, b.nbytes = off, nbytes
        for o in all_bufs:
            if b.off < o.off + o.nbytes and o.off < b.off + b.nbytes:
                mk.overlap(b, o)
        all_bufs.append(b)
        return b

    ident_f = alloc("ident_f", [128, 128], F32)
    ident_b = alloc("ident_b", [128, 128], BF16)
    U_f = alloc("U_f", [128, 128], F32)
    U_b = alloc("U_b", [128, 128], BF16)
    SL_f = alloc("SL_f", [128, 128], F32)
    ones_f = alloc("ones_f", [128, 128], F32)
    ones_b = alloc("ones_b", [128, 128], BF16)
    zeros_b = alloc("zeros_b", [128, 512], BF16)
    cst = alloc("cst", [128, 4], F32)
    NVR = sum(r for _, r in VEC_ROWS) * DEPTH
    NVB = (NVR + 127) // 128
    vecsT = alloc("vecsT", [128, NVB * 128], F32)
    bc16 = alloc("bc16", [128, 3, DEPTH, NH], F32)
    cs2 = alloc("cs2", [128, S], F32)
    sn2 = alloc("sn2", [128, S], F32)
    memT = alloc("memT", [128, KC, MEM], BF16)
    stat = alloc("stat", [128, 96], F32)
    WSLOT = 6144
    wslots = [alloc("wslot%d" % i, [128, WSLOT], BF16) for i in range(2)]
    T0 = cur[0]

    PB = [mk.psum("pb%d" % i, [128, 512], F32) for i in range(8)]
    bank_rr = [0]
    tbank_rr = [0]

    def nextbank():
        b = PB[bank_rr[0] % 4]
        bank_rr[0] += 1
        return b

    def nexttbank():
        b = PB[6 + tbank_rr[0] % 2]
        tbank_rr[0] += 1
        return b

    stat_rr = [0]

    def statcol(n=3):
        c = stat_rr[0]
        if c + n > 96:
            c = 0
        stat_rr[0] = c + n
        return c

    vec_r0 = {}
    r = 0
    for name, rows in VEC_ROWS:
        vec_r0[name] = r
        r += rows * DEPTH
    VR = dict(VEC_ROWS)

    def vcol(name, l, idx):
        c = vec_r0[name] + l * VR[name] + idx
        return vecsT[:, c:c + 1]

    def vcols(name, l, i0, n):
        c = vec_r0[name] + l * VR[name] + i0
        return vecsT[:, c:c + n]

    wrr = [0]

    def nextslot():
        s_ = wslots[wrr[0] % 2]
        wrr[0] += 1
        return s_

    def wload(src_ap, kcn, ncols, src_buf):
        slot = nextslot()
        assert kcn * ncols <= WSLOT
        view = slot[:, 0:kcn * ncols].rearrange("p (k n) -> p k n", k=kcn)
        mk.dma("pool", view, src_ap, [src_buf], [slot], ("w", slot.name))
        return slot, view

    def wsrc(buf, l, c0, ncols, k0=0, kcn=KC):
        return buf[l][k0 * 128:(k0 + kcn) * 128, c0:c0 + ncols].rearrange("(kc p) n -> p kc n", p=128)

    stg = alloc("stg", [128, NVB, 128], F32)
    ang = alloc("ang", [128, S], F32)
    ang2 = alloc("ang2", [128, S], F32)
    kf = alloc("kf", [128, S], F32)
    ki = alloc("ki", [128, S], I32)
    posi = alloc("posi", [128, S], I32)
    crope = alloc("crope", [128, 2], F32)
    memt = alloc("memt", [128, 2, D], F32)
    memn = alloc("memn", [128, 2, D], BF16)
    junk0 = alloc("junk0", [128, D], BF16)

    mk.dma("sp", ident_f[:], c_ident[:], [c_ident], [ident_f], "c0")
    mk.dma("sp", U_f[:], c_U[:], [c_U], [U_f], "c1")
    mk.dma("sp", SL_f[:], c_SL[:], [c_SL], [SL_f], "c2")
    mk.dma("sp", crope[:], c_rope[:], [c_rope], [crope], "c3")
    I("dve", "tensor_copy", [ident_f], [ident_b], out=ident_b[:], in_=ident_f[:])
    I("dve", "tensor_copy", [U_f], [U_b], out=U_b[:], in_=U_f[:])
    I("dve", "memset", [], [ones_f], ones_f[:], 1.0)
    I("dve", "memset", [], [ones_b], ones_b[:], 1.0)
    I("dve", "memset", [], [zeros_b], zeros_b[:], 0.0)
    I("dve", "memset", [], [cst], cst[:, 0:1], EPS)
    I("dve", "memset", [cst], [cst], cst[:, 1:2], 1.0)
    I("dve", "memset", [], [stg], stg[:], 0.0)
    I("dve", "memset", [], [cs2], cs2[:], 0.0)
    I("dve", "memset", [], [sn2], sn2[:], 0.0)
    di = 0
    for name, rows in VEC_ROWS:
        src = P[name]
        shp = dict(PARAMS)[name]
        if len(shp) == 2:
            flat = src[:].rearrange("l (c q) -> (l c) q", q=128)
        else:
            flat = src[:].rearrange("l j (c q) -> (l j c) q", q=128)
        total = rows * DEPTH
        r0 = vec_r0[name]
        done = 0
        while done < total:
            rr = r0 + done
            blk, pp = rr // 128, rr % 128
            n = min(total - done, 128 - pp)
            mk.dma("sp", stg[pp:pp + n, blk, :], flat[done:done + n, :], [src], [stg], "v%d" % (di % 4))
            di += 1
            done += n
    for blk in range(NVB):
        pb = nextbank()
        I("pe", "transpose", [stg, ident_f], [pb], out=pb[:, 0:128], in_=stg[:, blk, :], identity=ident_f[:])
        I("dve", "tensor_copy", [pb], [(vecsT, (blk,))], out=vecsT[:, blk * 128:(blk + 1) * 128], in_=pb[:, 0:128])
    for i, name in enumerate(("dt_bias", "a_log", "d_skip")):
        mk.dma("sp", bc16[:, i, :, :].rearrange("p l h -> p (l h)"),
               P[name][:].rearrange("l h -> (l h)").unsqueeze(0).broadcast_to([128, DEPTH * NH]),
               [P[name]], [bc16], "b%d" % i)
    I("act", "activation", [bc16], [bc16], out=bc16[:, 1, :, :], in_=bc16[:, 1, :, :], func=AF.Exp)
    I("dve", "tensor_scalar", [bc16], [bc16], out=bc16[:, 1, :, :], in0=bc16[:, 1, :, :], scalar1=-1.0,
      scalar2=None, op0=ALU.mult)

    R0, R1 = 64, 128
    mk.dma("sp", posi[R0:R1, :], pos_in[0:1, :].broadcast_to([64, S]), [pos_in], [posi], "c4")
    I("dve", "tensor_copy", [posi], [ang], out=ang[R0:R1, :], in_=posi[R0:R1, :])
    I("dve", "tensor_scalar", [ang, crope], [ang], out=ang[R0:R1, :], in0=ang[R0:R1, :],
      scalar1=crope[R0:R1, 0:1], scalar2=None, op0=ALU.mult)
    TWO_PI = 2.0 * math.pi
    C1 = 6.28125
    C2 = TWO_PI - C1

    def sin_of(src, dst, shift):
        I("dve", "tensor_scalar", [src], [ang2], out=ang2[R0:R1, :], in0=src[R0:R1, :], scalar1=shift,
          scalar2=None, op0=ALU.add)
        I("dve", "tensor_scalar", [ang2], [ki], out=ki[R0:R1, :], in0=ang2[R0:R1, :], scalar1=1.0 / TWO_PI,
          scalar2=None, op0=ALU.mult)
        I("dve", "tensor_copy", [ki], [kf], out=kf[R0:R1, :], in_=ki[R0:R1, :])
        I("dve", "scalar_tensor_tensor", [kf, ang2], [ang2], out=ang2[R0:R1, :], in0=kf[R0:R1, :], scalar=-C1,
          in1=ang2[R0:R1, :], op0=ALU.mult, op1=ALU.add)
        I("dve", "scalar_tensor_tensor", [kf, ang2], [ang2], out=ang2[R0:R1, :], in0=kf[R0:R1, :], scalar=-C2,
          in1=ang2[R0:R1, :], op0=ALU.mult, op1=ALU.add)
        I("dve", "tensor_scalar", [ang2], [ang2], out=ang2[R0:R1, :], in0=ang2[R0:R1, :], scalar1=math.pi,
          scalar2=-math.pi, op0=ALU.min, op1=ALU.max)
        I("act", "activation", [ang2], [dst], out=dst[R0:R1, :], in_=ang2[R0:R1, :], func=AF.Sin)

    sin_of(ang, cs2, math.pi / 2.0)
    sin_of(ang, sn2, 0.0)
    I("dve", "tensor_scalar", [sn2, crope], [sn2], out=sn2[R0:R1, :], in0=sn2[R0:R1, :],
      scalar1=crope[R0:R1, 1:2], scalar2=None, op0=ALU.mult)

    for kt in range(2):
        mk.dma("sp", memt[:, kt, :], mem_in[kt * 128:(kt + 1) * 128, :], [mem_in], [(memt, (kt,))], "m%d" % kt)
        c = statcol(3)
        I("act", "activation", [(memt, (kt,))], [junk0, (stat, (c,))], out=junk0[:], in_=memt[:, kt, :], func=AF.Square,
          accum_out=stat[:, c:c + 1])
        I("act", "activation", [(stat, (c,)), cst], [(stat, (c,))], out=stat[:, c + 1:c + 2], in_=stat[:, c:c + 1],
          func=AF.Sqrt, scale=1.0 / D, bias=cst[:, 0:1])
        I("dve", "reciprocal", [(stat, (c,))], [(stat, (c,))], out=stat[:, c + 2:c + 3], in_=stat[:, c + 1:c + 2])
        I("dve", "tensor_scalar", [(memt, (kt,)), (stat, (c,))], [(memn, (kt,))], out=memn[:, kt, :], in0=memt[:, kt, :],
          scalar1=stat[:, c + 2:c + 3], scalar2=None, op0=ALU.mult)
        tb_ = nexttbank()
        tv = tb_[:].bitcast(BF16)
        for kc in range(KC):
            I("pe", "transpose", [(memn, (kt,)), ident_b], [tb_], out=tv[:, kc * 128:(kc + 1) * 128],
              in_=memn[:, kt, kc * 128:(kc + 1) * 128], identity=ident_b[:])
        I("dve", "tensor_copy", [tb_], [memT], out=memT[:, :, kt * 128:(kt + 1) * 128],
          in_=tv[:, 0:1024].rearrange("p (k t) -> p k t", k=KC))

    cur[0] = T0
    hT = alloc("hT", [128, KC, S], BF16)
    R_A = cur[0]
    xt = [alloc("xt%d" % i, [128, D], F32) for i in range(2)]
    xn = [alloc("xn%d" % i, [128, D], BF16) for i in range(2)]
    junk = alloc("junk", [128, D], BF16)
    R_A_END = cur[0]
    rx = [alloc("rx%d" % i, [128, 256], F32) for i in range(3)]
    R_B = cur[0]

    raw = [alloc("raw%d" % i, [128, S], F32) for i in range(2)]
    acc = [alloc("acc%d" % i, [128, S], F32) for i in range(2)]
    tmpT = [alloc("tmpT%d" % i, [128, S], BF16) for i in range(2)]
    zst = [alloc("zst%d" % i, [128, 512], BF16) for i in range(2)]
    tkst = [alloc("tkst%d" % i, [128, 8, 128], BF16) for i in range(2)]
    R_X_END = cur[0]
    cqnT = alloc("cqnT", [128, 3, S], BF16)
    ckvnT = alloc("ckvnT", [128, 2, S], BF16)
    kpeT = alloc("kpeT", [128, S], BF16)
    dtt = alloc("dtt", [128, NT, NH], F32)
    adt = alloc("adt", [128, NT, NH], F32)
    ssqa = alloc("ssqa", [128, NT], F32)
    rsta = alloc("rsta", [128, NT], F32)
    R_Y = cur[0]
    sqb = [alloc("sqb%d" % i, [128, S], BF16) for i in range(2)]
    rstd = alloc("rstd", [128, S], F32)
    wkr = alloc("wkr", [128, KC, 128], BF16)
    rtmp = alloc("rtmp", [128, 2, 512], F32)
    cur[0] = R_Y
    yattT = alloc("yattT", [128, KC, S], BF16)
    MIX_END = cur[0]

    cur[0] = R_B
    st_f = alloc("st_f", [128, D], F32)
    st_b = alloc("st_b", [128, D], BF16)
    adU = alloc("adU", [128, NH, 128], F32)
    Eexp = [alloc("Eexp%d" % i, [128, 512], F32) for i in range(2)]
    MT = [alloc("MT%d" % i, [128, 512], BF16) for i in range(2)]
    cbm = [alloc("cbm%d" % i, [128, 128], F32) for i in range(2)]
    xsc = [alloc("xsc%d" % i, [128, D], BF16) for i in range(2)]
    btc = [alloc("btc%d" % i, [128, 512], BF16) for i in range(2)]
    bTc = [alloc("bTc%d" % i, [128, 4, 128], BF16) for i in range(2)]
    cTc = [alloc("cTc%d" % i, [128, 4, 128], BF16) for i in range(2)]
    szc = [alloc("szc%d" % i, [128, D], BF16) for i in range(2)]
    assert cur[0] <= R_X_END, (cur[0], R_X_END)
    cur[0] = R_Y
    xd = alloc("xd", [128, D], BF16)
    xdd = alloc("xdd", [128, D], BF16)
    xsk = alloc("xsk", [128, D], F32)
    yf = alloc("yf", [128, D], F32)
    yo = alloc("yo", [128, D], F32)
    ygb = alloc("ygb", [128, D], BF16)
    sm16 = alloc("sm16", [128, 8, NH], F32)
    assert cur[0] <= MIX_END, (cur[0], MIX_END)
    cur[0] = R_B
    QT = [alloc("QT%d" % i, [128, S], BF16) for i in range(2)]
    KT = [alloc("KT%d" % i, [128, S], BF16) for i in range(2)]
    Vh = [alloc("Vh%d" % i, [128, NT, 96], BF16) for i in range(2)]
    PT = [alloc("PT%d" % i, [128, 512], BF16) for i in range(4)]
    qtmp = alloc("qtmp", [128, 4, 512], F32)
    rsum = alloc("rsum", [128, 8], F32)
    onrm = alloc("onrm", [128, 4, HD], F32)
    osq = alloc("osq", [128, 4, HD], F32)
    ypair = [alloc("ypair%d" % i, [128, NT, 128], BF16) for i in range(2)]
    assert cur[0] <= R_X_END, (cur[0], R_X_END)

    cur[0] = R_B
    qT = alloc("qT", [128, KC, S], BF16)
    oT = alloc("oT", [128, KC, S], BF16)
    mnT = alloc("mnT", [128, KC, MEM], BF16)
    KmT = alloc("KmT", [128, KC, MEM], BF16)
    Vm = alloc("Vm", [128, 2, D], BF16)
    PTm = [alloc("PTm%d" % i, [128, 2, 512], BF16) for i in range(2)]
    rinvb = [alloc("rinvb%d" % i, [128, 512], F32) for i in range(2)]

    cur[0] = R_A
    fraw = alloc("fraw", [128, S], F32)
    assert cur[0] <= R_A_END
    cur[0] = R_B
    aT = alloc("aT", [128, NFF, S], BF16)
    facc = alloc("facc", [128, S], F32)
    sgate = alloc("sgate", [128, S], BF16)
    cur[0] = R_B
    gfin = alloc("gfin", [128, D], F32)
    ofin = [alloc("ofin%d" % i, [128, D], F32) for i in range(2)]

    def rms_tile_stats(src_ap, src_reads, width, jbuf=None):
        jb = junk if jbuf is None else jbuf
        c = statcol(3)
        I("act", "activation", src_reads, [jb, (stat, (c,))], out=jb[:, 0:width], in_=src_ap, func=AF.Square,
          accum_out=stat[:, c:c + 1])
        I("act", "activation", [(stat, (c,)), cst], [(stat, (c,))], out=stat[:, c + 1:c + 2], in_=stat[:, c:c + 1],
          func=AF.Sqrt, scale=1.0 / width, bias=cst[:, 0:1])
        I("dve", "reciprocal", [(stat, (c,))], [(stat, (c,))], out=stat[:, c + 2:c + 3], in_=stat[:, c + 1:c + 2])
        return stat[:, c + 2:c + 3], (stat, (c,))

    def transpose_tile_to(src_buf, dstT, dkey, tile_i, gain_name, l):
        tb_ = nexttbank()
        tv = tb_[:].bitcast(BF16)
        for kc in range(KC):
            I("pe", "transpose", [src_buf, ident_b], [tb_], out=tv[:, kc * 128:(kc + 1) * 128],
              in_=src_buf[:, kc * 128:(kc + 1) * 128], identity=ident_b[:])
        g = vcols(gain_name, l, 0, KC).unsqueeze(2).broadcast_to([128, KC, 128])
        I("dve", "tensor_tensor", [tb_, vecsT], [(dstT, dkey)],
          out=dstT[:, :, tile_i * 128:(tile_i + 1) * 128],
          in0=tv[:, 0:1024].rearrange("p (k t) -> p k t", k=KC), in1=g, op=ALU.mult)

    def norm_phase(xsrc, l, gain_name):
        for i in range(NT):
            t = xt[i % 2]
            mk.dma("sp", t[:], xsrc[i * 128:(i + 1) * 128, :], [(xsrc, (i,))], [t], ("xt", i % 2))
            rinv, skey = rms_tile_stats(t[:], [t], D)
            n = xn[i % 2]
            I("dve", "tensor_scalar", [t, skey], [n], out=n[:], in0=t[:], scalar1=rinv, scalar2=None, op0=ALU.mult)
            transpose_tile_to(n, hT, (None, i // 4), i, gain_name, l)

    def resid_phase(xsrc, l, wname, parts, scaled_last=None):
        NB = 256
        nk_tot = sum(nk for _, nk in parts)
        for cb in range(D // NB):
            slot, wv = wload(wsrc(P[wname], l, cb * NB, NB, 0, nk_tot), nk_tot, NB, P[wname])
            for i in range(NT):
                u = cb * NT + i
                t = rx[u % 3]
                mk.dma("sp", t[:], xsrc[i * 128:(i + 1) * 128, cb * NB:(cb + 1) * NB], [(xsrc, (i,))], [t], ("rx", u % 3))
                k0 = 0
                for pi, (actT, nk) in enumerate(parts):
                    last = (pi == len(parts) - 1)
                    sep = last and scaled_last is not None
                    if pi == 0 or sep:
                        pb = nextbank()
                    for kc in range(nk):
                        st_flag = (kc == 0) if (pi == 0 or sep) else False
                        sp_flag = (kc == nk - 1) and (last or (scaled_last is not None and pi == len(parts) - 2))
                        I("pe", "matmul", [(actT, (None, i // 4)), slot], [pb], pb[:, 0:NB],
                          lhsT=actT[:, kc, i * 128:(i + 1) * 128], rhs=wv[:, k0 + kc, :], start=st_flag, stop=sp_flag)
                    k0 += nk
                    if sep:
                        I("dve", "scalar_tensor_tensor", [pb, t, scaled_last], [t], out=t[:], in0=pb[:, 0:NB],
                          scalar=scaled_last[:, i:i + 1], in1=t[:], op0=ALU.mult, op1=ALU.add)
                    elif last or (scaled_last is not None and pi == len(parts) - 2):
                        I("dve", "tensor_tensor", [pb, t], [t], out=t[:], in0=pb[:, 0:NB], in1=t[:], op=ALU.add)
                mk.dma("sp", xres[i * 128:(i + 1) * 128, cb * NB:(cb + 1) * NB], t[:], [t], [(xres, (i,))], ("rxo", u % 3))

    def proj_fm(wv, col0, m, nk, actT, slot, cb_fn):
        for tb in range(4):
            pb = nextbank()
            for kc in range(nk):
                I("pe", "matmul", [slot, (actT, (None, tb))], [pb], pb[0:m, :],
                  lhsT=wv[:, kc, col0:col0 + m], rhs=actT[:, kc, tb * 512:(tb + 1) * 512],
                  start=(kc == 0), stop=(kc == nk - 1))
            cb_fn(tb, pb)

    def rope_rows(ps, dst, dkey, ts_):
        I("dve", "tensor_tensor", [ps, cs2], [qtmp], out=qtmp[64:96, 0, :], in0=ps[64:96, :], in1=cs2[64:96, ts_], op=ALU.mult)
        I("dve", "tensor_tensor", [ps, sn2], [qtmp], out=qtmp[64:96, 1, :], in0=ps[96:128, :], in1=sn2[96:128, ts_], op=ALU.mult)
        I("dve", "tensor_tensor", [qtmp], [(dst, dkey)], out=dst[64:96, ts_], in0=qtmp[64:96, 0, :], in1=qtmp[64:96, 1, :],
          op=ALU.add)
        I("dve", "tensor_tensor", [ps, cs2], [qtmp], out=qtmp[96:128, 2, :], in0=ps[96:128, :], in1=cs2[96:128, ts_], op=ALU.mult)
        I("dve", "tensor_tensor", [ps, sn2], [qtmp], out=qtmp[96:128, 3, :], in0=ps[64:96, :], in1=sn2[64:96, ts_], op=ALU.mult)
        I("dve", "tensor_tensor", [qtmp], [(dst, dkey)], out=dst[96:128, ts_], in0=qtmp[96:128, 2, :], in1=qtmp[96:128, 3, :],
          op=ALU.add)

    xsrc = x_in
    STAGES = ["norm", "z", "xbc", "prep", "ssd", "mla", "wout", "memattn", "ffn"]
    sub_stop = 99
    if stop_after and ":" in stop_after:
        stop_after, ss_ = stop_after.split(":")
        sub_stop = int(ss_)
    stop_idx = STAGES.index(stop_after) if stop_after else 99

    class _Stop(Exception):
        pass

    def layer_body(l, xsrc):
        norm_phase(xsrc, l, "norm_mix")
        if stop_idx <= 0:
            raise _Stop()

        for cb in range(2):
            slot, wv = wload(wsrc(P["w_in"], l, O_Z + cb * 512, 512), KC, 512, P["w_in"])
            for i in range(NT):
                pb = nextbank()
                for kc in range(KC):
                    I("pe", "matmul", [(hT, (None, i // 4)), slot], [pb], pb[:, :],
                      lhsT=hT[:, kc, i * 128:(i + 1) * 128], rhs=wv[:, kc, :], start=(kc == 0), stop=(kc == KC - 1))
                u = cb * NT + i
                z = zst[u % 2]
                I("act", "activation", [pb], [z], out=z[:], in_=pb[:, :], func=AF.Silu)
                mk.dma("sp", sz_d[i * 128:(i + 1) * 128, cb * 512:(cb + 1) * 512], z[:], [z], [(sz_d, (i,))], ("zst", u % 2))

        if stop_idx <= 1:
            raise _Stop()
        for cg in range(8):
            slot, wv = wload(wsrc(P["w_in"], l, O_XBC + cg * 256, 256), KC, 256, P["w_in"])
            for sub in range(2):
                c = cg * 2 + sub
                rw = raw[c % 2]
                ac = acc[c % 2]

                def evac(tb, pb, rw=rw):
                    I("act", "copy", [pb], [(rw, (tb,))], out=rw[:, tb * 512:(tb + 1) * 512], in_=pb[:, :])
                proj_fm(wv, sub * 128, 128, KC, hT, slot, evac)
                w3, w2, w1, w0 = (vcol("ssm_conv_w", l, j * 16 + c) for j in (3, 2, 1, 0))
                bb = vcol("ssm_conv_b", l, c)
                I("dve", "tensor_scalar", [rw, vecsT], [ac], out=ac[:], in0=rw[:], scalar1=w3, scalar2=bb,
                  op0=ALU.mult, op1=ALU.add)
                for sh, wj in ((1, w2), (2, w1), (3, w0)):
                    I("dve", "scalar_tensor_tensor", [rw, ac, vecsT], [ac], out=ac[:, sh:S], in0=rw[:, 0:S - sh],
                      scalar=wj, in1=ac[:, sh:S], op0=ALU.mult, op1=ALU.add)
                tt = tmpT[c % 2]
                I("act", "activation", [ac], [tt], out=tt[:], in_=ac[:], func=AF.Silu)
                if c < 12:
                    for half in range(2):
                        tk = tkst[half]
                        tb_ = nexttbank()
                        tv = tb_[:].bitcast(BF16)
                        for j in range(8):
                            i = half * 8 + j
                            I("pe", "transpose", [tt, ident_b], [tb_], out=tv[:, j * 128:(j + 1) * 128],
                              in_=tt[:, i * 128:(i + 1) * 128], identity=ident_b[:])
                        I("dve", "tensor_copy", [tb_], [tk], out=tk[:], in_=tv[:, 0:1024].rearrange("p (j t) -> p j t", j=8))
                        rows = slice(half * 1024, (half + 1) * 1024)
                        if c < 8:
                            dst = xs_d[rows, c * 128:(c + 1) * 128].rearrange("(i p) f -> p i f", p=128)
                            mk.dma("sp", dst, tk[:], [tk], [xs_d], ("tk", half))
                        else:
                            dst = bt_d[rows, (c - 8) * 128:(c - 7) * 128].rearrange("(i p) f -> p i f", p=128)
                            mk.dma("sp", dst, tk[:], [tk], [bt_d], ("tk", half))
                if 8 <= c < 12:
                    mk.dma("sp", bT_d[c - 8], tt[:], [tt], [bT_d], ("tT", c % 2))
                if c >= 12:
                    mk.dma("sp", cT_d[c - 12], tt[:], [tt], [cT_d], ("tT", c % 2))

        if stop_idx <= 2:
            raise _Stop()
        slot = nextslot()
        wv = slot[:, 0:KC * 64].rearrange("p (k n) -> p k n", k=KC)
        mk.dma("pool", wv[:, :, 0:NH], wsrc(P["w_in"], l, O_DT, NH), [P["w_in"]], [slot], ("w", slot.name))
        import os
        DTV = int(os.environ.get("MK_DT", "9"))
        for i in range(NT if DTV >= 2 else 0):
            pb = nextbank()
            for kc in range(KC):
                I("pe", "matmul", [(hT, (None, i // 4)), slot], [pb], pb[:, 0:NH],
                  lhsT=hT[:, kc, i * 128:(i + 1) * 128], rhs=wv[:, kc, 0:NH], start=(kc == 0), stop=(kc == KC - 1))
            I("dve", "tensor_tensor", [pb, bc16], [(dtt, (i,))], out=dtt[:, i, :], in0=pb[:, 0:NH], in1=bc16[:, 0, l, :],
              op=ALU.add)
            if DTV >= 3:
                I("act", "activation", [(dtt, (i,))], [(dtt, (i,))], out=dtt[:, i, :], in_=dtt[:, i, :], func=AF.Exp)
            if DTV >= 4:
                I("act", "activation", [(dtt, (i,)), cst], [(dtt, (i,))], out=dtt[:, i, :], in_=dtt[:, i, :], func=AF.Ln,
                  bias=cst[:, 1:2], scale=1.0)
            if DTV >= 5:
                I("dve", "tensor_tensor", [(dtt, (i,)), bc16], [(adt, (i,))], out=adt[:, i, :], in0=dtt[:, i, :],
                  in1=bc16[:, 1, l, :], op=ALU.mult)

        if stop_idx == 3 and sub_stop <= 0:
            raise _Stop()

        def latent(wv, slot, col0, nch, dstT, gname, width):
            for c in range(nch):
                sq = sqb[c % 2]

                def evac(tb, pb, c=c, sq=sq):
                    I("act", "activation", [pb], [(sq, (tb,))], out=sq[:, tb * 512:(tb + 1) * 512], in_=pb[:, :],
                      func=AF.Square)
                    I("dve", "tensor_scalar", [pb, vecsT], [(dstT, (c, tb))], out=dstT[:, c, tb * 512:(tb + 1) * 512],
                      in0=pb[:, :], scalar1=vcol(gname, l, c), scalar2=None, op0=ALU.mult)
                proj_fm(wv, col0 + c * 128, 128, KC, hT, slot, evac)
                for tb in range(4):
                    pb = nextbank()
                    I("pe", "matmul", [(sq, (tb,)), ones_b], [pb], pb[:, :], lhsT=ones_b[:],
                      rhs=sq[:, tb * 512:(tb + 1) * 512], start=True, stop=True)
                    if c == 0:
                        I("dve", "tensor_copy", [pb], [(rstd, (tb,))], out=rstd[:, tb * 512:(tb + 1) * 512], in_=pb[:, :])
                    else:
                        I("dve", "tensor_tensor", [pb, (rstd, (tb,))], [(rstd, (tb,))],
                          out=rstd[:, tb * 512:(tb + 1) * 512], in0=pb[:, :], in1=rstd[:, tb * 512:(tb + 1) * 512],
                          op=ALU.add)
            I("act", "activation", [rstd, cst], [rstd], out=rstd[:], in_=rstd[:], func=AF.Sqrt, scale=1.0 / width,
              bias=cst[:, 0:1])
            I("dve", "reciprocal", [rstd], [rstd], out=rstd[:], in_=rstd[:])
            for c in range(nch):
                I("dve", "tensor_tensor", [(dstT, (c, None)), rstd], [(dstT, (c, None))], out=dstT[:, c, :],
                  in0=dstT[:, c, :], in1=rstd[:], op=ALU.mult)

        slot, wv = wload(wsrc(P["w_in"], l, O_CQ, 384), KC, 384, P["w_in"])
        latent(wv, slot, 0, 3, cqnT, "q_norm", Q_LORA)
        if stop_idx == 3 and sub_stop <= 1:
            raise _Stop()
        slot = nextslot()
        wv = slot[:, 0:KC * 64].rearrange("p (k n) -> p k n", k=KC)
        mk.dma("pool", wv[:, :, 0:ROPE], wsrc(P["w_in"], l, O_KR, ROPE), [P["w_in"]], [slot], ("w", slot.name))
        I("dve", "memset", [], [wkr], wkr[:], 0.0)
        I("dve", "tensor_copy", [slot, wkr], [wkr], out=wkr[:, :, 64:80], in_=wv[:, :, 0:16])
        I("dve", "tensor_copy", [slot, wkr], [wkr], out=wkr[:, :, 96:112], in_=wv[:, :, 16:32])
        slot, wv = wload(wsrc(P["w_in"], l, O_CKV, 256), KC, 256, P["w_in"])
        if stop_idx == 3 and sub_stop <= 2:
            raise _Stop()
        latent(wv, slot, 0, 2, ckvnT, "kv_norm", KV_LORA)
        if stop_idx == 3 and sub_stop <= 3:
            raise _Stop()
        for tb in range(4):
            pa = nextbank()
            for kc in range(KC):
                I("pe", "matmul", [wkr, (hT, (None, tb))], [pa], pa[:, :], lhsT=wkr[:, kc, :],
                  rhs=hT[:, kc, tb * 512:(tb + 1) * 512], start=(kc == 0), stop=(kc == KC - 1))
            rope_rows(pa, kpeT, (tb,), slice(tb * 512, (tb + 1) * 512))

        if stop_idx <= 3:
            raise _Stop()
        I("dve", "memset", [], [st_f], st_f[:], 0.0)
        I("dve", "memset", [], [st_b], st_b[:], 0.0)
        for c in range(NT):
            cs_ = slice(c * 128, (c + 1) * 128)
            xs_c, bt_c, bT_c, cT_c, sz_c = xsc[c % 2], btc[c % 2], bTc[c % 2], cTc[c % 2], szc[c % 2]
            mk.dma("sp", xs_c[:], xs_d[cs_, :], [xs_d], [xs_c], ("xsc", c % 2))
            mk.dma("sp", bt_c[:], bt_d[cs_, :], [bt_d], [bt_c], ("btc", c % 2))
            mk.dma("sp", bT_c[:], bT_d[:, :, cs_].rearrange("g n t -> n g t"), [bT_d], [bT_c], ("bTc", c % 2))
            mk.dma("sp", cT_c[:], cT_d[:, :, cs_].rearrange("g n t -> n g t"), [cT_d], [cT_c], ("cTc", c % 2))
            mk.dma("sp", sz_c[:], sz_d[cs_, :], [(sz_d, (c,))], [sz_c], ("szc", c % 2))
            I("pool", "tensor_tensor", [(adt, (c,)), U_f], [adU], out=adU[:],
              in0=adt[:, c, :].unsqueeze(2).broadcast_to([128, NH, 128]),
              in1=U_f[:].unsqueeze(1).broadcast_to([128, NH, 128]), op=ALU.mult)
            pc = nextbank()
            I("pe", "matmul", [U_f, (adt, (c,))], [pc], pc[:, 0:NH], lhsT=U_f[:], rhs=adt[:, c, :], start=True, stop=True)
            I("pe", "matmul", [ones_f, (adt, (c,))], [pc], pc[:, 64:64 + NH], lhsT=ones_f[:], rhs=adt[:, c, :],
              start=True, stop=True, skip_group_check=True)
            I("dve", "tensor_copy", [pc], [sm16], out=sm16[:, 3:5, :], in_=pc[:, 0:128].rearrange("p (a h) -> p a h", a=2)[:, :, 0:NH])
            I("dve", "tensor_tensor", [sm16], [sm16], out=sm16[:, 5, :], in0=sm16[:, 4, :], in1=sm16[:, 3, :], op=ALU.subtract)
            I("act", "activation", [sm16], [sm16], out=sm16[:, 0:3, :], in_=sm16[:, 3:6, :], func=AF.Exp)
            x3 = xs_c[:].rearrange("p (h d) -> p h d", h=NH)
            I("dve", "tensor_tensor", [xs_c, (dtt, (c,))], [xd], out=xd[:].rearrange("p (h d) -> p h d", h=NH), in0=x3,
              in1=dtt[:, c, :].unsqueeze(2).broadcast_to([128, NH, HD]), op=ALU.mult)
            I("dve", "tensor_tensor", [xd, sm16], [xdd], out=xdd[:].rearrange("p (h d) -> p h d", h=NH),
              in0=xd[:].rearrange("p (h d) -> p h d", h=NH),
              in1=sm16[:, 2, :].unsqueeze(2).broadcast_to([128, NH, HD]), op=ALU.mult)
            I("pool", "tensor_tensor", [xs_c, bc16], [xsk], out=xsk[:].rearrange("p (h d) -> p h d", h=NH), in0=x3,
              in1=bc16[:, 2, l, :].unsqueeze(2).broadcast_to([128, NH, HD]), op=ALU.mult)
            for hh in range(2):
                hs = slice(hh * 512, (hh + 1) * 512)
                py, po = PB[4], PB[5]
                for g in (2 * hh, 2 * hh + 1):
                    pd = nextbank()
                    I("pe", "matmul", [SL_f, adU], [pd], pd[:, :], lhsT=SL_f[:],
                      rhs=adU[:, g * 4:(g + 1) * 4, :].rearrange("p h l -> p (h l)"), start=True, stop=True)
                    E = Eexp[g % 2]
                    I("act", "activation", [pd], [E], out=E[:], in_=pd[:, :], func=AF.Exp)
                    pcb = nextbank()
                    I("pe", "matmul", [bT_c, cT_c], [pcb], pcb[:, 0:128], lhsT=bT_c[:, g, :], rhs=cT_c[:, g, :],
                      start=True, stop=True)
                    cb_ = cbm[g % 2]
                    I("dve", "tensor_tensor", [pcb, U_f], [cb_], out=cb_[:], in0=pcb[:, 0:128], in1=U_f[:], op=ALU.mult)
                    M = MT[g % 2]
                    I("dve", "tensor_tensor", [E, cb_], [M], out=M[:].rearrange("p (r l) -> p r l", r=4),
                      in0=E[:].rearrange("p (r l) -> p r l", r=4), in1=cb_[:].unsqueeze(1).broadcast_to([128, 4, 128]),
                      op=ALU.mult)
                    for r_ in range(4):
                        h = g * 4 + r_
                        oc = (g % 2) * 256 + r_ * 64
                        I("pe", "matmul", [M, xd], [py], py[:, oc:oc + 64], lhsT=M[:, r_ * 128:(r_ + 1) * 128],
                          rhs=xd[:, h * HD:(h + 1) * HD], start=True, stop=True, skip_group_check=True)
                    oc = (g % 2) * 256
                    I("pe", "matmul", [cT_c, st_b], [po], po[:, oc:oc + 256], lhsT=cT_c[:, g, :],
                      rhs=st_b[:, g * 256:(g + 1) * 256], start=True, stop=True, skip_group_check=True)
                I("dve", "tensor_tensor", [po, sm16], [yo], out=yo[:, hs].rearrange("p (h d) -> p h d", h=8),
                  in0=po[:, :].rearrange("p (h d) -> p h d", h=8),
                  in1=sm16[:, 0, hh * 8:(hh + 1) * 8].unsqueeze(2).broadcast_to([128, 8, HD]), op=ALU.mult)
                I("dve", "tensor_tensor", [py, yo], [yf], out=yf[:, hs], in0=py[:, :], in1=yo[:, hs], op=ALU.add)
            I("pool", "tensor_tensor", [yf, xsk], [yf], out=yf[:], in0=yf[:], in1=xsk[:], op=ALU.add)
            I("pool", "tensor_tensor", [st_f, sm16], [st_f], out=st_f[:].rearrange("p (h d) -> p h d", h=NH),
              in0=st_f[:].rearrange("p (h d) -> p h d", h=NH),
              in1=sm16[:, 1, :].unsqueeze(2).broadcast_to([128, NH, HD]), op=ALU.mult)
            for hh in range(2):
                hs = slice(hh * 512, (hh + 1) * 512)
                ps_ = nextbank()
                for g in (2 * hh, 2 * hh + 1):
                    oc = (g % 2) * 256
                    I("pe", "matmul", [bt_c, xdd], [ps_], ps_[:, oc:oc + 256], lhsT=bt_c[:, g * 128:(g + 1) * 128],
                      rhs=xdd[:, g * 256:(g + 1) * 256], start=True, stop=True, skip_group_check=True)
                I("dve", "tensor_tensor", [ps_, st_f], [st_f], out=st_f[:, hs], in0=ps_[:, :], in1=st_f[:, hs], op=ALU.add)
            I("act", "copy", [st_f], [st_b], out=st_b[:], in_=st_f[:])
            I("dve", "tensor_tensor", [yf, sz_c], [yf], out=yf[:], in0=yf[:], in1=sz_c[:], op=ALU.mult)
            rinv, skey = rms_tile_stats(yf[:], [yf], D)
            I("dve", "tensor_scalar", [yf, skey], [ygb], out=ygb[:], in0=yf[:], scalar1=rinv, scalar2=None, op0=ALU.mult)
            transpose_tile_to(ygb, hT, (None, c // 4), c, "ssm_norm", l)

        if stop_idx <= 4:
            raise _Stop()
        wqs_ = nextslot()
        wq4 = wqs_[:, 0:3 * NH * 128].rearrange("p (k h d) -> p k h d", k=3, h=NH)
        I("dve", "memset", [], [wqs_], wqs_[:], 0.0)
        for kc in range(3):
            s4 = P["w_uq"][l][kc * 128:(kc + 1) * 128, :].rearrange("p (h d) -> p h d", h=NH)
            mk.dma("pool", wq4[:, kc, :, 0:80], s4[:, :, 0:80], [P["w_uq"]], [wqs_], ("w", wqs_.name))
            mk.dma("pool", wq4[:, kc, :, 96:112], s4[:, :, 80:96], [P["w_uq"]], [wqs_], ("w", wqs_.name))
        wkvs_ = nextslot()
        wkv = wkvs_[:, 0:2 * NH * 128].rearrange("p (k n) -> p k n", k=2)
        for kc in range(2):
            mk.dma("pool", wkv[:, kc, :], P["w_ukv"][l][kc * 128:(kc + 1) * 128, :], [P["w_ukv"]], [wkvs_],
                   ("w", wkvs_.name))
        I("dve", "memset", [], [ssqa], ssqa[:], 0.0)
        scale = 96.0 ** -0.5
        def mla_proj(h):
            Qh, Kh, V_ = QT[h % 2], KT[h % 2], Vh[h % 2]
            for tb in range(4):
                ts_ = slice(tb * 512, (tb + 1) * 512)
                pa = nextbank()
                for kc in range(3):
                    I("pe", "matmul", [wqs_, (cqnT, (None, tb))], [pa], pa[:, :],
                      lhsT=wq4[:, kc, h, :], rhs=cqnT[:, kc, ts_], start=(kc == 0), stop=(kc == 2))
                I("act", "copy", [pa], [(Qh, (tb, 0))], out=Qh[0:64, ts_], in_=pa[0:64, :])
                rope_rows(pa, Qh, (tb, 1), ts_)
                pk = nextbank()
                for kc in range(2):
                    I("pe", "matmul", [wkvs_, (ckvnT, (None, tb))], [pk], pk[0:64, :],
                      lhsT=wkv[:, kc, h * 128:h * 128 + 64], rhs=ckvnT[:, kc, ts_], start=(kc == 0), stop=(kc == 1))
                I("act", "copy", [pk], [(Kh, (tb, 0))], out=Kh[0:64, ts_], in_=pk[0:64, :])
                I("pool", "tensor_copy", [(kpeT, (tb,))], [(Kh, (tb, 1))], out=Kh[64:128, ts_], in_=kpeT[64:128, ts_])
            I("pool", "memset", [], [V_], V_[:, :, 64:65], 1.0)
            for i in range(NT):
                pv = nextbank()
                for kc in range(2):
                    I("pe", "matmul", [(ckvnT, (None, i // 4)), wkvs_], [pv], pv[:, 0:64],
                      lhsT=ckvnT[:, kc, i * 128:(i + 1) * 128], rhs=wkv[:, kc, h * 128 + 64:h * 128 + 128],
                      start=(kc == 0), stop=(kc == 1))
                I("act", "copy", [pv], [V_], out=V_[:, i, 0:64], in_=pv[:, 0:64])

        def mla_attn(h):
            Qh, Kh, V_ = QT[h % 2], KT[h % 2], Vh[h % 2]
            yp = ypair[(h // 2) % 2]

            def emit_scores(qb, kt, sidx):
                q0 = max(qb * 512, kt * 128)
                q1 = (qb + 1) * 512
                n = q1 - q0
                psc = nextbank()
                I("pe", "matmul", [Kh, Qh], [psc], psc[:, 0:n], lhsT=Kh[:, kt * 128:(kt + 1) * 128],
                  rhs=Qh[:, q0:q1], start=True, stop=True)
                pt = PT[sidx % 4]
                I("act", "activation", [psc], [pt], out=pt[:, 0:n], in_=psc[:, 0:n], func=AF.Exp, scale=scale)
                if kt * 128 >= qb * 512:
                    I("pool", "tensor_tensor", [pt, U_b], [pt], out=pt[:, 0:128], in0=pt[:, 0:128], in1=U_b[:], op=ALU.mult)
                return pt, n, q0

            def emit_pv(qb, kt, pt, n, q0):
                pacc = PB[4 + (h * 4 + qb) % 2]
                nkt = (qb + 1) * 4
                if kt == 0:
                    I("pe", "matmul", [zeros_b], [pacc], pacc[:, :], lhsT=zeros_b[:, 0:128], rhs=zeros_b[:, :],
                      start=True, stop=False)
                for j in range(n // 128):
                    qt = (q0 - qb * 512) // 128 + j
                    I("pe", "matmul", [pt, V_], [pacc], pacc[:, qt * 128:qt * 128 + 65], lhsT=pt[:, j * 128:(j + 1) * 128],
                      rhs=V_[:, kt, 0:65], start=False, stop=False, skip_group_check=True)
                if kt != nkt - 1:
                    return
                I("pe", "matmul", [zeros_b], [pacc], pacc[:, :], lhsT=zeros_b[:, 0:128], rhs=zeros_b[:, :],
                  start=False, stop=True)
                a3 = pacc[:, :].rearrange("p (q e) -> p q e", q=4)
                I("dve", "reciprocal", [pacc], [rsum], out=rsum[:, 0:4], in_=a3[:, :, 64])
                I("dve", "tensor_tensor", [pacc, rsum], [onrm], out=onrm[:], in0=a3[:, :, 0:64],
                  in1=rsum[:, 0:4].unsqueeze(2).broadcast_to([128, 4, HD]), op=ALU.mult)
                I("pool", "tensor_copy", [onrm], [(yp, (qb, h % 2))],
                  out=yp[:, qb * 4:(qb + 1) * 4, (h % 2) * 64:(h % 2) * 64 + 64], in_=onrm[:])
                I("dve", "tensor_tensor", [onrm], [osq], out=osq[:], in0=onrm[:], in1=onrm[:], op=ALU.mult)
                I("dve", "tensor_reduce", [osq], [rsum], out=rsum[:, 4:8], in_=osq[:], axis=mybir.AxisListType.X, op=ALU.add)
                I("dve", "tensor_tensor", [rsum, ssqa], [ssqa], out=ssqa[:, qb * 4:(qb + 1) * 4], in0=ssqa[:, qb * 4:(qb + 1) * 4],
                  in1=rsum[:, 4:8], op=ALU.add)

            steps = [(qb, kt) for qb in range(4) for kt in range((qb + 1) * 4)]
            pend = []
            for sidx, (qb, kt) in enumerate(steps):
                cur = emit_scores(qb, kt, sidx)
                pend.append((qb, kt) + cur)
                if len(pend) > 2:
                    emit_pv(*pend.pop(0))
            while pend:
                emit_pv(*pend.pop(0))
            if h % 2 == 1:
                hp = h // 2
                gcol = vcol("attn_out_norm", l, hp)
                for half in range(2):
                    tb_ = nexttbank()
                    tv = tb_[:].bitcast(BF16)
                    for j in range(8):
                        i = half * 8 + j
                        I("pe", "transpose", [yp, ident_b], [tb_], out=tv[:, j * 128:(j + 1) * 128], in_=yp[:, i, :],
                          identity=ident_b[:])
                    I("dve", "tensor_scalar", [tb_, vecsT], [(yattT, (hp, None))],
                      out=yattT[:, hp, half * 1024:(half + 1) * 1024], in0=tv[:, 0:1024], scalar1=gcol, scalar2=None,
                      op0=ALU.mult)
        mla_proj(0)
        for h in range(NH):
            if h + 1 < NH:
                mla_proj(h + 1)
            mla_attn(h)
        I("act", "activation", [ssqa, cst], [rsta], out=rsta[:], in_=ssqa[:], func=AF.Sqrt, scale=1.0 / D, bias=cst[:, 0:1])
        I("dve", "reciprocal", [rsta], [rsta], out=rsta[:], in_=rsta[:])

        if stop_idx <= 5:
            raise _Stop()
        resid_phase(xsrc, l, "w_out", [(hT, KC), (yattT, KC)], scaled_last=rsta)
        xsrc = xres
        cur_x[0] = xres

        if stop_idx <= 6:
            raise _Stop()
        norm_phase(xsrc, l, "norm_mem_q")
        I("dve", "tensor_tensor", [memT, vecsT], [mnT], out=mnT[:], in0=memT[:],
          in1=vcols("norm_mem_kv", l, 0, KC).unsqueeze(2).broadcast_to([128, KC, MEM]), op=ALU.mult)
        for cg in range(4):
            slot, wv = wload(wsrc(P["w_mk"], l, cg * 256, 256), KC, 256, P["w_mk"])
            for sub in range(2):
                dc = cg * 2 + sub
                pb = nextbank()
                for kc in range(KC):
                    I("pe", "matmul", [slot, mnT], [pb], pb[:, 0:MEM], lhsT=wv[:, kc, sub * 128:(sub + 1) * 128],
                      rhs=mnT[:, kc, :], start=(kc == 0), stop=(kc == KC - 1))
                I("act", "copy", [pb], [(KmT, (dc,))], out=KmT[:, dc, :], in_=pb[:, 0:MEM])
        for cb in range(2):
            slot, wv = wload(wsrc(P["w_mv"], l, cb * 512, 512), KC, 512, P["w_mv"])
            for kt in range(2):
                pb = nextbank()
                for kc in range(KC):
                    I("pe", "matmul", [mnT, slot], [pb], pb[:, :], lhsT=mnT[:, kc, kt * 128:(kt + 1) * 128], rhs=wv[:, kc, :],
                      start=(kc == 0), stop=(kc == KC - 1))
                I("act", "copy", [pb], [(Vm, (kt, cb))], out=Vm[:, kt, cb * 512:(cb + 1) * 512], in_=pb[:, :])
        for cg in range(4):
            slot, wv = wload(wsrc(P["w_mq"], l, cg * 256, 256), KC, 256, P["w_mq"])
            for sub in range(2):
                dc = cg * 2 + sub

                def evac(tb, pb, dc=dc):
                    I("act", "copy", [pb], [(qT, (dc, tb))], out=qT[:, dc, tb * 512:(tb + 1) * 512], in_=pb[:, :])
                proj_fm(wv, sub * 128, 128, KC, hT, slot, evac)
        mscale = 256.0 ** -0.5
        for hd in range(4):
            for tb in range(4):
                ts_ = slice(tb * 512, (tb + 1) * 512)
                ptm = PTm[(hd * 4 + tb) % 2]
                for kt in range(2):
                    psc = nextbank()
                    for d2 in range(2):
                        dc = hd * 2 + d2
                        I("pe", "matmul", [(KmT, (dc,)), (qT, (dc, tb))], [psc], psc[:, :],
                          lhsT=KmT[:, dc, kt * 128:(kt + 1) * 128], rhs=qT[:, dc, ts_], start=(d2 == 0), stop=(d2 == 1))
                    I("act", "activation", [psc], [(ptm, (kt,))], out=ptm[:, kt, :], in_=psc[:, :], func=AF.Exp, scale=mscale)
                pr = nextbank()
                for kt in range(2):
                    I("pe", "matmul", [ones_b, (ptm, (kt,))], [pr], pr[:, :], lhsT=ones_b[:], rhs=ptm[:, kt, :],
                      start=(kt == 0), stop=(kt == 1))
                rb = rinvb[(hd * 4 + tb) % 2]
                I("dve", "reciprocal", [pr], [rb], out=rb[:], in_=pr[:, :])
                for d2 in range(2):
                    dc = hd * 2 + d2
                    po_ = nextbank()
                    for kt in range(2):
                        I("pe", "matmul", [(Vm, (kt, None)), (ptm, (kt,))], [po_], po_[:, :],
                          lhsT=Vm[:, kt, dc * 128:(dc + 1) * 128], rhs=ptm[:, kt, :], start=(kt == 0), stop=(kt == 1))
                    I("dve", "tensor_tensor", [po_, rb], [(oT, (dc, tb))], out=oT[:, dc, ts_], in0=po_[:, :], in1=rb[:],
                      op=ALU.mult)
        resid_phase(xsrc, l, "w_mo", [(oT, KC)])

        if stop_idx <= 7:
            raise _Stop()
        norm_phase(xsrc, l, "norm_ffn")
        for jg in range(NFF // 2):
            slot_g, wg = wload(wsrc(P["w_up"], l, jg * 256, 256), KC, 256, P["w_up"])
            slot_v, wvv = wload(wsrc(P["w_up"], l, D_FF + jg * 256, 256), KC, 256, P["w_up"])
            for sub in range(2):
                j = jg * 2 + sub
                for which, (slot, wv_) in enumerate(((slot_g, wg), (slot_v, wvv))):
                    cidx = which * NFF + j

                    def evac(tb, pb):
                        I("act", "copy", [pb], [(fraw, (tb,))], out=fraw[:, tb * 512:(tb + 1) * 512], in_=pb[:, :])
                    proj_fm(wv_, sub * 128, 128, KC, hT, slot, evac)
                    w2, w1, w0 = (vcol("ffn_conv_w", l, jj * 44 + cidx) for jj in (2, 1, 0))
                    bb = vcol("ffn_conv_b", l, cidx)
                    I("dve", "tensor_scalar", [fraw, vecsT], [facc], out=facc[:], in0=fraw[:], scalar1=w2, scalar2=bb,
                      op0=ALU.mult, op1=ALU.add)
                    for sh, wj in ((1, w1), (2, w0)):
                        I("dve", "scalar_tensor_tensor", [fraw, facc, vecsT], [facc], out=facc[:, sh:S], in0=fraw[:, 0:S - sh],
                          scalar=wj, in1=facc[:, sh:S], op0=ALU.mult, op1=ALU.add)
                    if which == 0:
                        I("act", "activation", [facc], [sgate], out=sgate[:], in_=facc[:], func=AF.Silu)
                    else:
                        I("dve", "tensor_tensor", [facc, sgate], [(aT, (j, None))], out=aT[:, j, :], in0=facc[:], in1=sgate[:],
                          op=ALU.mult)
        resid_phase(xsrc, l, "w_down", [(aT, NFF)])


    cur_x = [x_in]
    try:
        for l in range(depth):
            layer_body(l, cur_x[0])
            cur_x[0] = xres
    except _Stop:
        pass
    xsrc = cur_x[0]

    mk.dma("sp", gfin[:], P["final_norm"][:].unsqueeze(0).broadcast_to([128, D]), [P["final_norm"]], [gfin], "gf")
    for i in range(NT):
        t = xt[i % 2]
        mk.dma("sp", t[:], xsrc[i * 128:(i + 1) * 128, :], [(xsrc, (i,))], [t], ("xt", i % 2))
        rinv, skey = rms_tile_stats(t[:], [t], D)
        o_ = ofin[i % 2]
        I("dve", "scalar_tensor_tensor", [t, skey, gfin], [o_], out=o_[:], in0=t[:], scalar=rinv, in1=gfin[:],
          op0=ALU.mult, op1=ALU.mult)
        mk.dma("sp", out_d[i * 128:(i + 1) * 128, :], o_[:], [o_], [(out_d, (i,))], ("of", i % 2))

    mk.emit()
    nc.mk_bufs = {b.name: b.t.name for b in all_bufs}
    return nc


_CACHE = {}


def _consts():
    k = np.arange(128)
    U = (k[:, None] <= k[None, :]).astype(np.float32)
    SL = (k[:, None] > k[None, :]).astype(np.float32)
    rope = np.zeros((128, 2), np.float32)
    inv_freq = (1.0 / (np.float32(10000.0) ** (np.arange(0, ROPE, 2, dtype=np.float32) / np.float32(ROPE)))).astype(np.float32)
    rope[64:80, 0] = inv_freq
    rope[96:112, 0] = inv_freq
    rope[64:80, 1] = 1.0
    rope[96:112, 1] = -1.0
    return {"c_ident": np.eye(128, dtype=np.float32), "c_U": U, "c_SL": SL, "c_rope": rope}


def kernel(**inputs):
    return run_depth(inputs, DEPTH)


def run_depth(inputs, depth, trace=False, ncores=N_CORES, stop_after=None):
    key = (depth, stop_after)
    if key not in _CACHE:
        _CACHE[key] = build_program(depth, stop_after=stop_after)
    nc = _CACHE[key]
    consts = _consts()
    x = np.ascontiguousarray(np.asarray(inputs["x"], dtype=np.float32))
    mem = np.ascontiguousarray(np.asarray(inputs["mem"], dtype=np.float32))
    pos = np.ascontiguousarray(np.asarray(inputs["positions"], dtype=np.int32))
    dd = max(depth, 1)
    shared = {}
    for name, shp in PARAMS:
        a = np.asarray(inputs[name], dtype=np.float32)
        if len(shp) > 1 and dd != DEPTH:
            a = a[:dd]
        shared[name] = np.ascontiguousarray(a)
    shared.update(consts)
    in_maps = []
    for b in range(ncores):
        m = dict(shared)
        m["x"] = x[b]
        m["mem"] = mem[b]
        m["positions"] = pos[b:b + 1]
        in_maps.append(m)
    res = run_bass_kernel_spmd(nc, in_maps, core_ids=list(range(ncores)), trace=trace)
    if trace:
        print("exec_time_ns", res.exec_time_ns)
    out = np.stack([np.asarray(r["out"]) for r in res.results], axis=0).astype(np.float32)
    return out
```

```python
import contextlib
import math
import numpy as np

import concourse.bass as bass
import concourse.mybir as mybir
from concourse.alu_op_type import AluOpType as ALU
from concourse.bass_utils import run_bass_kernel_spmd

DT = mybir.dt
AF = mybir.ActivationFunctionType
F32, BF16, I32 = DT.float32, DT.bfloat16, DT.int32

DEPTH = 4
S = 2048
D = 1024
NT = S // 128
KC = D // 128
MEM = 256
EPS = 1e-6
NH = 16
HD = 64
NG = 4
NST = 128
Q_LORA, KV_LORA, ROPE = 384, 256, 32
D_IN = 3760
O_Z, O_XBC, O_DT, O_CQ, O_CKV, O_KR = 0, 1024, 3072, 3088, 3472, 3728
D_FF = 2816
NFF = D_FF // 128
N_CORES = 8

ENGS = ("sp", "pe", "dve", "act", "pool")


class Op:
    __slots__ = ("eng", "fn", "deps", "signal", "sigval", "dma_sem", "dma_val", "id")


class Buf:
    def __init__(self, name, t):
        self.name = name
        self.t = t
        self.writers = []
        self.readers = []
        self.overlaps = []
        self.excl = False

    def __getitem__(self, idx):
        return self.t[idx]


def _conf(a, b):
    if a is None or b is None:
        return True
    for x, y in zip(a, b):
        if x is not None and y is not None and x != y:
            return False
    return True


def _covers(a, b):
    if a is None:
        return True
    if b is None:
        return False
    for x, y in zip(a, b):
        if x is not None and x != y:
            return False
    return True


class MK:
    def __init__(self, nc):
        self.nc = nc
        self.streams = {e: [] for e in ENGS}
        self.eng_sem = {}
        self.dma_sems = {}
        self.nops = 0

    def sbuf_at(self, name, shape, dtype, offset):
        return Buf(name, self.nc.alloc_sbuf_tensor_at(name, list(shape), dtype, offset=offset))

    def psum(self, name, shape, dtype=F32):
        b = Buf(name, self.nc.alloc_psum_tensor(name, list(shape), dtype))
        b.excl = True
        return b

    def dram(self, name, shape, dtype, kind="Internal"):
        return Buf(name, self.nc.dram_tensor(name, list(shape), dtype, kind=kind))

    def overlap(self, a, b):
        a.overlaps.append(b)
        b.overlaps.append(a)

    @staticmethod
    def _norm(lst):
        out = []
        for x in lst:
            if isinstance(x, Buf):
                out.append((x, None))
            else:
                out.append((x[0], tuple(x[1]) if x[1] is not None else None))
        return out

    @staticmethod
    def _absorb(b):
        for y in b.overlaps:
            if y.writers or y.readers:
                ops = [p for _, p in y.writers] + [p for _, p in y.readers]
                y.writers = []
                y.readers = []
                for p in ops:
                    b.writers.append((None, p))
                latest = {}
                keep = []
                for k2, p in b.writers:
                    if k2 is None and p.dma_sem is None:
                        if p.eng not in latest or latest[p.eng].id < p.id:
                            latest[p.eng] = p
                    else:
                        keep.append((k2, p))
                b.writers = keep + [(None, p) for p in latest.values()]

    def op(self, eng, fn, reads=(), writes=(), is_dma=False):
        o = Op()
        o.eng, o.fn, o.deps = eng, fn, []
        o.signal, o.sigval, o.dma_sem, o.dma_val = False, None, None, None
        o.id = self.nops
        self.nops += 1
        reads = self._norm(reads)
        writes = self._norm(writes)
        for b, _ in reads + writes:
            if b.overlaps:
                self._absorb(b)
        seen = set()

        def add(p, raw):
            if p is o or (p.id, raw) in seen:
                return
            seen.add((p.id, raw))
            o.deps.append((p, raw))

        for b, key in reads:
            for k2, p in b.writers:
                if _conf(key, k2):
                    add(p, True)
            if b.excl:
                for k2, p in b.readers:
                    if p.eng != eng:
                        add(p, False)
        for b, key in writes:
            for k2, p in b.writers:
                if _conf(key, k2):
                    add(p, False)
            for k2, p in b.readers:
                if _conf(key, k2):
                    add(p, False)
        for b, key in reads:
            if not is_dma:
                b.readers = [(k2, p) for (k2, p) in b.readers
                             if not (k2 == key and p.eng == eng and p.dma_sem is None)]
            b.readers.append((key, o))
        for b, key in writes:
            b.writers = [(k2, p) for (k2, p) in b.writers if not _covers(key, k2)]
            b.readers = [(k2, p) for (k2, p) in b.readers if (not _covers(key, k2)) or p is o]
            b.writers.append((key, o))
        self.streams[eng].append(o)
        return o

    def I(self, eng, name, reads, writes, *args, **kw):
        def fn(e):
            return getattr(e, name)(*args, **kw)
        return self.op(eng, fn, reads, writes)

    def dma(self, queue, out_ap, in_ap, reads, writes, sem_key, **kw):
        ent = self.dma_sems.get(sem_key)
        if ent is None:
            ent = [None, 0, None]
            self.dma_sems[sem_key] = ent
        prev = ent[2]

        def fn(e):
            return e.dma_start(out=out_ap, in_=in_ap, **kw)

        o = self.op(queue, fn, reads, writes, is_dma=True)
        ent[1] += 16
        o.dma_sem = sem_key
        o.dma_val = ent[1]
        if prev is not None:
            o.deps.append((prev, True))
        ent[2] = o
        return o

    def emit(self):
        nc = self.nc
        for e, ops in self.streams.items():
            for o in ops:
                for p, raw in o.deps:
                    if p.dma_sem is not None:
                        continue
                    if p.eng == o.eng:
                        if o.eng != "pe":
                            p.signal = True
                    else:
                        p.signal = True
        for e, ops in self.streams.items():
            c = 0
            for o in ops:
                if o.dma_sem is None and o.signal:
                    c += 1
                    o.sigval = c
        with contextlib.ExitStack() as st:
            for e in ENGS:
                self.eng_sem[e] = st.enter_context(nc.semaphore("sem_" + e))
            for i, (k, ent) in enumerate(self.dma_sems.items()):
                ent[0] = st.enter_context(nc.semaphore("dsem_%d" % i))
            block = st.enter_context(nc.Block())
            mk = self

            def make(ename):
                def body(eng):
                    waited = {}
                    for o in mk.streams[ename]:
                        for p, raw in o.deps:
                            if p.dma_sem is not None:
                                sem, val, key = mk.dma_sems[p.dma_sem][0], p.dma_val, ("d", p.dma_sem)
                            else:
                                if p.eng == ename and ename == "pe":
                                    continue
                                sem, val, key = mk.eng_sem[p.eng], p.sigval, ("e", p.eng)
                            if waited.get(key, 0) >= val:
                                continue
                            waited[key] = val
                            eng.wait_ge(sem, val)
                        inst = o.fn(eng)
                        if o.dma_sem is not None:
                            inst.then_inc(mk.dma_sems[o.dma_sem][0], 16)
                        elif o.signal:
                            inst.then_inc(mk.eng_sem[ename], 1)
                    for k, ent in mk.dma_sems.items():
                        if ent[2] is not None and ent[2].eng == ename and waited.get(("d", k), 0) < ent[1]:
                            eng.wait_ge(ent[0], ent[1])
                return body

            block.sync(make("sp"))
            block.tensor(make("pe"))
            block.vector(make("dve"))
            block.scalar(make("act"))
            block.gpsimd(make("pool"))


def make_params(DEPTH):
  return [
    ("norm_mix", (DEPTH, D)), ("w_in", (DEPTH, D, D_IN)), ("ssm_conv_w", (DEPTH, 4, 2048)),
    ("ssm_conv_b", (DEPTH, 2048)), ("dt_bias", (DEPTH, NH)), ("a_log", (DEPTH, NH)),
    ("d_skip", (DEPTH, NH)), ("ssm_norm", (DEPTH, D)), ("q_norm", (DEPTH, Q_LORA)),
    ("w_uq", (DEPTH, Q_LORA, NH * 96)), ("kv_norm", (DEPTH, KV_LORA)),
    ("w_ukv", (DEPTH, KV_LORA, NH * 128)), ("attn_out_norm", (DEPTH, D)),
    ("w_out", (DEPTH, 2 * D, D)), ("norm_mem_q", (DEPTH, D)), ("norm_mem_kv", (DEPTH, D)),
    ("w_mq", (DEPTH, D, D)), ("w_mk", (DEPTH, D, D)), ("w_mv", (DEPTH, D, D)), ("w_mo", (DEPTH, D, D)),
    ("norm_ffn", (DEPTH, D)), ("w_up", (DEPTH, D, 2 * D_FF)), ("ffn_conv_w", (DEPTH, 3, 2 * D_FF)),
    ("ffn_conv_b", (DEPTH, 2 * D_FF)), ("w_down", (DEPTH, D_FF, D)), ("final_norm", (D,)),
  ]


PARAMS = make_params(DEPTH)
VEC_ROWS = [("norm_mix", 8), ("ssm_conv_w", 64), ("ssm_conv_b", 16), ("ssm_norm", 8), ("q_norm", 3),
            ("kv_norm", 2), ("attn_out_norm", 8), ("norm_mem_q", 8), ("norm_mem_kv", 8),
            ("norm_ffn", 8), ("ffn_conv_w", 132), ("ffn_conv_b", 44)]


def build_program(depth=DEPTH, debug=None, stop_after=None):
    DEPTH = max(depth, 1)
    PARAMS = make_params(DEPTH)
    nc = bass.Bass("TRN2", target_bir_lowering=False)
    mk = MK(nc)
    I = mk.I

    x_in = mk.dram("x", [S, D], F32, kind="ExternalInput")
    mem_in = mk.dram("mem", [MEM, D], F32, kind="ExternalInput")
    pos_in = mk.dram("positions", [1, S], I32, kind="ExternalInput")
    P = {}
    for name, shp in PARAMS:
        P[name] = mk.dram(name, list(shp), F32, kind="ExternalInput")
    c_ident = mk.dram("c_ident", [128, 128], F32, kind="ExternalInput")
    c_U = mk.dram("c_U", [128, 128], F32, kind="ExternalInput")
    c_SL = mk.dram("c_SL", [128, 128], F32, kind="ExternalInput")
    c_rope = mk.dram("c_rope", [128, 2], F32, kind="ExternalInput")
    out_d = mk.dram("out", [S, D], F32, kind="ExternalOutput")
    xres = mk.dram("xres", [S, D], F32)
    sz_d = mk.dram("sz_d", [S, D], BF16)
    xs_d = mk.dram("xs_d", [S, D], BF16)
    bt_d = mk.dram("bt_d", [S, 512], BF16)
    bT_d = mk.dram("bT_d", [4, 128, S], BF16)
    cT_d = mk.dram("cT_d", [4, 128, S], BF16)
    dbg = {}
    if debug:
        for nm, shp in debug.items():
            dbg[nm] = mk.dram("dbg_" + nm, list(shp), F32, kind="ExternalOutput")

    BASE = 16512
    TOP = 229344
    cur = [BASE]
    all_bufs = []

    def alloc(name, shape, dtype):
        esz = 4 if dtype in (F32, I32) else 2
        n = 1
        for s_ in shape[1:]:
            n *= s_
        nbytes = (n * esz + 31) // 32 * 32
        off = cur[0]
        cur[0] += nbytes
        assert off + nbytes <= TOP, (name, off, nbytes, TOP)
        b = mk.sbuf_at(name, shape, dtype, off)
        b.off, b.nbytes = off, nbytes
        for o in all_bufs:
            if b.off < o.off + o.nbytes and o.off < b.off + b.nbytes:
                mk.overlap(b, o)
        all_bufs.append(b)
        return b

    ident_f = alloc("ident_f", [128, 128], F32)
    ident_b = alloc("ident_b", [128, 128], BF16)
    U_f = alloc("U_f", [128, 128], F32)
    U_b = alloc("U_b", [128, 128], BF16)
    SL_f = alloc("SL_f", [128, 128], F32)
    ones_f = alloc("ones_f", [128, 128], F32)
    ones_b = alloc("ones_b", [128, 128], BF16)
    zeros_b = alloc("zeros_b", [128, 512], BF16)
    cst = alloc("cst", [128, 4], F32)
    NVR = sum(r for _, r in VEC_ROWS) * DEPTH
    NVB = (NVR + 127) // 128
    vecsT = alloc("vecsT", [128, NVB * 128], F32)
    bc16 = alloc("bc16", [128, 3, DEPTH, NH], F32)
    cs2 = alloc("cs2", [128, S], F32)
    sn2 = alloc("sn2", [128, S], F32)
    memT = alloc("memT", [128, KC, MEM], BF16)
    stat = alloc("stat", [128, 96], F32)
    WSLOT = 6144
    wslots = [alloc("wslot%d" % i, [128, WSLOT], BF16) for i in range(2)]
    T0 = cur[0]

    PB = [mk.psum("pb%d" % i, [128, 512], F32) for i in range(8)]
    bank_rr = [0]
    tbank_rr = [0]

    def nextbank():
        b = PB[bank_rr[0] % 4]
        bank_rr[0] += 1
        return b

    def nexttbank():
        b = PB[6 + tbank_rr[0] % 2]
        tbank_rr[0] += 1
        return b

    stat_rr = [0]

    def statcol(n=3):
        c = stat_rr[0]
        if c + n > 96:
            c = 0
        stat_rr[0] = c + n
        return c

    vec_r0 = {}
    r = 0
    for name, rows in VEC_ROWS:
        vec_r0[name] = r
        r += rows * DEPTH
    VR = dict(VEC_ROWS)

    def vcol(name, l, idx):
        c = vec_r0[name] + l * VR[name] + idx
        return vecsT[:, c:c + 1]

    def vcols(name, l, i0, n):
        c = vec_r0[name] + l * VR[name] + i0
        return vecsT[:, c:c + n]

    wrr = [0]

    def nextslot():
        s_ = wslots[wrr[0] % 2]
        wrr[0] += 1
        return s_

    def wload(src_ap, kcn, ncols, src_buf):
        slot = nextslot()
        assert kcn * ncols <= WSLOT
        view = slot[:, 0:kcn * ncols].rearrange("p (k n) -> p k n", k=kcn)
        mk.dma("pool", view, src_ap, [src_buf], [slot], ("w", slot.name))
        return slot, view

    def wsrc(buf, l, c0, ncols, k0=0, kcn=KC):
        return buf[l][k0 * 128:(k0 + kcn) * 128, c0:c0 + ncols].rearrange("(kc p) n -> p kc n", p=128)

    stg = alloc("stg", [128, NVB, 128], F32)
    ang = alloc("ang", [128, S], F32)
    ang2 = alloc("ang2", [128, S], F32)
    kf = alloc("kf", [128, S], F32)
    ki = alloc("ki", [128, S], I32)
    posi = alloc("posi", [128, S], I32)
    crope = alloc("crope", [128, 2], F32)
    memt = alloc("memt", [128, 2, D], F32)
    memn = alloc("memn", [128, 2, D], BF16)
    junk0 = alloc("junk0", [128, D], BF16)

    mk.dma("sp", ident_f[:], c_ident[:], [c_ident], [ident_f], "c0")
    mk.dma("sp", U_f[:], c_U[:], [c_U], [U_f], "c1")
    mk.dma("sp", SL_f[:], c_SL[:], [c_SL], [SL_f], "c2")
    mk.dma("sp", crope[:], c_rope[:], [c_rope], [crope], "c3")
    I("dve", "tensor_copy", [ident_f], [ident_b], out=ident_b[:], in_=ident_f[:])
    I("dve", "tensor_copy", [U_f], [U_b], out=U_b[:], in_=U_f[:])
    I("dve", "memset", [], [ones_f], ones_f[:], 1.0)
    I("dve", "memset", [], [ones_b], ones_b[:], 1.0)
    I("dve", "memset", [], [zeros_b], zeros_b[:], 0.0)
    I("dve", "memset", [], [cst], cst[:, 0:1], EPS)
    I("dve", "memset", [cst], [cst], cst[:, 1:2], 1.0)
    I("dve", "memset", [], [stg], stg[:], 0.0)
    I("dve", "memset", [], [cs2], cs2[:], 0.0)
    I("dve", "memset", [], [sn2], sn2[:], 0.0)
    di = 0
    for name, rows in VEC_ROWS:
        src = P[name]
        shp = dict(PARAMS)[name]
        if len(shp) == 2:
            flat = src[:].rearrange("l (c q) -> (l c) q", q=128)
        else:
            flat = src[:].rearrange("l j (c q) -> (l j c) q", q=128)
        total = rows * DEPTH
        r0 = vec_r0[name]
        done = 0
        while done < total:
            rr = r0 + done
            blk, pp = rr // 128, rr % 128
            n = min(total - done, 128 - pp)
            mk.dma("sp", stg[pp:pp + n, blk, :], flat[done:done + n, :], [src], [stg], "v%d" % (di % 4))
            di += 1
            done += n
    for blk in range(NVB):
        pb = nextbank()
        I("pe", "transpose", [stg, ident_f], [pb], out=pb[:, 0:128], in_=stg[:, blk, :], identity=ident_f[:])
        I("dve", "tensor_copy", [pb], [(vecsT, (blk,))], out=vecsT[:, blk * 128:(blk + 1) * 128], in_=pb[:, 0:128])
    for i, name in enumerate(("dt_bias", "a_log", "d_skip")):
        mk.dma("sp", bc16[:, i, :, :].rearrange("p l h -> p (l h)"),
               P[name][:].rearrange("l h -> (l h)").unsqueeze(0).broadcast_to([128, DEPTH * NH]),
               [P[name]], [bc16], "b%d" % i)
    I("act", "activation", [bc16], [bc16], out=bc16[:, 1, :, :], in_=bc16[:, 1, :, :], func=AF.Exp)
    I("dve", "tensor_scalar", [bc16], [bc16], out=bc16[:, 1, :, :], in0=bc16[:, 1, :, :], scalar1=-1.0,
      scalar2=None, op0=ALU.mult)

    R0, R1 = 64, 128
    mk.dma("sp", posi[R0:R1, :], pos_in[0:1, :].broadcast_to([64, S]), [pos_in], [posi], "c4")
    I("dve", "tensor_copy", [posi], [ang], out=ang[R0:R1, :], in_=posi[R0:R1, :])
    I("dve", "tensor_scalar", [ang, crope], [ang], out=ang[R0:R1, :], in0=ang[R0:R1, :],
      scalar1=crope[R0:R1, 0:1], scalar2=None, op0=ALU.mult)
    TWO_PI = 2.0 * math.pi
    C1 = 6.28125
    C2 = TWO_PI - C1

    def sin_of(src, dst, shift):
        I("dve", "tensor_scalar", [src], [ang2], out=ang2[R0:R1, :], in0=src[R0:R1, :], scalar1=shift,
          scalar2=None, op0=ALU.add)
        I("dve", "tensor_scalar", [ang2], [ki], out=ki[R0:R1, :], in0=ang2[R0:R1, :], scalar1=1.0 / TWO_PI,
          scalar2=None, op0=ALU.mult)
        I("dve", "tensor_copy", [ki], [kf], out=kf[R0:R1, :], in_=ki[R0:R1, :])
        I("dve", "scalar_tensor_tensor", [kf, ang2], [ang2], out=ang2[R0:R1, :], in0=kf[R0:R1, :], scalar=-C1,
          in1=ang2[R0:R1, :], op0=ALU.mult, op1=ALU.add)
        I("dve", "scalar_tensor_tensor", [kf, ang2], [ang2], out=ang2[R0:R1, :], in0=kf[R0:R1, :], scalar=-C2,
          in1=ang2[R0:R1, :], op0=ALU.mult, op1=ALU.add)
        I("dve", "tensor_scalar", [ang2], [ang2], out=ang2[R0:R1, :], in0=ang2[R0:R1, :], scalar1=math.pi,
          scalar2=-math.pi, op0=ALU.min, op1=ALU.max)
        I("act", "activation", [ang2], [dst], out=dst[R0:R1, :], in_=ang2[R0:R1, :], func=AF.Sin)

    sin_of(ang, cs2, math.pi / 2.0)
    sin_of(ang, sn2, 0.0)
    I("dve", "tensor_scalar", [sn2, crope], [sn2], out=sn2[R0:R1, :], in0=sn2[R0:R1, :],
      scalar1=crope[R0:R1, 1:2], scalar2=None, op0=ALU.mult)

    for kt in range(2):
        mk.dma("sp", memt[:, kt, :], mem_in[kt * 128:(kt + 1) * 128, :], [mem_in], [(memt, (kt,))], "m%d" % kt)
        c = statcol(3)
        I("act", "activation", [(memt, (kt,))], [junk0, (stat, (c,))], out=junk0[:], in_=memt[:, kt, :], func=AF.Square,
          accum_out=stat[:, c:c + 1])
        I("act", "activation", [(stat, (c,)), cst], [(stat, (c,))], out=stat[:, c + 1:c + 2], in_=stat[:, c:c + 1],
          func=AF.Sqrt, scale=1.0 / D, bias=cst[:, 0:1])
        I("dve", "reciprocal", [(stat, (c,))], [(stat, (c,))], out=stat[:, c + 2:c + 3], in_=stat[:, c + 1:c + 2])
        I("dve", "tensor_scalar", [(memt, (kt,)), (stat, (c,))], [(memn, (kt,))], out=memn[:, kt, :], in0=memt[:, kt, :],
          scalar1=stat[:, c + 2:c + 3], scalar2=None, op0=ALU.mult)
        tb_ = nexttbank()
        tv = tb_[:].bitcast(BF16)
        for kc in range(KC):
            I("pe", "transpose", [(memn, (kt,)), ident_b], [tb_], out=tv[:, kc * 128:(kc + 1) * 128],
              in_=memn[:, kt, kc * 128:(kc + 1) * 128], identity=ident_b[:])
        I("dve", "tensor_copy", [tb_], [memT], out=memT[:, :, kt * 128:(kt + 1) * 128],
          in_=tv[:, 0:1024].rearrange("p (k t) -> p k t", k=KC))

    cur[0] = T0
    hT = alloc("hT", [128, KC, S], BF16)
    R_A = cur[0]
    xt = [alloc("xt%d" % i, [128, D], F32) for i in range(2)]
    xn = [alloc("xn%d" % i, [128, D], BF16) for i in range(2)]
    junk = alloc("junk", [128, D], BF16)
    R_A_END = cur[0]
    rx = [alloc("rx%d" % i, [128, 256], F32) for i in range(3)]
    R_B = cur[0]

    raw = [alloc("raw%d" % i, [128, S], F32) for i in range(2)]
    acc = [alloc("acc%d" % i, [128, S], F32) for i in range(2)]
    tmpT = [alloc("tmpT%d" % i, [128, S], BF16) for i in range(2)]
    zst = [alloc("zst%d" % i, [128, 512], BF16) for i in range(2)]
    tkst = [alloc("tkst%d" % i, [128, 8, 128], BF16) for i in range(2)]
    R_X_END = cur[0]
    cqnT = alloc("cqnT", [128, 3, S], BF16)
    ckvnT = alloc("ckvnT", [128, 2, S], BF16)
    kpeT = alloc("kpeT", [128, S], BF16)
    dtt = alloc("dtt", [128, NT, NH], F32)
    adt = alloc("adt", [128, NT, NH], F32)
    ssqa = alloc("ssqa", [128, NT], F32)
    rsta = alloc("rsta", [128, NT], F32)
    R_Y = cur[0]
    sqb = [alloc("sqb%d" % i, [128, S], BF16) for i in range(2)]
    rstd = alloc("rstd", [128, S], F32)
    wkr = alloc("wkr", [128, KC, 128], BF16)
    rtmp = alloc("rtmp", [128, 2, 512], F32)
    cur[0] = R_Y
    yattT = alloc("yattT", [128, KC, S], BF16)
    MIX_END = cur[0]

    cur[0] = R_B
    st_f = alloc("st_f", [128, D], F32)
    st_b = alloc("st_b", [128, D], BF16)
    adUs = [alloc("adU%d" % i, [128, NH, 128], F32) for i in range(2)]
    Eexp = [alloc("Eexp%d" % i, [128, 512], F32) for i in range(2)]
    MT = [alloc("MT%d" % i, [128, 512], BF16) for i in range(2)]
    cbm = [alloc("cbm%d" % i, [128, 128], F32) for i in range(2)]
    xsc = [alloc("xsc%d" % i, [128, D], BF16) for i in range(2)]
    btc = [alloc("btc%d" % i, [128, 512], BF16) for i in range(2)]
    bTc = [alloc("bTc%d" % i, [128, 4, 128], BF16) for i in range(2)]
    cTc = [alloc("cTc%d" % i, [128, 4, 128], BF16) for i in range(2)]
    szc = [alloc("szc%d" % i, [128, D], BF16) for i in range(2)]
    assert cur[0] <= R_X_END, (cur[0], R_X_END)
    cur[0] = R_Y
    xd = alloc("xd", [128, D], BF16)
    xdd = alloc("xdd", [128, D], BF16)
    xsk = alloc("xsk", [128, D], F32)
    yfs = [alloc("yf%d" % i, [128, D], F32) for i in range(2)]
    yo = alloc("yo", [128, D], F32)
    ygb = alloc("ygb", [128, D], BF16)
    sm16s = [alloc("sm16_%d" % i, [128, 8, NH], F32) for i in range(2)]
    assert cur[0] <= MIX_END, (cur[0], MIX_END)
    cur[0] = R_B
    QT = [alloc("QT%d" % i, [128, S], BF16) for i in range(2)]
    KT = [alloc("KT%d" % i, [128, S], BF16) for i in range(2)]
    Vh = [alloc("Vh%d" % i, [128, NT, 96], BF16) for i in range(2)]
    PT = [alloc("PT%d" % i, [128, 512], BF16) for i in range(5)]
    qtmp = alloc("qtmp", [128, 4, 512], F32)
    rsum = alloc("rsum", [128, 8], F32)
    onrm = alloc("onrm", [128, 4, HD], F32)
    osq = alloc("osq", [128, 4, HD], F32)
    ypair = [alloc("ypair%d" % i, [128, NT, 128], BF16) for i in range(2)]
    assert cur[0] <= R_X_END, (cur[0], R_X_END)

    cur[0] = R_B
    qT = alloc("qT", [128, KC, S], BF16)
    oT = alloc("oT", [128, KC, S], BF16)
    mnT = alloc("mnT", [128, KC, MEM], BF16)
    KmT = alloc("KmT", [128, KC, MEM], BF16)
    Vm = alloc("Vm", [128, 2, D], BF16)
    PTm = [alloc("PTm%d" % i, [128, 2, 512], BF16) for i in range(2)]
    rinvb = [alloc("rinvb%d" % i, [128, 512], F32) for i in range(2)]

    cur[0] = R_A
    fraw = alloc("fraw", [128, S], F32)
    assert cur[0] <= R_A_END
    cur[0] = R_B
    aT = alloc("aT", [128, NFF, S], BF16)
    facc = alloc("facc", [128, S], F32)
    sgate = alloc("sgate", [128, S], BF16)
    cur[0] = R_B
    gfin = alloc("gfin", [128, D], F32)
    ofin = [alloc("ofin%d" % i, [128, D], F32) for i in range(2)]

    def rms_tile_stats(src_ap, src_reads, width, jbuf=None):
        jb = junk if jbuf is None else jbuf
        c = statcol(3)
        I("act", "activation", src_reads, [jb, (stat, (c,))], out=jb[:, 0:width], in_=src_ap, func=AF.Square,
          accum_out=stat[:, c:c + 1])
        I("act", "activation", [(stat, (c,)), cst], [(stat, (c,))], out=stat[:, c + 1:c + 2], in_=stat[:, c:c + 1],
          func=AF.Sqrt, scale=1.0 / width, bias=cst[:, 0:1])
        I("dve", "reciprocal", [(stat, (c,))], [(stat, (c,))], out=stat[:, c + 2:c + 3], in_=stat[:, c + 1:c + 2])
        return stat[:, c + 2:c + 3], (stat, (c,))

    def transpose_tile_to(src_buf, dstT, dkey, tile_i, gain_name, l):
        tb_ = nexttbank()
        tv = tb_[:].bitcast(BF16)
        for kc in range(KC):
            I("pe", "transpose", [src_buf, ident_b], [tb_], out=tv[:, kc * 128:(kc + 1) * 128],
              in_=src_buf[:, kc * 128:(kc + 1) * 128], identity=ident_b[:])
        g = vcols(gain_name, l, 0, KC).unsqueeze(2).broadcast_to([128, KC, 128])
        I("dve", "tensor_tensor", [tb_, vecsT], [(dstT, dkey)],
          out=dstT[:, :, tile_i * 128:(tile_i + 1) * 128],
          in0=tv[:, 0:1024].rearrange("p (k t) -> p k t", k=KC), in1=g, op=ALU.mult)

    def norm_phase(xsrc, l, gain_name):
        for i in range(NT):
            t = xt[i % 2]
            mk.dma("sp", t[:], xsrc[i * 128:(i + 1) * 128, :], [(xsrc, (i,))], [t], ("xt", i % 2))
            rinv, skey = rms_tile_stats(t[:], [t], D)
            n = xn[i % 2]
            I("dve", "tensor_scalar", [t, skey], [n], out=n[:], in0=t[:], scalar1=rinv, scalar2=None, op0=ALU.mult)
            transpose_tile_to(n, hT, (None, i // 4), i, gain_name, l)

    def resid_phase(xsrc, l, wname, parts, scaled_last=None):
        NB = 256
        nk_tot = sum(nk for _, nk in parts)
        for cb in range(D // NB):
            slot, wv = wload(wsrc(P[wname], l, cb * NB, NB, 0, nk_tot), nk_tot, NB, P[wname])
            for i in range(NT):
                u = cb * NT + i
                t = rx[u % 3]
                mk.dma("sp", t[:], xsrc[i * 128:(i + 1) * 128, cb * NB:(cb + 1) * NB], [(xsrc, (i,))], [t], ("rx", u % 3))
                k0 = 0
                for pi, (actT, nk) in enumerate(parts):
                    last = (pi == len(parts) - 1)
                    sep = last and scaled_last is not None
                    if pi == 0 or sep:
                        pb = nextbank()
                    for kc in range(nk):
                        st_flag = (kc == 0) if (pi == 0 or sep) else False
                        sp_flag = (kc == nk - 1) and (last or (scaled_last is not None and pi == len(parts) - 2))
                        I("pe", "matmul", [(actT, (None, i // 4)), slot], [pb], pb[:, 0:NB],
                          lhsT=actT[:, kc, i * 128:(i + 1) * 128], rhs=wv[:, k0 + kc, :], start=st_flag, stop=sp_flag)
                    k0 += nk
                    if sep:
                        I("dve", "scalar_tensor_tensor", [pb, t, scaled_last], [t], out=t[:], in0=pb[:, 0:NB],
                          scalar=scaled_last[:, i:i + 1], in1=t[:], op0=ALU.mult, op1=ALU.add)
                    elif last or (scaled_last is not None and pi == len(parts) - 2):
                        I("dve", "tensor_tensor", [pb, t], [t], out=t[:], in0=pb[:, 0:NB], in1=t[:], op=ALU.add)
                mk.dma("sp", xres[i * 128:(i + 1) * 128, cb * NB:(cb + 1) * NB], t[:], [t], [(xres, (i,))], ("rxo", u % 3))

    def proj_fm(wv, col0, m, nk, actT, slot, cb_fn):
        for tb in range(4):
            pb = nextbank()
            for kc in range(nk):
                I("pe", "matmul", [slot, (actT, (None, tb))], [pb], pb[0:m, :],
                  lhsT=wv[:, kc, col0:col0 + m], rhs=actT[:, kc, tb * 512:(tb + 1) * 512],
                  start=(kc == 0), stop=(kc == nk - 1))
            cb_fn(tb, pb)

    def rope_rows(ps, dst, dkey, ts_):
        I("dve", "tensor_tensor", [ps, cs2], [qtmp], out=qtmp[64:96, 0, :], in0=ps[64:96, :], in1=cs2[64:96, ts_], op=ALU.mult)
        I("dve", "tensor_tensor", [ps, sn2], [qtmp], out=qtmp[64:96, 1, :], in0=ps[96:128, :], in1=sn2[96:128, ts_], op=ALU.mult)
        I("dve", "tensor_tensor", [qtmp], [(dst, dkey)], out=dst[64:96, ts_], in0=qtmp[64:96, 0, :], in1=qtmp[64:96, 1, :],
          op=ALU.add)
        I("dve", "tensor_tensor", [ps, cs2], [qtmp], out=qtmp[96:128, 2, :], in0=ps[96:128, :], in1=cs2[96:128, ts_], op=ALU.mult)
        I("dve", "tensor_tensor", [ps, sn2], [qtmp], out=qtmp[96:128, 3, :], in0=ps[64:96, :], in1=sn2[64:96, ts_], op=ALU.mult)
        I("dve", "tensor_tensor", [qtmp], [(dst, dkey)], out=dst[96:128, ts_], in0=qtmp[96:128, 2, :], in1=qtmp[96:128, 3, :],
          op=ALU.add)

    xsrc = x_in
    STAGES = ["norm", "z", "xbc", "prep", "ssd", "mla", "wout", "memattn", "ffn"]
    sub_stop = 99
    if stop_after and ":" in stop_after:
        stop_after, ss_ = stop_after.split(":")
        sub_stop = int(ss_)
    stop_idx = STAGES.index(stop_after) if stop_after else 99

    class _Stop(Exception):
        pass

    def layer_body(l, xsrc):
        norm_phase(xsrc, l, "norm_mix")
        if stop_idx <= 0:
            raise _Stop()

        for cb in range(2):
            slot, wv = wload(wsrc(P["w_in"], l, O_Z + cb * 512, 512), KC, 512, P["w_in"])
            for i in range(NT):
                pb = nextbank()
                for kc in range(KC):
                    I("pe", "matmul", [(hT, (None, i // 4)), slot], [pb], pb[:, :],
                      lhsT=hT[:, kc, i * 128:(i + 1) * 128], rhs=wv[:, kc, :], start=(kc == 0), stop=(kc == KC - 1))
                u = cb * NT + i
                z = zst[u % 2]
                I("act", "activation", [pb], [z], out=z[:], in_=pb[:, :], func=AF.Silu)
                mk.dma("sp", sz_d[i * 128:(i + 1) * 128, cb * 512:(cb + 1) * 512], z[:], [z], [(sz_d, (i,))], ("zst", u % 2))

        if stop_idx <= 1:
            raise _Stop()
        for cg in range(8):
            slot, wv = wload(wsrc(P["w_in"], l, O_XBC + cg * 256, 256), KC, 256, P["w_in"])
            for sub in range(2):
                c = cg * 2 + sub
                rw = raw[c % 2]
                ac = acc[c % 2]

                def evac(tb, pb, rw=rw):
                    I("act", "copy", [pb], [(rw, (tb,))], out=rw[:, tb * 512:(tb + 1) * 512], in_=pb[:, :])
                proj_fm(wv, sub * 128, 128, KC, hT, slot, evac)
                w3, w2, w1, w0 = (vcol("ssm_conv_w", l, j * 16 + c) for j in (3, 2, 1, 0))
                bb = vcol("ssm_conv_b", l, c)
                I("dve", "tensor_scalar", [rw, vecsT], [ac], out=ac[:], in0=rw[:], scalar1=w3, scalar2=bb,
                  op0=ALU.mult, op1=ALU.add)
                for sh, wj in ((1, w2), (2, w1), (3, w0)):
                    I("dve", "scalar_tensor_tensor", [rw, ac, vecsT], [ac], out=ac[:, sh:S], in0=rw[:, 0:S - sh],
                      scalar=wj, in1=ac[:, sh:S], op0=ALU.mult, op1=ALU.add)
                tt = tmpT[c % 2]
                I("act", "activation", [ac], [tt], out=tt[:], in_=ac[:], func=AF.Silu)
                if c < 12:
                    for half in range(2):
                        tk = tkst[half]
                        tb_ = nexttbank()
                        tv = tb_[:].bitcast(BF16)
                        for j in range(8):
                            i = half * 8 + j
                            I("pe", "transpose", [tt, ident_b], [tb_], out=tv[:, j * 128:(j + 1) * 128],
                              in_=tt[:, i * 128:(i + 1) * 128], identity=ident_b[:])
                        I("dve", "tensor_copy", [tb_], [tk], out=tk[:], in_=tv[:, 0:1024].rearrange("p (j t) -> p j t", j=8))
                        rows = slice(half * 1024, (half + 1) * 1024)
                        if c < 8:
                            dst = xs_d[rows, c * 128:(c + 1) * 128].rearrange("(i p) f -> p i f", p=128)
                            mk.dma("sp", dst, tk[:], [tk], [xs_d], ("tk", half))
                        else:
                            dst = bt_d[rows, (c - 8) * 128:(c - 7) * 128].rearrange("(i p) f -> p i f", p=128)
                            mk.dma("sp", dst, tk[:], [tk], [bt_d], ("tk", half))
                if 8 <= c < 12:
                    mk.dma("sp", bT_d[c - 8], tt[:], [tt], [bT_d], ("tT", c % 2))
                if c >= 12:
                    mk.dma("sp", cT_d[c - 12], tt[:], [tt], [cT_d], ("tT", c % 2))

        if stop_idx <= 2:
            raise _Stop()
        slot = nextslot()
        wv = slot[:, 0:KC * 64].rearrange("p (k n) -> p k n", k=KC)
        mk.dma("pool", wv[:, :, 0:NH], wsrc(P["w_in"], l, O_DT, NH), [P["w_in"]], [slot], ("w", slot.name))
        import os
        DTV = int(os.environ.get("MK_DT", "9"))
        for i in range(NT if DTV >= 2 else 0):
            pb = nextbank()
            for kc in range(KC):
                I("pe", "matmul", [(hT, (None, i // 4)), slot], [pb], pb[:, 0:NH],
                  lhsT=hT[:, kc, i * 128:(i + 1) * 128], rhs=wv[:, kc, 0:NH], start=(kc == 0), stop=(kc == KC - 1))
            I("dve", "tensor_tensor", [pb, bc16], [(dtt, (i,))], out=dtt[:, i, :], in0=pb[:, 0:NH], in1=bc16[:, 0, l, :],
              op=ALU.add)
            if DTV >= 3:
                I("act", "activation", [(dtt, (i,))], [(dtt, (i,))], out=dtt[:, i, :], in_=dtt[:, i, :], func=AF.Exp)
            if DTV >= 4:
                I("act", "activation", [(dtt, (i,)), cst], [(dtt, (i,))], out=dtt[:, i, :], in_=dtt[:, i, :], func=AF.Ln,
                  bias=cst[:, 1:2], scale=1.0)
            if DTV >= 5:
                I("dve", "tensor_tensor", [(dtt, (i,)), bc16], [(adt, (i,))], out=adt[:, i, :], in0=dtt[:, i, :],
                  in1=bc16[:, 1, l, :], op=ALU.mult)

        if stop_idx == 3 and sub_stop <= 0:
            raise _Stop()

        def latent(wv, slot, col0, nch, dstT, gname, width):
            for c in range(nch):
                sq = sqb[c % 2]

                def evac(tb, pb, c=c, sq=sq):
                    I("act", "activation", [pb], [(sq, (tb,))], out=sq[:, tb * 512:(tb + 1) * 512], in_=pb[:, :],
                      func=AF.Square)
                    I("dve", "tensor_scalar", [pb, vecsT], [(dstT, (c, tb))], out=dstT[:, c, tb * 512:(tb + 1) * 512],
                      in0=pb[:, :], scalar1=vcol(gname, l, c), scalar2=None, op0=ALU.mult)
                proj_fm(wv, col0 + c * 128, 128, KC, hT, slot, evac)
                for tb in range(4):
                    pb = nextbank()
                    I("pe", "matmul", [(sq, (tb,)), ones_b], [pb], pb[:, :], lhsT=ones_b[:],
                      rhs=sq[:, tb * 512:(tb + 1) * 512], start=True, stop=True)
                    if c == 0:
                        I("dve", "tensor_copy", [pb], [(rstd, (tb,))], out=rstd[:, tb * 512:(tb + 1) * 512], in_=pb[:, :])
                    else:
                        I("dve", "tensor_tensor", [pb, (rstd, (tb,))], [(rstd, (tb,))],
                          out=rstd[:, tb * 512:(tb + 1) * 512], in0=pb[:, :], in1=rstd[:, tb * 512:(tb + 1) * 512],
                          op=ALU.add)
            I("act", "activation", [rstd, cst], [rstd], out=rstd[:], in_=rstd[:], func=AF.Sqrt, scale=1.0 / width,
              bias=cst[:, 0:1])
            I("dve", "reciprocal", [rstd], [rstd], out=rstd[:], in_=rstd[:])
            for c in range(nch):
                I("dve", "tensor_tensor", [(dstT, (c, None)), rstd], [(dstT, (c, None))], out=dstT[:, c, :],
                  in0=dstT[:, c, :], in1=rstd[:], op=ALU.mult)

        slot, wv = wload(wsrc(P["w_in"], l, O_CQ, 384), KC, 384, P["w_in"])
        latent(wv, slot, 0, 3, cqnT, "q_norm", Q_LORA)
        if stop_idx == 3 and sub_stop <= 1:
            raise _Stop()
        slot = nextslot()
        wv = slot[:, 0:KC * 64].rearrange("p (k n) -> p k n", k=KC)
        mk.dma("pool", wv[:, :, 0:ROPE], wsrc(P["w_in"], l, O_KR, ROPE), [P["w_in"]], [slot], ("w", slot.name))
        I("dve", "memset", [], [wkr], wkr[:], 0.0)
        I("dve", "tensor_copy", [slot, wkr], [wkr], out=wkr[:, :, 64:80], in_=wv[:, :, 0:16])
        I("dve", "tensor_copy", [slot, wkr], [wkr], out=wkr[:, :, 96:112], in_=wv[:, :, 16:32])
        slot, wv = wload(wsrc(P["w_in"], l, O_CKV, 256), KC, 256, P["w_in"])
        if stop_idx == 3 and sub_stop <= 2:
            raise _Stop()
        latent(wv, slot, 0, 2, ckvnT, "kv_norm", KV_LORA)
        if stop_idx == 3 and sub_stop <= 3:
            raise _Stop()
        for tb in range(4):
            pa = nextbank()
            for kc in range(KC):
                I("pe", "matmul", [wkr, (hT, (None, tb))], [pa], pa[:, :], lhsT=wkr[:, kc, :],
                  rhs=hT[:, kc, tb * 512:(tb + 1) * 512], start=(kc == 0), stop=(kc == KC - 1))
            rope_rows(pa, kpeT, (tb,), slice(tb * 512, (tb + 1) * 512))

        if stop_idx <= 3:
            raise _Stop()
        I("dve", "memset", [], [st_f], st_f[:], 0.0)
        I("dve", "memset", [], [st_b], st_b[:], 0.0)
        pending_tail = [None]
        def ssd_decay_prep(c):
            adU = adUs[c % 2]
            sm16 = sm16s[c % 2]
            I("pool", "tensor_tensor", [(adt, (c,)), U_f], [adU], out=adU[:],
              in0=adt[:, c, :].unsqueeze(2).broadcast_to([128, NH, 128]),
              in1=U_f[:].unsqueeze(1).broadcast_to([128, NH, 128]), op=ALU.mult)
            pc = nextbank()
            I("pe", "matmul", [U_f, (adt, (c,))], [pc], pc[:, 0:NH], lhsT=U_f[:], rhs=adt[:, c, :], start=True, stop=True)
            I("pe", "matmul", [ones_f, (adt, (c,))], [pc], pc[:, 64:64 + NH], lhsT=ones_f[:], rhs=adt[:, c, :],
              start=True, stop=True, skip_group_check=True)
            I("dve", "tensor_copy", [pc], [sm16], out=sm16[:, 3:5, :], in_=pc[:, 0:128].rearrange("p (a h) -> p a h", a=2)[:, :, 0:NH])
            I("dve", "tensor_tensor", [sm16], [sm16], out=sm16[:, 5, :], in0=sm16[:, 4, :], in1=sm16[:, 3, :], op=ALU.subtract)
            I("act", "activation", [sm16], [sm16], out=sm16[:, 0:3, :], in_=sm16[:, 3:6, :], func=AF.Exp)

        ssd_decay_prep(0)
        for c in range(NT):
            cs_ = slice(c * 128, (c + 1) * 128)
            xs_c, bt_c, bT_c, cT_c, sz_c = xsc[c % 2], btc[c % 2], bTc[c % 2], cTc[c % 2], szc[c % 2]
            yf = yfs[c % 2]
            mk.dma("sp", xs_c[:], xs_d[cs_, :], [xs_d], [xs_c], ("xsc", c % 2))
            mk.dma("sp", bt_c[:], bt_d[cs_, :], [bt_d], [bt_c], ("btc", c % 2))
            mk.dma("sp", bT_c[:], bT_d[:, :, cs_].rearrange("g n t -> n g t"), [bT_d], [bT_c], ("bTc", c % 2))
            mk.dma("sp", cT_c[:], cT_d[:, :, cs_].rearrange("g n t -> n g t"), [cT_d], [cT_c], ("cTc", c % 2))
            mk.dma("sp", sz_c[:], sz_d[cs_, :], [(sz_d, (c,))], [sz_c], ("szc", c % 2))
            adU = adUs[c % 2]
            sm16 = sm16s[c % 2]
            if c + 1 < NT:
                ssd_decay_prep(c + 1)
            x3 = xs_c[:].rearrange("p (h d) -> p h d", h=NH)
            I("dve", "tensor_tensor", [xs_c, (dtt, (c,))], [xd], out=xd[:].rearrange("p (h d) -> p h d", h=NH), in0=x3,
              in1=dtt[:, c, :].unsqueeze(2).broadcast_to([128, NH, HD]), op=ALU.mult)
            I("dve", "tensor_tensor", [xd, sm16], [xdd], out=xdd[:].rearrange("p (h d) -> p h d", h=NH),
              in0=xd[:].rearrange("p (h d) -> p h d", h=NH),
              in1=sm16[:, 2, :].unsqueeze(2).broadcast_to([128, NH, HD]), op=ALU.mult)
            I("pool", "tensor_tensor", [xs_c, bc16], [xsk], out=xsk[:].rearrange("p (h d) -> p h d", h=NH), in0=x3,
              in1=bc16[:, 2, l, :].unsqueeze(2).broadcast_to([128, NH, HD]), op=ALU.mult)
            for hh in range(2):
                hs = slice(hh * 512, (hh + 1) * 512)
                py, po = PB[4], PB[5]
                for g in (2 * hh, 2 * hh + 1):
                    pd = nextbank()
                    I("pe", "matmul", [SL_f, adU], [pd], pd[:, :], lhsT=SL_f[:],
                      rhs=adU[:, g * 4:(g + 1) * 4, :].rearrange("p h l -> p (h l)"), start=True, stop=True)
                    E = Eexp[g % 2]
                    I("act", "activation", [pd], [E], out=E[:], in_=pd[:, :], func=AF.Exp)
                    pcb = nextbank()
                    I("pe", "matmul", [bT_c, cT_c], [pcb], pcb[:, 0:128], lhsT=bT_c[:, g, :], rhs=cT_c[:, g, :],
                      start=True, stop=True)
                    cb_ = cbm[g % 2]
                    I("dve", "tensor_tensor", [pcb, U_f], [cb_], out=cb_[:], in0=pcb[:, 0:128], in1=U_f[:], op=ALU.mult)
                    M = MT[g % 2]
                    I("dve", "tensor_tensor", [E, cb_], [M], out=M[:].rearrange("p (r l) -> p r l", r=4),
                      in0=E[:].rearrange("p (r l) -> p r l", r=4), in1=cb_[:].unsqueeze(1).broadcast_to([128, 4, 128]),
                      op=ALU.mult)
                    for r_ in range(4):
                        h = g * 4 + r_
                        oc = (g % 2) * 256 + r_ * 64
                        I("pe", "matmul", [M, xd], [py], py[:, oc:oc + 64], lhsT=M[:, r_ * 128:(r_ + 1) * 128],
                          rhs=xd[:, h * HD:(h + 1) * HD], start=True, stop=True, skip_group_check=True)
                    oc = (g % 2) * 256
                    I("pe", "matmul", [cT_c, st_b], [po], po[:, oc:oc + 256], lhsT=cT_c[:, g, :],
                      rhs=st_b[:, g * 256:(g + 1) * 256], start=True, stop=True, skip_group_check=True)
                I("dve", "tensor_tensor", [po, sm16], [yo], out=yo[:, hs].rearrange("p (h d) -> p h d", h=8),
                  in0=po[:, :].rearrange("p (h d) -> p h d", h=8),
                  in1=sm16[:, 0, hh * 8:(hh + 1) * 8].unsqueeze(2).broadcast_to([128, 8, HD]), op=ALU.mult)
                I("dve", "tensor_tensor", [py, yo], [yf], out=yf[:, hs], in0=py[:, :], in1=yo[:, hs], op=ALU.add)
            I("pool", "tensor_tensor", [yf, xsk], [yf], out=yf[:], in0=yf[:], in1=xsk[:], op=ALU.add)
            I("pool", "tensor_tensor", [st_f, sm16], [st_f], out=st_f[:].rearrange("p (h d) -> p h d", h=NH),
              in0=st_f[:].rearrange("p (h d) -> p h d", h=NH),
              in1=sm16[:, 1, :].unsqueeze(2).broadcast_to([128, NH, HD]), op=ALU.mult)
            for hh in range(2):
                hs = slice(hh * 512, (hh + 1) * 512)
                ps_ = nextbank()
                for g in (2 * hh, 2 * hh + 1):
                    oc = (g % 2) * 256
                    I("pe", "matmul", [bt_c, xdd], [ps_], ps_[:, oc:oc + 256], lhsT=bt_c[:, g * 128:(g + 1) * 128],
                      rhs=xdd[:, g * 256:(g + 1) * 256], start=True, stop=True, skip_group_check=True)
                I("dve", "tensor_tensor", [ps_, st_f], [st_f], out=st_f[:, hs], in0=ps_[:, :], in1=st_f[:, hs], op=ALU.add)
            I("act", "copy", [st_f], [st_b], out=st_b[:], in_=st_f[:])
            def ssd_tail(c=c, yf=yf, sz_c=sz_c):
                I("pool", "tensor_tensor", [yf, sz_c], [yf], out=yf[:], in0=yf[:], in1=sz_c[:], op=ALU.mult)
                rinv, skey = rms_tile_stats(yf[:], [yf], D)
                I("act", "activation", [yf, skey], [ygb], out=ygb[:], in_=yf[:], func=AF.Copy, scale=rinv)
                transpose_tile_to(ygb, hT, (None, c // 4), c, "ssm_norm", l)
            if pending_tail[0] is not None:
                pending_tail[0]()
            pending_tail[0] = ssd_tail
        pending_tail[0]()
        pending_tail[0] = None

        if stop_idx <= 4:
            raise _Stop()
        wqs_ = nextslot()
        wq4 = wqs_[:, 0:3 * NH * 128].rearrange("p (k h d) -> p k h d", k=3, h=NH)
        I("dve", "memset", [], [wqs_], wqs_[:], 0.0)
        for kc in range(3):
            s4 = P["w_uq"][l][kc * 128:(kc + 1) * 128, :].rearrange("p (h d) -> p h d", h=NH)
            mk.dma("pool", wq4[:, kc, :, 0:80], s4[:, :, 0:80], [P["w_uq"]], [wqs_], ("w", wqs_.name))
            mk.dma("pool", wq4[:, kc, :, 96:112], s4[:, :, 80:96], [P["w_uq"]], [wqs_], ("w", wqs_.name))
        wkvs_ = nextslot()
        wkv = wkvs_[:, 0:2 * NH * 128].rearrange("p (k n) -> p k n", k=2)
        for kc in range(2):
            mk.dma("pool", wkv[:, kc, :], P["w_ukv"][l][kc * 128:(kc + 1) * 128, :], [P["w_ukv"]], [wkvs_],
                   ("w", wkvs_.name))
        I("dve", "memset", [], [ssqa], ssqa[:], 0.0)
        scale = 96.0 ** -0.5
        def mla_proj(h):
            Qh, Kh, V_ = QT[h % 2], KT[h % 2], Vh[h % 2]
            for tb in range(4):
                ts_ = slice(tb * 512, (tb + 1) * 512)
                pa = nextbank()
                for kc in range(3):
                    I("pe", "matmul", [wqs_, (cqnT, (None, tb))], [pa], pa[:, :],
                      lhsT=wq4[:, kc, h, :], rhs=cqnT[:, kc, ts_], start=(kc == 0), stop=(kc == 2))
                I("act", "copy", [pa], [(Qh, (tb, 0))], out=Qh[0:64, ts_], in_=pa[0:64, :])
                rope_rows(pa, Qh, (tb, 1), ts_)
                pk = nextbank()
                for kc in range(2):
                    I("pe", "matmul", [wkvs_, (ckvnT, (None, tb))], [pk], pk[0:64, :],
                      lhsT=wkv[:, kc, h * 128:h * 128 + 64], rhs=ckvnT[:, kc, ts_], start=(kc == 0), stop=(kc == 1))
                I("act", "copy", [pk], [(Kh, (tb, 0))], out=Kh[0:64, ts_], in_=pk[0:64, :])
                I("pool", "tensor_copy", [(kpeT, (tb,))], [(Kh, (tb, 1))], out=Kh[64:128, ts_], in_=kpeT[64:128, ts_])
            I("pool", "memset", [], [V_], V_[:, :, 64:65], 1.0)
            for i in range(NT):
                pv = nextbank()
                for kc in range(2):
                    I("pe", "matmul", [(ckvnT, (None, i // 4)), wkvs_], [pv], pv[:, 0:64],
                      lhsT=ckvnT[:, kc, i * 128:(i + 1) * 128], rhs=wkv[:, kc, h * 128 + 64:h * 128 + 128],
                      start=(kc == 0), stop=(kc == 1))
                I("act", "copy", [pv], [V_], out=V_[:, i, 0:64], in_=pv[:, 0:64])

        def mla_attn(h):
            Qh, Kh, V_ = QT[h % 2], KT[h % 2], Vh[h % 2]
            yp = ypair[(h // 2) % 2]

            def emit_scores(qb, kt, sidx):
                q0 = max(qb * 512, kt * 128)
                q1 = (qb + 1) * 512
                n = q1 - q0
                psc = nextbank()
                I("pe", "matmul", [Kh, Qh], [psc], psc[:, 0:n], lhsT=Kh[:, kt * 128:(kt + 1) * 128],
                  rhs=Qh[:, q0:q1], start=True, stop=True)
                pt = PT[sidx % 5]
                I("act", "activation", [psc], [pt], out=pt[:, 0:n], in_=psc[:, 0:n], func=AF.Exp, scale=scale)
                if kt * 128 >= qb * 512:
                    I("pool", "tensor_tensor", [pt, U_b], [pt], out=pt[:, 0:128], in0=pt[:, 0:128], in1=U_b[:], op=ALU.mult)
                return pt, n, q0

            def emit_pv(qb, kt, pt, n, q0):
                pacc = PB[4 + (h * 4 + qb) % 2]
                nkt = (qb + 1) * 4
                if kt == 0:
                    I("pe", "matmul", [zeros_b], [pacc], pacc[:, :], lhsT=zeros_b[:, 0:128], rhs=zeros_b[:, :],
                      start=True, stop=False)
                for j in range(n // 128):
                    qt = (q0 - qb * 512) // 128 + j
                    I("pe", "matmul", [pt, V_], [pacc], pacc[:, qt * 128:qt * 128 + 65], lhsT=pt[:, j * 128:(j + 1) * 128],
                      rhs=V_[:, kt, 0:65], start=False, stop=False, skip_group_check=True)
                if kt != nkt - 1:
                    return
                I("pe", "matmul", [zeros_b], [pacc], pacc[:, :], lhsT=zeros_b[:, 0:128], rhs=zeros_b[:, :],
                  start=False, stop=True)
                a3 = pacc[:, :].rearrange("p (q e) -> p q e", q=4)
                I("dve", "reciprocal", [pacc], [rsum], out=rsum[:, 0:4], in_=a3[:, :, 64])
                I("dve", "tensor_tensor", [pacc, rsum], [onrm], out=onrm[:], in0=a3[:, :, 0:64],
                  in1=rsum[:, 0:4].unsqueeze(2).broadcast_to([128, 4, HD]), op=ALU.mult)
                I("pool", "tensor_copy", [onrm], [(yp, (qb, h % 2))],
                  out=yp[:, qb * 4:(qb + 1) * 4, (h % 2) * 64:(h % 2) * 64 + 64], in_=onrm[:])
                I("dve", "tensor_tensor", [onrm], [osq], out=osq[:], in0=onrm[:], in1=onrm[:], op=ALU.mult)
                I("dve", "tensor_reduce", [osq], [rsum], out=rsum[:, 4:8], in_=osq[:], axis=mybir.AxisListType.X, op=ALU.add)
                I("dve", "tensor_tensor", [rsum, ssqa], [ssqa], out=ssqa[:, qb * 4:(qb + 1) * 4], in0=ssqa[:, qb * 4:(qb + 1) * 4],
                  in1=rsum[:, 4:8], op=ALU.add)

            steps = [(qb, kt) for qb in range(4) for kt in range((qb + 1) * 4)]
            pend = []
            for sidx, (qb, kt) in enumerate(steps):
                cur = emit_scores(qb, kt, sidx)
                pend.append((qb, kt) + cur)
                if len(pend) > 3:
                    emit_pv(*pend.pop(0))
            while pend:
                emit_pv(*pend.pop(0))
            if h % 2 == 1:
                hp = h // 2
                gcol = vcol("attn_out_norm", l, hp)
                for half in range(2):
                    tb_ = nexttbank()
                    tv = tb_[:].bitcast(BF16)
                    for j in range(8):
                        i = half * 8 + j
                        I("pe", "transpose", [yp, ident_b], [tb_], out=tv[:, j * 128:(j + 1) * 128], in_=yp[:, i, :],
                          identity=ident_b[:])
                    I("dve", "tensor_scalar", [tb_, vecsT], [(yattT, (hp, None))],
                      out=yattT[:, hp, half * 1024:(half + 1) * 1024], in0=tv[:, 0:1024], scalar1=gcol, scalar2=None,
                      op0=ALU.mult)
        mla_proj(0)
        for h in range(NH):
            if h + 1 < NH:
                mla_proj(h + 1)
            mla_attn(h)
        I("act", "activation", [ssqa, cst], [rsta], out=rsta[:], in_=ssqa[:], func=AF.Sqrt, scale=1.0 / D, bias=cst[:, 0:1])
        I("dve", "reciprocal", [rsta], [rsta], out=rsta[:], in_=rsta[:])

        if stop_idx <= 5:
            raise _Stop()
        resid_phase(xsrc, l, "w_out", [(hT, KC), (yattT, KC)], scaled_last=rsta)
        xsrc = xres
        cur_x[0] = xres

        if stop_idx <= 6:
            raise _Stop()
        norm_phase(xsrc, l, "norm_mem_q")
        I("dve", "tensor_tensor", [memT, vecsT], [mnT], out=mnT[:], in0=memT[:],
          in1=vcols("norm_mem_kv", l, 0, KC).unsqueeze(2).broadcast_to([128, KC, MEM]), op=ALU.mult)
        for cg in range(4):
            slot, wv = wload(wsrc(P["w_mk"], l, cg * 256, 256), KC, 256, P["w_mk"])
            for sub in range(2):
                dc = cg * 2 + sub
                pb = nextbank()
                for kc in range(KC):
                    I("pe", "matmul", [slot, mnT], [pb], pb[:, 0:MEM], lhsT=wv[:, kc, sub * 128:(sub + 1) * 128],
                      rhs=mnT[:, kc, :], start=(kc == 0), stop=(kc == KC - 1))
                I("act", "copy", [pb], [(KmT, (dc,))], out=KmT[:, dc, :], in_=pb[:, 0:MEM])
        for cb in range(2):
            slot, wv = wload(wsrc(P["w_mv"], l, cb * 512, 512), KC, 512, P["w_mv"])
            for kt in range(2):
                pb = nextbank()
                for kc in range(KC):
                    I("pe", "matmul", [mnT, slot], [pb], pb[:, :], lhsT=mnT[:, kc, kt * 128:(kt + 1) * 128], rhs=wv[:, kc, :],
                      start=(kc == 0), stop=(kc == KC - 1))
                I("act", "copy", [pb], [(Vm, (kt, cb))], out=Vm[:, kt, cb * 512:(cb + 1) * 512], in_=pb[:, :])
        for cg in range(4):
            slot, wv = wload(wsrc(P["w_mq"], l, cg * 256, 256), KC, 256, P["w_mq"])
            for sub in range(2):
                dc = cg * 2 + sub

                def evac(tb, pb, dc=dc):
                    I("act", "copy", [pb], [(qT, (dc, tb))], out=qT[:, dc, tb * 512:(tb + 1) * 512], in_=pb[:, :])
                proj_fm(wv, sub * 128, 128, KC, hT, slot, evac)
        mscale = 256.0 ** -0.5
        for hd in range(4):
            for tb in range(4):
                ts_ = slice(tb * 512, (tb + 1) * 512)
                ptm = PTm[(hd * 4 + tb) % 2]
                for kt in range(2):
                    psc = nextbank()
                    for d2 in range(2):
                        dc = hd * 2 + d2
                        I("pe", "matmul", [(KmT, (dc,)), (qT, (dc, tb))], [psc], psc[:, :],
                          lhsT=KmT[:, dc, kt * 128:(kt + 1) * 128], rhs=qT[:, dc, ts_], start=(d2 == 0), stop=(d2 == 1))
                    I("act", "activation", [psc], [(ptm, (kt,))], out=ptm[:, kt, :], in_=psc[:, :], func=AF.Exp, scale=mscale)
                pr = nextbank()
                for kt in range(2):
                    I("pe", "matmul", [ones_b, (ptm, (kt,))], [pr], pr[:, :], lhsT=ones_b[:], rhs=ptm[:, kt, :],
                      start=(kt == 0), stop=(kt == 1))
                rb = rinvb[(hd * 4 + tb) % 2]
                I("dve", "reciprocal", [pr], [rb], out=rb[:], in_=pr[:, :])
                for d2 in range(2):
                    dc = hd * 2 + d2
                    po_ = nextbank()
                    for kt in range(2):
                        I("pe", "matmul", [(Vm, (kt, None)), (ptm, (kt,))], [po_], po_[:, :],
                          lhsT=Vm[:, kt, dc * 128:(dc + 1) * 128], rhs=ptm[:, kt, :], start=(kt == 0), stop=(kt == 1))
                    I("dve", "tensor_tensor", [po_, rb], [(oT, (dc, tb))], out=oT[:, dc, ts_], in0=po_[:, :], in1=rb[:],
                      op=ALU.mult)
        resid_phase(xsrc, l, "w_mo", [(oT, KC)])

        if stop_idx <= 7:
            raise _Stop()
        norm_phase(xsrc, l, "norm_ffn")
        for jg in range(NFF // 2):
            slot_g, wg = wload(wsrc(P["w_up"], l, jg * 256, 256), KC, 256, P["w_up"])
            slot_v, wvv = wload(wsrc(P["w_up"], l, D_FF + jg * 256, 256), KC, 256, P["w_up"])
            for sub in range(2):
                j = jg * 2 + sub
                for which, (slot, wv_) in enumerate(((slot_g, wg), (slot_v, wvv))):
                    cidx = which * NFF + j

                    def evac(tb, pb):
                        I("act", "copy", [pb], [(fraw, (tb,))], out=fraw[:, tb * 512:(tb + 1) * 512], in_=pb[:, :])
                    proj_fm(wv_, sub * 128, 128, KC, hT, slot, evac)
                    w2, w1, w0 = (vcol("ffn_conv_w", l, jj * 44 + cidx) for jj in (2, 1, 0))
                    bb = vcol("ffn_conv_b", l, cidx)
                    I("dve", "tensor_scalar", [fraw, vecsT], [facc], out=facc[:], in0=fraw[:], scalar1=w2, scalar2=bb,
                      op0=ALU.mult, op1=ALU.add)
                    for sh, wj in ((1, w1), (2, w0)):
                        I("dve", "scalar_tensor_tensor", [fraw, facc, vecsT], [facc], out=facc[:, sh:S], in0=fraw[:, 0:S - sh],
                          scalar=wj, in1=facc[:, sh:S], op0=ALU.mult, op1=ALU.add)
                    if which == 0:
                        I("act", "activation", [facc], [sgate], out=sgate[:], in_=facc[:], func=AF.Silu)
                    else:
                        I("dve", "tensor_tensor", [facc, sgate], [(aT, (j, None))], out=aT[:, j, :], in0=facc[:], in1=sgate[:],
                          op=ALU.mult)
        resid_phase(xsrc, l, "w_down", [(aT, NFF)])


    cur_x = [x_in]
    try:
        for l in range(depth):
            layer_body(l, cur_x[0])
            cur_x[0] = xres
    except _Stop:
        pass
    xsrc = cur_x[0]

    mk.dma("sp", gfin[:], P["final_norm"][:].unsqueeze(0).broadcast_to([128, D]), [P["final_norm"]], [gfin], "gf")
    for i in range(NT):
        t = xt[i % 2]
        mk.dma("sp", t[:], xsrc[i * 128:(i + 1) * 128, :], [(xsrc, (i,))], [t], ("xt", i % 2))
        rinv, skey = rms_tile_stats(t[:], [t], D)
        o_ = ofin[i % 2]
        I("dve", "scalar_tensor_tensor", [t, skey, gfin], [o_], out=o_[:], in0=t[:], scalar=rinv, in1=gfin[:],
          op0=ALU.mult, op1=ALU.mult)
        mk.dma("sp", out_d[i * 128:(i + 1) * 128, :], o_[:], [o_], [(out_d, (i,))], ("of", i % 2))

    mk.emit()
    nc.mk_bufs = {b.name: b.t.name for b in all_bufs}
    return nc


_CACHE = {}


def _consts():
    k = np.arange(128)
    U = (k[:, None] <= k[None, :]).astype(np.float32)
    SL = (k[:, None] > k[None, :]).astype(np.float32)
    rope = np.zeros((128, 2), np.float32)
    inv_freq = (1.0 / (np.float32(10000.0) ** (np.arange(0, ROPE, 2, dtype=np.float32) / np.float32(ROPE)))).astype(np.float32)
    rope[64:80, 0] = inv_freq
    rope[96:112, 0] = inv_freq
    rope[64:80, 1] = 1.0
    rope[96:112, 1] = -1.0
    return {"c_ident": np.eye(128, dtype=np.float32), "c_U": U, "c_SL": SL, "c_rope": rope}


def kernel(**inputs):
    return run_depth(inputs, DEPTH)


def run_depth(inputs, depth, trace=False, ncores=N_CORES, stop_after=None):
    key = (depth, stop_after)
    if key not in _CACHE:
        _CACHE[key] = build_program(depth, stop_after=stop_after)
    nc = _CACHE[key]
    consts = _consts()
    x = np.ascontiguousarray(np.asarray(inputs["x"], dtype=np.float32))
    mem = np.ascontiguousarray(np.asarray(inputs["mem"], dtype=np.float32))
    pos = np.ascontiguousarray(np.asarray(inputs["positions"], dtype=np.int32))
    dd = max(depth, 1)
    shared = {}
    for name, shp in PARAMS:
        a = np.asarray(inputs[name], dtype=np.float32)
        if len(shp) > 1 and dd != DEPTH:
            a = a[:dd]
        shared[name] = np.ascontiguousarray(a)
    shared.update(consts)
    in_maps = []
    for b in range(ncores):
        m = dict(shared)
        m["x"] = x[b]
        m["mem"] = mem[b]
        m["positions"] = pos[b:b + 1]
        in_maps.append(m)
    res = run_bass_kernel_spmd(nc, in_maps, core_ids=list(range(ncores)), trace=trace)
    if trace:
        print("exec_time_ns", res.exec_time_ns)
    out = np.stack([np.asarray(r["out"]) for r in res.results], axis=0).astype(np.float32)
    return out
```

```python
import contextlib
import math
import numpy as np

import concourse.bass as bass
import concourse.mybir as mybir
from concourse.alu_op_type import AluOpType as ALU
from concourse.bass_utils import run_bass_kernel_spmd

DT = mybir.dt
AF = mybir.ActivationFunctionType
F32, BF16, I32 = DT.float32, DT.bfloat16, DT.int32

DEPTH = 4
S = 2048
D = 1024
NT = S // 128
KC = D // 128
MEM = 256
EPS = 1e-6
NH = 16
HD = 64
NG = 4
NST = 128
Q_LORA, KV_LORA, ROPE = 384, 256, 32
D_IN = 3760
O_Z, O_XBC, O_DT, O_CQ, O_CKV, O_KR = 0, 1024, 3072, 3088, 3472, 3728
D_FF = 2816
NFF = D_FF // 128
N_CORES = 8

ENGS = ("sp", "pe", "dve", "act", "pool")


class Op:
    __slots__ = ("eng", "fn", "deps", "signal", "sigval", "dma_sem", "dma_val", "id")


class Buf:
    def __init__(self, name, t):
        self.name = name
        self.t = t
        self.writers = []
        self.readers = []
        self.overlaps = []
        self.excl = False

    def __getitem__(self, idx):
        return self.t[idx]


def _conf(a, b):
    if a is None or b is None:
        return True
    for x, y in zip(a, b):
        if x is not None and y is not None and x != y:
            return False
    return True


def _covers(a, b):
    if a is None:
        return True
    if b is None:
        return False
    for x, y in zip(a, b):
        if x is not None and x != y:
            return False
    return True


class MK:
    def __init__(self, nc):
        self.nc = nc
        self.streams = {e: [] for e in ENGS}
        self.eng_sem = {}
        self.dma_sems = {}
        self.nops = 0

    def sbuf_at(self, name, shape, dtype, offset):
        return Buf(name, self.nc.alloc_sbuf_tensor_at(name, list(shape), dtype, offset=offset))

    def psum(self, name, shape, dtype=F32):
        b = Buf(name, self.nc.alloc_psum_tensor(name, list(shape), dtype))
        b.excl = True
        return b

    def dram(self, name, shape, dtype, kind="Internal"):
        return Buf(name, self.nc.dram_tensor(name, list(shape), dtype, kind=kind))

    def overlap(self, a, b):
        a.overlaps.append(b)
        b.overlaps.append(a)

    @staticmethod
    def _norm(lst):
        out = []
        for x in lst:
            if isinstance(x, Buf):
                out.append((x, None))
            else:
                out.append((x[0], tuple(x[1]) if x[1] is not None else None))
        return out

    @staticmethod
    def _absorb(b):
        for y in b.overlaps:
            if y.writers or y.readers:
                ops = [p for _, p in y.writers] + [p for _, p in y.readers]
                y.writers = []
                y.readers = []
                for p in ops:
                    b.writers.append((None, p))
                latest = {}
                keep = []
                for k2, p in b.writers:
                    if k2 is None and p.dma_sem is None:
                        if p.eng not in latest or latest[p.eng].id < p.id:
                            latest[p.eng] = p
                    else:
                        keep.append((k2, p))
                b.writers = keep + [(None, p) for p in latest.values()]

    def op(self, eng, fn, reads=(), writes=(), is_dma=False):
        o = Op()
        o.eng, o.fn, o.deps = eng, fn, []
        o.signal, o.sigval, o.dma_sem, o.dma_val = False, None, None, None
        o.id = self.nops
        self.nops += 1
        reads = self._norm(reads)
        writes = self._norm(writes)
        for b, _ in reads + writes:
            if b.overlaps:
                self._absorb(b)
        seen = set()

        def add(p, raw):
            if p is o or (p.id, raw) in seen:
                return
            seen.add((p.id, raw))
            o.deps.append((p, raw))

        for b, key in reads:
            for k2, p in b.writers:
                if _conf(key, k2):
                    add(p, True)
            if b.excl:
                for k2, p in b.readers:
                    if p.eng != eng:
                        add(p, False)
        for b, key in writes:
            for k2, p in b.writers:
                if _conf(key, k2):
                    add(p, False)
            for k2, p in b.readers:
                if _conf(key, k2):
                    add(p, False)
        for b, key in reads:
            if not is_dma:
                b.readers = [(k2, p) for (k2, p) in b.readers
                             if not (k2 == key and p.eng == eng and p.dma_sem is None)]
            b.readers.append((key, o))
        for b, key in writes:
            b.writers = [(k2, p) for (k2, p) in b.writers if not _covers(key, k2)]
            b.readers = [(k2, p) for (k2, p) in b.readers if (not _covers(key, k2)) or p is o]
            b.writers.append((key, o))
        self.streams[eng].append(o)
        return o

    def I(self, eng, name, reads, writes, *args, **kw):
        def fn(e):
            return getattr(e, name)(*args, **kw)
        return self.op(eng, fn, reads, writes)

    def dma(self, queue, out_ap, in_ap, reads, writes, sem_key, **kw):
        ent = self.dma_sems.get(sem_key)
        if ent is None:
            ent = [None, 0, None]
            self.dma_sems[sem_key] = ent
        prev = ent[2]

        def fn(e):
            return e.dma_start(out=out_ap, in_=in_ap, **kw)

        o = self.op(queue, fn, reads, writes, is_dma=True)
        ent[1] += 16
        o.dma_sem = sem_key
        o.dma_val = ent[1]
        if prev is not None:
            o.deps.append((prev, True))
        ent[2] = o
        return o

    def emit(self):
        nc = self.nc
        for e, ops in self.streams.items():
            for o in ops:
                for p, raw in o.deps:
                    if p.dma_sem is not None:
                        continue
                    if p.eng == o.eng:
                        if o.eng != "pe":
                            p.signal = True
                    else:
                        p.signal = True
        for e, ops in self.streams.items():
            c = 0
            for o in ops:
                if o.dma_sem is None and o.signal:
                    c += 1
                    o.sigval = c
        with contextlib.ExitStack() as st:
            for e in ENGS:
                self.eng_sem[e] = st.enter_context(nc.semaphore("sem_" + e))
            for i, (k, ent) in enumerate(self.dma_sems.items()):
                ent[0] = st.enter_context(nc.semaphore("dsem_%d" % i))
            block = st.enter_context(nc.Block())
            mk = self

            def make(ename):
                def body(eng):
                    waited = {}
                    for o in mk.streams[ename]:
                        for p, raw in o.deps:
                            if p.dma_sem is not None:
                                sem, val, key = mk.dma_sems[p.dma_sem][0], p.dma_val, ("d", p.dma_sem)
                            else:
                                if p.eng == ename and ename == "pe":
                                    continue
                                sem, val, key = mk.eng_sem[p.eng], p.sigval, ("e", p.eng)
                            if waited.get(key, 0) >= val:
                                continue
                            waited[key] = val
                            eng.wait_ge(sem, val)
                        inst = o.fn(eng)
                        if o.dma_sem is not None:
                            inst.then_inc(mk.dma_sems[o.dma_sem][0], 16)
                        elif o.signal:
                            inst.then_inc(mk.eng_sem[ename], 1)
                    for k, ent in mk.dma_sems.items():
                        if ent[2] is not None and ent[2].eng == ename and waited.get(("d", k), 0) < ent[1]:
                            eng.wait_ge(ent[0], ent[1])
                return body

            block.sync(make("sp"))
            block.tensor(make("pe"))
            block.vector(make("dve"))
            block.scalar(make("act"))
            block.gpsimd(make("pool"))


def make_params(DEPTH):
  return [
    ("norm_mix", (DEPTH, D)), ("w_in", (DEPTH, D, D_IN)), ("ssm_conv_w", (DEPTH, 4, 2048)),
    ("ssm_conv_b", (DEPTH, 2048)), ("dt_bias", (DEPTH, NH)), ("a_log", (DEPTH, NH)),
    ("d_skip", (DEPTH, NH)), ("ssm_norm", (DEPTH, D)), ("q_norm", (DEPTH, Q_LORA)),
    ("w_uq", (DEPTH, Q_LORA, NH * 96)), ("kv_norm", (DEPTH, KV_LORA)),
    ("w_ukv", (DEPTH, KV_LORA, NH * 128)), ("attn_out_norm", (DEPTH, D)),
    ("w_out", (DEPTH, 2 * D, D)), ("norm_mem_q", (DEPTH, D)), ("norm_mem_kv", (DEPTH, D)),
    ("w_mq", (DEPTH, D, D)), ("w_mk", (DEPTH, D, D)), ("w_mv", (DEPTH, D, D)), ("w_mo", (DEPTH, D, D)),
    ("norm_ffn", (DEPTH, D)), ("w_up", (DEPTH, D, 2 * D_FF)), ("ffn_conv_w", (DEPTH, 3, 2 * D_FF)),
    ("ffn_conv_b", (DEPTH, 2 * D_FF)), ("w_down", (DEPTH, D_FF, D)), ("final_norm", (D,)),
  ]


PARAMS = make_params(DEPTH)
VEC_ROWS = [("norm_mix", 8), ("ssm_conv_w", 64), ("ssm_conv_b", 16), ("ssm_norm", 8), ("q_norm", 3),
            ("kv_norm", 2), ("attn_out_norm", 8), ("norm_mem_q", 8), ("norm_mem_kv", 8),
            ("norm_ffn", 8), ("ffn_conv_w", 132), ("ffn_conv_b", 44)]


def build_program(depth=DEPTH, debug=None, stop_after=None):
    DEPTH = max(depth, 1)
    PARAMS = make_params(DEPTH)
    nc = bass.Bass("TRN2", target_bir_lowering=False)
    mk = MK(nc)
    I = mk.I

    x_in = mk.dram("x", [S, D], F32, kind="ExternalInput")
    mem_in = mk.dram("mem", [MEM, D], F32, kind="ExternalInput")
    pos_in = mk.dram("positions", [1, S], I32, kind="ExternalInput")
    P = {}
    for name, shp in PARAMS:
        P[name] = mk.dram(name, list(shp), F32, kind="ExternalInput")
    c_ident = mk.dram("c_ident", [128, 128], F32, kind="ExternalInput")
    c_U = mk.dram("c_U", [128, 128], F32, kind="ExternalInput")
    c_SL = mk.dram("c_SL", [128, 128], F32, kind="ExternalInput")
    c_rope = mk.dram("c_rope", [128, 2], F32, kind="ExternalInput")
    out_d = mk.dram("out", [S, D], F32, kind="ExternalOutput")
    xres = mk.dram("xres", [S, D], F32)
    sz_d = mk.dram("sz_d", [S, D], BF16)
    xs_d = mk.dram("xs_d", [S, D], BF16)
    bt_d = mk.dram("bt_d", [S, 512], BF16)
    bT_d = mk.dram("bT_d", [4, 128, S], BF16)
    cT_d = mk.dram("cT_d", [4, 128, S], BF16)
    dbg = {}
    if debug:
        for nm, shp in debug.items():
            dbg[nm] = mk.dram("dbg_" + nm, list(shp), F32, kind="ExternalOutput")

    BASE = 16512
    TOP = 229344
    cur = [BASE]
    all_bufs = []

    def alloc(name, shape, dtype):
        esz = 4 if dtype in (F32, I32) else 2
        n = 1
        for s_ in shape[1:]:
            n *= s_
        nbytes = (n * esz + 31) // 32 * 32
        off = cur[0]
        cur[0] += nbytes
        assert off + nbytes <= TOP, (name, off, nbytes, TOP)
        b = mk.sbuf_at(name, shape, dtype, off)
        b.off, b.nbytes = off, nbytes
        for o in all_bufs:
            if b.off < o.off + o.nbytes and o.off < b.off + b.nbytes:
                mk.overlap(b, o)
        all_bufs.append(b)
        return b

    ident_f = alloc("ident_f", [128, 128], F32)
    ident_b = alloc("ident_b", [128, 128], BF16)
    U_f = alloc("U_f", [128, 128], F32)
    U_b = alloc("U_b", [128, 128], BF16)
    SL_f = alloc("SL_f", [128, 128], F32)
    ones_f = alloc("ones_f", [128, 128], F32)
    ones_b = alloc("ones_b", [128, 128], BF16)
    zeros_b = alloc("zeros_b", [128, 512], BF16)
    cst = alloc("cst", [128, 4], F32)
    NVR = sum(r for _, r in VEC_ROWS) * DEPTH
    NVB = (NVR + 127) // 128
    vecsT = alloc("vecsT", [128, NVB * 128], F32)
    bc16 = alloc("bc16", [128, 3, DEPTH, NH], F32)
    cs2 = alloc("cs2", [128, S], F32)
    sn2 = alloc("sn2", [128, S], F32)
    memT = alloc("memT", [128, KC, MEM], BF16)
    stat = alloc("stat", [128, 96], F32)
    WSLOT = 6144
    wslots = [alloc("wslot%d" % i, [128, WSLOT], BF16) for i in range(2)]
    T0 = cur[0]

    PB = [mk.psum("pb%d" % i, [128, 512], F32) for i in range(8)]
    bank_rr = [0]
    tbank_rr = [0]

    def nextbank():
        b = PB[bank_rr[0] % 4]
        bank_rr[0] += 1
        return b

    def nexttbank():
        b = PB[6 + tbank_rr[0] % 2]
        tbank_rr[0] += 1
        return b

    stat_rr = [0]

    def statcol(n=3):
        c = stat_rr[0]
        if c + n > 96:
            c = 0
        stat_rr[0] = c + n
        return c

    vec_r0 = {}
    r = 0
    for name, rows in VEC_ROWS:
        vec_r0[name] = r
        r += rows * DEPTH
    VR = dict(VEC_ROWS)

    def vcol(name, l, idx):
        c = vec_r0[name] + l * VR[name] + idx
        return vecsT[:, c:c + 1]

    def vcols(name, l, i0, n):
        c = vec_r0[name] + l * VR[name] + i0
        return vecsT[:, c:c + n]

    wrr = [0]

    def nextslot():
        s_ = wslots[wrr[0] % 2]
        wrr[0] += 1
        return s_

    def wload(src_ap, kcn, ncols, src_buf):
        slot = nextslot()
        assert kcn * ncols <= WSLOT
        view = slot[:, 0:kcn * ncols].rearrange("p (k n) -> p k n", k=kcn)
        mk.dma("pool", view, src_ap, [src_buf], [slot], ("w", slot.name))
        return slot, view

    def wsrc(buf, l, c0, ncols, k0=0, kcn=KC):
        return buf[l][k0 * 128:(k0 + kcn) * 128, c0:c0 + ncols].rearrange("(kc p) n -> p kc n", p=128)

    stg = alloc("stg", [128, NVB, 128], F32)
    ang = alloc("ang", [128, S], F32)
    ang2 = alloc("ang2", [128, S], F32)
    kf = alloc("kf", [128, S], F32)
    ki = alloc("ki", [128, S], I32)
    posi = alloc("posi", [128, S], I32)
    crope = alloc("crope", [128, 2], F32)
    memt = alloc("memt", [128, 2, D], F32)
    memn = alloc("memn", [128, 2, D], BF16)
    junk0 = alloc("junk0", [128, D], BF16)

    mk.dma("sp", ident_f[:], c_ident[:], [c_ident], [ident_f], "c0")
    mk.dma("sp", U_f[:], c_U[:], [c_U], [U_f], "c1")
    mk.dma("sp", SL_f[:], c_SL[:], [c_SL], [SL_f], "c2")
    mk.dma("sp", crope[:], c_rope[:], [c_rope], [crope], "c3")
    I("dve", "tensor_copy", [ident_f], [ident_b], out=ident_b[:], in_=ident_f[:])
    I("dve", "tensor_copy", [U_f], [U_b], out=U_b[:], in_=U_f[:])
    I("dve", "memset", [], [ones_f], ones_f[:], 1.0)
    I("dve", "memset", [], [ones_b], ones_b[:], 1.0)
    I("dve", "memset", [], [zeros_b], zeros_b[:], 0.0)
    I("dve", "memset", [], [cst], cst[:, 0:1], EPS)
    I("dve", "memset", [cst], [cst], cst[:, 1:2], 1.0)
    I("dve", "memset", [], [stg], stg[:], 0.0)
    I("dve", "memset", [], [cs2], cs2[:], 0.0)
    I("dve", "memset", [], [sn2], sn2[:], 0.0)
    di = 0
    for name, rows in VEC_ROWS:
        src = P[name]
        shp = dict(PARAMS)[name]
        if len(shp) == 2:
            flat = src[:].rearrange("l (c q) -> (l c) q", q=128)
        else:
            flat = src[:].rearrange("l j (c q) -> (l j c) q", q=128)
        total = rows * DEPTH
        r0 = vec_r0[name]
        done = 0
        while done < total:
            rr = r0 + done
            blk, pp = rr // 128, rr % 128
            n = min(total - done, 128 - pp)
            mk.dma("sp", stg[pp:pp + n, blk, :], flat[done:done + n, :], [src], [stg], "v%d" % (di % 4))
            di += 1
            done += n
    for blk in range(NVB):
        pb = nextbank()
        I("pe", "transpose", [stg, ident_f], [pb], out=pb[:, 0:128], in_=stg[:, blk, :], identity=ident_f[:])
        I("dve", "tensor_copy", [pb], [(vecsT, (blk,))], out=vecsT[:, blk * 128:(blk + 1) * 128], in_=pb[:, 0:128])
    for i, name in enumerate(("dt_bias", "a_log", "d_skip")):
        mk.dma("sp", bc16[:, i, :, :].rearrange("p l h -> p (l h)"),
               P[name][:].rearrange("l h -> (l h)").unsqueeze(0).broadcast_to([128, DEPTH * NH]),
               [P[name]], [bc16], "b%d" % i)
    I("act", "activation", [bc16], [bc16], out=bc16[:, 1, :, :], in_=bc16[:, 1, :, :], func=AF.Exp)
    I("dve", "tensor_scalar", [bc16], [bc16], out=bc16[:, 1, :, :], in0=bc16[:, 1, :, :], scalar1=-1.0,
      scalar2=None, op0=ALU.mult)

    R0, R1 = 64, 128
    mk.dma("sp", posi[R0:R1, :], pos_in[0:1, :].broadcast_to([64, S]), [pos_in], [posi], "c4")
    I("dve", "tensor_copy", [posi], [ang], out=ang[R0:R1, :], in_=posi[R0:R1, :])
    I("dve", "tensor_scalar", [ang, crope], [ang], out=ang[R0:R1, :], in0=ang[R0:R1, :],
      scalar1=crope[R0:R1, 0:1], scalar2=None, op0=ALU.mult)
    TWO_PI = 2.0 * math.pi
    C1 = 6.28125
    C2 = TWO_PI - C1

    def sin_of(src, dst, shift):
        I("dve", "tensor_scalar", [src], [ang2], out=ang2[R0:R1, :], in0=src[R0:R1, :], scalar1=shift,
          scalar2=None, op0=ALU.add)
        I("dve", "tensor_scalar", [ang2], [ki], out=ki[R0:R1, :], in0=ang2[R0:R1, :], scalar1=1.0 / TWO_PI,
          scalar2=None, op0=ALU.mult)
        I("dve", "tensor_copy", [ki], [kf], out=kf[R0:R1, :], in_=ki[R0:R1, :])
        I("dve", "scalar_tensor_tensor", [kf, ang2], [ang2], out=ang2[R0:R1, :], in0=kf[R0:R1, :], scalar=-C1,
          in1=ang2[R0:R1, :], op0=ALU.mult, op1=ALU.add)
        I("dve", "scalar_tensor_tensor", [kf, ang2], [ang2], out=ang2[R0:R1, :], in0=kf[R0:R1, :], scalar=-C2,
          in1=ang2[R0:R1, :], op0=ALU.mult, op1=ALU.add)
        I("dve", "tensor_scalar", [ang2], [ang2], out=ang2[R0:R1, :], in0=ang2[R0:R1, :], scalar1=math.pi,
          scalar2=-math.pi, op0=ALU.min, op1=ALU.max)
        I("act", "activation", [ang2], [dst], out=dst[R0:R1, :], in_=ang2[R0:R1, :], func=AF.Sin)

    sin_of(ang, cs2, math.pi / 2.0)
    sin_of(ang, sn2, 0.0)
    I("dve", "tensor_scalar", [sn2, crope], [sn2], out=sn2[R0:R1, :], in0=sn2[R0:R1, :],
      scalar1=crope[R0:R1, 1:2], scalar2=None, op0=ALU.mult)

    for kt in range(2):
        mk.dma("sp", memt[:, kt, :], mem_in[kt * 128:(kt + 1) * 128, :], [mem_in], [(memt, (kt,))], "m%d" % kt)
        c = statcol(3)
        I("act", "activation", [(memt, (kt,))], [junk0, (stat, (c,))], out=junk0[:], in_=memt[:, kt, :], func=AF.Square,
          accum_out=stat[:, c:c + 1])
        I("act", "activation", [(stat, (c,)), cst], [(stat, (c,))], out=stat[:, c + 1:c + 2], in_=stat[:, c:c + 1],
          func=AF.Sqrt, scale=1.0 / D, bias=cst[:, 0:1])
        I("dve", "reciprocal", [(stat, (c,))], [(stat, (c,))], out=stat[:, c + 2:c + 3], in_=stat[:, c + 1:c + 2])
        I("dve", "tensor_scalar", [(memt, (kt,)), (stat, (c,))], [(memn, (kt,))], out=memn[:, kt, :], in0=memt[:, kt, :],
          scalar1=stat[:, c + 2:c + 3], scalar2=None, op0=ALU.mult)
        tb_ = nexttbank()
        tv = tb_[:].bitcast(BF16)
        for kc in range(KC):
            I("pe", "transpose", [(memn, (kt,)), ident_b], [tb_], out=tv[:, kc * 128:(kc + 1) * 128],
              in_=memn[:, kt, kc * 128:(kc + 1) * 128], identity=ident_b[:])
        I("dve", "tensor_copy", [tb_], [memT], out=memT[:, :, kt * 128:(kt + 1) * 128],
          in_=tv[:, 0:1024].rearrange("p (k t) -> p k t", k=KC))

    cur[0] = T0
    hT = alloc("hT", [128, KC, S], BF16)
    R_A = cur[0]
    xt = [alloc("xt%d" % i, [128, D], F32) for i in range(2)]
    xn = [alloc("xn%d" % i, [128, D], BF16) for i in range(2)]
    junk = alloc("junk", [128, D], BF16)
    R_A_END = cur[0]
    rx = [alloc("rx%d" % i, [128, 256], F32) for i in range(3)]
    R_B = cur[0]

    raw = [alloc("raw%d" % i, [128, S], F32) for i in range(2)]
    acc = [alloc("acc%d" % i, [128, S], F32) for i in range(2)]
    tmpT = [alloc("tmpT%d" % i, [128, S], BF16) for i in range(2)]
    zst = [alloc("zst%d" % i, [128, 512], BF16) for i in range(2)]
    tkst = [alloc("tkst%d" % i, [128, 8, 128], BF16) for i in range(2)]
    R_X_END = cur[0]
    cqnT = alloc("cqnT", [128, 3, S], BF16)
    ckvnT = alloc("ckvnT", [128, 2, S], BF16)
    kpeT = alloc("kpeT", [128, S], BF16)
    dtt = alloc("dtt", [128, NT, NH], F32)
    adt = alloc("adt", [128, NT, NH], F32)
    ssqa = alloc("ssqa", [128, NT], F32)
    rsta = alloc("rsta", [128, NT], F32)
    R_Y = cur[0]
    sqb = [alloc("sqb%d" % i, [128, S], BF16) for i in range(2)]
    rstd = alloc("rstd", [128, S], F32)
    wkr = alloc("wkr", [128, KC, 128], BF16)
    rtmp = alloc("rtmp", [128, 2, 512], F32)
    cur[0] = R_Y
    yattT = alloc("yattT", [128, KC, S], BF16)
    MIX_END = cur[0]

    cur[0] = R_B
    st_f = alloc("st_f", [128, D], F32)
    st_b = alloc("st_b", [128, D], BF16)
    adUs = [alloc("adU%d" % i, [128, NH, 128], F32) for i in range(2)]
    Eexp = [alloc("Eexp%d" % i, [128, 512], F32) for i in range(2)]
    MT = [alloc("MT%d" % i, [128, 512], BF16) for i in range(2)]
    cbm = [alloc("cbm%d" % i, [128, 128], F32) for i in range(2)]
    xsc = [alloc("xsc%d" % i, [128, D], BF16) for i in range(2)]
    btc = [alloc("btc%d" % i, [128, 512], BF16) for i in range(2)]
    bTc = [alloc("bTc%d" % i, [128, 4, 128], BF16) for i in range(2)]
    cTc = [alloc("cTc%d" % i, [128, 4, 128], BF16) for i in range(2)]
    szc = [alloc("szc%d" % i, [128, D], BF16) for i in range(2)]
    assert cur[0] <= R_X_END, (cur[0], R_X_END)
    cur[0] = R_Y
    xd = alloc("xd", [128, D], BF16)
    xdd = alloc("xdd", [128, D], BF16)
    xsk = alloc("xsk", [128, D], F32)
    yfs = [alloc("yf%d" % i, [128, D], F32) for i in range(2)]
    yo = alloc("yo", [128, D], F32)
    ygb = alloc("ygb", [128, D], BF16)
    sm16s = [alloc("sm16_%d" % i, [128, 8, NH], F32) for i in range(2)]
    assert cur[0] <= MIX_END, (cur[0], MIX_END)
    cur[0] = R_B
    QT = [alloc("QT%d" % i, [128, S], BF16) for i in range(2)]
    KT = [alloc("KT%d" % i, [128, S], BF16) for i in range(2)]
    Vh = [alloc("Vh%d" % i, [128, NT, 96], BF16) for i in range(2)]
    PT = [alloc("PT%d" % i, [128, 512], BF16) for i in range(5)]
    qtmp = alloc("qtmp", [128, 4, 512], F32)
    rsum = alloc("rsum", [128, 8], F32)
    onrm = alloc("onrm", [128, 4, HD], F32)
    osq = alloc("osq", [128, 4, HD], F32)
    ypair = [alloc("ypair%d" % i, [128, NT, 128], BF16) for i in range(2)]
    assert cur[0] <= R_X_END, (cur[0], R_X_END)

    cur[0] = R_B
    qT = alloc("qT", [128, KC, S], BF16)
    oT = alloc("oT", [128, KC, S], BF16)
    mnT = alloc("mnT", [128, KC, MEM], BF16)
    KmT = alloc("KmT", [128, KC, MEM], BF16)
    Vm = alloc("Vm", [128, 2, D], BF16)
    PTm = [alloc("PTm%d" % i, [128, 2, 512], BF16) for i in range(2)]
    rinvb = [alloc("rinvb%d" % i, [128, 512], F32) for i in range(2)]

    cur[0] = R_A
    fraw = alloc("fraw", [128, S], F32)
    assert cur[0] <= R_A_END
    cur[0] = R_B
    aT = alloc("aT", [128, NFF, S], BF16)
    facc = alloc("facc", [128, S], F32)
    sgate = alloc("sgate", [128, S], BF16)
    cur[0] = R_B
    gfin = alloc("gfin", [128, D], F32)
    ofin = [alloc("ofin%d" % i, [128, D], F32) for i in range(2)]

    def rms_tile_stats(src_ap, src_reads, width, jbuf=None):
        jb = junk if jbuf is None else jbuf
        c = statcol(3)
        I("act", "activation", src_reads, [jb, (stat, (c,))], out=jb[:, 0:width], in_=src_ap, func=AF.Square,
          accum_out=stat[:, c:c + 1])
        I("act", "activation", [(stat, (c,)), cst], [(stat, (c,))], out=stat[:, c + 1:c + 2], in_=stat[:, c:c + 1],
          func=AF.Sqrt, scale=1.0 / width, bias=cst[:, 0:1])
        I("dve", "reciprocal", [(stat, (c,))], [(stat, (c,))], out=stat[:, c + 2:c + 3], in_=stat[:, c + 1:c + 2])
        return stat[:, c + 2:c + 3], (stat, (c,))

    def transpose_tile_to(src_buf, dstT, dkey, tile_i, gain_name, l):
        tb_ = nexttbank()
        tv = tb_[:].bitcast(BF16)
        for kc in range(KC):
            I("pe", "transpose", [src_buf, ident_b], [tb_], out=tv[:, kc * 128:(kc + 1) * 128],
              in_=src_buf[:, kc * 128:(kc + 1) * 128], identity=ident_b[:])
        g = vcols(gain_name, l, 0, KC).unsqueeze(2).broadcast_to([128, KC, 128])
        I("dve", "tensor_tensor", [tb_, vecsT], [(dstT, dkey)],
          out=dstT[:, :, tile_i * 128:(tile_i + 1) * 128],
          in0=tv[:, 0:1024].rearrange("p (k t) -> p k t", k=KC), in1=g, op=ALU.mult)

    def norm_phase(xsrc, l, gain_name):
        for i in range(NT):
            t = xt[i % 2]
            mk.dma("sp", t[:], xsrc[i * 128:(i + 1) * 128, :], [(xsrc, (i,))], [t], ("xt", i % 2))
            rinv, skey = rms_tile_stats(t[:], [t], D)
            n = xn[i % 2]
            I("dve", "tensor_scalar", [t, skey], [n], out=n[:], in0=t[:], scalar1=rinv, scalar2=None, op0=ALU.mult)
            transpose_tile_to(n, hT, (None, i // 4), i, gain_name, l)

    def resid_phase(xsrc, l, wname, parts, scaled_last=None):
        NB = 256
        nk_tot = sum(nk for _, nk in parts)
        n_units = (D // NB) * NT

        def xload(u):
            cb_, i_ = divmod(u, NT)
            t_ = rx[u % 3]
            mk.dma("sp", t_[:], xsrc[i_ * 128:(i_ + 1) * 128, cb_ * NB:(cb_ + 1) * NB], [(xsrc, (i_,))], [t_], ("rx", u % 3))

        xload(0)
        for cb in range(D // NB):
            slot, wv = wload(wsrc(P[wname], l, cb * NB, NB, 0, nk_tot), nk_tot, NB, P[wname])
            for i in range(NT):
                u = cb * NT + i
                t = rx[u % 3]
                if u + 1 < n_units:
                    xload(u + 1)
                k0 = 0
                for pi, (actT, nk) in enumerate(parts):
                    last = (pi == len(parts) - 1)
                    sep = last and scaled_last is not None
                    if pi == 0 or sep:
                        pb = nextbank()
                    for kc in range(nk):
                        st_flag = (kc == 0) if (pi == 0 or sep) else False
                        sp_flag = (kc == nk - 1) and (last or (scaled_last is not None and pi == len(parts) - 2))
                        I("pe", "matmul", [(actT, (None, i // 4)), slot], [pb], pb[:, 0:NB],
                          lhsT=actT[:, kc, i * 128:(i + 1) * 128], rhs=wv[:, k0 + kc, :], start=st_flag, stop=sp_flag)
                    k0 += nk
                    if sep:
                        I("dve", "scalar_tensor_tensor", [pb, t, scaled_last], [t], out=t[:], in0=pb[:, 0:NB],
                          scalar=scaled_last[:, i:i + 1], in1=t[:], op0=ALU.mult, op1=ALU.add)
                    elif last or (scaled_last is not None and pi == len(parts) - 2):
                        I("dve", "tensor_tensor", [pb, t], [t], out=t[:], in0=pb[:, 0:NB], in1=t[:], op=ALU.add)
                mk.dma("sp", xres[i * 128:(i + 1) * 128, cb * NB:(cb + 1) * NB], t[:], [t], [(xres, (i,))], ("rxo", u % 3))

    def proj_fm(wv, col0, m, nk, actT, slot, cb_fn):
        for tb in range(4):
            pb = nextbank()
            for kc in range(nk):
                I("pe", "matmul", [slot, (actT, (None, tb))], [pb], pb[0:m, :],
                  lhsT=wv[:, kc, col0:col0 + m], rhs=actT[:, kc, tb * 512:(tb + 1) * 512],
                  start=(kc == 0), stop=(kc == nk - 1))
            cb_fn(tb, pb)

    def rope_rows(ps, dst, dkey, ts_):
        I("dve", "tensor_tensor", [ps, cs2], [qtmp], out=qtmp[64:96, 0, :], in0=ps[64:96, :], in1=cs2[64:96, ts_], op=ALU.mult)
        I("dve", "tensor_tensor", [ps, sn2], [qtmp], out=qtmp[64:96, 1, :], in0=ps[96:128, :], in1=sn2[96:128, ts_], op=ALU.mult)
        I("dve", "tensor_tensor", [qtmp], [(dst, dkey)], out=dst[64:96, ts_], in0=qtmp[64:96, 0, :], in1=qtmp[64:96, 1, :],
          op=ALU.add)
        I("dve", "tensor_tensor", [ps, cs2], [qtmp], out=qtmp[96:128, 2, :], in0=ps[96:128, :], in1=cs2[96:128, ts_], op=ALU.mult)
        I("dve", "tensor_tensor", [ps, sn2], [qtmp], out=qtmp[96:128, 3, :], in0=ps[64:96, :], in1=sn2[64:96, ts_], op=ALU.mult)
        I("dve", "tensor_tensor", [qtmp], [(dst, dkey)], out=dst[96:128, ts_], in0=qtmp[96:128, 2, :], in1=qtmp[96:128, 3, :],
          op=ALU.add)

    xsrc = x_in
    STAGES = ["norm", "z", "xbc", "prep", "ssd", "mla", "wout", "memattn", "ffn"]
    sub_stop = 99
    if stop_after and ":" in stop_after:
        stop_after, ss_ = stop_after.split(":")
        sub_stop = int(ss_)
    stop_idx = STAGES.index(stop_after) if stop_after else 99

    class _Stop(Exception):
        pass

    def layer_body(l, xsrc):
        norm_phase(xsrc, l, "norm_mix")
        if stop_idx <= 0:
            raise _Stop()

        for cb in range(2):
            slot, wv = wload(wsrc(P["w_in"], l, O_Z + cb * 512, 512), KC, 512, P["w_in"])
            for i in range(NT):
                pb = nextbank()
                for kc in range(KC):
                    I("pe", "matmul", [(hT, (None, i // 4)), slot], [pb], pb[:, :],
                      lhsT=hT[:, kc, i * 128:(i + 1) * 128], rhs=wv[:, kc, :], start=(kc == 0), stop=(kc == KC - 1))
                u = cb * NT + i
                z = zst[u % 2]
                I("act", "activation", [pb], [z], out=z[:], in_=pb[:, :], func=AF.Silu)
                mk.dma("sp", sz_d[i * 128:(i + 1) * 128, cb * 512:(cb + 1) * 512], z[:], [z], [(sz_d, (i,))], ("zst", u % 2))

        if stop_idx <= 1:
            raise _Stop()
        for cg in range(8):
            slot, wv = wload(wsrc(P["w_in"], l, O_XBC + cg * 256, 256), KC, 256, P["w_in"])
            for sub in range(2):
                c = cg * 2 + sub
                rw = raw[c % 2]
                ac = acc[c % 2]

                def evac(tb, pb, rw=rw):
                    I("act", "copy", [pb], [(rw, (tb,))], out=rw[:, tb * 512:(tb + 1) * 512], in_=pb[:, :])
                proj_fm(wv, sub * 128, 128, KC, hT, slot, evac)
                w3, w2, w1, w0 = (vcol("ssm_conv_w", l, j * 16 + c) for j in (3, 2, 1, 0))
                bb = vcol("ssm_conv_b", l, c)
                I("dve", "tensor_scalar", [rw, vecsT], [ac], out=ac[:], in0=rw[:], scalar1=w3, scalar2=bb,
                  op0=ALU.mult, op1=ALU.add)
                for sh, wj in ((1, w2), (2, w1), (3, w0)):
                    I("dve", "scalar_tensor_tensor", [rw, ac, vecsT], [ac], out=ac[:, sh:S], in0=rw[:, 0:S - sh],
                      scalar=wj, in1=ac[:, sh:S], op0=ALU.mult, op1=ALU.add)
                tt = tmpT[c % 2]
                I("act", "activation", [ac], [tt], out=tt[:], in_=ac[:], func=AF.Silu)
                if c < 12:
                    for half in range(2):
                        tk = tkst[half]
                        tb_ = nexttbank()
                        tv = tb_[:].bitcast(BF16)
                        for j in range(8):
                            i = half * 8 + j
                            I("pe", "transpose", [tt, ident_b], [tb_], out=tv[:, j * 128:(j + 1) * 128],
                              in_=tt[:, i * 128:(i + 1) * 128], identity=ident_b[:])
                        I("dve", "tensor_copy", [tb_], [tk], out=tk[:], in_=tv[:, 0:1024].rearrange("p (j t) -> p j t", j=8))
                        rows = slice(half * 1024, (half + 1) * 1024)
                        if c < 8:
                            dst = xs_d[rows, c * 128:(c + 1) * 128].rearrange("(i p) f -> p i f", p=128)
                            mk.dma("sp", dst, tk[:], [tk], [xs_d], ("tk", half))
                        else:
                            dst = bt_d[rows, (c - 8) * 128:(c - 7) * 128].rearrange("(i p) f -> p i f", p=128)
                            mk.dma("sp", dst, tk[:], [tk], [bt_d], ("tk", half))
                if 8 <= c < 12:
                    mk.dma("sp", bT_d[c - 8], tt[:], [tt], [bT_d], ("tT", c % 2))
                if c >= 12:
                    mk.dma("sp", cT_d[c - 12], tt[:], [tt], [cT_d], ("tT", c % 2))

        if stop_idx <= 2:
            raise _Stop()
        slot = nextslot()
        wv = slot[:, 0:KC * 64].rearrange("p (k n) -> p k n", k=KC)
        mk.dma("pool", wv[:, :, 0:NH], wsrc(P["w_in"], l, O_DT, NH), [P["w_in"]], [slot], ("w", slot.name))
        import os
        DTV = int(os.environ.get("MK_DT", "9"))
        for i in range(NT if DTV >= 2 else 0):
            pb = nextbank()
            for kc in range(KC):
                I("pe", "matmul", [(hT, (None, i // 4)), slot], [pb], pb[:, 0:NH],
                  lhsT=hT[:, kc, i * 128:(i + 1) * 128], rhs=wv[:, kc, 0:NH], start=(kc == 0), stop=(kc == KC - 1))
            I("dve", "tensor_tensor", [pb, bc16], [(dtt, (i,))], out=dtt[:, i, :], in0=pb[:, 0:NH], in1=bc16[:, 0, l, :],
              op=ALU.add)
            if DTV >= 3:
                I("act", "activation", [(dtt, (i,))], [(dtt, (i,))], out=dtt[:, i, :], in_=dtt[:, i, :], func=AF.Exp)
            if DTV >= 4:
                I("act", "activation", [(dtt, (i,)), cst], [(dtt, (i,))], out=dtt[:, i, :], in_=dtt[:, i, :], func=AF.Ln,
                  bias=cst[:, 1:2], scale=1.0)
            if DTV >= 5:
                I("dve", "tensor_tensor", [(dtt, (i,)), bc16], [(adt, (i,))], out=adt[:, i, :], in0=dtt[:, i, :],
                  in1=bc16[:, 1, l, :], op=ALU.mult)

        if stop_idx == 3 and sub_stop <= 0:
            raise _Stop()

        def latent(wv, slot, col0, nch, dstT, gname, width):
            for c in range(nch):
                sq = sqb[c % 2]

                def evac(tb, pb, c=c, sq=sq):
                    I("act", "activation", [pb], [(sq, (tb,))], out=sq[:, tb * 512:(tb + 1) * 512], in_=pb[:, :],
                      func=AF.Square)
                    I("dve", "tensor_scalar", [pb, vecsT], [(dstT, (c, tb))], out=dstT[:, c, tb * 512:(tb + 1) * 512],
                      in0=pb[:, :], scalar1=vcol(gname, l, c), scalar2=None, op0=ALU.mult)
                proj_fm(wv, col0 + c * 128, 128, KC, hT, slot, evac)
                for tb in range(4):
                    pb = nextbank()
                    I("pe", "matmul", [(sq, (tb,)), ones_b], [pb], pb[:, :], lhsT=ones_b[:],
                      rhs=sq[:, tb * 512:(tb + 1) * 512], start=True, stop=True)
                    if c == 0:
                        I("dve", "tensor_copy", [pb], [(rstd, (tb,))], out=rstd[:, tb * 512:(tb + 1) * 512], in_=pb[:, :])
                    else:
                        I("dve", "tensor_tensor", [pb, (rstd, (tb,))], [(rstd, (tb,))],
                          out=rstd[:, tb * 512:(tb + 1) * 512], in0=pb[:, :], in1=rstd[:, tb * 512:(tb + 1) * 512],
                          op=ALU.add)
            I("act", "activation", [rstd, cst], [rstd], out=rstd[:], in_=rstd[:], func=AF.Sqrt, scale=1.0 / width,
              bias=cst[:, 0:1])
            I("dve", "reciprocal", [rstd], [rstd], out=rstd[:], in_=rstd[:])
            for c in range(nch):
                I("dve", "tensor_tensor", [(dstT, (c, None)), rstd], [(dstT, (c, None))], out=dstT[:, c, :],
                  in0=dstT[:, c, :], in1=rstd[:], op=ALU.mult)

        slot, wv = wload(wsrc(P["w_in"], l, O_CQ, 384), KC, 384, P["w_in"])
        latent(wv, slot, 0, 3, cqnT, "q_norm", Q_LORA)
        if stop_idx == 3 and sub_stop <= 1:
            raise _Stop()
        slot = nextslot()
        wv = slot[:, 0:KC * 64].rearrange("p (k n) -> p k n", k=KC)
        mk.dma("pool", wv[:, :, 0:ROPE], wsrc(P["w_in"], l, O_KR, ROPE), [P["w_in"]], [slot], ("w", slot.name))
        I("dve", "memset", [], [wkr], wkr[:], 0.0)
        I("dve", "tensor_copy", [slot, wkr], [wkr], out=wkr[:, :, 64:80], in_=wv[:, :, 0:16])
        I("dve", "tensor_copy", [slot, wkr], [wkr], out=wkr[:, :, 96:112], in_=wv[:, :, 16:32])
        slot, wv = wload(wsrc(P["w_in"], l, O_CKV, 256), KC, 256, P["w_in"])
        if stop_idx == 3 and sub_stop <= 2:
            raise _Stop()
        latent(wv, slot, 0, 2, ckvnT, "kv_norm", KV_LORA)
        if stop_idx == 3 and sub_stop <= 3:
            raise _Stop()
        for tb in range(4):
            pa = nextbank()
            for kc in range(KC):
                I("pe", "matmul", [wkr, (hT, (None, tb))], [pa], pa[:, :], lhsT=wkr[:, kc, :],
                  rhs=hT[:, kc, tb * 512:(tb + 1) * 512], start=(kc == 0), stop=(kc == KC - 1))
            rope_rows(pa, kpeT, (tb,), slice(tb * 512, (tb + 1) * 512))

        if stop_idx <= 3:
            raise _Stop()
        I("dve", "memset", [], [st_f], st_f[:], 0.0)
        I("dve", "memset", [], [st_b], st_b[:], 0.0)
        pending_tail = [None]
        def ssd_decay_prep(c):
            adU = adUs[c % 2]
            sm16 = sm16s[c % 2]
            I("pool", "tensor_tensor", [(adt, (c,)), U_f], [adU], out=adU[:],
              in0=adt[:, c, :].unsqueeze(2).broadcast_to([128, NH, 128]),
              in1=U_f[:].unsqueeze(1).broadcast_to([128, NH, 128]), op=ALU.mult)
            pc = nextbank()
            I("pe", "matmul", [U_f, (adt, (c,))], [pc], pc[:, 0:NH], lhsT=U_f[:], rhs=adt[:, c, :], start=True, stop=True)
            I("pe", "matmul", [ones_f, (adt, (c,))], [pc], pc[:, 64:64 + NH], lhsT=ones_f[:], rhs=adt[:, c, :],
              start=True, stop=True, skip_group_check=True)
            I("dve", "tensor_copy", [pc], [sm16], out=sm16[:, 3:5, :], in_=pc[:, 0:128].rearrange("p (a h) -> p a h", a=2)[:, :, 0:NH])
            I("dve", "tensor_tensor", [sm16], [sm16], out=sm16[:, 5, :], in0=sm16[:, 4, :], in1=sm16[:, 3, :], op=ALU.subtract)
            I("act", "activation", [sm16], [sm16], out=sm16[:, 0:3, :], in_=sm16[:, 3:6, :], func=AF.Exp)

        ssd_decay_prep(0)
        for c in range(NT):
            cs_ = slice(c * 128, (c + 1) * 128)
            xs_c, bt_c, bT_c, cT_c, sz_c = xsc[c % 2], btc[c % 2], bTc[c % 2], cTc[c % 2], szc[c % 2]
            yf = yfs[c % 2]
            mk.dma("sp", xs_c[:], xs_d[cs_, :], [xs_d], [xs_c], ("xsc", c % 2))
            mk.dma("sp", bt_c[:], bt_d[cs_, :], [bt_d], [bt_c], ("btc", c % 2))
            mk.dma("sp", bT_c[:], bT_d[:, :, cs_].rearrange("g n t -> n g t"), [bT_d], [bT_c], ("bTc", c % 2))
            mk.dma("sp", cT_c[:], cT_d[:, :, cs_].rearrange("g n t -> n g t"), [cT_d], [cT_c], ("cTc", c % 2))
            mk.dma("sp", sz_c[:], sz_d[cs_, :], [(sz_d, (c,))], [sz_c], ("szc", c % 2))
            adU = adUs[c % 2]
            sm16 = sm16s[c % 2]
            if c + 1 < NT:
                ssd_decay_prep(c + 1)
            x3 = xs_c[:].rearrange("p (h d) -> p h d", h=NH)
            I("dve", "tensor_tensor", [xs_c, (dtt, (c,))], [xd], out=xd[:].rearrange("p (h d) -> p h d", h=NH), in0=x3,
              in1=dtt[:, c, :].unsqueeze(2).broadcast_to([128, NH, HD]), op=ALU.mult)
            I("dve", "tensor_tensor", [xd, sm16], [xdd], out=xdd[:].rearrange("p (h d) -> p h d", h=NH),
              in0=xd[:].rearrange("p (h d) -> p h d", h=NH),
              in1=sm16[:, 2, :].unsqueeze(2).broadcast_to([128, NH, HD]), op=ALU.mult)
            I("pool", "tensor_tensor", [xs_c, bc16], [xsk], out=xsk[:].rearrange("p (h d) -> p h d", h=NH), in0=x3,
              in1=bc16[:, 2, l, :].unsqueeze(2).broadcast_to([128, NH, HD]), op=ALU.mult)
            for hh in range(2):
                hs = slice(hh * 512, (hh + 1) * 512)
                py, po = PB[4], PB[5]
                for g in (2 * hh, 2 * hh + 1):
                    pd = nextbank()
                    I("pe", "matmul", [SL_f, adU], [pd], pd[:, :], lhsT=SL_f[:],
                      rhs=adU[:, g * 4:(g + 1) * 4, :].rearrange("p h l -> p (h l)"), start=True, stop=True)
                    E = Eexp[g % 2]
                    I("act", "activation", [pd], [E], out=E[:], in_=pd[:, :], func=AF.Exp)
                    pcb = nextbank()
                    I("pe", "matmul", [bT_c, cT_c], [pcb], pcb[:, 0:128], lhsT=bT_c[:, g, :], rhs=cT_c[:, g, :],
                      start=True, stop=True)
                    cb_ = cbm[g % 2]
                    I("dve", "tensor_tensor", [pcb, U_f], [cb_], out=cb_[:], in0=pcb[:, 0:128], in1=U_f[:], op=ALU.mult)
                    M = MT[g % 2]
                    I("dve", "tensor_tensor", [E, cb_], [M], out=M[:].rearrange("p (r l) -> p r l", r=4),
                      in0=E[:].rearrange("p (r l) -> p r l", r=4), in1=cb_[:].unsqueeze(1).broadcast_to([128, 4, 128]),
                      op=ALU.mult)
                    for r_ in range(4):
                        h = g * 4 + r_
                        oc = (g % 2) * 256 + r_ * 64
                        I("pe", "matmul", [M, xd], [py], py[:, oc:oc + 64], lhsT=M[:, r_ * 128:(r_ + 1) * 128],
                          rhs=xd[:, h * HD:(h + 1) * HD], start=True, stop=True, skip_group_check=True)
                    oc = (g % 2) * 256
                    I("pe", "matmul", [cT_c, st_b], [po], po[:, oc:oc + 256], lhsT=cT_c[:, g, :],
                      rhs=st_b[:, g * 256:(g + 1) * 256], start=True, stop=True, skip_group_check=True)
                I("dve", "tensor_tensor", [po, sm16], [yo], out=yo[:, hs].rearrange("p (h d) -> p h d", h=8),
                  in0=po[:, :].rearrange("p (h d) -> p h d", h=8),
                  in1=sm16[:, 0, hh * 8:(hh + 1) * 8].unsqueeze(2).broadcast_to([128, 8, HD]), op=ALU.mult)
                I("dve", "tensor_tensor", [py, yo], [yf], out=yf[:, hs], in0=py[:, :], in1=yo[:, hs], op=ALU.add)
            I("pool", "tensor_tensor", [yf, xsk], [yf], out=yf[:], in0=yf[:], in1=xsk[:], op=ALU.add)
            I("pool", "tensor_tensor", [st_f, sm16], [st_f], out=st_f[:].rearrange("p (h d) -> p h d", h=NH),
              in0=st_f[:].rearrange("p (h d) -> p h d", h=NH),
              in1=sm16[:, 1, :].unsqueeze(2).broadcast_to([128, NH, HD]), op=ALU.mult)
            for hh in range(2):
                hs = slice(hh * 512, (hh + 1) * 512)
                ps_ = nextbank()
                for g in (2 * hh, 2 * hh + 1):
                    oc = (g % 2) * 256
                    I("pe", "matmul", [bt_c, xdd], [ps_], ps_[:, oc:oc + 256], lhsT=bt_c[:, g * 128:(g + 1) * 128],
                      rhs=xdd[:, g * 256:(g + 1) * 256], start=True, stop=True, skip_group_check=True)
                I("dve", "tensor_tensor", [ps_, st_f], [st_f], out=st_f[:, hs], in0=ps_[:, :], in1=st_f[:, hs], op=ALU.add)
            I("act", "copy", [st_f], [st_b], out=st_b[:], in_=st_f[:])
            def ssd_tail(c=c, yf=yf, sz_c=sz_c):
                I("pool", "tensor_tensor", [yf, sz_c], [yf], out=yf[:], in0=yf[:], in1=sz_c[:], op=ALU.mult)
                rinv, skey = rms_tile_stats(yf[:], [yf], D)
                I("act", "activation", [yf, skey], [ygb], out=ygb[:], in_=yf[:], func=AF.Copy, scale=rinv)
                transpose_tile_to(ygb, hT, (None, c // 4), c, "ssm_norm", l)
            if pending_tail[0] is not None:
                pending_tail[0]()
            pending_tail[0] = ssd_tail
        pending_tail[0]()
        pending_tail[0] = None

        if stop_idx <= 4:
            raise _Stop()
        wqs_ = nextslot()
        wq4 = wqs_[:, 0:3 * NH * 128].rearrange("p (k h d) -> p k h d", k=3, h=NH)
        I("dve", "memset", [], [wqs_], wqs_[:], 0.0)
        for kc in range(3):
            s4 = P["w_uq"][l][kc * 128:(kc + 1) * 128, :].rearrange("p (h d) -> p h d", h=NH)
            mk.dma("pool", wq4[:, kc, :, 0:80], s4[:, :, 0:80], [P["w_uq"]], [wqs_], ("w", wqs_.name))
            mk.dma("pool", wq4[:, kc, :, 96:112], s4[:, :, 80:96], [P["w_uq"]], [wqs_], ("w", wqs_.name))
        wkvs_ = nextslot()
        wkv = wkvs_[:, 0:2 * NH * 128].rearrange("p (k n) -> p k n", k=2)
        for kc in range(2):
            mk.dma("pool", wkv[:, kc, :], P["w_ukv"][l][kc * 128:(kc + 1) * 128, :], [P["w_ukv"]], [wkvs_],
                   ("w", wkvs_.name))
        I("dve", "memset", [], [ssqa], ssqa[:], 0.0)
        scale = 96.0 ** -0.5
        def mla_proj(h):
            Qh, Kh, V_ = QT[h % 2], KT[h % 2], Vh[h % 2]
            for tb in range(4):
                ts_ = slice(tb * 512, (tb + 1) * 512)
                pa = nextbank()
                for kc in range(3):
                    I("pe", "matmul", [wqs_, (cqnT, (None, tb))], [pa], pa[:, :],
                      lhsT=wq4[:, kc, h, :], rhs=cqnT[:, kc, ts_], start=(kc == 0), stop=(kc == 2))
                I("act", "copy", [pa], [(Qh, (tb, 0))], out=Qh[0:64, ts_], in_=pa[0:64, :])
                rope_rows(pa, Qh, (tb, 1), ts_)
                pk = nextbank()
                for kc in range(2):
                    I("pe", "matmul", [wkvs_, (ckvnT, (None, tb))], [pk], pk[0:64, :],
                      lhsT=wkv[:, kc, h * 128:h * 128 + 64], rhs=ckvnT[:, kc, ts_], start=(kc == 0), stop=(kc == 1))
                I("act", "copy", [pk], [(Kh, (tb, 0))], out=Kh[0:64, ts_], in_=pk[0:64, :])
                I("pool", "tensor_copy", [(kpeT, (tb,))], [(Kh, (tb, 1))], out=Kh[64:128, ts_], in_=kpeT[64:128, ts_])
            I("pool", "memset", [], [V_], V_[:, :, 64:65], 1.0)
            for i in range(NT):
                pv = nextbank()
                for kc in range(2):
                    I("pe", "matmul", [(ckvnT, (None, i // 4)), wkvs_], [pv], pv[:, 0:64],
                      lhsT=ckvnT[:, kc, i * 128:(i + 1) * 128], rhs=wkv[:, kc, h * 128 + 64:h * 128 + 128],
                      start=(kc == 0), stop=(kc == 1))
                I("act", "copy", [pv], [V_], out=V_[:, i, 0:64], in_=pv[:, 0:64])

        def mla_attn(h):
            Qh, Kh, V_ = QT[h % 2], KT[h % 2], Vh[h % 2]
            yp = ypair[(h // 2) % 2]

            def emit_scores(qb, kt, sidx):
                q0 = max(qb * 512, kt * 128)
                q1 = (qb + 1) * 512
                n = q1 - q0
                psc = nextbank()
                I("pe", "matmul", [Kh, Qh], [psc], psc[:, 0:n], lhsT=Kh[:, kt * 128:(kt + 1) * 128],
                  rhs=Qh[:, q0:q1], start=True, stop=True)
                pt = PT[sidx % 5]
                I("act", "activation", [psc], [pt], out=pt[:, 0:n], in_=psc[:, 0:n], func=AF.Exp, scale=scale)
                if kt * 128 >= qb * 512:
                    I("pool", "tensor_tensor", [pt, U_b], [pt], out=pt[:, 0:128], in0=pt[:, 0:128], in1=U_b[:], op=ALU.mult)
                return pt, n, q0

            def emit_pv(qb, kt, pt, n, q0):
                pacc = PB[4 + (h * 4 + qb) % 2]
                nkt = (qb + 1) * 4
                if kt == 0:
                    I("pe", "matmul", [zeros_b], [pacc], pacc[:, :], lhsT=zeros_b[:, 0:128], rhs=zeros_b[:, :],
                      start=True, stop=False)
                for j in range(n // 128):
                    qt = (q0 - qb * 512) // 128 + j
                    I("pe", "matmul", [pt, V_], [pacc], pacc[:, qt * 128:qt * 128 + 65], lhsT=pt[:, j * 128:(j + 1) * 128],
                      rhs=V_[:, kt, 0:65], start=False, stop=False, skip_group_check=True)
                if kt != nkt - 1:
                    return
                I("pe", "matmul", [zeros_b], [pacc], pacc[:, :], lhsT=zeros_b[:, 0:128], rhs=zeros_b[:, :],
                  start=False, stop=True)
                a3 = pacc[:, :].rearrange("p (q e) -> p q e", q=4)
                I("dve", "reciprocal", [pacc], [rsum], out=rsum[:, 0:4], in_=a3[:, :, 64])
                I("dve", "tensor_tensor", [pacc, rsum], [onrm], out=onrm[:], in0=a3[:, :, 0:64],
                  in1=rsum[:, 0:4].unsqueeze(2).broadcast_to([128, 4, HD]), op=ALU.mult)
                I("pool", "tensor_copy", [onrm], [(yp, (qb, h % 2))],
                  out=yp[:, qb * 4:(qb + 1) * 4, (h % 2) * 64:(h % 2) * 64 + 64], in_=onrm[:])
                I("dve", "tensor_tensor", [onrm], [osq], out=osq[:], in0=onrm[:], in1=onrm[:], op=ALU.mult)
                I("dve", "tensor_reduce", [osq], [rsum], out=rsum[:, 4:8], in_=osq[:], axis=mybir.AxisListType.X, op=ALU.add)
                I("dve", "tensor_tensor", [rsum, ssqa], [ssqa], out=ssqa[:, qb * 4:(qb + 1) * 4], in0=ssqa[:, qb * 4:(qb + 1) * 4],
                  in1=rsum[:, 4:8], op=ALU.add)

            steps = [(qb, kt) for qb in range(4) for kt in range((qb + 1) * 4)]
            pend = []
            for sidx, (qb, kt) in enumerate(steps):
                cur = emit_scores(qb, kt, sidx)
                pend.append((qb, kt) + cur)
                if len(pend) > 3:
                    emit_pv(*pend.pop(0))
            while pend:
                emit_pv(*pend.pop(0))
            if h % 2 == 1:
                hp = h // 2
                gcol = vcol("attn_out_norm", l, hp)
                for half in range(2):
                    tb_ = nexttbank()
                    tv = tb_[:].bitcast(BF16)
                    for j in range(8):
                        i = half * 8 + j
                        I("pe", "transpose", [yp, ident_b], [tb_], out=tv[:, j * 128:(j + 1) * 128], in_=yp[:, i, :],
                          identity=ident_b[:])
                    I("dve", "tensor_scalar", [tb_, vecsT], [(yattT, (hp, None))],
                      out=yattT[:, hp, half * 1024:(half + 1) * 1024], in0=tv[:, 0:1024], scalar1=gcol, scalar2=None,
                      op0=ALU.mult)
        mla_proj(0)
        for h in range(NH):
            if h + 1 < NH:
                mla_proj(h + 1)
            mla_attn(h)
        I("act", "activation", [ssqa, cst], [rsta], out=rsta[:], in_=ssqa[:], func=AF.Sqrt, scale=1.0 / D, bias=cst[:, 0:1])
        I("dve", "reciprocal", [rsta], [rsta], out=rsta[:], in_=rsta[:])

        if stop_idx <= 5:
            raise _Stop()
        resid_phase(xsrc, l, "w_out", [(hT, KC), (yattT, KC)], scaled_last=rsta)
        xsrc = xres
        cur_x[0] = xres

        if stop_idx <= 6:
            raise _Stop()
        norm_phase(xsrc, l, "norm_mem_q")
        I("dve", "tensor_tensor", [memT, vecsT], [mnT], out=mnT[:], in0=memT[:],
          in1=vcols("norm_mem_kv", l, 0, KC).unsqueeze(2).broadcast_to([128, KC, MEM]), op=ALU.mult)
        for cg in range(4):
            slot, wv = wload(wsrc(P["w_mk"], l, cg * 256, 256), KC, 256, P["w_mk"])
            for sub in range(2):
                dc = cg * 2 + sub
                pb = nextbank()
                for kc in range(KC):
                    I("pe", "matmul", [slot, mnT], [pb], pb[:, 0:MEM], lhsT=wv[:, kc, sub * 128:(sub + 1) * 128],
                      rhs=mnT[:, kc, :], start=(kc == 0), stop=(kc == KC - 1))
                I("act", "copy", [pb], [(KmT, (dc,))], out=KmT[:, dc, :], in_=pb[:, 0:MEM])
        for cb in range(2):
            slot, wv = wload(wsrc(P["w_mv"], l, cb * 512, 512), KC, 512, P["w_mv"])
            for kt in range(2):
                pb = nextbank()
                for kc in range(KC):
                    I("pe", "matmul", [mnT, slot], [pb], pb[:, :], lhsT=mnT[:, kc, kt * 128:(kt + 1) * 128], rhs=wv[:, kc, :],
                      start=(kc == 0), stop=(kc == KC - 1))
                I("act", "copy", [pb], [(Vm, (kt, cb))], out=Vm[:, kt, cb * 512:(cb + 1) * 512], in_=pb[:, :])
        for cg in range(4):
            slot, wv = wload(wsrc(P["w_mq"], l, cg * 256, 256), KC, 256, P["w_mq"])
            for sub in range(2):
                dc = cg * 2 + sub

                def evac(tb, pb, dc=dc):
                    I("act", "copy", [pb], [(qT, (dc, tb))], out=qT[:, dc, tb * 512:(tb + 1) * 512], in_=pb[:, :])
                proj_fm(wv, sub * 128, 128, KC, hT, slot, evac)
        mscale = 256.0 ** -0.5
        for hd in range(4):
            for tb in range(4):
                ts_ = slice(tb * 512, (tb + 1) * 512)
                ptm = PTm[(hd * 4 + tb) % 2]
                for kt in range(2):
                    psc = nextbank()
                    for d2 in range(2):
                        dc = hd * 2 + d2
                        I("pe", "matmul", [(KmT, (dc,)), (qT, (dc, tb))], [psc], psc[:, :],
                          lhsT=KmT[:, dc, kt * 128:(kt + 1) * 128], rhs=qT[:, dc, ts_], start=(d2 == 0), stop=(d2 == 1))
                    I("act", "activation", [psc], [(ptm, (kt,))], out=ptm[:, kt, :], in_=psc[:, :], func=AF.Exp, scale=mscale)
                pr = nextbank()
                for kt in range(2):
                    I("pe", "matmul", [ones_b, (ptm, (kt,))], [pr], pr[:, :], lhsT=ones_b[:], rhs=ptm[:, kt, :],
                      start=(kt == 0), stop=(kt == 1))
                rb = rinvb[(hd * 4 + tb) % 2]
                I("dve", "reciprocal", [pr], [rb], out=rb[:], in_=pr[:, :])
                for d2 in range(2):
                    dc = hd * 2 + d2
                    po_ = nextbank()
                    for kt in range(2):
                        I("pe", "matmul", [(Vm, (kt, None)), (ptm, (kt,))], [po_], po_[:, :],
                          lhsT=Vm[:, kt, dc * 128:(dc + 1) * 128], rhs=ptm[:, kt, :], start=(kt == 0), stop=(kt == 1))
                    I("dve", "tensor_tensor", [po_, rb], [(oT, (dc, tb))], out=oT[:, dc, ts_], in0=po_[:, :], in1=rb[:],
                      op=ALU.mult)
        resid_phase(xsrc, l, "w_mo", [(oT, KC)])

        if stop_idx <= 7:
            raise _Stop()
        norm_phase(xsrc, l, "norm_ffn")
        for jg in range(NFF // 2):
            slot_g, wg = wload(wsrc(P["w_up"], l, jg * 256, 256), KC, 256, P["w_up"])
            slot_v, wvv = wload(wsrc(P["w_up"], l, D_FF + jg * 256, 256), KC, 256, P["w_up"])
            for sub in range(2):
                j = jg * 2 + sub
                for which, (slot, wv_) in enumerate(((slot_g, wg), (slot_v, wvv))):
                    cidx = which * NFF + j

                    def evac(tb, pb):
                        I("act", "copy", [pb], [(fraw, (tb,))], out=fraw[:, tb * 512:(tb + 1) * 512], in_=pb[:, :])
                    proj_fm(wv_, sub * 128, 128, KC, hT, slot, evac)
                    w2, w1, w0 = (vcol("ffn_conv_w", l, jj * 44 + cidx) for jj in (2, 1, 0))
                    bb = vcol("ffn_conv_b", l, cidx)
                    I("dve", "tensor_scalar", [fraw, vecsT], [facc], out=facc[:], in0=fraw[:], scalar1=w2, scalar2=bb,
                      op0=ALU.mult, op1=ALU.add)
                    for sh, wj in ((1, w1), (2, w0)):
                        I("dve", "scalar_tensor_tensor", [fraw, facc, vecsT], [facc], out=facc[:, sh:S], in0=fraw[:, 0:S - sh],
                          scalar=wj, in1=facc[:, sh:S], op0=ALU.mult, op1=ALU.add)
                    if which == 0:
                        I("act", "activation", [facc], [sgate], out=sgate[:], in_=facc[:], func=AF.Silu)
                    else:
                        I("dve", "tensor_tensor", [facc, sgate], [(aT, (j, None))], out=aT[:, j, :], in0=facc[:], in1=sgate[:],
                          op=ALU.mult)
        resid_phase(xsrc, l, "w_down", [(aT, NFF)])


    cur_x = [x_in]
    try:
        for l in range(depth):
            layer_body(l, cur_x[0])
            cur_x[0] = xres
    except _Stop:
        pass
    xsrc = cur_x[0]

    mk.dma("sp", gfin[:], P["final_norm"][:].unsqueeze(0).broadcast_to([128, D]), [P["final_norm"]], [gfin], "gf")
    for i in range(NT):
        t = xt[i % 2]
        mk.dma("sp", t[:], xsrc[i * 128:(i + 1) * 128, :], [(xsrc, (i,))], [t], ("xt", i % 2))
        rinv, skey = rms_tile_stats(t[:], [t], D)
        o_ = ofin[i % 2]
        I("dve", "scalar_tensor_tensor", [t, skey, gfin], [o_], out=o_[:], in0=t[:], scalar=rinv, in1=gfin[:],
          op0=ALU.mult, op1=ALU.mult)
        mk.dma("sp", out_d[i * 128:(i + 1) * 128, :], o_[:], [o_], [(out_d, (i,))], ("of", i % 2))

    mk.emit()
    nc.mk_bufs = {b.name: b.t.name for b in all_bufs}
    return nc


_CACHE = {}


def _consts():
    k = np.arange(128)
    U = (k[:, None] <= k[None, :]).astype(np.float32)
    SL = (k[:, None] > k[None, :]).astype(np.float32)
    rope = np.zeros((128, 2), np.float32)
    inv_freq = (1.0 / (np.float32(10000.0) ** (np.arange(0, ROPE, 2, dtype=np.float32) / np.float32(ROPE)))).astype(np.float32)
    rope[64:80, 0] = inv_freq
    rope[96:112, 0] = inv_freq
    rope[64:80, 1] = 1.0
    rope[96:112, 1] = -1.0
    return {"c_ident": np.eye(128, dtype=np.float32), "c_U": U, "c_SL": SL, "c_rope": rope}


def kernel(**inputs):
    return run_depth(inputs, DEPTH)


def run_depth(inputs, depth, trace=False, ncores=N_CORES, stop_after=None):
    key = (depth, stop_after)
    if key not in _CACHE:
        _CACHE[key] = build_program(depth, stop_after=stop_after)
    nc = _CACHE[key]
    consts = _consts()
    x = np.ascontiguousarray(np.asarray(inputs["x"], dtype=np.float32))
    mem = np.ascontiguousarray(np.asarray(inputs["mem"], dtype=np.float32))
    pos = np.ascontiguousarray(np.asarray(inputs["positions"], dtype=np.int32))
    dd = max(depth, 1)
    shared = {}
    for name, shp in PARAMS:
        a = np.asarray(inputs[name], dtype=np.float32)
        if len(shp) > 1 and dd != DEPTH:
            a = a[:dd]
        shared[name] = np.ascontiguousarray(a)
    shared.update(consts)
    in_maps = []
    for b in range(ncores):
        m = dict(shared)
        m["x"] = x[b]
        m["mem"] = mem[b]
        m["positions"] = pos[b:b + 1]
        in_maps.append(m)
    res = run_bass_kernel_spmd(nc, in_maps, core_ids=list(range(ncores)), trace=trace)
    if trace:
        print("exec_time_ns", res.exec_time_ns)
    out = np.stack([np.asarray(r["out"]) for r in res.results], axis=0).astype(np.float32)
    return out
```

```python
import contextlib
import math
import numpy as np

import concourse.bass as bass
import concourse.mybir as mybir
from concourse.alu_op_type import AluOpType as ALU
from concourse.bass_utils import run_bass_kernel_spmd

DT = mybir.dt
AF = mybir.ActivationFunctionType
F32, BF16, I32 = DT.float32, DT.bfloat16, DT.int32

DEPTH = 4
S = 2048
D = 1024
NT = S // 128
KC = D // 128
MEM = 256
EPS = 1e-6
NH = 16
HD = 64
NG = 4
NST = 128
Q_LORA, KV_LORA, ROPE = 384, 256, 32
D_IN = 3760
O_Z, O_XBC, O_DT, O_CQ, O_CKV, O_KR = 0, 1024, 3072, 3088, 3472, 3728
D_FF = 2816
NFF = D_FF // 128
N_CORES = 8

ENGS = ("sp", "pe", "dve", "act", "pool")


class Op:
    __slots__ = ("eng", "fn", "deps", "signal", "sigval", "dma_sem", "dma_val", "id")


class Buf:
    def __init__(self, name, t):
        self.name = name
        self.t = t
        self.writers = []
        self.readers = []
        self.overlaps = []
        self.excl = False

    def __getitem__(self, idx):
        return self.t[idx]


def _conf(a, b):
    if a is None or b is None:
        return True
    for x, y in zip(a, b):
        if x is not None and y is not None and x != y:
            return False
    return True


def _covers(a, b):
    if a is None:
        return True
    if b is None:
        return False
    for x, y in zip(a, b):
        if x is not None and x != y:
            return False
    return True


class MK:
    def __init__(self, nc):
        self.nc = nc
        self.streams = {e: [] for e in ENGS}
        self.eng_sem = {}
        self.dma_sems = {}
        self.nops = 0

    def sbuf_at(self, name, shape, dtype, offset):
        return Buf(name, self.nc.alloc_sbuf_tensor_at(name, list(shape), dtype, offset=offset))

    def psum(self, name, shape, dtype=F32):
        b = Buf(name, self.nc.alloc_psum_tensor(name, list(shape), dtype))
        b.excl = True
        return b

    def dram(self, name, shape, dtype, kind="Internal"):
        return Buf(name, self.nc.dram_tensor(name, list(shape), dtype, kind=kind))

    def overlap(self, a, b):
        a.overlaps.append(b)
        b.overlaps.append(a)

    @staticmethod
    def _norm(lst):
        out = []
        for x in lst:
            if isinstance(x, Buf):
                out.append((x, None))
            else:
                out.append((x[0], tuple(x[1]) if x[1] is not None else None))
        return out

    @staticmethod
    def _absorb(b):
        for y in b.overlaps:
            if y.writers or y.readers:
                ops = [p for _, p in y.writers] + [p for _, p in y.readers]
                y.writers = []
                y.readers = []
                for p in ops:
                    b.writers.append((None, p))
                latest = {}
                keep = []
                for k2, p in b.writers:
                    if k2 is None and p.dma_sem is None:
                        if p.eng not in latest or latest[p.eng].id < p.id:
                            latest[p.eng] = p
                    else:
                        keep.append((k2, p))
                b.writers = keep + [(None, p) for p in latest.values()]

    def op(self, eng, fn, reads=(), writes=(), is_dma=False):
        o = Op()
        o.eng, o.fn, o.deps = eng, fn, []
        o.signal, o.sigval, o.dma_sem, o.dma_val = False, None, None, None
        o.id = self.nops
        self.nops += 1
        reads = self._norm(reads)
        writes = self._norm(writes)
        for b, _ in reads + writes:
            if b.overlaps:
                self._absorb(b)
        seen = set()

        def add(p, raw):
            if p is o or (p.id, raw) in seen:
                return
            seen.add((p.id, raw))
            o.deps.append((p, raw))

        for b, key in reads:
            for k2, p in b.writers:
                if _conf(key, k2):
                    add(p, True)
            if b.excl:
                for k2, p in b.readers:
                    if p.eng != eng:
                        add(p, False)
        for b, key in writes:
            for k2, p in b.writers:
                if _conf(key, k2):
                    add(p, False)
            for k2, p in b.readers:
                if _conf(key, k2):
                    add(p, False)
        for b, key in reads:
            if not is_dma:
                b.readers = [(k2, p) for (k2, p) in b.readers
                             if not (k2 == key and p.eng == eng and p.dma_sem is None)]
            b.readers.append((key, o))
        for b, key in writes:
            b.writers = [(k2, p) for (k2, p) in b.writers if not _covers(key, k2)]
            b.readers = [(k2, p) for (k2, p) in b.readers if (not _covers(key, k2)) or p is o]
            b.writers.append((key, o))
        self.streams[eng].append(o)
        return o

    def I(self, eng, name, reads, writes, *args, **kw):
        def fn(e):
            return getattr(e, name)(*args, **kw)
        return self.op(eng, fn, reads, writes)

    def dma(self, queue, out_ap, in_ap, reads, writes, sem_key, **kw):
        ent = self.dma_sems.get(sem_key)
        if ent is None:
            ent = [None, 0, None]
            self.dma_sems[sem_key] = ent
        prev = ent[2]

        def fn(e):
            return e.dma_start(out=out_ap, in_=in_ap, **kw)

        o = self.op(queue, fn, reads, writes, is_dma=True)
        ent[1] += 16
        o.dma_sem = sem_key
        o.dma_val = ent[1]
        if prev is not None:
            o.deps.append((prev, True))
        ent[2] = o
        return o

    def emit(self):
        nc = self.nc
        for e, ops in self.streams.items():
            for o in ops:
                for p, raw in o.deps:
                    if p.dma_sem is not None:
                        continue
                    if p.eng == o.eng:
                        if o.eng != "pe":
                            p.signal = True
                    else:
                        p.signal = True
        for e, ops in self.streams.items():
            c = 0
            for o in ops:
                if o.dma_sem is None and o.signal:
                    c += 1
                    o.sigval = c
        with contextlib.ExitStack() as st:
            for e in ENGS:
                self.eng_sem[e] = st.enter_context(nc.semaphore("sem_" + e))
            for i, (k, ent) in enumerate(self.dma_sems.items()):
                ent[0] = st.enter_context(nc.semaphore("dsem_%d" % i))
            block = st.enter_context(nc.Block())
            mk = self

            def make(ename):
                def body(eng):
                    waited = {}
                    for o in mk.streams[ename]:
                        for p, raw in o.deps:
                            if p.dma_sem is not None:
                                sem, val, key = mk.dma_sems[p.dma_sem][0], p.dma_val, ("d", p.dma_sem)
                            else:
                                if p.eng == ename and ename == "pe":
                                    continue
                                sem, val, key = mk.eng_sem[p.eng], p.sigval, ("e", p.eng)
                            if waited.get(key, 0) >= val:
                                continue
                            waited[key] = val
                            eng.wait_ge(sem, val)
                        inst = o.fn(eng)
                        if o.dma_sem is not None:
                            inst.then_inc(mk.dma_sems[o.dma_sem][0], 16)
                        elif o.signal:
                            inst.then_inc(mk.eng_sem[ename], 1)
                    for k, ent in mk.dma_sems.items():
                        if ent[2] is not None and ent[2].eng == ename and waited.get(("d", k), 0) < ent[1]:
                            eng.wait_ge(ent[0], ent[1])
                return body

            block.sync(make("sp"))
            block.tensor(make("pe"))
            block.vector(make("dve"))
            block.scalar(make("act"))
            block.gpsimd(make("pool"))


def make_params(DEPTH):
  return [
    ("norm_mix", (DEPTH, D)), ("w_in", (DEPTH, D, D_IN)), ("ssm_conv_w", (DEPTH, 4, 2048)),
    ("ssm_conv_b", (DEPTH, 2048)), ("dt_bias", (DEPTH, NH)), ("a_log", (DEPTH, NH)),
    ("d_skip", (DEPTH, NH)), ("ssm_norm", (DEPTH, D)), ("q_norm", (DEPTH, Q_LORA)),
    ("w_uq", (DEPTH, Q_LORA, NH * 96)), ("kv_norm", (DEPTH, KV_LORA)),
    ("w_ukv", (DEPTH, KV_LORA, NH * 128)), ("attn_out_norm", (DEPTH, D)),
    ("w_out", (DEPTH, 2 * D, D)), ("norm_mem_q", (DEPTH, D)), ("norm_mem_kv", (DEPTH, D)),
    ("w_mq", (DEPTH, D, D)), ("w_mk", (DEPTH, D, D)), ("w_mv", (DEPTH, D, D)), ("w_mo", (DEPTH, D, D)),
    ("norm_ffn", (DEPTH, D)), ("w_up", (DEPTH, D, 2 * D_FF)), ("ffn_conv_w", (DEPTH, 3, 2 * D_FF)),
    ("ffn_conv_b", (DEPTH, 2 * D_FF)), ("w_down", (DEPTH, D_FF, D)), ("final_norm", (D,)),
  ]


PARAMS = make_params(DEPTH)
VEC_ROWS = [("norm_mix", 8), ("ssm_conv_w", 64), ("ssm_conv_b", 16), ("ssm_norm", 8), ("q_norm", 3),
            ("kv_norm", 2), ("attn_out_norm", 8), ("norm_mem_q", 8), ("norm_mem_kv", 8),
            ("norm_ffn", 8), ("ffn_conv_w", 132), ("ffn_conv_b", 44)]


def build_program(depth=DEPTH, debug=None, stop_after=None):
    DEPTH = max(depth, 1)
    PARAMS = make_params(DEPTH)
    nc = bass.Bass("TRN2", target_bir_lowering=False)
    mk = MK(nc)
    I = mk.I

    x_in = mk.dram("x", [S, D], F32, kind="ExternalInput")
    mem_in = mk.dram("mem", [MEM, D], F32, kind="ExternalInput")
    pos_in = mk.dram("positions", [1, S], I32, kind="ExternalInput")
    P = {}
    for name, shp in PARAMS:
        P[name] = mk.dram(name, list(shp), F32, kind="ExternalInput")
    c_ident = mk.dram("c_ident", [128, 128], F32, kind="ExternalInput")
    c_U = mk.dram("c_U", [128, 128], F32, kind="ExternalInput")
    c_SL = mk.dram("c_SL", [128, 128], F32, kind="ExternalInput")
    c_rope = mk.dram("c_rope", [128, 2], F32, kind="ExternalInput")
    out_d = mk.dram("out", [S, D], F32, kind="ExternalOutput")
    xres = mk.dram("xres", [S, D], F32)
    sz_d = mk.dram("sz_d", [S, D], BF16)
    xs_d = mk.dram("xs_d", [S, D], BF16)
    bt_d = mk.dram("bt_d", [S, 512], BF16)
    bT_d = mk.dram("bT_d", [4, 128, S], BF16)
    cT_d = mk.dram("cT_d", [4, 128, S], BF16)
    dbg = {}
    if debug:
        for nm, shp in debug.items():
            dbg[nm] = mk.dram("dbg_" + nm, list(shp), F32, kind="ExternalOutput")

    BASE = 16512
    TOP = 229344
    cur = [BASE]
    all_bufs = []

    def alloc(name, shape, dtype):
        esz = 4 if dtype in (F32, I32) else 2
        n = 1
        for s_ in shape[1:]:
            n *= s_
        nbytes = (n * esz + 31) // 32 * 32
        off = cur[0]
        cur[0] += nbytes
        assert off + nbytes <= TOP, (name, off, nbytes, TOP)
        b = mk.sbuf_at(name, shape, dtype, off)
        b.off, b.nbytes = off, nbytes
        for o in all_bufs:
            if b.off < o.off + o.nbytes and o.off < b.off + b.nbytes:
                mk.overlap(b, o)
        all_bufs.append(b)
        return b

    ident_f = alloc("ident_f", [128, 128], F32)
    ident_b = alloc("ident_b", [128, 128], BF16)
    U_f = alloc("U_f", [128, 128], F32)
    U_b = alloc("U_b", [128, 128], BF16)
    SL_f = alloc("SL_f", [128, 128], F32)
    ones_f = alloc("ones_f", [128, 128], F32)
    ones_b = alloc("ones_b", [128, 128], BF16)
    zeros_b = alloc("zeros_b", [128, 512], BF16)
    cst = alloc("cst", [128, 4], F32)
    NVR = sum(r for _, r in VEC_ROWS) * DEPTH
    NVB = (NVR + 127) // 128
    vecsT = alloc("vecsT", [128, NVB * 128], F32)
    bc16 = alloc("bc16", [128, 3, DEPTH, NH], F32)
    cs2 = alloc("cs2", [128, S], F32)
    sn2 = alloc("sn2", [128, S], F32)
    memT = alloc("memT", [128, KC, MEM], BF16)
    stat = alloc("stat", [128, 96], F32)
    WSLOT = 6144
    wslots = [alloc("wslot%d" % i, [128, WSLOT], BF16) for i in range(2)]
    T0 = cur[0]

    PB = [mk.psum("pb%d" % i, [128, 512], F32) for i in range(8)]
    bank_rr = [0]
    tbank_rr = [0]

    def nextbank():
        b = PB[bank_rr[0] % 4]
        bank_rr[0] += 1
        return b

    def nexttbank():
        b = PB[6 + tbank_rr[0] % 2]
        tbank_rr[0] += 1
        return b

    stat_rr = [0]

    def statcol(n=3):
        c = stat_rr[0]
        if c + n > 96:
            c = 0
        stat_rr[0] = c + n
        return c

    vec_r0 = {}
    r = 0
    for name, rows in VEC_ROWS:
        vec_r0[name] = r
        r += rows * DEPTH
    VR = dict(VEC_ROWS)

    def vcol(name, l, idx):
        c = vec_r0[name] + l * VR[name] + idx
        return vecsT[:, c:c + 1]

    def vcols(name, l, i0, n):
        c = vec_r0[name] + l * VR[name] + i0
        return vecsT[:, c:c + n]

    wrr = [0]

    def nextslot():
        s_ = wslots[wrr[0] % 2]
        wrr[0] += 1
        return s_

    def wload(src_ap, kcn, ncols, src_buf):
        slot = nextslot()
        assert kcn * ncols <= WSLOT
        view = slot[:, 0:kcn * ncols].rearrange("p (k n) -> p k n", k=kcn)
        mk.dma("pool", view, src_ap, [src_buf], [slot], ("w", slot.name))
        return slot, view

    def wsrc(buf, l, c0, ncols, k0=0, kcn=KC):
        return buf[l][k0 * 128:(k0 + kcn) * 128, c0:c0 + ncols].rearrange("(kc p) n -> p kc n", p=128)

    stg = alloc("stg", [128, NVB, 128], F32)
    ang = alloc("ang", [128, S], F32)
    ang2 = alloc("ang2", [128, S], F32)
    kf = alloc("kf", [128, S], F32)
    ki = alloc("ki", [128, S], I32)
    posi = alloc("posi", [128, S], I32)
    crope = alloc("crope", [128, 2], F32)
    memt = alloc("memt", [128, 2, D], F32)
    memn = alloc("memn", [128, 2, D], BF16)
    junk0 = alloc("junk0", [128, D], BF16)

    mk.dma("sp", ident_f[:], c_ident[:], [c_ident], [ident_f], "c0")
    mk.dma("sp", U_f[:], c_U[:], [c_U], [U_f], "c1")
    mk.dma("sp", SL_f[:], c_SL[:], [c_SL], [SL_f], "c2")
    mk.dma("sp", crope[:], c_rope[:], [c_rope], [crope], "c3")
    I("dve", "tensor_copy", [ident_f], [ident_b], out=ident_b[:], in_=ident_f[:])
    I("dve", "tensor_copy", [U_f], [U_b], out=U_b[:], in_=U_f[:])
    I("dve", "memset", [], [ones_f], ones_f[:], 1.0)
    I("dve", "memset", [], [ones_b], ones_b[:], 1.0)
    I("dve", "memset", [], [zeros_b], zeros_b[:], 0.0)
    I("dve", "memset", [], [cst], cst[:, 0:1], EPS)
    I("dve", "memset", [cst], [cst], cst[:, 1:2], 1.0)
    I("dve", "memset", [], [stg], stg[:], 0.0)
    I("dve", "memset", [], [cs2], cs2[:], 0.0)
    I("dve", "memset", [], [sn2], sn2[:], 0.0)
    di = 0
    for name, rows in VEC_ROWS:
        src = P[name]
        shp = dict(PARAMS)[name]
        if len(shp) == 2:
            flat = src[:].rearrange("l (c q) -> (l c) q", q=128)
        else:
            flat = src[:].rearrange("l j (c q) -> (l j c) q", q=128)
        total = rows * DEPTH
        r0 = vec_r0[name]
        done = 0
        while done < total:
            rr = r0 + done
            blk, pp = rr // 128, rr % 128
            n = min(total - done, 128 - pp)
            mk.dma("sp", stg[pp:pp + n, blk, :], flat[done:done + n, :], [src], [stg], "v%d" % (di % 4))
            di += 1
            done += n
    for blk in range(NVB):
        pb = nextbank()
        I("pe", "transpose", [stg, ident_f], [pb], out=pb[:, 0:128], in_=stg[:, blk, :], identity=ident_f[:])
        I("dve", "tensor_copy", [pb], [(vecsT, (blk,))], out=vecsT[:, blk * 128:(blk + 1) * 128], in_=pb[:, 0:128])
    for i, name in enumerate(("dt_bias", "a_log", "d_skip")):
        mk.dma("sp", bc16[:, i, :, :].rearrange("p l h -> p (l h)"),
               P[name][:].rearrange("l h -> (l h)").unsqueeze(0).broadcast_to([128, DEPTH * NH]),
               [P[name]], [bc16], "b%d" % i)
    I("act", "activation", [bc16], [bc16], out=bc16[:, 1, :, :], in_=bc16[:, 1, :, :], func=AF.Exp)
    I("dve", "tensor_scalar", [bc16], [bc16], out=bc16[:, 1, :, :], in0=bc16[:, 1, :, :], scalar1=-1.0,
      scalar2=None, op0=ALU.mult)

    R0, R1 = 64, 128
    mk.dma("sp", posi[R0:R1, :], pos_in[0:1, :].broadcast_to([64, S]), [pos_in], [posi], "c4")
    I("dve", "tensor_copy", [posi], [ang], out=ang[R0:R1, :], in_=posi[R0:R1, :])
    I("dve", "tensor_scalar", [ang, crope], [ang], out=ang[R0:R1, :], in0=ang[R0:R1, :],
      scalar1=crope[R0:R1, 0:1], scalar2=None, op0=ALU.mult)
    TWO_PI = 2.0 * math.pi
    C1 = 6.28125
    C2 = TWO_PI - C1

    def sin_of(src, dst, shift):
        I("dve", "tensor_scalar", [src], [ang2], out=ang2[R0:R1, :], in0=src[R0:R1, :], scalar1=shift,
          scalar2=None, op0=ALU.add)
        I("dve", "tensor_scalar", [ang2], [ki], out=ki[R0:R1, :], in0=ang2[R0:R1, :], scalar1=1.0 / TWO_PI,
          scalar2=None, op0=ALU.mult)
        I("dve", "tensor_copy", [ki], [kf], out=kf[R0:R1, :], in_=ki[R0:R1, :])
        I("dve", "scalar_tensor_tensor", [kf, ang2], [ang2], out=ang2[R0:R1, :], in0=kf[R0:R1, :], scalar=-C1,
          in1=ang2[R0:R1, :], op0=ALU.mult, op1=ALU.add)
        I("dve", "scalar_tensor_tensor", [kf, ang2], [ang2], out=ang2[R0:R1, :], in0=kf[R0:R1, :], scalar=-C2,
          in1=ang2[R0:R1, :], op0=ALU.mult, op1=ALU.add)
        I("dve", "tensor_scalar", [ang2], [ang2], out=ang2[R0:R1, :], in0=ang2[R0:R1, :], scalar1=math.pi,
          scalar2=-math.pi, op0=ALU.min, op1=ALU.max)
        I("act", "activation", [ang2], [dst], out=dst[R0:R1, :], in_=ang2[R0:R1, :], func=AF.Sin)

    sin_of(ang, cs2, math.pi / 2.0)
    sin_of(ang, sn2, 0.0)
    I("dve", "tensor_scalar", [sn2, crope], [sn2], out=sn2[R0:R1, :], in0=sn2[R0:R1, :],
      scalar1=crope[R0:R1, 1:2], scalar2=None, op0=ALU.mult)

    for kt in range(2):
        mk.dma("sp", memt[:, kt, :], mem_in[kt * 128:(kt + 1) * 128, :], [mem_in], [(memt, (kt,))], "m%d" % kt)
        c = statcol(3)
        I("act", "activation", [(memt, (kt,))], [junk0, (stat, (c,))], out=junk0[:], in_=memt[:, kt, :], func=AF.Square,
          accum_out=stat[:, c:c + 1])
        I("act", "activation", [(stat, (c,)), cst], [(stat, (c,))], out=stat[:, c + 1:c + 2], in_=stat[:, c:c + 1],
          func=AF.Sqrt, scale=1.0 / D, bias=cst[:, 0:1])
        I("dve", "reciprocal", [(stat, (c,))], [(stat, (c,))], out=stat[:, c + 2:c + 3], in_=stat[:, c + 1:c + 2])
        I("dve", "tensor_scalar", [(memt, (kt,)), (stat, (c,))], [(memn, (kt,))], out=memn[:, kt, :], in0=memt[:, kt, :],
          scalar1=stat[:, c + 2:c + 3], scalar2=None, op0=ALU.mult)
        tb_ = nexttbank()
        tv = tb_[:].bitcast(BF16)
        for kc in range(KC):
            I("pe", "transpose", [(memn, (kt,)), ident_b], [tb_], out=tv[:, kc * 128:(kc + 1) * 128],
              in_=memn[:, kt, kc * 128:(kc + 1) * 128], identity=ident_b[:])
        I("dve", "tensor_copy", [tb_], [memT], out=memT[:, :, kt * 128:(kt + 1) * 128],
          in_=tv[:, 0:1024].rearrange("p (k t) -> p k t", k=KC))

    cur[0] = T0
    hT = alloc("hT", [128, KC, S], BF16)
    R_A = cur[0]
    xt = [alloc("xt%d" % i, [128, D], F32) for i in range(2)]
    xn = [alloc("xn%d" % i, [128, D], BF16) for i in range(2)]
    junk = alloc("junk", [128, D], BF16)
    R_A_END = cur[0]
    rx = [alloc("rx%d" % i, [128, 256], F32) for i in range(3)]
    R_B = cur[0]

    raw = [alloc("raw%d" % i, [128, S], F32) for i in range(2)]
    acc = [alloc("acc%d" % i, [128, S], F32) for i in range(2)]
    tmpT = [alloc("tmpT%d" % i, [128, S], BF16) for i in range(2)]
    zst = [alloc("zst%d" % i, [128, 512], BF16) for i in range(2)]
    tkst = [alloc("tkst%d" % i, [128, 8, 128], BF16) for i in range(2)]
    R_X_END = cur[0]
    cqnT = alloc("cqnT", [128, 3, S], BF16)
    ckvnT = alloc("ckvnT", [128, 2, S], BF16)
    kpeT = alloc("kpeT", [128, S], BF16)
    dtt = alloc("dtt", [128, NT, NH], F32)
    adt = alloc("adt", [128, NT, NH], F32)
    ssqa = alloc("ssqa", [128, NT], F32)
    rsta = alloc("rsta", [128, NT], F32)
    R_Y = cur[0]
    sqb = [alloc("sqb%d" % i, [128, S], BF16) for i in range(2)]
    rstd = alloc("rstd", [128, S], F32)
    wkr = alloc("wkr", [128, KC, 128], BF16)
    rtmp = alloc("rtmp", [128, 2, 512], F32)
    cur[0] = R_Y
    yattT = alloc("yattT", [128, KC, S], BF16)
    MIX_END = cur[0]

    cur[0] = R_B
    st_f = alloc("st_f", [128, D], F32)
    st_b = alloc("st_b", [128, D], BF16)
    adUs = [alloc("adU%d" % i, [128, NH, 128], F32) for i in range(2)]
    Eexp = [alloc("Eexp%d" % i, [128, 512], F32) for i in range(2)]
    MT = [alloc("MT%d" % i, [128, 512], BF16) for i in range(2)]
    cbm = [alloc("cbm%d" % i, [128, 128], F32) for i in range(2)]
    xsc = [alloc("xsc%d" % i, [128, D], BF16) for i in range(2)]
    btc = [alloc("btc%d" % i, [128, 512], BF16) for i in range(2)]
    bTc = [alloc("bTc%d" % i, [128, 4, 128], BF16) for i in range(2)]
    cTc = [alloc("cTc%d" % i, [128, 4, 128], BF16) for i in range(2)]
    szc = [alloc("szc%d" % i, [128, D], BF16) for i in range(2)]
    assert cur[0] <= R_X_END, (cur[0], R_X_END)
    cur[0] = R_Y
    xd = alloc("xd", [128, D], BF16)
    xdd = alloc("xdd", [128, D], BF16)
    xsk = alloc("xsk", [128, D], F32)
    yfs = [alloc("yf%d" % i, [128, D], F32) for i in range(2)]
    yo = alloc("yo", [128, D], F32)
    ygb = alloc("ygb", [128, D], BF16)
    sm16s = [alloc("sm16_%d" % i, [128, 8, NH], F32) for i in range(2)]
    assert cur[0] <= MIX_END, (cur[0], MIX_END)
    cur[0] = R_B
    QT = [alloc("QT%d" % i, [128, S], BF16) for i in range(2)]
    KT = [alloc("KT%d" % i, [128, S], BF16) for i in range(2)]
    Vh = [alloc("Vh%d" % i, [128, NT, 96], BF16) for i in range(2)]
    PT = [alloc("PT%d" % i, [128, 512], BF16) for i in range(5)]
    qtmp = alloc("qtmp", [128, 4, 512], F32)
    rsum = alloc("rsum", [128, 8], F32)
    onrm = alloc("onrm", [128, 4, HD], F32)
    osq = alloc("osq", [128, 4, HD], F32)
    ypair = [alloc("ypair%d" % i, [128, NT, 128], BF16) for i in range(2)]
    assert cur[0] <= R_X_END, (cur[0], R_X_END)

    cur[0] = R_B
    qT = alloc("qT", [128, KC, S], BF16)
    oT = alloc("oT", [128, KC, S], BF16)
    mnT = alloc("mnT", [128, KC, MEM], BF16)
    KmT = alloc("KmT", [128, KC, MEM], BF16)
    Vm = alloc("Vm", [128, 2, D], BF16)
    PTm = [alloc("PTm%d" % i, [128, 2, 512], BF16) for i in range(2)]
    rinvb = [alloc("rinvb%d" % i, [128, 512], F32) for i in range(2)]

    cur[0] = R_A
    fraw = alloc("fraw", [128, S], F32)
    assert cur[0] <= R_A_END
    cur[0] = R_B
    aT = alloc("aT", [128, NFF, S], BF16)
    facc = alloc("facc", [128, S], F32)
    sgate = alloc("sgate", [128, S], BF16)
    cur[0] = R_B
    gfin = alloc("gfin", [128, D], F32)
    ofin = [alloc("ofin%d" % i, [128, D], F32) for i in range(2)]

    def rms_tile_stats(src_ap, src_reads, width, jbuf=None):
        jb = junk if jbuf is None else jbuf
        c = statcol(3)
        I("act", "activation", src_reads, [jb, (stat, (c,))], out=jb[:, 0:width], in_=src_ap, func=AF.Square,
          accum_out=stat[:, c:c + 1])
        I("act", "activation", [(stat, (c,)), cst], [(stat, (c,))], out=stat[:, c + 1:c + 2], in_=stat[:, c:c + 1],
          func=AF.Sqrt, scale=1.0 / width, bias=cst[:, 0:1])
        I("dve", "reciprocal", [(stat, (c,))], [(stat, (c,))], out=stat[:, c + 2:c + 3], in_=stat[:, c + 1:c + 2])
        return stat[:, c + 2:c + 3], (stat, (c,))

    def transpose_tile_to(src_buf, dstT, dkey, tile_i, gain_name, l):
        tb_ = nexttbank()
        tv = tb_[:].bitcast(BF16)
        for kc in range(KC):
            I("pe", "transpose", [src_buf, ident_b], [tb_], out=tv[:, kc * 128:(kc + 1) * 128],
              in_=src_buf[:, kc * 128:(kc + 1) * 128], identity=ident_b[:])
        g = vcols(gain_name, l, 0, KC).unsqueeze(2).broadcast_to([128, KC, 128])
        I("dve", "tensor_tensor", [tb_, vecsT], [(dstT, dkey)],
          out=dstT[:, :, tile_i * 128:(tile_i + 1) * 128],
          in0=tv[:, 0:1024].rearrange("p (k t) -> p k t", k=KC), in1=g, op=ALU.mult)

    def norm_phase(xsrc, l, gain_name):
        for i in range(NT):
            t = xt[i % 2]
            mk.dma("sp", t[:], xsrc[i * 128:(i + 1) * 128, :], [(xsrc, (i,))], [t], ("xt", i % 2))
            rinv, skey = rms_tile_stats(t[:], [t], D)
            n = xn[i % 2]
            I("dve", "tensor_scalar", [t, skey], [n], out=n[:], in0=t[:], scalar1=rinv, scalar2=None, op0=ALU.mult)
            transpose_tile_to(n, hT, (None, i // 4), i, gain_name, l)

    def resid_phase(xsrc, l, wname, parts, scaled_last=None):
        NB = 256
        nk_tot = sum(nk for _, nk in parts)
        n_units = (D // NB) * NT

        def xload(u):
            cb_, i_ = divmod(u, NT)
            t_ = rx[u % 3]
            mk.dma("sp", t_[:], xsrc[i_ * 128:(i_ + 1) * 128, cb_ * NB:(cb_ + 1) * NB], [(xsrc, (i_,))], [t_], ("rx", u % 3))

        xload(0)
        for cb in range(D // NB):
            slot, wv = wload(wsrc(P[wname], l, cb * NB, NB, 0, nk_tot), nk_tot, NB, P[wname])
            for i in range(NT):
                u = cb * NT + i
                t = rx[u % 3]
                if u + 1 < n_units:
                    xload(u + 1)
                k0 = 0
                for pi, (actT, nk) in enumerate(parts):
                    last = (pi == len(parts) - 1)
                    sep = last and scaled_last is not None
                    if pi == 0 or sep:
                        pb = nextbank()
                    for kc in range(nk):
                        st_flag = (kc == 0) if (pi == 0 or sep) else False
                        sp_flag = (kc == nk - 1) and (last or (scaled_last is not None and pi == len(parts) - 2))
                        I("pe", "matmul", [(actT, (None, i // 4)), slot], [pb], pb[:, 0:NB],
                          lhsT=actT[:, kc, i * 128:(i + 1) * 128], rhs=wv[:, k0 + kc, :], start=st_flag, stop=sp_flag)
                    k0 += nk
                    if sep:
                        I("dve", "scalar_tensor_tensor", [pb, t, scaled_last], [t], out=t[:], in0=pb[:, 0:NB],
                          scalar=scaled_last[:, i:i + 1], in1=t[:], op0=ALU.mult, op1=ALU.add)
                    elif last or (scaled_last is not None and pi == len(parts) - 2):
                        I("dve", "tensor_tensor", [pb, t], [t], out=t[:], in0=pb[:, 0:NB], in1=t[:], op=ALU.add)
                mk.dma("sp", xres[i * 128:(i + 1) * 128, cb * NB:(cb + 1) * NB], t[:], [t], [(xres, (i,))], ("rxo", u % 3))

    def proj_fm(wv, col0, m, nk, actT, slot, cb_fn):
        for tb in range(4):
            pb = nextbank()
            for kc in range(nk):
                I("pe", "matmul", [slot, (actT, (None, tb))], [pb], pb[0:m, :],
                  lhsT=wv[:, kc, col0:col0 + m], rhs=actT[:, kc, tb * 512:(tb + 1) * 512],
                  start=(kc == 0), stop=(kc == nk - 1))
            cb_fn(tb, pb)

    def rope_rows(ps, dst, dkey, ts_):
        I("dve", "tensor_tensor", [ps, cs2], [qtmp], out=qtmp[64:128, 0, :], in0=ps[64:128, :], in1=cs2[64:128, ts_], op=ALU.mult)
        I("dve", "tensor_tensor", [ps, sn2], [qtmp], out=qtmp[64:96, 1, :], in0=ps[96:128, :], in1=sn2[96:128, ts_], op=ALU.mult)
        I("dve", "tensor_tensor", [ps, sn2], [qtmp], out=qtmp[96:128, 1, :], in0=ps[64:96, :], in1=sn2[64:96, ts_], op=ALU.mult)
        I("dve", "tensor_tensor", [qtmp], [(dst, dkey)], out=dst[64:128, ts_], in0=qtmp[64:128, 0, :], in1=qtmp[64:128, 1, :],
          op=ALU.add)

    xsrc = x_in
    STAGES = ["norm", "z", "xbc", "prep", "ssd", "mla", "wout", "memattn", "ffn"]
    sub_stop = 99
    if stop_after and ":" in stop_after:
        stop_after, ss_ = stop_after.split(":")
        sub_stop = int(ss_)
    stop_idx = STAGES.index(stop_after) if stop_after else 99

    class _Stop(Exception):
        pass

    def layer_body(l, xsrc):
        norm_phase(xsrc, l, "norm_mix")
        if stop_idx <= 0:
            raise _Stop()

        for cb in range(2):
            slot, wv = wload(wsrc(P["w_in"], l, O_Z + cb * 512, 512), KC, 512, P["w_in"])
            for i in range(NT):
                pb = nextbank()
                for kc in range(KC):
                    I("pe", "matmul", [(hT, (None, i // 4)), slot], [pb], pb[:, :],
                      lhsT=hT[:, kc, i * 128:(i + 1) * 128], rhs=wv[:, kc, :], start=(kc == 0), stop=(kc == KC - 1))
                u = cb * NT + i
                z = zst[u % 2]
                I("act", "activation", [pb], [z], out=z[:], in_=pb[:, :], func=AF.Silu)
                mk.dma("sp", sz_d[i * 128:(i + 1) * 128, cb * 512:(cb + 1) * 512], z[:], [z], [(sz_d, (i,))], ("zst", u % 2))

        if stop_idx <= 1:
            raise _Stop()
        for cg in range(8):
            slot, wv = wload(wsrc(P["w_in"], l, O_XBC + cg * 256, 256), KC, 256, P["w_in"])
            for sub in range(2):
                c = cg * 2 + sub
                rw = raw[c % 2]
                ac = acc[c % 2]

                def evac(tb, pb, rw=rw):
                    I("act", "copy", [pb], [(rw, (tb,))], out=rw[:, tb * 512:(tb + 1) * 512], in_=pb[:, :])
                proj_fm(wv, sub * 128, 128, KC, hT, slot, evac)
                w3, w2, w1, w0 = (vcol("ssm_conv_w", l, j * 16 + c) for j in (3, 2, 1, 0))
                bb = vcol("ssm_conv_b", l, c)
                I("dve", "tensor_scalar", [rw, vecsT], [ac], out=ac[:], in0=rw[:], scalar1=w3, scalar2=bb,
                  op0=ALU.mult, op1=ALU.add)
                for sh, wj in ((1, w2), (2, w1), (3, w0)):
                    I("dve", "scalar_tensor_tensor", [rw, ac, vecsT], [ac], out=ac[:, sh:S], in0=rw[:, 0:S - sh],
                      scalar=wj, in1=ac[:, sh:S], op0=ALU.mult, op1=ALU.add)
                tt = tmpT[c % 2]
                I("act", "activation", [ac], [tt], out=tt[:], in_=ac[:], func=AF.Silu)
                if c < 12:
                    for half in range(2):
                        tk = tkst[half]
                        tb_ = nexttbank()
                        tv = tb_[:].bitcast(BF16)
                        for j in range(8):
                            i = half * 8 + j
                            I("pe", "transpose", [tt, ident_b], [tb_], out=tv[:, j * 128:(j + 1) * 128],
                              in_=tt[:, i * 128:(i + 1) * 128], identity=ident_b[:])
                        I("dve", "tensor_copy", [tb_], [tk], out=tk[:], in_=tv[:, 0:1024].rearrange("p (j t) -> p j t", j=8))
                        rows = slice(half * 1024, (half + 1) * 1024)
                        if c < 8:
                            dst = xs_d[rows, c * 128:(c + 1) * 128].rearrange("(i p) f -> p i f", p=128)
                            mk.dma("sp", dst, tk[:], [tk], [xs_d], ("tk", half))
                        else:
                            dst = bt_d[rows, (c - 8) * 128:(c - 7) * 128].rearrange("(i p) f -> p i f", p=128)
                            mk.dma("sp", dst, tk[:], [tk], [bt_d], ("tk", half))
                if 8 <= c < 12:
                    mk.dma("sp", bT_d[c - 8], tt[:], [tt], [bT_d], ("tT", c % 2))
                if c >= 12:
                    mk.dma("sp", cT_d[c - 12], tt[:], [tt], [cT_d], ("tT", c % 2))

        if stop_idx <= 2:
            raise _Stop()
        slot = nextslot()
        wv = slot[:, 0:KC * 64].rearrange("p (k n) -> p k n", k=KC)
        mk.dma("pool", wv[:, :, 0:NH], wsrc(P["w_in"], l, O_DT, NH), [P["w_in"]], [slot], ("w", slot.name))
        import os
        DTV = int(os.environ.get("MK_DT", "9"))
        for i in range(NT if DTV >= 2 else 0):
            pb = nextbank()
            for kc in range(KC):
                I("pe", "matmul", [(hT, (None, i // 4)), slot], [pb], pb[:, 0:NH],
                  lhsT=hT[:, kc, i * 128:(i + 1) * 128], rhs=wv[:, kc, 0:NH], start=(kc == 0), stop=(kc == KC - 1))
            I("dve", "tensor_tensor", [pb, bc16], [(dtt, (i,))], out=dtt[:, i, :], in0=pb[:, 0:NH], in1=bc16[:, 0, l, :],
              op=ALU.add)
            if DTV >= 3:
                I("act", "activation", [(dtt, (i,))], [(dtt, (i,))], out=dtt[:, i, :], in_=dtt[:, i, :], func=AF.Exp)
            if DTV >= 4:
                I("act", "activation", [(dtt, (i,)), cst], [(dtt, (i,))], out=dtt[:, i, :], in_=dtt[:, i, :], func=AF.Ln,
                  bias=cst[:, 1:2], scale=1.0)
            if DTV >= 5:
                I("dve", "tensor_tensor", [(dtt, (i,)), bc16], [(adt, (i,))], out=adt[:, i, :], in0=dtt[:, i, :],
                  in1=bc16[:, 1, l, :], op=ALU.mult)

        if stop_idx == 3 and sub_stop <= 0:
            raise _Stop()

        def latent(wv, slot, col0, nch, dstT, gname, width):
            for c in range(nch):
                sq = sqb[c % 2]

                def evac(tb, pb, c=c, sq=sq):
                    I("act", "activation", [pb], [(sq, (tb,))], out=sq[:, tb * 512:(tb + 1) * 512], in_=pb[:, :],
                      func=AF.Square)
                    I("dve", "tensor_scalar", [pb, vecsT], [(dstT, (c, tb))], out=dstT[:, c, tb * 512:(tb + 1) * 512],
                      in0=pb[:, :], scalar1=vcol(gname, l, c), scalar2=None, op0=ALU.mult)
                proj_fm(wv, col0 + c * 128, 128, KC, hT, slot, evac)
                for tb in range(4):
                    pb = nextbank()
                    I("pe", "matmul", [(sq, (tb,)), ones_b], [pb], pb[:, :], lhsT=ones_b[:],
                      rhs=sq[:, tb * 512:(tb + 1) * 512], start=True, stop=True)
                    if c == 0:
                        I("dve", "tensor_copy", [pb], [(rstd, (tb,))], out=rstd[:, tb * 512:(tb + 1) * 512], in_=pb[:, :])
                    else:
                        I("dve", "tensor_tensor", [pb, (rstd, (tb,))], [(rstd, (tb,))],
                          out=rstd[:, tb * 512:(tb + 1) * 512], in0=pb[:, :], in1=rstd[:, tb * 512:(tb + 1) * 512],
                          op=ALU.add)
            I("act", "activation", [rstd, cst], [rstd], out=rstd[:], in_=rstd[:], func=AF.Sqrt, scale=1.0 / width,
              bias=cst[:, 0:1])
            I("dve", "reciprocal", [rstd], [rstd], out=rstd[:], in_=rstd[:])
            for c in range(nch):
                I("dve", "tensor_tensor", [(dstT, (c, None)), rstd], [(dstT, (c, None))], out=dstT[:, c, :],
                  in0=dstT[:, c, :], in1=rstd[:], op=ALU.mult)

        slot, wv = wload(wsrc(P["w_in"], l, O_CQ, 384), KC, 384, P["w_in"])
        latent(wv, slot, 0, 3, cqnT, "q_norm", Q_LORA)
        if stop_idx == 3 and sub_stop <= 1:
            raise _Stop()
        slot = nextslot()
        wv = slot[:, 0:KC * 64].rearrange("p (k n) -> p k n", k=KC)
        mk.dma("pool", wv[:, :, 0:ROPE], wsrc(P["w_in"], l, O_KR, ROPE), [P["w_in"]], [slot], ("w", slot.name))
        I("dve", "memset", [], [wkr], wkr[:], 0.0)
        I("dve", "tensor_copy", [slot, wkr], [wkr], out=wkr[:, :, 64:80], in_=wv[:, :, 0:16])
        I("dve", "tensor_copy", [slot, wkr], [wkr], out=wkr[:, :, 96:112], in_=wv[:, :, 16:32])
        slot, wv = wload(wsrc(P["w_in"], l, O_CKV, 256), KC, 256, P["w_in"])
        if stop_idx == 3 and sub_stop <= 2:
            raise _Stop()
        latent(wv, slot, 0, 2, ckvnT, "kv_norm", KV_LORA)
        if stop_idx == 3 and sub_stop <= 3:
            raise _Stop()
        for tb in range(4):
            pa = nextbank()
            for kc in range(KC):
                I("pe", "matmul", [wkr, (hT, (None, tb))], [pa], pa[:, :], lhsT=wkr[:, kc, :],
                  rhs=hT[:, kc, tb * 512:(tb + 1) * 512], start=(kc == 0), stop=(kc == KC - 1))
            rope_rows(pa, kpeT, (tb,), slice(tb * 512, (tb + 1) * 512))

        if stop_idx <= 3:
            raise _Stop()
        I("dve", "memset", [], [st_f], st_f[:], 0.0)
        I("dve", "memset", [], [st_b], st_b[:], 0.0)
        pending_tail = [None]
        def ssd_decay_prep(c):
            adU = adUs[c % 2]
            sm16 = sm16s[c % 2]
            I("pool", "tensor_tensor", [(adt, (c,)), U_f], [adU], out=adU[:],
              in0=adt[:, c, :].unsqueeze(2).broadcast_to([128, NH, 128]),
              in1=U_f[:].unsqueeze(1).broadcast_to([128, NH, 128]), op=ALU.mult)
            pc = nextbank()
            I("pe", "matmul", [U_f, (adt, (c,))], [pc], pc[:, 0:NH], lhsT=U_f[:], rhs=adt[:, c, :], start=True, stop=True)
            I("pe", "matmul", [ones_f, (adt, (c,))], [pc], pc[:, 64:64 + NH], lhsT=ones_f[:], rhs=adt[:, c, :],
              start=True, stop=True, skip_group_check=True)
            I("dve", "tensor_copy", [pc], [sm16], out=sm16[:, 3:5, :], in_=pc[:, 0:128].rearrange("p (a h) -> p a h", a=2)[:, :, 0:NH])
            I("dve", "tensor_tensor", [sm16], [sm16], out=sm16[:, 5, :], in0=sm16[:, 4, :], in1=sm16[:, 3, :], op=ALU.subtract)
            I("act", "activation", [sm16], [sm16], out=sm16[:, 0:3, :], in_=sm16[:, 3:6, :], func=AF.Exp)

        ssd_decay_prep(0)
        for c in range(NT):
            cs_ = slice(c * 128, (c + 1) * 128)
            xs_c, bt_c, bT_c, cT_c, sz_c = xsc[c % 2], btc[c % 2], bTc[c % 2], cTc[c % 2], szc[c % 2]
            yf = yfs[c % 2]
            mk.dma("sp", xs_c[:], xs_d[cs_, :], [xs_d], [xs_c], ("xsc", c % 2))
            mk.dma("sp", bt_c[:], bt_d[cs_, :], [bt_d], [bt_c], ("btc", c % 2))
            mk.dma("sp", bT_c[:], bT_d[:, :, cs_].rearrange("g n t -> n g t"), [bT_d], [bT_c], ("bTc", c % 2))
            mk.dma("sp", cT_c[:], cT_d[:, :, cs_].rearrange("g n t -> n g t"), [cT_d], [cT_c], ("cTc", c % 2))
            mk.dma("sp", sz_c[:], sz_d[cs_, :], [(sz_d, (c,))], [sz_c], ("szc", c % 2))
            adU = adUs[c % 2]
            sm16 = sm16s[c % 2]
            if c + 1 < NT:
                ssd_decay_prep(c + 1)
            x3 = xs_c[:].rearrange("p (h d) -> p h d", h=NH)
            I("dve", "tensor_tensor", [xs_c, (dtt, (c,))], [xd], out=xd[:].rearrange("p (h d) -> p h d", h=NH), in0=x3,
              in1=dtt[:, c, :].unsqueeze(2).broadcast_to([128, NH, HD]), op=ALU.mult)
            I("dve", "tensor_tensor", [xd, sm16], [xdd], out=xdd[:].rearrange("p (h d) -> p h d", h=NH),
              in0=xd[:].rearrange("p (h d) -> p h d", h=NH),
              in1=sm16[:, 2, :].unsqueeze(2).broadcast_to([128, NH, HD]), op=ALU.mult)
            I("pool", "tensor_tensor", [xs_c, bc16], [xsk], out=xsk[:].rearrange("p (h d) -> p h d", h=NH), in0=x3,
              in1=bc16[:, 2, l, :].unsqueeze(2).broadcast_to([128, NH, HD]), op=ALU.mult)
            for hh in range(2):
                hs = slice(hh * 512, (hh + 1) * 512)
                py, po = PB[4], PB[5]
                for g in (2 * hh, 2 * hh + 1):
                    pd = nextbank()
                    I("pe", "matmul", [SL_f, adU], [pd], pd[:, :], lhsT=SL_f[:],
                      rhs=adU[:, g * 4:(g + 1) * 4, :].rearrange("p h l -> p (h l)"), start=True, stop=True)
                    E = Eexp[g % 2]
                    I("act", "activation", [pd], [E], out=E[:], in_=pd[:, :], func=AF.Exp)
                    pcb = nextbank()
                    I("pe", "matmul", [bT_c, cT_c], [pcb], pcb[:, 0:128], lhsT=bT_c[:, g, :], rhs=cT_c[:, g, :],
                      start=True, stop=True)
                    cb_ = cbm[g % 2]
                    I("dve", "tensor_tensor", [pcb, U_f], [cb_], out=cb_[:], in0=pcb[:, 0:128], in1=U_f[:], op=ALU.mult)
                    M = MT[g % 2]
                    I("dve", "tensor_tensor", [E, cb_], [M], out=M[:].rearrange("p (r l) -> p r l", r=4),
                      in0=E[:].rearrange("p (r l) -> p r l", r=4), in1=cb_[:].unsqueeze(1).broadcast_to([128, 4, 128]),
                      op=ALU.mult)
                    for r_ in range(4):
                        h = g * 4 + r_
                        oc = (g % 2) * 256 + r_ * 64
                        I("pe", "matmul", [M, xd], [py], py[:, oc:oc + 64], lhsT=M[:, r_ * 128:(r_ + 1) * 128],
                          rhs=xd[:, h * HD:(h + 1) * HD], start=True, stop=True, skip_group_check=True)
                    oc = (g % 2) * 256
                    I("pe", "matmul", [cT_c, st_b], [po], po[:, oc:oc + 256], lhsT=cT_c[:, g, :],
                      rhs=st_b[:, g * 256:(g + 1) * 256], start=True, stop=True, skip_group_check=True)
                I("dve", "tensor_tensor", [po, sm16], [yo], out=yo[:, hs].rearrange("p (h d) -> p h d", h=8),
                  in0=po[:, :].rearrange("p (h d) -> p h d", h=8),
                  in1=sm16[:, 0, hh * 8:(hh + 1) * 8].unsqueeze(2).broadcast_to([128, 8, HD]), op=ALU.mult)
                I("dve", "tensor_tensor", [py, yo], [yf], out=yf[:, hs], in0=py[:, :], in1=yo[:, hs], op=ALU.add)
            I("pool", "tensor_tensor", [yf, xsk], [yf], out=yf[:], in0=yf[:], in1=xsk[:], op=ALU.add)
            I("pool", "tensor_tensor", [st_f, sm16], [st_f], out=st_f[:].rearrange("p (h d) -> p h d", h=NH),
              in0=st_f[:].rearrange("p (h d) -> p h d", h=NH),
              in1=sm16[:, 1, :].unsqueeze(2).broadcast_to([128, NH, HD]), op=ALU.mult)
            for hh in range(2):
                hs = slice(hh * 512, (hh + 1) * 512)
                ps_ = nextbank()
                for g in (2 * hh, 2 * hh + 1):
                    oc = (g % 2) * 256
                    I("pe", "matmul", [bt_c, xdd], [ps_], ps_[:, oc:oc + 256], lhsT=bt_c[:, g * 128:(g + 1) * 128],
                      rhs=xdd[:, g * 256:(g + 1) * 256], start=True, stop=True, skip_group_check=True)
                I("dve", "tensor_tensor", [ps_, st_f], [st_f], out=st_f[:, hs], in0=ps_[:, :], in1=st_f[:, hs], op=ALU.add)
            I("act", "copy", [st_f], [st_b], out=st_b[:], in_=st_f[:])
            def ssd_tail(c=c, yf=yf, sz_c=sz_c):
                I("pool", "tensor_tensor", [yf, sz_c], [yf], out=yf[:], in0=yf[:], in1=sz_c[:], op=ALU.mult)
                rinv, skey = rms_tile_stats(yf[:], [yf], D)
                I("act", "activation", [yf, skey], [ygb], out=ygb[:], in_=yf[:], func=AF.Copy, scale=rinv)
                transpose_tile_to(ygb, hT, (None, c // 4), c, "ssm_norm", l)
            if pending_tail[0] is not None:
                pending_tail[0]()
            pending_tail[0] = ssd_tail
        pending_tail[0]()
        pending_tail[0] = None

        if stop_idx <= 4:
            raise _Stop()
        wqs_ = nextslot()
        wq4 = wqs_[:, 0:3 * NH * 128].rearrange("p (k h d) -> p k h d", k=3, h=NH)
        I("dve", "memset", [], [wqs_], wqs_[:], 0.0)
        for kc in range(3):
            s4 = P["w_uq"][l][kc * 128:(kc + 1) * 128, :].rearrange("p (h d) -> p h d", h=NH)
            mk.dma("pool", wq4[:, kc, :, 0:80], s4[:, :, 0:80], [P["w_uq"]], [wqs_], ("w", wqs_.name))
            mk.dma("pool", wq4[:, kc, :, 96:112], s4[:, :, 80:96], [P["w_uq"]], [wqs_], ("w", wqs_.name))
        wkvs_ = nextslot()
        wkv = wkvs_[:, 0:2 * NH * 128].rearrange("p (k n) -> p k n", k=2)
        for kc in range(2):
            mk.dma("pool", wkv[:, kc, :], P["w_ukv"][l][kc * 128:(kc + 1) * 128, :], [P["w_ukv"]], [wkvs_],
                   ("w", wkvs_.name))
        I("dve", "memset", [], [ssqa], ssqa[:], 0.0)
        scale = 96.0 ** -0.5
        def mla_proj(h):
            Qh, Kh, V_ = QT[h % 2], KT[h % 2], Vh[h % 2]
            for tb in range(4):
                ts_ = slice(tb * 512, (tb + 1) * 512)
                pa = nextbank()
                for kc in range(3):
                    I("pe", "matmul", [wqs_, (cqnT, (None, tb))], [pa], pa[:, :],
                      lhsT=wq4[:, kc, h, :], rhs=cqnT[:, kc, ts_], start=(kc == 0), stop=(kc == 2))
                I("act", "copy", [pa], [(Qh, (tb, 0))], out=Qh[0:64, ts_], in_=pa[0:64, :])
                rope_rows(pa, Qh, (tb, 1), ts_)
                pk = nextbank()
                for kc in range(2):
                    I("pe", "matmul", [wkvs_, (ckvnT, (None, tb))], [pk], pk[0:64, :],
                      lhsT=wkv[:, kc, h * 128:h * 128 + 64], rhs=ckvnT[:, kc, ts_], start=(kc == 0), stop=(kc == 1))
                I("act", "copy", [pk], [(Kh, (tb, 0))], out=Kh[0:64, ts_], in_=pk[0:64, :])
                I("pool", "tensor_copy", [(kpeT, (tb,))], [(Kh, (tb, 1))], out=Kh[64:128, ts_], in_=kpeT[64:128, ts_])
            I("pool", "memset", [], [V_], V_[:, :, 64:65], 1.0)
            for i in range(NT):
                pv = nextbank()
                for kc in range(2):
                    I("pe", "matmul", [(ckvnT, (None, i // 4)), wkvs_], [pv], pv[:, 0:64],
                      lhsT=ckvnT[:, kc, i * 128:(i + 1) * 128], rhs=wkv[:, kc, h * 128 + 64:h * 128 + 128],
                      start=(kc == 0), stop=(kc == 1))
                I("act", "copy", [pv], [V_], out=V_[:, i, 0:64], in_=pv[:, 0:64])

        def mla_attn(h):
            Qh, Kh, V_ = QT[h % 2], KT[h % 2], Vh[h % 2]
            yp = ypair[(h // 2) % 2]

            def emit_scores(qb, kt, sidx):
                q0 = max(qb * 512, kt * 128)
                q1 = (qb + 1) * 512
                n = q1 - q0
                psc = nextbank()
                I("pe", "matmul", [Kh, Qh], [psc], psc[:, 0:n], lhsT=Kh[:, kt * 128:(kt + 1) * 128],
                  rhs=Qh[:, q0:q1], start=True, stop=True)
                pt = PT[sidx % 5]
                I("act", "activation", [psc], [pt], out=pt[:, 0:n], in_=psc[:, 0:n], func=AF.Exp, scale=scale)
                if kt * 128 >= qb * 512:
                    I("pool", "tensor_tensor", [pt, U_b], [pt], out=pt[:, 0:128], in0=pt[:, 0:128], in1=U_b[:], op=ALU.mult)
                return pt, n, q0

            def emit_pv(qb, kt, pt, n, q0):
                pacc = PB[4 + (h * 4 + qb) % 2]
                nkt = (qb + 1) * 4
                if kt == 0:
                    I("pe", "matmul", [zeros_b], [pacc], pacc[:, :], lhsT=zeros_b[:, 0:128], rhs=zeros_b[:, :],
                      start=True, stop=False)
                for j in range(n // 128):
                    qt = (q0 - qb * 512) // 128 + j
                    I("pe", "matmul", [pt, V_], [pacc], pacc[:, qt * 128:qt * 128 + 65], lhsT=pt[:, j * 128:(j + 1) * 128],
                      rhs=V_[:, kt, 0:65], start=False, stop=False, skip_group_check=True)
                if kt != nkt - 1:
                    return
                I("pe", "matmul", [zeros_b], [pacc], pacc[:, :], lhsT=zeros_b[:, 0:128], rhs=zeros_b[:, :],
                  start=False, stop=True)
                a3 = pacc[:, :].rearrange("p (q e) -> p q e", q=4)
                I("dve", "reciprocal", [pacc], [rsum], out=rsum[:, 0:4], in_=a3[:, :, 64])
                I("dve", "tensor_tensor", [pacc, rsum], [onrm], out=onrm[:], in0=a3[:, :, 0:64],
                  in1=rsum[:, 0:4].unsqueeze(2).broadcast_to([128, 4, HD]), op=ALU.mult)
                I("pool", "tensor_copy", [onrm], [(yp, (qb, h % 2))],
                  out=yp[:, qb * 4:(qb + 1) * 4, (h % 2) * 64:(h % 2) * 64 + 64], in_=onrm[:])
                I("dve", "tensor_tensor", [onrm], [osq], out=osq[:], in0=onrm[:], in1=onrm[:], op=ALU.mult)
                I("dve", "tensor_reduce", [osq], [rsum], out=rsum[:, 4:8], in_=osq[:], axis=mybir.AxisListType.X, op=ALU.add)
                I("dve", "tensor_tensor", [rsum, ssqa], [ssqa], out=ssqa[:, qb * 4:(qb + 1) * 4], in0=ssqa[:, qb * 4:(qb + 1) * 4],
                  in1=rsum[:, 4:8], op=ALU.add)

            steps = [(qb, kt) for qb in range(4) for kt in range((qb + 1) * 4)]
            pend = []
            for sidx, (qb, kt) in enumerate(steps):
                cur = emit_scores(qb, kt, sidx)
                pend.append((qb, kt) + cur)
                if len(pend) > 3:
                    emit_pv(*pend.pop(0))
            while pend:
                emit_pv(*pend.pop(0))
            if h % 2 == 1:
                hp = h // 2
                gcol = vcol("attn_out_norm", l, hp)
                for half in range(2):
                    tb_ = nexttbank()
                    tv = tb_[:].bitcast(BF16)
                    for j in range(8):
                        i = half * 8 + j
                        I("pe", "transpose", [yp, ident_b], [tb_], out=tv[:, j * 128:(j + 1) * 128], in_=yp[:, i, :],
                          identity=ident_b[:])
                    I("dve", "tensor_scalar", [tb_, vecsT], [(yattT, (hp, None))],
                      out=yattT[:, hp, half * 1024:(half + 1) * 1024], in0=tv[:, 0:1024], scalar1=gcol, scalar2=None,
                      op0=ALU.mult)
        mla_proj(0)
        for h in range(NH):
            if h + 1 < NH:
                mla_proj(h + 1)
            mla_attn(h)
        I("act", "activation", [ssqa, cst], [rsta], out=rsta[:], in_=ssqa[:], func=AF.Sqrt, scale=1.0 / D, bias=cst[:, 0:1])
        I("dve", "reciprocal", [rsta], [rsta], out=rsta[:], in_=rsta[:])

        if stop_idx <= 5:
            raise _Stop()
        resid_phase(xsrc, l, "w_out", [(hT, KC), (yattT, KC)], scaled_last=rsta)
        xsrc = xres
        cur_x[0] = xres

        if stop_idx <= 6:
            raise _Stop()
        norm_phase(xsrc, l, "norm_mem_q")
        I("dve", "tensor_tensor", [memT, vecsT], [mnT], out=mnT[:], in0=memT[:],
          in1=vcols("norm_mem_kv", l, 0, KC).unsqueeze(2).broadcast_to([128, KC, MEM]), op=ALU.mult)
        for cg in range(4):
            slot, wv = wload(wsrc(P["w_mk"], l, cg * 256, 256), KC, 256, P["w_mk"])
            for sub in range(2):
                dc = cg * 2 + sub
                pb = nextbank()
                for kc in range(KC):
                    I("pe", "matmul", [slot, mnT], [pb], pb[:, 0:MEM], lhsT=wv[:, kc, sub * 128:(sub + 1) * 128],
                      rhs=mnT[:, kc, :], start=(kc == 0), stop=(kc == KC - 1))
                I("act", "copy", [pb], [(KmT, (dc,))], out=KmT[:, dc, :], in_=pb[:, 0:MEM])
        for cb in range(2):
            slot, wv = wload(wsrc(P["w_mv"], l, cb * 512, 512), KC, 512, P["w_mv"])
            for kt in range(2):
                pb = nextbank()
                for kc in range(KC):
                    I("pe", "matmul", [mnT, slot], [pb], pb[:, :], lhsT=mnT[:, kc, kt * 128:(kt + 1) * 128], rhs=wv[:, kc, :],
                      start=(kc == 0), stop=(kc == KC - 1))
                I("act", "copy", [pb], [(Vm, (kt, cb))], out=Vm[:, kt, cb * 512:(cb + 1) * 512], in_=pb[:, :])
        for cg in range(4):
            slot, wv = wload(wsrc(P["w_mq"], l, cg * 256, 256), KC, 256, P["w_mq"])
            for sub in range(2):
                dc = cg * 2 + sub

                def evac(tb, pb, dc=dc):
                    I("act", "copy", [pb], [(qT, (dc, tb))], out=qT[:, dc, tb * 512:(tb + 1) * 512], in_=pb[:, :])
                proj_fm(wv, sub * 128, 128, KC, hT, slot, evac)
        mscale = 256.0 ** -0.5
        for hd in range(4):
            for tb in range(4):
                ts_ = slice(tb * 512, (tb + 1) * 512)
                ptm = PTm[(hd * 4 + tb) % 2]
                for kt in range(2):
                    psc = nextbank()
                    for d2 in range(2):
                        dc = hd * 2 + d2
                        I("pe", "matmul", [(KmT, (dc,)), (qT, (dc, tb))], [psc], psc[:, :],
                          lhsT=KmT[:, dc, kt * 128:(kt + 1) * 128], rhs=qT[:, dc, ts_], start=(d2 == 0), stop=(d2 == 1))
                    I("act", "activation", [psc], [(ptm, (kt,))], out=ptm[:, kt, :], in_=psc[:, :], func=AF.Exp, scale=mscale)
                pr = nextbank()
                for kt in range(2):
                    I("pe", "matmul", [ones_b, (ptm, (kt,))], [pr], pr[:, :], lhsT=ones_b[:], rhs=ptm[:, kt, :],
                      start=(kt == 0), stop=(kt == 1))
                rb = rinvb[(hd * 4 + tb) % 2]
                I("dve", "reciprocal", [pr], [rb], out=rb[:], in_=pr[:, :])
                for d2 in range(2):
                    dc = hd * 2 + d2
                    po_ = nextbank()
                    for kt in range(2):
                        I("pe", "matmul", [(Vm, (kt, None)), (ptm, (kt,))], [po_], po_[:, :],
                          lhsT=Vm[:, kt, dc * 128:(dc + 1) * 128], rhs=ptm[:, kt, :], start=(kt == 0), stop=(kt == 1))
                    I("dve", "tensor_tensor", [po_, rb], [(oT, (dc, tb))], out=oT[:, dc, ts_], in0=po_[:, :], in1=rb[:],
                      op=ALU.mult)
        resid_phase(xsrc, l, "w_mo", [(oT, KC)])

        if stop_idx <= 7:
            raise _Stop()
        norm_phase(xsrc, l, "norm_ffn")
        for jg in range(NFF // 2):
            slot_g, wg = wload(wsrc(P["w_up"], l, jg * 256, 256), KC, 256, P["w_up"])
            slot_v, wvv = wload(wsrc(P["w_up"], l, D_FF + jg * 256, 256), KC, 256, P["w_up"])
            for sub in range(2):
                j = jg * 2 + sub
                for which, (slot, wv_) in enumerate(((slot_g, wg), (slot_v, wvv))):
                    cidx = which * NFF + j

                    def evac(tb, pb):
                        I("act", "copy", [pb], [(fraw, (tb,))], out=fraw[:, tb * 512:(tb + 1) * 512], in_=pb[:, :])
                    proj_fm(wv_, sub * 128, 128, KC, hT, slot, evac)
                    w2, w1, w0 = (vcol("ffn_conv_w", l, jj * 44 + cidx) for jj in (2, 1, 0))
                    bb = vcol("ffn_conv_b", l, cidx)
                    I("dve", "tensor_scalar", [fraw, vecsT], [facc], out=facc[:], in0=fraw[:], scalar1=w2, scalar2=bb,
                      op0=ALU.mult, op1=ALU.add)
                    for sh, wj in ((1, w1), (2, w0)):
                        I("dve", "scalar_tensor_tensor", [fraw, facc, vecsT], [facc], out=facc[:, sh:S], in0=fraw[:, 0:S - sh],
                          scalar=wj, in1=facc[:, sh:S], op0=ALU.mult, op1=ALU.add)
                    if which == 0:
                        I("act", "activation", [facc], [sgate], out=sgate[:], in_=facc[:], func=AF.Silu)
                    else:
                        I("dve", "tensor_tensor", [facc, sgate], [(aT, (j, None))], out=aT[:, j, :], in0=facc[:], in1=sgate[:],
                          op=ALU.mult)
        resid_phase(xsrc, l, "w_down", [(aT, NFF)])


    cur_x = [x_in]
    try:
        for l in range(depth):
            layer_body(l, cur_x[0])
            cur_x[0] = xres
    except _Stop:
        pass
    xsrc = cur_x[0]

    mk.dma("sp", gfin[:], P["final_norm"][:].unsqueeze(0).broadcast_to([128, D]), [P["final_norm"]], [gfin], "gf")
    mk.dma("sp", xt[0][:], xsrc[0:128, :], [(xsrc, (0,))], [xt[0]], ("xt", 0))
    for i in range(NT):
        t = xt[i % 2]
        if i + 1 < NT:
            mk.dma("sp", xt[(i + 1) % 2][:], xsrc[(i + 1) * 128:(i + 2) * 128, :], [(xsrc, (i + 1,))], [xt[(i + 1) % 2]],
                   ("xt", (i + 1) % 2))
        rinv, skey = rms_tile_stats(t[:], [t], D)
        o_ = ofin[i % 2]
        I("dve", "scalar_tensor_tensor", [t, skey, gfin], [o_], out=o_[:], in0=t[:], scalar=rinv, in1=gfin[:],
          op0=ALU.mult, op1=ALU.mult)
        mk.dma("sp", out_d[i * 128:(i + 1) * 128, :], o_[:], [o_], [(out_d, (i,))], ("of", i % 2))

    mk.emit()
    nc.mk_bufs = {b.name: b.t.name for b in all_bufs}
    return nc


_CACHE = {}


def _consts():
    k = np.arange(128)
    U = (k[:, None] <= k[None, :]).astype(np.float32)
    SL = (k[:, None] > k[None, :]).astype(np.float32)
    rope = np.zeros((128, 2), np.float32)
    inv_freq = (1.0 / (np.float32(10000.0) ** (np.arange(0, ROPE, 2, dtype=np.float32) / np.float32(ROPE)))).astype(np.float32)
    rope[64:80, 0] = inv_freq
    rope[96:112, 0] = inv_freq
    rope[64:80, 1] = 1.0
    rope[96:112, 1] = -1.0
    return {"c_ident": np.eye(128, dtype=np.float32), "c_U": U, "c_SL": SL, "c_rope": rope}


def kernel(**inputs):
    return run_depth(inputs, DEPTH)


def run_depth(inputs, depth, trace=False, ncores=N_CORES, stop_after=None):
    key = (depth, stop_after)
    if key not in _CACHE:
        _CACHE[key] = build_program(depth, stop_after=stop_after)
    nc = _CACHE[key]
    consts = _consts()
    x = np.ascontiguousarray(np.asarray(inputs["x"], dtype=np.float32))
    mem = np.ascontiguousarray(np.asarray(inputs["mem"], dtype=np.float32))
    pos = np.ascontiguousarray(np.asarray(inputs["positions"], dtype=np.int32))
    dd = max(depth, 1)
    shared = {}
    for name, shp in PARAMS:
        a = np.asarray(inputs[name], dtype=np.float32)
        if len(shp) > 1 and dd != DEPTH:
            a = a[:dd]
        shared[name] = np.ascontiguousarray(a)
    shared.update(consts)
    in_maps = []
    for b in range(ncores):
        m = dict(shared)
        m["x"] = x[b]
        m["mem"] = mem[b]
        m["positions"] = pos[b:b + 1]
        in_maps.append(m)
    res = run_bass_kernel_spmd(nc, in_maps, core_ids=list(range(ncores)), trace=trace)
    if trace:
        print("exec_time_ns", res.exec_time_ns)
    out = np.stack([np.asarray(r["out"]) for r in res.results], axis=0).astype(np.float32)
    return out
```
